# Optimizing a Trainium2 kernel written in Bass

```python
import jax, jax.numpy as jnp
from jax import lax
import numpy as np

D_MODEL = 1024
BATCH = 2
SEQ = 8192
DEPTH = 2
DEC_BATCH = 32
DEC_SEQ = 16
PAST_LEN = 1024

CHUNK = 64
LEFT_CHUNKS = 8
ATTN_REACH = LEFT_CHUNKS * CHUNK
N_HEADS_A = 8
HEAD_DIM_A = 64
D_A = N_HEADS_A * HEAD_DIM_A
REL_CLIP = 128
N_REL = 2 * REL_CLIP + 1
MLP_CHUNK = 128
N_GROUPS_B = 4
D_B = 512
GROUP_DIM_B = D_B // N_GROUPS_B
D_FF = ((8 * D_MODEL // 3 + 255) // 256) * 256
CONV_W = 3
N_BRANCH = 2
D_IN = 3 * D_A + 2 * D_B + N_BRANCH * D_MODEL
EPS = 1e-6

kernel_name = 'streaming_band_attn_spatial_gating_convffn'


def rms_norm(x, g):
    xf = x.astype(jnp.float32)
    y = xf * lax.rsqrt(jnp.mean(xf * xf, axis=-1, keepdims=True) + EPS)
    return (y * g.astype(jnp.float32)).astype(x.dtype)


def ada_params(c, w_ada, b_ada):
    mod = jax.nn.silu(c) @ w_ada + b_ada
    return jnp.split(mod[:, None, :], 6, axis=-1)


def modulate(h, shift, scale):
    return h * (1 + scale) + shift


def mixer_inputs(h, w_in, q_norm_g, k_norm_g, v_norm_g):
    B, T, _ = h.shape
    z = h @ w_in
    q, k, v, zb, gates = jnp.split(z, [D_A, 2 * D_A, 3 * D_A, 3 * D_A + 2 * D_B], axis=-1)
    q = rms_norm(q.reshape(B, T, N_HEADS_A, HEAD_DIM_A), q_norm_g)
    k = rms_norm(k.reshape(B, T, N_HEADS_A, HEAD_DIM_A), k_norm_g)
    v = v.reshape(B, T, N_HEADS_A, HEAD_DIM_A)
    ub, vb = jnp.split(jax.nn.gelu(zb), 2, axis=-1)
    vb = rms_norm(vb, v_norm_g).reshape(B, T, N_GROUPS_B, GROUP_DIM_B)
    g_a, g_b = jnp.split(jax.nn.sigmoid(gates), 2, axis=-1)
    return q, k, v, ub, vb, g_a, g_b


def band_attention_prompt(q, k, v, rel_bias):
    B, S, H, dh = q.shape
    nC = S // CHUNK
    band = (LEFT_CHUNKS + 1) * CHUNK
    qc = q.reshape(B, nC, CHUNK, H, dh)
    pad = jnp.zeros((B, ATTN_REACH, H, dh), k.dtype)
    kp = jnp.concatenate([pad, k], axis=1).reshape(B, nC + LEFT_CHUNKS, CHUNK, H, dh)
    vp = jnp.concatenate([pad.astype(v.dtype), v], axis=1).reshape(B, nC + LEFT_CHUNKS, CHUNK, H, dh)
    kb = jnp.concatenate([kp[:, j:j + nC] for j in range(LEFT_CHUNKS + 1)], axis=2)
    vb = jnp.concatenate([vp[:, j:j + nC] for j in range(LEFT_CHUNKS + 1)], axis=2)
    s = jnp.einsum('bcqhd,bckhd->bchqk', qc, kb, preferred_element_type=jnp.float32) * (dh ** -0.5)
    i = jnp.arange(CHUNK)[:, None]
    kk = jnp.arange(band)[None, :]
    idx = jnp.clip(ATTN_REACH + i - kk, -REL_CLIP, REL_CLIP) + REL_CLIP
    s = s + rel_bias.astype(jnp.float32)[:, idx][None, None]
    key_chunk = jnp.arange(nC)[:, None] - LEFT_CHUNKS + jnp.arange(LEFT_CHUNKS + 1)[None, :]
    valid = jnp.repeat(key_chunk >= 0, CHUNK, axis=1)
    s = jnp.where(valid[None, :, None, None, :], s, -jnp.inf)
    p = jax.nn.softmax(s, axis=-1)
    o = jnp.einsum('bchqk,bckhd->bcqhd', p.astype(v.dtype), vb)
    return o.reshape(B, S, H * dh)


def band_attention_sample(q, k, v, k_cache, v_cache, rel_bias):
    B, T, H, dh = q.shape
    L = k_cache.shape[1]
    kk_all = jnp.concatenate([k_cache.astype(k.dtype), k], axis=1)
    vv_all = jnp.concatenate([v_cache.astype(v.dtype), v], axis=1)
    s = jnp.einsum('bqhd,bkhd->bhqk', q, kk_all, preferred_element_type=jnp.float32) * (dh ** -0.5)
    i = jnp.arange(T)[:, None]
    kk = jnp.arange(L + T)[None, :]
    idx = jnp.clip(L + i - kk, -REL_CLIP, REL_CLIP) + REL_CLIP
    s = s + rel_bias.astype(jnp.float32)[:, idx][None]
    p = jax.nn.softmax(s, axis=-1)
    o = jnp.einsum('bhqk,bkhd->bqhd', p.astype(v.dtype), vv_all)
    return o.reshape(B, T, H * dh)


def causal_spatial(w_spatial):
    return w_spatial * jnp.tril(jnp.ones((MLP_CHUNK, MLP_CHUNK), w_spatial.dtype))


def spatial_gating_prompt(ub, vb, w_spatial, b_spatial):
    B, S, _ = ub.shape
    vc = vb.reshape(B, S // MLP_CHUNK, MLP_CHUNK, N_GROUPS_B, GROUP_DIM_B)
    mix = jnp.einsum('gts,bcsgd->bctgd', causal_spatial(w_spatial), vc)
    mix = mix + b_spatial.T[None, None, :, :, None]
    return ub * mix.reshape(B, S, D_B)


def spatial_gating_sample(ub, vb, w_spatial, b_spatial):
    B, T, _ = ub.shape
    ws = causal_spatial(w_spatial)[:, :T, :T]
    mix = jnp.einsum('gts,bsgd->btgd', ws, vb) + b_spatial[:, :T].T[None, :, :, None]
    return ub * mix.reshape(B, T, D_B)


def conv_ffn(h, conv_cache, w_ffn_in, ffn_conv_w, ffn_conv_b, w_ffn_out):
    B, T, _ = h.shape
    g, u = jnp.split(h @ w_ffn_in, 2, axis=-1)
    if conv_cache is None:
        conv_cache = jnp.zeros((B, CONV_W - 1, D_FF), g.dtype)
    gp = jnp.concatenate([conv_cache.astype(g.dtype), g], axis=1)
    gc = ffn_conv_b
    for j in range(CONV_W):
        gc = gc + ffn_conv_w[j] * gp[:, j:j + T]
    out = (jax.nn.gelu(gc) * u) @ w_ffn_out
    return out, gp[:, -(CONV_W - 1):]


def run_layer(x, c, k_cache, v_cache, conv_cache, norm1_g, norm2_g, w_ada, b_ada, w_in,
              q_norm_g, k_norm_g, rel_bias, v_norm_g, w_spatial, b_spatial,
              w_out_a, w_out_b, w_out, w_ffn_in, ffn_conv_w, ffn_conv_b, w_ffn_out):
    sh1, sc1, gt1, sh2, sc2, gt2 = ada_params(c, w_ada, b_ada)
    h = modulate(rms_norm(x, norm1_g), sh1, sc1)
    q, k, v, ub, vb, g_a, g_b = mixer_inputs(h, w_in, q_norm_g, k_norm_g, v_norm_g)
    if k_cache is None:
        o_a = band_attention_prompt(q, k, v, rel_bias)
        o_b = spatial_gating_prompt(ub, vb, w_spatial, b_spatial)
        keep = min(ATTN_REACH, x.shape[1])
        k_state, v_state, vb_state = k[:, -keep:], v[:, -keep:], None
    else:
        o_a = band_attention_sample(q, k, v, k_cache, v_cache, rel_bias)
        o_b = spatial_gating_sample(ub, vb, w_spatial, b_spatial)
        k_state, v_state, vb_state = k, v, vb
    mixed = (g_a * (o_a @ w_out_a) + g_b * (o_b @ w_out_b)) @ w_out
    x = x + gt1 * mixed
    h2 = modulate(rms_norm(x, norm2_g), sh2, sc2)
    f, conv_state = conv_ffn(h2, conv_cache, w_ffn_in, ffn_conv_w, ffn_conv_b, w_ffn_out)
    x = x + gt2 * f
    return x, k_state, v_state, vb_state, conv_state


def setup_inputs(seed: int = 0) -> dict:
    key = jax.random.key(seed)
    ks = jax.random.split(key, 32)
    f32 = jnp.float32

    def nrm(k, shape, scale):
        return jax.random.normal(k, shape, f32) * scale

    cache_len = min(ATTN_REACH, PAST_LEN)
    return {
        'x_prompt': nrm(ks[0], (BATCH, SEQ, D_MODEL), 1.0),
        'x_sample': nrm(ks[1], (DEC_BATCH, DEC_SEQ, D_MODEL), 1.0),
        'cache_attn_k': nrm(ks[2], (DEPTH, DEC_BATCH, cache_len, N_HEADS_A, HEAD_DIM_A), 1.0),
        'cache_attn_v': nrm(ks[3], (DEPTH, DEC_BATCH, cache_len, N_HEADS_A, HEAD_DIM_A), 1.0),
        'cache_ffn_conv': nrm(ks[4], (DEPTH, DEC_BATCH, CONV_W - 1, D_FF), 1.0),
        'c_prompt': nrm(ks[5], (BATCH, D_MODEL), 1.0),
        'c_sample': nrm(ks[6], (DEC_BATCH, D_MODEL), 1.0),
        'norm1_g': 1.0 + nrm(ks[7], (DEPTH, D_MODEL), 0.02),
        'norm2_g': 1.0 + nrm(ks[8], (DEPTH, D_MODEL), 0.02),
        'w_ada': nrm(ks[9], (DEPTH, D_MODEL, 6 * D_MODEL), 0.5 * D_MODEL ** -0.5),
        'b_ada': nrm(ks[10], (DEPTH, 6 * D_MODEL), 0.02),
        'w_in': nrm(ks[11], (DEPTH, D_MODEL, D_IN), D_MODEL ** -0.5),
        'q_norm_g': 1.0 + nrm(ks[12], (DEPTH, HEAD_DIM_A), 0.02),
        'k_norm_g': 1.0 + nrm(ks[13], (DEPTH, HEAD_DIM_A), 0.02),
        'rel_bias': nrm(ks[14], (DEPTH, N_HEADS_A, N_REL), 0.1),
        'v_norm_g': 1.0 + nrm(ks[15], (DEPTH, D_B), 0.02),
        'w_spatial': nrm(ks[16], (DEPTH, N_GROUPS_B, MLP_CHUNK, MLP_CHUNK), MLP_CHUNK ** -0.5),
        'b_spatial': 1.0 + nrm(ks[17], (DEPTH, N_GROUPS_B, MLP_CHUNK), 0.02),
        'w_out_a': nrm(ks[18], (DEPTH, D_A, D_MODEL), D_A ** -0.5),
        'w_out_b': nrm(ks[19], (DEPTH, D_B, D_MODEL), D_B ** -0.5),
        'w_out': nrm(ks[20], (DEPTH, D_MODEL, D_MODEL), D_MODEL ** -0.5),
        'w_ffn_in': nrm(ks[21], (DEPTH, D_MODEL, 2 * D_FF), D_MODEL ** -0.5),
        'ffn_conv_w': nrm(ks[22], (DEPTH, CONV_W, D_FF), CONV_W ** -0.5),
        'ffn_conv_b': nrm(ks[23], (DEPTH, D_FF), 0.02),
        'w_ffn_out': nrm(ks[24], (DEPTH, D_FF, D_MODEL), D_FF ** -0.5),
    }


def reference(x_prompt, x_sample, cache_attn_k, cache_attn_v, cache_ffn_conv, c_prompt, c_sample,
              norm1_g, norm2_g, w_ada, b_ada, w_in, q_norm_g, k_norm_g, rel_bias, v_norm_g,
              w_spatial, b_spatial, w_out_a, w_out_b, w_out, w_ffn_in, ffn_conv_w, ffn_conv_b,
              w_ffn_out):
    xp, xs = x_prompt, x_sample
    kp_l, vp_l, cp_l, ks_l, vs_l, bs_l, cs_l = [], [], [], [], [], [], []
    for l in range(DEPTH):
        lw = (norm1_g[l], norm2_g[l], w_ada[l], b_ada[l], w_in[l], q_norm_g[l], k_norm_g[l],
              rel_bias[l], v_norm_g[l], w_spatial[l], b_spatial[l], w_out_a[l], w_out_b[l],
              w_out[l], w_ffn_in[l], ffn_conv_w[l], ffn_conv_b[l], w_ffn_out[l])
        xp, kp, vp, _, cp = run_layer(xp, c_prompt, None, None, None, *lw)
        xs, ks, vs, bs, cs = run_layer(xs, c_sample, cache_attn_k[l], cache_attn_v[l],
                                       cache_ffn_conv[l], *lw)
        kp_l.append(kp); vp_l.append(vp); cp_l.append(cp)
        ks_l.append(ks); vs_l.append(vs); bs_l.append(bs); cs_l.append(cs)
    new_k_prompt = jnp.stack(kp_l)
    new_v_prompt = jnp.stack(vp_l)
    new_conv_prompt = jnp.stack(cp_l)
    new_k_sample = jnp.stack(ks_l)
    new_v_sample = jnp.stack(vs_l)
    new_spatial_v_sample = jnp.stack(bs_l)
    new_conv_sample = jnp.stack(cs_l)
    return (xp, xs, new_k_prompt, new_v_prompt, new_conv_prompt, new_k_sample, new_v_sample, new_spatial_v_sample, new_conv_sample)
```

```python
import contextlib
import numpy as np
import concourse.bass as bass
import concourse.mybir as mybir
from concourse.bass_utils import run_bass_kernel_spmd

F32 = mybir.dt.float32
BF16 = mybir.dt.bfloat16
AF = mybir.ActivationFunctionType
ALU = mybir.AluOpType
AX = mybir.AxisListType

NCORES = 8
D = 1024
SEG = 2048
HALO = 1152
W = SEG + HALO
NT = W // 128
DFF = 2816
EPS = 1e-6
NB = 4
L0_BLOCKS = [(4, 4), (8, 4), (12, 4), (16, 3), (19, 3), (22, 3)]
L1_BLOCKS = [(8, 4), (12, 4), (16, 3), (19, 3), (22, 3)]
OUT_T0 = 9
KEEP_T0 = 21
_DBG = {}


class Prog:
    STREAMS = ("sp", "act", "dve", "pool", "pe")

    def __init__(self):
        self.ops = []
        self.keyw = {}
        self.keyr = {}
        self.dcnt = {}

    def op(self, stream, method, *args, r=(), w=(), semkey=None, **kw):
        idx = len(self.ops)
        dom = ("d", semkey) if semkey is not None else ("s", stream)
        deps = {}

        def add(d, i):
            if d[0] == "d":
                i = self.dcnt[d]
            if deps.get(d, -1) < i:
                deps[d] = i

        for k in r:
            for d, i in self.keyw.get(k, {}).items():
                add(d, i)
        skip_same = dom[0] == "d" or stream == "pe"
        for k in w:
            for d, i in self.keyr.get(k, {}).items():
                if d == dom and (skip_same or i == idx):
                    continue
                add(d, i)
            for d, i in self.keyw.get(k, {}).items():
                if d == dom and skip_same:
                    continue
                add(d, i)
        for k in r:
            self.keyr.setdefault(k, {})[dom] = idx
        for k in w:
            if self.keyr.get(k):
                self.keyw[k] = {dom: idx}
                self.keyr[k] = {}
            else:
                self.keyw.setdefault(k, {})[dom] = idx
        if dom[0] == "d":
            self.dcnt[dom] = self.dcnt.get(dom, 0) + 16
        self.ops.append(dict(stream=stream, method=method, args=args, kw=kw, dom=dom,
                             deps=list(deps.items())))
        return idx

    def emit(self, nc, stack):
        ops = self.ops
        needs = set()
        for o in ops:
            for d, i in o["deps"]:
                if d[0] == "s":
                    needs.add(i)
        cnt = {}
        sems = {}
        issuer = {}
        for i, o in enumerate(ops):
            d = o["dom"]
            if d[0] == "d":
                cnt[d] = cnt.get(d, 0) + 16
                o["done"] = cnt[d]
                assert issuer.setdefault(d, o["stream"]) == o["stream"], d
            elif i in needs:
                cnt[d] = cnt.get(d, 0) + 1
                o["done"] = cnt[d]
            else:
                o["done"] = None
        for d in cnt:
            sems[d] = stack.enter_context(nc.semaphore("s%d" % len(sems)))
        block = stack.enter_context(nc.Block())
        self.nsem = len(sems)

        def run(stream, eng):
            waited = {}
            for o in ops:
                if o["stream"] != stream:
                    continue
                for d, i in o["deps"]:
                    v = i if d[0] == "d" else ops[i]["done"]
                    if waited.get(d, 0) < v:
                        eng.wait_ge(sems[d], v)
                        waited[d] = v
                ins = getattr(eng, o["method"])(*o["args"], **o["kw"])
                if o["done"] is not None:
                    d = o["dom"]
                    ins.then_inc(sems[d], 16 if d[0] == "d" else 1)
            for d, v in cnt.items():
                if d[0] == "d" and issuer[d] == stream and waited.get(d, 0) < v:
                    eng.wait_ge(sems[d], v)

        @block.sync
        def _(e):
            run("sp", e)

        @block.scalar
        def _(e):
            run("act", e)

        @block.vector
        def _(e):
            run("dve", e)

        @block.gpsimd
        def _(e):
            run("pool", e)

        @block.tensor
        def _(e):
            run("pe", e)


def build_nc(stop=None):
    class _Stop(Exception):
        pass

    nblk = [0]

    def ck(st):
        if stop is not None and nblk[0] + st / 10.0 >= stop - 1e-9 and stop > 0:
            raise _Stop()

    nc = bass.Bass("TRN2", target_bir_lowering=False)
    P = Prog()
    stack = contextlib.ExitStack()

    def din(name, shape):
        return nc.dram_tensor(name, list(shape), F32, kind="ExternalInput").ap()

    def dout(name, shape):
        return nc.dram_tensor(name, list(shape), F32, kind="ExternalOutput").ap()

    def dint(name, shape, dt):
        return nc.dram_tensor(name, list(shape), dt, kind="Internal").ap()

    xin = din("xin", (128, 8, W))
    xs_d = din("xs", (128, 8, 64))
    vtok_d = din("vtok", (128, NT))
    flag_d = din("flag", (128, 1))
    cT_d = din("cT", (128, 8, 5))
    ckT_d = din("ckT", (2, 128, 4, 4, 512))
    cv_d = din("cv", (2, 4, 128, 4, 512))
    cconv_d = din("cconv", (2, 128, 22, 4, 2))
    wall_d = din("wall", (2, 24, 128, 4096))
    wfo_d = din("wfo", (2, 8, 128, 2816))
    wada_d = din("wada", (2, 12, 128, 4096))
    nrm_d = din("nrm", (2, 128, 16))
    bada_d = din("bada", (2, 128, 48))
    qkg_d = din("qkg", (2, 128, 2))
    vg_d = din("vg", (2, 128, 512))
    bsp_d = din("bsp", (2, 1, 512))
    bsps_d = din("bsps", (2, 1, 256))
    wspT_d = din("wspT", (2, 128, 4, 128))
    wspb_d = din("wspb", (2, 64, 4, 64))
    tril_d = din("tril", (128, 4, 128))
    trilb_d = din("trilb", (64, 4, 64))
    cw_d = din("cw", (2, 128, 22, 3))
    cb_d = din("cb", (2, 128, 22))
    Tb_d = din("Tb", (2, 128, 2, 8, 128))
    b256_d = din("b256", (2, 128, 8))

    yT_d = dout("yT", (128, 8, SEG))
    ysT_d = dout("ysT", (128, 8, 64))
    nkT_d = dout("nkT", (2, 128, 4, 512))
    nv_d = dout("nv", (2, 4, 128, 512))
    ncv_d = dout("ncv", (2, 128, 22, 2))
    nksT_d = dout("nksT", (2, 128, 4, 64))
    nvs_d = dout("nvs", (2, 16, 4, 512))
    nbs_d = dout("nbs", (2, 64, 512))
    ncs_d = dout("ncs", (2, 128, 22, 4, 2))

    wsc = dint("wsc", (2, 24, 128, 4096), BF16)
    wsc_fo = dint("wscfo", (2, 8, 128, 2816), BF16)
    wsc_ada = dint("wscada", (2, 12, 128, 4096), BF16)
    x1s = dint("x1s", (128, 8, 21 * 128), F32)

    def sb(name, shape, dt=F32):
        return stack.enter_context(nc.sbuf_tensor("sb_" + name, list(shape), dt))

    xT = sb("xT", (128, 8, 512))
    hT = sb("hT", (128, 8, 512), BF16)
    sq = sb("sq", (128, 3, 512), BF16)
    rstd = sb("rstd", (128, 512))
    tmpn = sb("tmpn", (128, 2, 512))
    qT = sb("qT", (128, 4, 512), BF16)
    kst = sb("kst", (128, 2, 512))
    rs = sb("rs", (128, 2, 512))
    kT = sb("kT", (128, 4, 1024), BF16)
    vt = sb("vt", (128, 8, 512), BF16)
    vst = sb("vst", (128, 2, 512))
    ubT = sb("ubT", (128, 4, 512), BF16)
    vbn = sb("vbn", (128, 4, 512), BF16)
    gl = sb("gl", (128, 2, 512))
    ssv = sb("ssv", (128, 4))
    SA = sb("SA", (128, 24, 512), BF16)
    oaT = sb("oaT", (128, 4, 512), BF16)
    obT = sb("obT", (128, 4, 512), BF16)
    Pt = sb("Pt", (128, 2, 2, 5, 128), BF16)
    ex = sb("ex", (128, 2, 2, 256))
    rden = sb("rden", (128, 2, 256))
    gb = sb("gb", (128, 3, 516))
    gc = sb("gc", (128, 2, 512))
    ge = sb("ge", (128, 2, 512))
    halo = sb("halo", (128, 22, 2))
    halos = sb("halos", (128, 22, 4, 2))
    ncs_st = sb("ncs_st", (128, 22, 4, 2))
    wslab = sb("wslab", (128, NB, 4096), BF16)
    ones_bf = sb("ones_bf", (128, 128), BF16)
    blk64 = sb("blk64", (128, 128), BF16)
    ones_row = sb("ones_row", (1, 128), BF16)
    vones = sb("vones", (128, 9, 128), BF16)
    vtok = sb("vtok", (128, NT))
    flag = sb("flag", (128, 1))
    cTs = sb("cTs", (128, 8, 5))
    cs = sb("cs", (128, 8, 5), BF16)
    E = sb("E", (128, 2, 8, 128))
    negb = sb("negb", (128, 8))
    modT = sb("modT", (128, 48, 5))
    A12 = sb("A12", (128, 2, 8, 5))
    nrm = sb("nrm", (128, 16))
    bada = sb("bada", (128, 48))
    qkg = sb("qkg", (128, 2))
    vg = sb("vg", (128, 512))
    bsp = sb("bsp", (1, 512), BF16)
    bsp_f = sb("bsp_f", (1, 768))
    bsps = sb("bsps", (1, 256), BF16)
    WcT = sb("WcT", (128, 4, 128), BF16)
    Wblk = sb("Wblk", (64, 4, 64), BF16)
    cw = sb("cw", (128, 22, 3))
    cb = sb("cb", (128, 22))
    kTs = sb("kTs", (128, 4, 64), BF16)
    vts = sb("vts", (16, 4, 512), BF16)
    vbns = sb("vbns", (64, 512), BF16)
    ckb = sb("ckb", (128, 1, 4, 512), BF16)
    cvb = sb("cvb", (128, 1, 4, 512), BF16)
    Pts = sb("Pts", (128, 2, 2, 80), BF16)
    exs = sb("exs", (128, 2, 2, 32))
    rdens = sb("rdens", (128, 2, 32))

    ps = [stack.enter_context(nc.psum_tensor("ps%d" % i, [128, 512], F32)) for i in range(8)]

    def bk(i):
        return [("ps", i)]

    bank_ctr = [0]

    def nb():
        b = bank_ctr[0] % 8
        bank_ctr[0] += 1
        return b

    rot = {}

    def nrot(name, n):
        v = rot.get(name, 0)
        rot[name] = v + 1
        return v % n

    ws = dict(seq=[], issued=0, consumed=0)

    def ws_issue(upto):
        while ws["issued"] < min(upto, len(ws["seq"])):
            src, ncols, cast, rkeys = ws["seq"][ws["issued"]]
            slot = ws["issued"] % NB
            P.op("pool" if cast else "sp", "dma_start", out=wslab[:, slot, 0:ncols], in_=src,
                 r=rkeys, w=[("wslab", slot)], semkey=("w", slot))
            ws["issued"] += 1

    def ws_get():
        assert ws["consumed"] < len(ws["seq"])
        ws_issue(ws["consumed"] + 1)
        slot = ws["consumed"] % NB
        ws["consumed"] += 1
        return slot

    def ws_done():
        ws_issue(ws["consumed"] + NB)

    def seq_layer_setup(l):
        for s in range(12):
            ws["seq"].append((wsc_ada[l, s], 4096, False, [("wscada", l, s)]))

    def seq_block(l, kind):
        if kind == "kv":
            for s in (1, 2):
                ws["seq"].append((wsc[l, s], 4096, False, [("wsc", l, s)]))
            return
        for s in range(24):
            ws["seq"].append((wsc[l, s], 4096, False, [("wsc", l, s)]))
        for s in range(8):
            ws["seq"].append((wsc_fo[l, s], 2816, False, [("wscfo", l, s)]))

    for l in range(2):
        seq_layer_setup(l)
        seq_block(l, "kv")
        for _ in (L0_BLOCKS if l == 0 else L1_BLOCKS):
            seq_block(l, "full")
        seq_block(l, "sample")

    P.op("pool", "memset", ones_bf[:, :], 1.0, w=["ones_bf"])
    P.op("pool", "memset", blk64[:, :], 0.0, w=["blk64"])
    P.op("pool", "memset", blk64[0:64, 0:64], 1.0, w=["blk64"])
    P.op("pool", "memset", blk64[64:128, 64:128], 1.0, w=["blk64"])
    P.op("pool", "memset", ones_row[:, :], 1.0, w=["ones_row"])
    P.op("pool", "memset", Pt[:, :, :, :, :].rearrange("p a b c d -> p (a b c d)"), 0.0, w=[("Pt", s, h) for s in range(2) for h in range(2)])
    P.op("pool", "memset", halo[:, :, :], 0.0, w=["halo"])
    P.op("sp", "dma_start", out=vtok[:, :], in_=vtok_d, w=["vtok"], semkey="c0")
    P.op("sp", "dma_start", out=flag[:, :], in_=flag_d, w=["flag"], semkey="c0")
    P.op("sp", "dma_start", out=cTs[:, :, :], in_=cT_d, w=["cTs"], semkey="c0")
    P.op("act", "activation", out=cs[:, :, :], in_=cTs[:, :, :], func=AF.Silu, r=["cTs"], w=["cs"])
    for t in range(9):
        P.op("act", "activation", out=vones[:, t, :], in_=ones_bf[:, :], func=AF.Copy,
             scale=vtok[:, t:t + 1], r=["ones_bf", "vtok"], w=["vones"])
    PC_GROUPS = [(0, 5), (5, 9), (9, 13), (13, 19), (19, 24)]
    for l in range(2):
        for g0 in (0, 6):
            P.op("pool", "dma_start", out=wsc_ada[l, g0:g0 + 6], in_=wada_d[l, g0:g0 + 6],
                 w=[("wscada", l, s) for s in range(g0, g0 + 6)], semkey=("pca", l, g0))
        for (g0, g1) in PC_GROUPS:
            P.op("pool", "dma_start", out=wsc[l, g0:g1], in_=wall_d[l, g0:g1],
                 w=[("wsc", l, s) for s in range(g0, g1)], semkey=("pc", l, g0))
        P.op("pool", "dma_start", out=wsc_fo[l], in_=wfo_d[l], w=[("wscfo", l, s) for s in range(8)],
             semkey=("pcf", l))

    def layer_setup(l):
        for dst, src, key in ((nrm, nrm_d[l], "nrm"), (bada, bada_d[l], "bada"), (qkg, qkg_d[l], "qkg"),
                              (vg, vg_d[l], "vg"), (cw, cw_d[l], "cw"), (cb, cb_d[l], "cb"),
                              (negb, b256_d[l], "negb")):
            full = tuple(slice(None) for _ in dst.shape)
            P.op("sp", "dma_start", out=dst[full], in_=src, w=[key], semkey="ls")
        P.op("sp", "dma_start", out=E[:, :, :, :], in_=Tb_d[l], w=["E"], semkey="ls")
        P.op("sp", "dma_start", out=halos[:, :, :, :], in_=cconv_d[l], w=["halos"], semkey="ls")
        P.op("sp", "dma_start", out=bsp_f[:, 0:512], in_=bsp_d[l], w=["bsp_f"], semkey="ls")
        P.op("sp", "dma_start", out=bsp_f[:, 512:768], in_=bsps_d[l], w=["bsp_f"], semkey="ls")
        P.op("sp", "dma_start", out=gc[:, 0, :], in_=wspT_d[l].rearrange("p g t -> p (g t)"),
             w=[("gc", 0)], semkey="ls3")
        P.op("sp", "dma_start", out=gc[:, 1, :], in_=tril_d.rearrange("p g t -> p (g t)"),
             w=[("gc", 1)], semkey="ls3")
        P.op("sp", "dma_start", out=ge[0:64, 0, 0:256], in_=wspb_d[l].rearrange("p g t -> p (g t)"),
             w=[("ge", 0)], semkey="ls3")
        P.op("sp", "dma_start", out=ge[0:64, 1, 0:256], in_=trilb_d.rearrange("p g t -> p (g t)"),
             w=[("ge", 1)], semkey="ls3")
        P.op("dve", "tensor_tensor", out=WcT[:, :, :].rearrange("p g t -> p (g t)"), in0=gc[:, 0, :],
             in1=gc[:, 1, :], op=ALU.mult, r=[("gc", 0), ("gc", 1)], w=["WcT"])
        P.op("dve", "tensor_tensor", out=Wblk[:, :, :].rearrange("p g t -> p (g t)"), in0=ge[0:64, 0, 0:256],
             in1=ge[0:64, 1, 0:256], op=ALU.mult, r=[("ge", 0), ("ge", 1)], w=["Wblk"])
        P.op("dve", "tensor_copy", bsp[:, :], bsp_f[:, 0:512], r=["bsp_f"], w=["bsp"])
        P.op("dve", "tensor_copy", bsps[:, :], bsp_f[:, 512:768], r=["bsp_f"], w=["bsps"])
        P.op("dve", "tensor_scalar_mul", qkg[:, 1:2], qkg[:, 1:2], 8.0, r=["qkg"], w=["qkg"])
        P.op("dve", "tensor_scalar_mul", negb[:, :], negb[:, :], -1.0, r=["negb"], w=["negb"])
        for t in range(2):
            for h in range(8):
                P.op("act", "activation", out=E[:, t, h, :], in_=E[:, t, h, :], func=AF.Exp,
                     bias=negb[:, h:h + 1], scale=1.0, r=["E", "negb"], w=["E"])
        P.op("pool", "memset", E[64:128, 1, :, 0:64], 0.0, r=["E"], w=["E"])
        b = nb()
        for s in range(12):
            slot = ws_get()
            for m in range(4):
                ci = s * 4 + m
                for kc in range(8):
                    P.op("pe", "matmul", ps[b][:, ci * 5:(ci + 1) * 5],
                         lhsT=wslab[:, slot, kc * 512 + m * 128: kc * 512 + (m + 1) * 128],
                         rhs=cs[:, kc, :], start=(kc == 0), stop=(kc == 7),
                         r=[("wslab", slot), "cs"], w=bk(b))
            ws_done()
        for bb in range(5):
            P.op("dve", "tensor_tensor", out=modT[:, :, bb],
                 in0=ps[b][:, 0:240].rearrange("p (c b) -> p c b", b=5)[:, :, bb], in1=bada[:, :],
                 op=ALU.add, r=bk(b) + ["bada"], w=["modT"])
        for which in range(2):
            sc0 = 8 if which == 0 else 32
            for bb in range(5):
                P.op("dve", "tensor_scalar", out=A12[:, which, :, bb], in0=modT[:, sc0:sc0 + 8, bb],
                     scalar1=1.0, scalar2=32.0, op0=ALU.add, op1=ALU.mult, r=["modT"], w=["A12"])
                P.op("dve", "tensor_tensor", out=A12[:, which, :, bb], in0=A12[:, which, :, bb],
                     in1=nrm[:, which * 8:(which + 1) * 8], op=ALU.mult, r=["A12", "nrm"], w=["A12"])

    def norm_mod(which, T, groups):
        b = nb()
        for c in range(8):
            r_ = nrot("sq", 3)
            P.op("act", "activation", out=sq[:, r_, 0:T], in_=xT[:, c, 0:T], func=AF.Square,
                 r=[("xT", c)], w=[("sq", r_)])
            P.op("pe", "matmul", ps[b][:, 0:T], lhsT=ones_bf[:, :], rhs=sq[:, r_, 0:T],
                 start=(c == 0), stop=(c == 7), r=[("sq", r_), "ones_bf"], w=bk(b))
        P.op("act", "activation", out=rstd[:, 0:T], in_=ps[b][:, 0:T], func=AF.Sqrt, bias=float(D * EPS),
             scale=1.0, r=bk(b), w=["rstd"])
        P.op("dve", "reciprocal", out=rstd[:, 0:T], in_=rstd[:, 0:T], r=["rstd"], w=["rstd"])
        shc = 0 if which == 0 else 24
        for c in range(8):
            r_ = nrot("tmpn", 2)
            for (c0, n, bb) in groups:
                P.op("dve", "scalar_tensor_tensor", out=tmpn[:, r_, c0:c0 + n], in0=xT[:, c, c0:c0 + n],
                     scalar=A12[:, which, c, bb:bb + 1], in1=rstd[:, c0:c0 + n], op0=ALU.mult, op1=ALU.mult,
                     r=[("xT", c), "A12", "rstd"], w=[("tmpn", r_)])
            for (c0, n, bb) in groups:
                P.op("act", "activation", out=hT[:, c, c0:c0 + n], in_=tmpn[:, r_, c0:c0 + n],
                     func=AF.Identity, bias=modT[:, shc + c, bb:bb + 1], scale=1.0,
                     r=[("tmpn", r_), "modT"], w=[("hT", c)])

    def proj_fm(b, slot, nkc, ms, col0, rhs_t, rkeys, T):
        for kc in range(nkc):
            P.op("pe", "matmul", ps[b][:, 0:T], lhsT=wslab[:, slot, kc * ms + col0: kc * ms + col0 + 128],
                 rhs=rhs_t(kc), start=(kc == 0), stop=(kc == nkc - 1),
                 r=[("wslab", slot)] + rkeys(kc), w=bk(b))

    def headnorm(b, T, gcol, out_ap, out_keys):
        r_ = nrot("sq", 3)
        P.op("act", "activation", out=sq[:, r_, 0:T], in_=ps[b][:, 0:T], func=AF.Square, r=bk(b), w=[("sq", r_)])
        b2 = nb()
        P.op("pe", "matmul", ps[b2][:, 0:T], lhsT=blk64[:, :], rhs=sq[:, r_, 0:T], start=True, stop=True,
             r=[("sq", r_), "blk64"], w=bk(b2))
        r2 = nrot("rs", 2)
        P.op("act", "activation", out=rs[:, r2, 0:T], in_=ps[b2][:, 0:T], func=AF.Sqrt, bias=float(64 * EPS),
             scale=1.0, r=bk(b2), w=[("rs", r2)])
        P.op("dve", "reciprocal", out=rs[:, r2, 0:T], in_=rs[:, r2, 0:T], r=[("rs", r2)], w=[("rs", r2)])
        P.op("dve", "scalar_tensor_tensor", out=out_ap, in0=ps[b][:, 0:T], scalar=qkg[:, gcol:gcol + 1],
             in1=rs[:, r2, 0:T], op0=ALU.mult, op1=ALU.mult, r=bk(b) + [("rs", r2), "qkg"], w=out_keys)

    def hT_r(T):
        return (lambda kc: hT[:, kc, 0:T]), (lambda kc: [("hT", kc)])

    def block(l, kind, t0, ntile):
        sample = kind == "sample"
        T = 64 if sample else ntile * 128
        tiles = [] if sample else list(range(t0, t0 + ntile))
        groups = [(bl * 16, 16, 1 + bl) for bl in range(4)] if sample else [(0, T, 0)]
        hr, hk = hT_r(T)
        if sample:
            src = xs_d if l == 0 else None
        elif l == 0:
            src = xin[:, :, t0 * 128: t0 * 128 + T]
        else:
            src = x1s[:, :, (t0 - 4) * 128: (t0 - 4) * 128 + T]
        if src is not None:
            rk = [("x1s", t) for t in tiles] if (l == 1 and not sample) else []
            P.op("sp", "dma_start", out=xT[:, :, 0:T], in_=src, r=rk, w=[("xT", c) for c in range(8)], semkey="xld")
        else:
            P.op("pool", "tensor_copy", xT[:, :, 0:64], xs_keep[:, :, :], r=["xs_keep"],
                 w=[("xT", c) for c in range(8)])
        norm_mod(0, T, groups)

        if kind == "kv":
            slabs = {1: ws_get()}
        else:
            slabs = {0: ws_get()}
            for c in range(4):
                b = nb()
                proj_fm(b, slabs[0], 8, 512, c * 128, hr, hk, T)
                headnorm(b, T, 0, qT[:, c, 0:T], [("qT", c)])
            ws_done()
            slabs[1] = ws_get()
        for c in range(4):
            b = nb()
            proj_fm(b, slabs[1], 8, 512, c * 128, hr, hk, T)
            r_ = nrot("kst", 2)
            headnorm(b, T, 1, kst[:, r_, 0:T], [("kst", r_)])
            if sample:
                P.op("act", "activation", out=kTs[:, c, :], in_=kst[:, r_, 0:64], func=AF.Copy,
                     r=[("kst", r_)], w=["kTs"])
                P.op("sp", "dma_start", out=nksT_d[l, :, c, :], in_=kst[:, r_, 0:64], r=[("kst", r_)],
                     semkey=("kst", r_))
            else:
                for ti, t in enumerate(tiles):
                    sl = t % 8
                    P.op("act", "activation", out=kT[:, c, sl * 128:(sl + 1) * 128],
                         in_=kst[:, r_, ti * 128:(ti + 1) * 128], func=AF.Copy, r=[("kst", r_)], w=[("kT", sl)])
                    if t >= KEEP_T0:
                        P.op("sp", "dma_start", out=nkT_d[l, :, c, (t - KEEP_T0) * 128:(t - KEEP_T0 + 1) * 128],
                             in_=kst[:, r_, ti * 128:(ti + 1) * 128], r=[("kst", r_)], semkey=("kst", r_))
        ws_done()
        slot = ws_get()
        if sample:
            for bl in range(4):
                b = nb()
                for kc in range(8):
                    P.op("pe", "matmul", ps[b][0:16, 0:512], lhsT=hT[:, kc, bl * 16:(bl + 1) * 16],
                         rhs=wslab[:, slot, kc * 512:(kc + 1) * 512], start=(kc == 0), stop=(kc == 7),
                         r=[("wslab", slot), ("hT", kc)], w=bk(b))
                r_ = nrot("vst", 2)
                P.op("act", "activation", out=vst[0:16, r_, :], in_=ps[b][0:16, 0:512], func=AF.Copy, r=bk(b),
                     w=[("vst", r_)])
                P.op("dve", "tensor_copy", vts[:, bl, :], vst[0:16, r_, :], r=[("vst", r_)], w=["vts"])
                P.op("sp", "dma_start", out=nvs_d[l, :, bl, :], in_=vst[0:16, r_, :], r=[("vst", r_)],
                     semkey=("vst", r_))
        else:
            for ti, t in enumerate(tiles):
                b = nb()
                for kc in range(8):
                    P.op("pe", "matmul", ps[b][:, 0:512], lhsT=hT[:, kc, ti * 128:(ti + 1) * 128],
                         rhs=wslab[:, slot, kc * 512:(kc + 1) * 512], start=(kc == 0), stop=(kc == 7),
                         r=[("wslab", slot), ("hT", kc)], w=bk(b))
                sl = t % 8
                if t >= KEEP_T0:
                    r_ = nrot("vst", 2)
                    P.op("act", "activation", out=vst[:, r_, :], in_=ps[b][:, 0:512], func=AF.Copy, r=bk(b),
                         w=[("vst", r_)])
                    P.op("dve", "tensor_scalar_mul", vt[:, sl, :], vst[:, r_, :], vtok[:, t:t + 1],
                         r=[("vst", r_), "vtok"], w=[("vt", sl)])
                    P.op("sp", "dma_start", out=nv_d[l, t - KEEP_T0], in_=vst[:, r_, :], r=[("vst", r_)],
                         semkey=("vst", r_))
                else:
                    P.op("dve", "tensor_scalar_mul", vt[:, sl, :], ps[b][:, 0:512], vtok[:, t:t + 1],
                         r=bk(b) + ["vtok"], w=[("vt", sl)])
        ws_done()
        if kind == "kv":
            return

        ck(1)
        slot = ws_get()
        for c in range(4):
            b = nb()
            proj_fm(b, slot, 8, 512, c * 128, hr, hk, T)
            P.op("act", "activation", out=ubT[:, c, 0:T], in_=ps[b][:, 0:T], func=AF.Gelu_apprx_tanh, r=bk(b),
                 w=[("ubT", c)])
        ws_done()
        ck(2)
        slot = ws_get()
        tl = [(0, 64)] if sample else [(ti, 128) for ti in range(ntile)]
        for ti, M in tl:
            b = nb()
            for kc in range(8):
                P.op("pe", "matmul", ps[b][0:M, 0:512], lhsT=hT[:, kc, ti * 128: ti * 128 + M],
                     rhs=wslab[:, slot, kc * 512:(kc + 1) * 512], start=(kc == 0), stop=(kc == 7),
                     r=[("wslab", slot), ("hT", kc)], w=bk(b))
            r_ = nrot("gl", 2)
            P.op("act", "activation", out=gl[0:M, r_, :], in_=ps[b][0:M, 0:512], func=AF.Gelu_apprx_tanh, r=bk(b),
                 w=[("gl", r_)])
            r2 = nrot("tmpn", 2)
            P.op("dve", "tensor_tensor", out=tmpn[0:M, r2, :], in0=gl[0:M, r_, :], in1=gl[0:M, r_, :], op=ALU.mult,
                 r=[("gl", r_)], w=[("tmpn", r2)])
            r3 = nrot("ssv", 4)
            P.op("dve", "reduce_sum", out=ssv[0:M, r3:r3 + 1], in_=tmpn[0:M, r2, :], axis=AX.X,
                 r=[("tmpn", r2)], w=[("ssv", r3)])
            P.op("act", "activation", out=ssv[0:M, r3:r3 + 1], in_=ssv[0:M, r3:r3 + 1], func=AF.Sqrt,
                 bias=float(EPS), scale=1.0 / 512.0, r=[("ssv", r3)], w=[("ssv", r3)])
            P.op("dve", "reciprocal", out=ssv[0:M, r3:r3 + 1], in_=ssv[0:M, r3:r3 + 1], r=[("ssv", r3)],
                 w=[("ssv", r3)])
            if sample:
                P.op("dve", "scalar_tensor_tensor", out=tmpn[0:64, r2, :], in0=gl[0:64, r_, :],
                     scalar=ssv[0:64, r3:r3 + 1], in1=vg[0:64, :], op0=ALU.mult, op1=ALU.mult,
                     r=[("gl", r_), ("ssv", r3), "vg"], w=[("tmpn", r2)])
                P.op("act", "activation", out=vbns[:, :], in_=tmpn[0:64, r2, :], func=AF.Copy, r=[("tmpn", r2)],
                     w=["vbns"])
                P.op("sp", "dma_start", out=nbs_d[l], in_=tmpn[0:64, r2, :], r=[("tmpn", r2)], semkey=("tmpn", r2))
            else:
                P.op("dve", "scalar_tensor_tensor", out=vbn[:, ti, :], in0=gl[:, r_, :], scalar=ssv[:, r3:r3 + 1],
                     in1=vg[:, :], op0=ALU.mult, op1=ALU.mult, r=[("gl", r_), ("ssv", r3), "vg"], w=[("vbn", ti)])
        ws_done()
        ck(3)
        for s in range(4):
            slot = ws_get()
            for m in range(4):
                c = s * 4 + m
                b = nb()
                proj_fm(b, slot, 8, 512, m * 128, hr, hk, T)
                P.op("act", "activation", out=SA[:, c, 0:T], in_=ps[b][:, 0:T], func=AF.Sigmoid, r=bk(b),
                     w=[("SA", c)])
            ws_done()

        ck(4)
        if sample:
            items = [(bl, hp) for bl in range(4) for hp in range(4)]

            def s1(it, st):
                bl, hp = it
                cr = 0
                for j in range(5):
                    for hh in range(2):
                        hs = slice(hh * 64, (hh + 1) * 64)
                        b = st * 4 + hh
                        if j < 4:
                            P.op("pe", "matmul", ps[b][:, j * 16:(j + 1) * 16],
                                 lhsT=ckb[hs, cr, hp, j * 128:(j + 1) * 128], rhs=qT[hs, hp, bl * 16:(bl + 1) * 16],
                                 start=True, stop=True, r=[("ckb", cr), ("qT", hp)], w=[("ps", b)])
                        else:
                            P.op("pe", "matmul", ps[b][0:16, 64:80],
                                 lhsT=kTs[hs, hp, bl * 16:(bl + 1) * 16], rhs=qT[hs, hp, bl * 16:(bl + 1) * 16],
                                 start=True, stop=True, r=["kTs", ("qT", hp)], w=[("ps", b)])
                for hh in range(2):
                    b = st * 4 + hh
                    h = 2 * hp + hh
                    P.op("act", "activation", out=Pts[:, st, hh, 0:48], in_=ps[b][:, 0:48], func=AF.Exp,
                         r=[("ps", b)], w=[("Pts", st, hh)])
                    P.op("act", "activation", out=exs[:, st, hh, 0:16], in_=ps[b][:, 48:64], func=AF.Exp,
                         r=[("ps", b)], w=[("exs", st, hh)])
                    P.op("act", "activation", out=exs[0:16, st, hh, 16:32], in_=ps[b][0:16, 64:80], func=AF.Exp,
                         r=[("ps", b)], w=[("exs", st, hh)])
                    P.op("dve", "tensor_tensor", out=Pts[:, st, hh, 48:64], in0=exs[:, st, hh, 0:16],
                         in1=E[:, 0, h, 0:16], op=ALU.mult, r=[("exs", st, hh), "E"], w=[("Pts", st, hh)])
                    P.op("dve", "tensor_tensor", out=Pts[0:16, st, hh, 64:80], in0=exs[0:16, st, hh, 16:32],
                         in1=E[0:16, 1, h, 0:16], op=ALU.mult, r=[("exs", st, hh), "E"], w=[("Pts", st, hh)])

            def s2(it, st):
                bl, hp = it
                cr = 0
                bd, bo = st * 4 + 2, st * 4 + 3
                for hh in range(2):
                    for j in range(5):
                        if j < 4:
                            P.op("pe", "matmul", ps[bd][:, hh * 16:(hh + 1) * 16], lhsT=ones_bf[:, :],
                                 rhs=Pts[:, st, hh, j * 16:(j + 1) * 16], start=(j == 0), stop=False,
                                 r=[("Pts", st, hh), "ones_bf"], w=[("ps", bd)])
                        else:
                            P.op("pe", "matmul", ps[bd][:, hh * 16:(hh + 1) * 16], lhsT=ones_bf[0:16, :],
                                 rhs=Pts[0:16, st, hh, 64:80], start=False, stop=True,
                                 r=[("Pts", st, hh), "ones_bf"], w=[("ps", bd)])
                for hh in range(2):
                    hs = slice(hh * 64, (hh + 1) * 64)
                    fc = (2 * hp + hh) * 64
                    for j in range(5):
                        if j < 4:
                            P.op("pe", "matmul", ps[bo][hs, 0:16], lhsT=cvb[:, cr, j, fc:fc + 64],
                                 rhs=Pts[:, st, hh, j * 16:(j + 1) * 16], start=(j == 0), stop=False,
                                 r=[("Pts", st, hh), ("cvb", cr)], w=[("ps", bo)])
                        else:
                            P.op("pe", "matmul", ps[bo][hs, 0:16], lhsT=vts[0:16, bl, fc:fc + 64],
                                 rhs=Pts[0:16, st, hh, 64:80], start=False, stop=True,
                                 r=[("Pts", st, hh), "vts"], w=[("ps", bo)])
                P.op("dve", "reciprocal", out=rdens[:, st, :], in_=ps[bd][:, 0:32], r=[("ps", bd)],
                     w=[("rdens", st)])
                for hh in range(2):
                    hs = slice(hh * 64, (hh + 1) * 64)
                    P.op("dve", "tensor_tensor", out=oaT[hs, hp, bl * 16:(bl + 1) * 16], in0=ps[bo][hs, 0:16],
                         in1=rdens[hs, st, hh * 16:(hh + 1) * 16], op=ALU.mult, r=[("ps", bo), ("rdens", st)],
                         w=[("oaT", hp)])
        else:
            items = [(qi, hp) for qi in range(ntile) for hp in range(4)]

            def sview(st, hh, j):
                if j < 4:
                    return st * 4 + hh, slice(j * 128, (j + 1) * 128)
                return st * 4 + 2 + hh, slice(0, 128)

            def s1(it, st):
                qi, hp = it
                qt = t0 + qi
                for j in range(5):
                    sl = (qt - 4 + j) % 8
                    for hh in range(2):
                        hs = slice(hh * 64, (hh + 1) * 64)
                        b, cs_ = sview(st, hh, j)
                        P.op("pe", "matmul", ps[b][:, cs_],
                             lhsT=kT[hs, hp, sl * 128:(sl + 1) * 128], rhs=qT[hs, hp, qi * 128:(qi + 1) * 128],
                             start=True, stop=True, r=[("kT", sl), ("qT", hp)], w=[("ps", b)])
                for hh in range(2):
                    h = 2 * hp + hh
                    b = st * 4 + hh
                    pk = [("Pt", st, hh)]
                    P.op("act", "activation", out=Pt[64:128, st, hh, 0, :], in_=ps[b][64:128, 0:128], func=AF.Exp,
                         r=[("ps", b)], w=pk)
                    P.op("act", "activation", out=Pt[0:64, st, hh, 0, 0:64], in_=ps[b][0:64, 0:64], func=AF.Exp,
                         r=[("ps", b)], w=pk)
                    P.op("act", "activation", out=Pt[:, st, hh, 1:3, :].rearrange("p j q -> p (j q)"),
                         in_=ps[b][:, 128:384], func=AF.Exp, r=[("ps", b)], w=pk)
                    P.op("act", "activation", out=ex[:, st, hh, 0:128], in_=ps[b][:, 384:512], func=AF.Exp,
                         r=[("ps", b)], w=[("ex", st, hh)])
                    b4 = st * 4 + 2 + hh
                    P.op("act", "activation", out=ex[:, st, hh, 128:256], in_=ps[b4][:, 0:128], func=AF.Exp,
                         r=[("ps", b4)], w=[("ex", st, hh)])
                    P.op("dve", "tensor_tensor", out=Pt[:, st, hh, 3:5, :],
                         in0=ex[:, st, hh, :].rearrange("p (j q) -> p j q", j=2), in1=E[:, :, h, :], op=ALU.mult,
                         r=[("ex", st, hh), "E"], w=pk)

            def s2(it, st):
                qi, hp = it
                qt = t0 + qi
                bd, bo = st * 4 + 2, st * 4 + 3
                for hh in range(2):
                    for j in range(5):
                        kt = qt - 4 + j
                        lw = vones[:, kt, :] if kt < 9 else ones_bf[:, :]
                        P.op("pe", "matmul", ps[bd][:, 256 + hh * 128:256 + (hh + 1) * 128], lhsT=lw,
                             rhs=Pt[:, st, hh, j, :], start=(j == 0), stop=(j == 4),
                             r=[("Pt", st, hh), "vones", "ones_bf"], w=[("ps", bd)])
                for hh in range(2):
                    hs = slice(hh * 64, (hh + 1) * 64)
                    fc = (2 * hp + hh) * 64
                    for j in range(5):
                        sl = (qt - 4 + j) % 8
                        P.op("pe", "matmul", ps[bo][hs, 256:384], lhsT=vt[:, sl, fc:fc + 64],
                             rhs=Pt[:, st, hh, j, :], start=(j == 0), stop=(j == 4),
                             r=[("Pt", st, hh), ("vt", sl)], w=[("ps", bo)])
                P.op("dve", "tensor_scalar_max", rden[:, st, :], ps[bd][:, 256:512], 1e-30, r=[("ps", bd)],
                     w=[("rden", st)])
                P.op("dve", "reciprocal", out=rden[:, st, :], in_=rden[:, st, :], r=[("rden", st)],
                     w=[("rden", st)])
                for hh in range(2):
                    hs = slice(hh * 64, (hh + 1) * 64)
                    P.op("dve", "tensor_tensor", out=oaT[hs, hp, qi * 128:(qi + 1) * 128], in0=ps[bo][hs, 256:384],
                         in1=rden[hs, st, hh * 128:(hh + 1) * 128], op=ALU.mult, r=[("ps", bo), ("rden", st)],
                         w=[("oaT", hp)])

        if sample:
            groups_it = [[(bl, hp) for hp in range(4)] for bl in range(4)]
        else:
            groups_it = [items]
        for its in groups_it:
            if sample:
                bl = its[0][0]
                P.op("pool", "dma_start", out=ckb[:, 0, :, :], in_=ckT_d[l, :, :, bl, :], w=[("ckb", 0)],
                     semkey=("ckb", 0))
                P.op("pool", "dma_start", out=cvb[:, 0, :, :], in_=cv_d[l, bl], w=[("cvb", 0)],
                     semkey=("cvb", 0))
            for i in range(len(its) + 1):
                if i < len(its):
                    s1(its[i], i % 2)
                if i >= 1:
                    s2(its[i - 1], (i - 1) % 2)

        ck(5)
        if sample:
            b = nb()
            for g in range(4):
                P.op("pe", "matmul", ps[b][:, g * 64:(g + 1) * 64], lhsT=vbns[0:64, g * 128:(g + 1) * 128],
                     rhs=Wblk[0:64, g, :], start=True, stop=False, r=["vbns", "Wblk"], w=bk(b))
                P.op("pe", "matmul", ps[b][:, g * 64:(g + 1) * 64], lhsT=ones_row[0:1, :],
                     rhs=bsps[0:1, g * 64:(g + 1) * 64], start=False, stop=True, r=["ones_row", "bsps"], w=bk(b))
            P.op("dve", "tensor_tensor", out=obT[:, :, 0:64], in0=ps[b][:, 0:256].rearrange("p (g t) -> p g t", g=4),
                 in1=ubT[:, :, 0:64], op=ALU.mult, r=bk(b) + [("ubT", c) for c in range(4)],
                 w=[("obT", c) for c in range(4)])
        else:
            for ti in range(ntile):
                b = nb()
                for g in range(4):
                    P.op("pe", "matmul", ps[b][:, g * 128:(g + 1) * 128], lhsT=vbn[:, ti, g * 128:(g + 1) * 128],
                         rhs=WcT[:, g, :], start=True, stop=False, r=[("vbn", ti), "WcT"], w=bk(b))
                    P.op("pe", "matmul", ps[b][:, g * 128:(g + 1) * 128], lhsT=ones_row[0:1, :],
                         rhs=bsp[0:1, g * 128:(g + 1) * 128], start=False, stop=True, r=["ones_row", "bsp"], w=bk(b))
                P.op("dve", "tensor_tensor", out=obT[:, :, ti * 128:(ti + 1) * 128],
                     in0=ps[b][:, 0:512].rearrange("p (g t) -> p g t", g=4), in1=ubT[:, :, ti * 128:(ti + 1) * 128],
                     op=ALU.mult, r=bk(b) + [("ubT", c) for c in range(4)], w=[("obT", c) for c in range(4)])

        ck(6)
        sa_, sb_ = ws_get(), ws_get()
        for c in range(8):
            ba, bb_ = nb(), nb()
            proj_fm(ba, sa_, 4, 1024, c * 128, lambda kc: oaT[:, kc, 0:T], lambda kc: [("oaT", kc)], T)
            proj_fm(bb_, sb_, 4, 1024, c * 128, lambda kc: obT[:, kc, 0:T], lambda kc: [("obT", kc)], T)
            P.op("dve", "tensor_tensor", out=tmpn[:, 0, 0:T], in0=ps[ba][:, 0:T], in1=SA[:, c, 0:T], op=ALU.mult,
                 r=bk(ba) + [("SA", c)], w=[("tmpn", 0)])
            P.op("dve", "tensor_tensor", out=tmpn[:, 1, 0:T], in0=ps[bb_][:, 0:T], in1=SA[:, 8 + c, 0:T], op=ALU.mult,
                 r=bk(bb_) + [("SA", 8 + c)], w=[("tmpn", 1)])
            P.op("pool", "tensor_tensor", out=SA[:, 16 + c, 0:T], in0=tmpn[:, 0, 0:T], in1=tmpn[:, 1, 0:T], op=ALU.add,
                 r=[("tmpn", 0), ("tmpn", 1)], w=[("SA", 16 + c)])
        ws_done()
        ws_done()
        ck(7)
        for s in range(2):
            slot = ws_get()
            for m in range(4):
                c = s * 4 + m
                b = nb()
                proj_fm(b, slot, 8, 512, m * 128, lambda kc: SA[:, 16 + kc, 0:T], lambda kc: [("SA", 16 + kc)], T)
                for (c0, n, bb) in groups:
                    P.op("dve", "scalar_tensor_tensor", out=xT[:, c, c0:c0 + n], in0=ps[b][:, c0:c0 + n],
                         scalar=modT[:, 16 + c, bb:bb + 1], in1=xT[:, c, c0:c0 + n], op0=ALU.mult, op1=ALU.add,
                         r=bk(b) + [("xT", c), "modT"], w=[("xT", c)])
            ws_done()

        ck(8)
        norm_mod(1, T, groups)
        GW = 72 if sample else T + 2
        for jj in range(11):
            slot = ws_get()
            for sub in range(2):
                j = 2 * jj + sub
                bg, bu = nb(), nb()
                proj_fm(bg, slot, 8, 512, sub * 128, hr, hk, T)
                proj_fm(bu, slot, 8, 512, 256 + sub * 128, hr, hk, T)
                r_ = nrot("gb", 3)
                if sample:
                    g3 = gb[:, r_, 0:72].rearrange("p (b t) -> p b t", t=18)
                    P.op("act", "activation", out=g3[:, :, 2:18],
                         in_=ps[bg][:, 0:64].rearrange("p (b t) -> p b t", t=16), func=AF.Copy, r=bk(bg),
                         w=[("gb", r_)])
                    P.op("pool", "tensor_copy", g3[:, :, 0:2], halos[:, j, :, :], r=["halos"], w=[("gb", r_)])
                    P.op("pool", "tensor_copy", ncs_st[:, j, :, :], g3[:, :, 16:18], r=[("gb", r_)], w=["ncs_st"])
                    views = [g3[:, :, k:k + 16] for k in range(3)]
                    r2 = nrot("gc", 2)
                    gcv = gc[:, r2, 0:64].rearrange("p (b t) -> p b t", t=16)
                else:
                    P.op("act", "activation", out=gb[:, r_, 2:2 + T], in_=ps[bg][:, 0:T], func=AF.Copy, r=bk(bg),
                         w=[("gb", r_)])
                    P.op("pool", "tensor_copy", gb[:, r_, 0:2], halo[:, j, :], r=["halo"], w=[("gb", r_)])
                    if t0 == 8:
                        P.op("pool", "tensor_scalar_mul", gb[:, r_, 128:130], gb[:, r_, 128:130], flag[:, 0:1],
                             r=[("gb", r_), "flag"], w=[("gb", r_)])
                    P.op("pool", "tensor_copy", halo[:, j, :], gb[:, r_, T:T + 2], r=[("gb", r_)], w=["halo"])
                    views = [gb[:, r_, k:k + T] for k in range(3)]
                    r2 = nrot("gc", 2)
                    gcv = gc[:, r2, 0:T]
                P.op("pool", "tensor_scalar", out=gcv, in0=views[0], scalar1=cw[:, j, 0:1], scalar2=cb[:, j:j + 1],
                     op0=ALU.mult, op1=ALU.add, r=[("gb", r_), "cw", "cb"], w=[("gc", r2)])
                for k in (1, 2):
                    P.op("dve", "scalar_tensor_tensor", out=gcv, in0=views[k], scalar=cw[:, j, k:k + 1], in1=gcv,
                         op0=ALU.mult, op1=ALU.add, r=[("gb", r_), ("gc", r2), "cw"], w=[("gc", r2)])
                r3 = nrot("ge", 2)
                P.op("act", "activation", out=ge[:, r3, 0:T], in_=gc[:, r2, 0:T], func=AF.Gelu_apprx_tanh,
                     r=[("gc", r2)], w=[("ge", r3)])
                P.op("dve", "tensor_tensor", out=SA[:, j, 0:T], in0=ps[bu][:, 0:T], in1=ge[:, r3, 0:T], op=ALU.mult,
                     r=bk(bu) + [("ge", r3)], w=[("SA", j)])
            ws_done()
        for c in range(8):
            slot = ws_get()
            b = nb()
            proj_fm(b, slot, 22, 128, 0, lambda kc: SA[:, kc, 0:T], lambda kc: [("SA", kc)], T)
            for (c0, n, bb) in groups:
                P.op("dve", "scalar_tensor_tensor", out=xT[:, c, c0:c0 + n], in0=ps[b][:, c0:c0 + n],
                     scalar=modT[:, 40 + c, bb:bb + 1], in1=xT[:, c, c0:c0 + n], op0=ALU.mult, op1=ALU.add,
                     r=bk(b) + [("xT", c), "modT"], w=[("xT", c)])
            ws_done()

        ck(9)
        xk = [("xT", c) for c in range(8)]
        if sample:
            if l == 0:
                P.op("pool", "tensor_copy", xs_keep[:, :, :], xT[:, :, 0:64], r=xk, w=["xs_keep"])
            else:
                P.op("sp", "dma_start", out=ysT_d, in_=xT[:, :, 0:64], r=xk, semkey="xst")
            P.op("sp", "dma_start", out=ncs_d[l], in_=ncs_st[:, :, :, :], r=["ncs_st"], semkey="ncs")
        elif l == 0:
            P.op("sp", "dma_start", out=x1s[:, :, (t0 - 4) * 128:(t0 - 4) * 128 + T], in_=xT[:, :, 0:T], r=xk,
                 w=[("x1s", t) for t in tiles], semkey="xst")
        else:
            lo = max(t0, OUT_T0)
            c0 = (lo - t0) * 128
            P.op("sp", "dma_start", out=yT_d[:, :, (lo - OUT_T0) * 128:(lo - OUT_T0) * 128 + T - c0],
                 in_=xT[:, :, c0:T], r=xk, semkey="xst")

    xs_keep = sb("xs_keep", (128, 8, 64))

    def blk(*a):
        block(*a)
        nblk[0] += 1
        if stop is not None and nblk[0] >= stop:
            raise _Stop()

    try:
        for l in range(2):
            if stop is not None and stop == -1:
                raise _Stop()
            layer_setup(l)
            if stop is not None and stop == 0:
                raise _Stop()
            if l == 1:
                P.op("pool", "memset", halo[:, :, :], 0.0, r=["halo"], w=["halo"])
            blk(l, "kv", 0 if l == 0 else 4, 4)
            for (t0, n) in (L0_BLOCKS if l == 0 else L1_BLOCKS):
                blk(l, "full", t0, n)
            P.op("sp", "dma_start", out=ncv_d[l], in_=halo[:, :, :], r=["halo"], semkey="ncv")
            blk(l, "sample", 0, 0)
        assert ws["consumed"] == len(ws["seq"]), (ws["consumed"], len(ws["seq"]))
    except _Stop:
        dbg = dout("dbgx", (128, 8, 512))
        P.op("sp", "dma_start", out=dbg, in_=xT[:, :, :], r=[("xT", c) for c in range(8)], semkey="dbg")
        dbg2 = dout("dbgh", (128, 8, 512))
        P.op("pool", "dma_start", out=dbg2, in_=tmpn[:, :, :].rearrange("p a b -> p (a b)"), r=[("tmpn", 0), ("tmpn", 1)], semkey="dbg") if False else None

    P.emit(nc, stack)
    stack.close()
    return nc


def _fm(a):
    t, f = a.shape
    return np.ascontiguousarray(a.reshape(t, f // 128, 128).transpose(2, 1, 0))


def _slab(wm):
    k, m = wm.shape
    return np.ascontiguousarray(wm.reshape(k // 128, 128, m).transpose(1, 0, 2)).reshape(128, (k // 128) * m)


_NC_CACHE = {}


def kernel(x_prompt, x_sample, cache_attn_k, cache_attn_v, cache_ffn_conv, c_prompt, c_sample,
           norm1_g, norm2_g, w_ada, b_ada, w_in, q_norm_g, k_norm_g, rel_bias, v_norm_g,
           w_spatial, b_spatial, w_out_a, w_out_b, w_out, w_ffn_in, ffn_conv_w, ffn_conv_b, w_ffn_out):
    f = lambda a: np.asarray(a, dtype=np.float32)
    x_prompt, x_sample, cache_attn_k, cache_attn_v, cache_ffn_conv = map(f, (x_prompt, x_sample, cache_attn_k, cache_attn_v, cache_ffn_conv))
    c_prompt, c_sample, norm1_g, norm2_g, w_ada, b_ada, w_in = map(f, (c_prompt, c_sample, norm1_g, norm2_g, w_ada, b_ada, w_in))
    q_norm_g, k_norm_g, rel_bias, v_norm_g, w_spatial, b_spatial = map(f, (q_norm_g, k_norm_g, rel_bias, v_norm_g, w_spatial, b_spatial))
    w_out_a, w_out_b, w_out, w_ffn_in, ffn_conv_w, ffn_conv_b, w_ffn_out = map(f, (w_out_a, w_out_b, w_out, w_ffn_in, ffn_conv_w, ffn_conv_b, w_ffn_out))

    in_maps = _prep(x_prompt, x_sample, cache_attn_k, cache_attn_v, cache_ffn_conv, c_prompt, c_sample,
                    norm1_g, norm2_g, w_ada, b_ada, w_in, q_norm_g, k_norm_g, rel_bias, v_norm_g,
                    w_spatial, b_spatial, w_out_a, w_out_b, w_out, w_ffn_in, ffn_conv_w, ffn_conv_b, w_ffn_out)
    if "nc" not in _NC_CACHE:
        _NC_CACHE["nc"] = build_nc()
    nc = _NC_CACHE["nc"]
    res = run_bass_kernel_spmd(nc, in_maps, core_ids=list(range(NCORES)))
    return _post(res.results)


def _prep(x_prompt, x_sample, cache_attn_k, cache_attn_v, cache_ffn_conv, c_prompt, c_sample,
          norm1_g, norm2_g, w_ada, b_ada, w_in, q_norm_g, k_norm_g, rel_bias, v_norm_g,
          w_spatial, b_spatial, w_out_a, w_out_b, w_out, w_ffn_in, ffn_conv_w, ffn_conv_b, w_ffn_out):

    wall = np.empty((2, 24, 128, 4096), np.float32)
    wfo = np.empty((2, 8, 128, 2816), np.float32)
    wada = np.empty((2, 12, 128, 4096), np.float32)
    for l in range(2):
        for s in range(9):
            wall[l, s] = _slab(w_in[l][:, s * 512:(s + 1) * 512])
        wall[l, 9] = _slab(w_out_a[l])
        wall[l, 10] = _slab(w_out_b[l])
        for s in range(2):
            wall[l, 11 + s] = _slab(w_out[l][:, s * 512:(s + 1) * 512])
        for jj in range(11):
            idx = np.concatenate([np.arange(2 * jj * 128, (2 * jj + 2) * 128), DFF + np.arange(2 * jj * 128, (2 * jj + 2) * 128)])
            wall[l, 13 + jj] = _slab(w_ffn_in[l][:, idx])
        for c in range(8):
            wfo[l, c] = _slab(w_ffn_out[l][:, c * 128:(c + 1) * 128])
        for s in range(12):
            wada[l, s] = _slab(w_ada[l][:, s * 512:(s + 1) * 512])
    col = lambda v: np.ascontiguousarray(v.reshape(-1, 128).T)
    nrm = np.stack([np.concatenate([col(norm1_g[l]), col(norm2_g[l])], 1) for l in range(2)])
    bada = np.stack([col(b_ada[l]) for l in range(2)])
    qkg = np.stack([np.stack([np.tile(q_norm_g[l], 2), np.tile(k_norm_g[l], 2)], 1) for l in range(2)])
    vg = np.stack([np.broadcast_to(v_norm_g[l][None, :], (128, 512)) for l in range(2)]).copy()
    bsp = b_spatial.reshape(2, 1, 512).copy()
    bsps = np.stack([np.tile(b_spatial[l][:, None, :16], (1, 4, 1)).reshape(1, 256) for l in range(2)])
    wspT = np.ascontiguousarray(w_spatial.transpose(0, 3, 1, 2))
    wspb = np.zeros((2, 64, 4, 64), np.float32)
    for bl in range(4):
        wspb[:, bl * 16:(bl + 1) * 16, :, bl * 16:(bl + 1) * 16] = wspT[:, 0:16, :, 0:16]
    s_i = np.arange(128)[:, None]
    t_i = np.arange(128)[None, :]
    tril = np.broadcast_to((s_i <= t_i).astype(np.float32)[:, None, :], (128, 4, 128)).copy()
    trilb = np.zeros((64, 4, 64), np.float32)
    for bl in range(4):
        trilb[bl * 16:(bl + 1) * 16, :, bl * 16:(bl + 1) * 16] = tril[0:16, :, 0:16]
    cw = np.ascontiguousarray(ffn_conv_w.reshape(2, 3, 22, 128).transpose(0, 3, 2, 1))
    cb = np.ascontiguousarray(ffn_conv_b.reshape(2, 22, 128).transpose(0, 2, 1))
    ki = np.arange(128)[:, None]
    qi = np.arange(128)[None, :]
    idx1 = np.clip(128 + qi - ki, -128, 128) + 128
    idx0 = np.clip(qi - ki, -128, 128) + 128
    Tb = np.stack([np.stack([rel_bias[l][:, idx1].transpose(1, 0, 2), rel_bias[l][:, idx0].transpose(1, 0, 2)], 1)
                   for l in range(2)])
    b256 = np.stack([np.broadcast_to(rel_bias[l][None, :, 256], (128, 8)) for l in range(2)]).copy()

    shared = dict(wall=wall, wfo=wfo, wada=wada, nrm=nrm, bada=bada, qkg=qkg, vg=vg, bsp=bsp, bsps=bsps, wspT=wspT,
                  wspb=wspb, tril=tril, trilb=trilb, cw=cw, cb=cb, Tb=np.ascontiguousarray(Tb), b256=b256)
    shared = {k: np.ascontiguousarray(v, dtype=np.float32) for k, v in shared.items()}

    in_maps = []
    for core in range(NCORES):
        b, seg = core // 4, core % 4
        s0 = seg * SEG
        a = s0 - HALO
        xw = np.zeros((W, D), np.float32)
        lo = max(a, 0)
        xw[lo - a:] = x_prompt[b, lo:s0 + SEG]
        valid = (np.arange(W) + a >= 0).astype(np.float32)
        m = dict(shared)
        m["xin"] = _fm(xw)
        m["xs"] = _fm(x_sample[4 * core:4 * core + 4].reshape(64, D))
        m["vtok"] = np.ascontiguousarray(valid.reshape(NT, 128).T)
        m["flag"] = np.full((128, 1), 1.0 if a >= 0 else 0.0, np.float32)
        cc = np.concatenate([c_prompt[b:b + 1], c_sample[4 * core:4 * core + 4]], 0)
        m["cT"] = np.ascontiguousarray(cc.reshape(5, 8, 128).transpose(2, 1, 0))
        ck = cache_attn_k[:, 4 * core:4 * core + 4]
        m["ckT"] = np.ascontiguousarray(ck.reshape(2, 4, 512, 4, 2, 64).transpose(0, 4, 5, 3, 1, 2)).reshape(2, 128, 4, 4, 512)
        cvv = cache_attn_v[:, 4 * core:4 * core + 4].reshape(2, 4, 4, 128, 512)
        m["cv"] = np.ascontiguousarray(cvv.transpose(0, 1, 3, 2, 4))
        cc2 = cache_ffn_conv[:, 4 * core:4 * core + 4].reshape(2, 4, 2, 22, 128)
        m["cconv"] = np.ascontiguousarray(cc2.transpose(0, 4, 3, 1, 2))
        in_maps.append(m)
    return in_maps


def _post(R):

    y_prompt = np.empty((2, 8192, D), np.float32)
    y_sample = np.empty((32, 16, D), np.float32)
    nkp = np.empty((2, 2, 512, 8, 64), np.float32)
    nvp = np.empty((2, 2, 512, 8, 64), np.float32)
    ncp = np.empty((2, 2, 2, DFF), np.float32)
    nks = np.empty((2, 32, 16, 8, 64), np.float32)
    nvs = np.empty((2, 32, 16, 8, 64), np.float32)
    nbs = np.empty((2, 32, 16, 4, 128), np.float32)
    ncs = np.empty((2, 32, 2, DFF), np.float32)
    for core in range(NCORES):
        r = R[core]
        b, seg = core // 4, core % 4
        y_prompt[b, seg * SEG:(seg + 1) * SEG] = r["yT"].transpose(2, 1, 0).reshape(SEG, D)
        y_sample[4 * core:4 * core + 4] = r["ysT"].transpose(2, 1, 0).reshape(4, 16, D)
        sl = slice(4 * core, 4 * core + 4)
        for l in range(2):
            if seg == 3:
                nkp[l, b] = r["nkT"][l].reshape(2, 64, 4, 512).transpose(3, 2, 0, 1).reshape(512, 8, 64)
                nvp[l, b] = r["nv"][l].reshape(512, 8, 64)
                ncp[l, b] = r["ncv"][l].transpose(2, 1, 0).reshape(2, DFF)
            nks[l, sl] = r["nksT"][l].reshape(2, 64, 4, 4, 16).transpose(3, 4, 2, 0, 1).reshape(4, 16, 8, 64)
            nvs[l, sl] = r["nvs"][l].transpose(1, 0, 2).reshape(4, 16, 8, 64)
            nbs[l, sl] = r["nbs"][l].reshape(4, 16, 4, 128)
            ncs[l, sl] = r["ncs"][l].transpose(2, 3, 1, 0).reshape(4, 2, DFF)
    return (y_prompt, y_sample, nkp, nvp, ncp, nks, nvs, nbs, ncs)
```

```python
import contextlib
import numpy as np
import concourse.bass as bass
import concourse.mybir as mybir
from concourse.bass_utils import run_bass_kernel_spmd

F32 = mybir.dt.float32
BF16 = mybir.dt.bfloat16
AF = mybir.ActivationFunctionType
ALU = mybir.AluOpType
AX = mybir.AxisListType

NCORES = 8
D = 1024
SEG = 2048
HALO = 1152
W = SEG + HALO
NT = W // 128
DFF = 2816
EPS = 1e-6
NB = 3
L0_BLOCKS = [(4, 4), (8, 4), (12, 4), (16, 3), (19, 3), (22, 3)]
L1_BLOCKS = [(8, 4), (12, 4), (16, 3), (19, 3), (22, 3)]
OUT_T0 = 9
KEEP_T0 = 21
_DBG = {}


class Prog:
    STREAMS = ("sp", "act", "dve", "pool", "pe")

    def __init__(self):
        self.ops = []
        self.keyw = {}
        self.keyr = {}
        self.dcnt = {}

    def op(self, stream, method, *args, r=(), w=(), semkey=None, **kw):
        idx = len(self.ops)
        dom = ("d", semkey) if semkey is not None else ("s", stream)
        deps = {}

        def add(d, i):
            if d[0] == "d":
                i = self.dcnt[d]
            if deps.get(d, -1) < i:
                deps[d] = i

        for k in r:
            for d, i in self.keyw.get(k, {}).items():
                add(d, i)
        skip_same = dom[0] == "d" or stream == "pe"
        for k in w:
            for d, i in self.keyr.get(k, {}).items():
                if d == dom and (skip_same or i == idx):
                    continue
                add(d, i)
            for d, i in self.keyw.get(k, {}).items():
                if d == dom and skip_same:
                    continue
                add(d, i)
        for k in r:
            self.keyr.setdefault(k, {})[dom] = idx
        for k in w:
            if self.keyr.get(k):
                self.keyw[k] = {dom: idx}
                self.keyr[k] = {}
            else:
                self.keyw.setdefault(k, {})[dom] = idx
        if dom[0] == "d":
            self.dcnt[dom] = self.dcnt.get(dom, 0) + 16
        self.ops.append(dict(stream=stream, method=method, args=args, kw=kw, dom=dom,
                             deps=list(deps.items())))
        return idx

    def emit(self, nc, stack):
        ops = self.ops
        needs = set()
        for o in ops:
            for d, i in o["deps"]:
                if d[0] == "s":
                    needs.add(i)
        cnt = {}
        sems = {}
        issuer = {}
        for i, o in enumerate(ops):
            d = o["dom"]
            if d[0] == "d":
                cnt[d] = cnt.get(d, 0) + 16
                o["done"] = cnt[d]
                assert issuer.setdefault(d, o["stream"]) == o["stream"], d
            elif i in needs:
                cnt[d] = cnt.get(d, 0) + 1
                o["done"] = cnt[d]
            else:
                o["done"] = None
        for d in cnt:
            sems[d] = stack.enter_context(nc.semaphore("s%d" % len(sems)))
        block = stack.enter_context(nc.Block())
        self.nsem = len(sems)

        def run(stream, eng):
            waited = {}
            for o in ops:
                if o["stream"] != stream:
                    continue
                for d, i in o["deps"]:
                    v = i if d[0] == "d" else ops[i]["done"]
                    if waited.get(d, 0) < v:
                        eng.wait_ge(sems[d], v)
                        waited[d] = v
                ins = getattr(eng, o["method"])(*o["args"], **o["kw"])
                if o["done"] is not None:
                    d = o["dom"]
                    ins.then_inc(sems[d], 16 if d[0] == "d" else 1)
            for d, v in cnt.items():
                if d[0] == "d" and issuer[d] == stream and waited.get(d, 0) < v:
                    eng.wait_ge(sems[d], v)

        @block.sync
        def _(e):
            run("sp", e)

        @block.scalar
        def _(e):
            run("act", e)

        @block.vector
        def _(e):
            run("dve", e)

        @block.gpsimd
        def _(e):
            run("pool", e)

        @block.tensor
        def _(e):
            run("pe", e)


def build_nc(stop=None):
    class _Stop(Exception):
        pass

    nblk = [0]

    def ck(st):
        if stop is not None and nblk[0] + st / 10.0 >= stop - 1e-9 and stop > 0:
            raise _Stop()

    nc = bass.Bass("TRN2", target_bir_lowering=False)
    P = Prog()
    stack = contextlib.ExitStack()

    def din(name, shape):
        return nc.dram_tensor(name, list(shape), F32, kind="ExternalInput").ap()

    def dout(name, shape):
        return nc.dram_tensor(name, list(shape), F32, kind="ExternalOutput").ap()

    def dint(name, shape, dt):
        return nc.dram_tensor(name, list(shape), dt, kind="Internal").ap()

    xin = din("xin", (128, 8, W))
    xs_d = din("xs", (128, 8, 64))
    vtok_d = din("vtok", (128, NT))
    flag_d = din("flag", (128, 1))
    cT_d = din("cT", (128, 8, 5))
    ckT_d = din("ckT", (2, 128, 4, 4, 512))
    cv_d = din("cv", (2, 4, 128, 4, 512))
    cconv_d = din("cconv", (2, 128, 22, 4, 2))
    wall_d = din("wall", (2, 24, 128, 4096))
    wfo_d = din("wfo", (2, 8, 128, 2816))
    wada_d = din("wada", (2, 12, 128, 4096))
    nrm_d = din("nrm", (2, 128, 16))
    bada_d = din("bada", (2, 128, 48))
    qkg_d = din("qkg", (2, 128, 2))
    vg_d = din("vg", (2, 128, 512))
    bsp_d = din("bsp", (2, 1, 512))
    bsps_d = din("bsps", (2, 1, 256))
    wspT_d = din("wspT", (2, 128, 4, 128))
    wspb_d = din("wspb", (2, 64, 4, 64))
    tril_d = din("tril", (128, 4, 128))
    trilb_d = din("trilb", (64, 4, 64))
    cw_d = din("cw", (2, 128, 22, 3))
    cb_d = din("cb", (2, 128, 22))
    Tb_d = din("Tb", (2, 128, 2, 8, 128))
    b256_d = din("b256", (2, 128, 8))

    yT_d = dout("yT", (128, 8, SEG))
    ysT_d = dout("ysT", (128, 8, 64))
    nkT_d = dout("nkT", (2, 128, 4, 512))
    nv_d = dout("nv", (2, 4, 128, 512))
    ncv_d = dout("ncv", (2, 128, 22, 2))
    nksT_d = dout("nksT", (2, 128, 4, 64))
    nvs_d = dout("nvs", (2, 16, 4, 512))
    nbs_d = dout("nbs", (2, 64, 512))
    ncs_d = dout("ncs", (2, 128, 22, 4, 2))

    wsc = dint("wsc", (2, 24, 128, 4096), BF16)
    wsc_fo = dint("wscfo", (2, 8, 128, 2816), BF16)
    x1s = dint("x1s", (128, 8, 21 * 128), F32)

    def sb(name, shape, dt=F32):
        return stack.enter_context(nc.sbuf_tensor("sb_" + name, list(shape), dt))

    xTt = sb("xT", (128, 2, 8, 512))
    cur = {"xT": xTt[:, 0], "par": 0, "idx": 0}
    hT = sb("hT", (128, 8, 512), BF16)
    sq = sb("sq", (128, 3, 512), BF16)
    rstd = sb("rstd", (128, 512))
    tmpn = sb("tmpn", (128, 2, 512))
    qT = sb("qT", (128, 4, 512), BF16)
    kst = sb("kst", (128, 2, 512))
    rs = sb("rs", (128, 2, 512))
    kT = sb("kT", (128, 4, 1024), BF16)
    vt = sb("vt", (128, 8, 512), BF16)
    ubT = sb("ubT", (128, 4, 512), BF16)
    vbn = sb("vbn", (128, 4, 512), BF16)
    gl = sb("gl", (128, 2, 512))
    ssv = sb("ssv", (128, 4))
    SA = sb("SA", (128, 24, 512), BF16)
    oaT = sb("oaT", (128, 4, 512), BF16)
    obT = sb("obT", (128, 4, 512), BF16)
    Pt = sb("Pt", (128, 2, 2, 5, 128), BF16)
    rden = sb("rden", (128, 2, 256))
    gb = sb("gb", (128, 3, 516))
    gc = sb("gc", (128, 2, 512))
    ge = sb("ge", (128, 2, 512))
    halo = sb("halo", (128, 22, 2))
    halos = sb("halos", (128, 22, 4, 2))
    ncs_st = sb("ncs_st", (128, 22, 4, 2))
    wslab = sb("wslab", (128, NB, 4096), BF16)
    ones_bf = sb("ones_bf", (128, 128), BF16)
    blk64 = sb("blk64", (128, 128), BF16)
    ones_row = sb("ones_row", (1, 128), BF16)
    vones = sb("vones", (128, 9, 128), BF16)
    vtok = sb("vtok", (128, NT))
    flag = sb("flag", (128, 1))
    cTs = sb("cTs", (128, 8, 5))
    cs = sb("cs", (128, 8, 5))
    E = sb("E", (128, 2, 8, 128))
    negb = sb("negb", (128, 8))
    modT = sb("modT", (128, 48, 5))
    A12 = sb("A12", (128, 2, 8, 5))
    nrm = sb("nrm", (128, 16))
    bada = sb("bada", (128, 48))
    qkg = sb("qkg", (128, 2))
    vg = sb("vg", (128, 512))
    bsp = sb("bsp", (1, 512), BF16)
    bsp_f = sb("bsp_f", (1, 768))
    bsps = sb("bsps", (1, 256), BF16)
    WcT = sb("WcT", (128, 4, 128), BF16)
    Wblk = sb("Wblk", (64, 4, 64), BF16)
    cw = sb("cw", (128, 22, 3))
    cb = sb("cb", (128, 22))
    kTs = sb("kTs", (128, 4, 64), BF16)
    vts = sb("vts", (16, 4, 512), BF16)
    vbns = sb("vbns", (64, 512), BF16)
    ckb = sb("ckb", (128, 1, 4, 512), BF16)
    cvb = sb("cvb", (128, 1, 4, 512), BF16)
    Pts = sb("Pts", (128, 2, 2, 80), BF16)
    exs = sb("exs", (128, 2, 2, 32))
    rdens = sb("rdens", (128, 2, 32))

    psd = [stack.enter_context(nc.psum_tensor("ps%d" % i, [128, 1024], F32)) for i in range(4)]
    ps = [psd[i // 2][:, (i % 2) * 512:(i % 2 + 1) * 512] for i in range(8)]

    def bk(i):
        return [("ps", i)]

    bank_ctr = [0]

    def nb():
        b = bank_ctr[0] % 8
        bank_ctr[0] += 1
        return b

    rot = {}

    def nrot(name, n):
        v = rot.get(name, 0)
        rot[name] = v + 1
        return v % n

    ws = dict(seq=[], issued=0, consumed=0)

    def ws_issue(upto):
        while ws["issued"] < min(upto, len(ws["seq"])):
            src, ncols, cast, rkeys = ws["seq"][ws["issued"]]
            slot = ws["issued"] % NB
            if ncols == "ada":
                dst = wslab[:, slot, :].bitcast(F32).rearrange("p (k m) -> p k m", k=8)
            else:
                dst = wslab[:, slot, 0:ncols]
            P.op("pool" if cast else "sp", "dma_start", out=dst, in_=src,
                 r=rkeys, w=[("wslab", slot)], semkey=("w", slot))
            ws["issued"] += 1

    def ws_get():
        assert ws["consumed"] < len(ws["seq"])
        ws_issue(ws["consumed"] + 1)
        slot = ws["consumed"] % NB
        ws["consumed"] += 1
        return slot

    def ws_done():
        ws_issue(ws["consumed"] + NB)

    def seq_layer_setup(l):
        for s in range(12):
            for h in range(2):
                src = wada_d[l, s].rearrange("p (k m) -> p k m", k=8)[:, :, h * 256:(h + 1) * 256]
                ws["seq"].append((src, "ada", False, []))

    def seq_block(l, kind):
        if kind == "kv":
            for s in (1, 2):
                ws["seq"].append((wsc[l, s], 4096, False, [("wsc", l, s)]))
            return
        for s in range(24):
            ws["seq"].append((wsc[l, s], 4096, False, [("wsc", l, s)]))
        for s in range(8):
            ws["seq"].append((wsc_fo[l, s], 2816, False, [("wscfo", l, s)]))

    for l in range(2):
        seq_layer_setup(l)
        seq_block(l, "kv")
        for _ in (L0_BLOCKS if l == 0 else L1_BLOCKS):
            seq_block(l, "full")
        seq_block(l, "sample")

    P.op("pool", "memset", ones_bf[:, :], 1.0, w=["ones_bf"])
    P.op("pool", "memset", blk64[:, :], 0.0, w=["blk64"])
    P.op("pool", "memset", blk64[0:64, 0:64], 1.0, w=["blk64"])
    P.op("pool", "memset", blk64[64:128, 64:128], 1.0, w=["blk64"])
    P.op("pool", "memset", ones_row[:, :], 1.0, w=["ones_row"])
    P.op("pool", "memset", Pt[:, :, :, :, :].rearrange("p a b c d -> p (a b c d)"), 0.0, w=[("Pt", s, h) for s in range(2) for h in range(2)])
    P.op("pool", "memset", halo[:, :, :], 0.0, w=["halo"])
    P.op("sp", "dma_start", out=vtok[:, :], in_=vtok_d, w=["vtok"], semkey="c0")
    P.op("sp", "dma_start", out=flag[:, :], in_=flag_d, w=["flag"], semkey="c0")
    P.op("sp", "dma_start", out=cTs[:, :, :], in_=cT_d, w=["cTs"], semkey="c0")
    P.op("act", "activation", out=cs[:, :, :], in_=cTs[:, :, :], func=AF.Silu, r=["cTs"], w=["cs"])
    for t in range(9):
        P.op("act", "activation", out=vones[:, t, :], in_=ones_bf[:, :], func=AF.Copy,
             scale=vtok[:, t:t + 1], r=["ones_bf", "vtok"], w=["vones"])
    PC_GROUPS = [(1, 3), (0, 1), (3, 6), (6, 9), (9, 13), (13, 19), (19, 24)]
    for l in range(2):
        for (g0, g1) in PC_GROUPS:
            P.op("pool", "dma_start", out=wsc[l, g0:g1], in_=wall_d[l, g0:g1],
                 w=[("wsc", l, s) for s in range(g0, g1)], semkey=("pc", l, g0))
        P.op("pool", "dma_start", out=wsc_fo[l], in_=wfo_d[l], w=[("wscfo", l, s) for s in range(8)],
             semkey=("pcf", l))

    def layer_setup(l):
        for dst, src, key in ((nrm, nrm_d[l], "nrm"), (bada, bada_d[l], "bada"), (qkg, qkg_d[l], "qkg"),
                              (vg, vg_d[l], "vg"), (cw, cw_d[l], "cw"), (cb, cb_d[l], "cb"),
                              (negb, b256_d[l], "negb")):
            full = tuple(slice(None) for _ in dst.shape)
            P.op("sp", "dma_start", out=dst[full], in_=src, w=[key], semkey="ls")
        P.op("sp", "dma_start", out=E[:, :, :, :], in_=Tb_d[l], w=["E"], semkey="ls")
        P.op("sp", "dma_start", out=halos[:, :, :, :], in_=cconv_d[l], w=["halos"], semkey="ls")
        P.op("sp", "dma_start", out=bsp_f[:, 0:512], in_=bsp_d[l], w=["bsp_f"], semkey="ls")
        P.op("sp", "dma_start", out=bsp_f[:, 512:768], in_=bsps_d[l], w=["bsp_f"], semkey="ls")
        P.op("sp", "dma_start", out=gc[:, 0, :], in_=wspT_d[l].rearrange("p g t -> p (g t)"),
             w=[("gc", 0)], semkey="ls3")
        P.op("sp", "dma_start", out=gc[:, 1, :], in_=tril_d.rearrange("p g t -> p (g t)"),
             w=[("gc", 1)], semkey="ls3")
        P.op("sp", "dma_start", out=ge[0:64, 0, 0:256], in_=wspb_d[l].rearrange("p g t -> p (g t)"),
             w=[("ge", 0)], semkey="ls3")
        P.op("sp", "dma_start", out=ge[0:64, 1, 0:256], in_=trilb_d.rearrange("p g t -> p (g t)"),
             w=[("ge", 1)], semkey="ls3")
        P.op("dve", "tensor_tensor", out=WcT[:, :, :].rearrange("p g t -> p (g t)"), in0=gc[:, 0, :],
             in1=gc[:, 1, :], op=ALU.mult, r=[("gc", 0), ("gc", 1)], w=["WcT"])
        P.op("dve", "tensor_tensor", out=Wblk[:, :, :].rearrange("p g t -> p (g t)"), in0=ge[0:64, 0, 0:256],
             in1=ge[0:64, 1, 0:256], op=ALU.mult, r=[("ge", 0), ("ge", 1)], w=["Wblk"])
        P.op("dve", "tensor_copy", bsp[:, :], bsp_f[:, 0:512], r=["bsp_f"], w=["bsp"])
        P.op("dve", "tensor_copy", bsps[:, :], bsp_f[:, 512:768], r=["bsp_f"], w=["bsps"])
        P.op("dve", "tensor_scalar_mul", qkg[:, 1:2], qkg[:, 1:2], 8.0, r=["qkg"], w=["qkg"])
        P.op("dve", "tensor_scalar_mul", negb[:, :], negb[:, :], -1.0, r=["negb"], w=["negb"])
        for t in range(2):
            for h in range(8):
                P.op("act", "activation", out=E[:, t, h, :], in_=E[:, t, h, :], func=AF.Exp,
                     bias=negb[:, h:h + 1], scale=1.0, r=["E", "negb"], w=["E"])
        P.op("pool", "memset", E[64:128, 1, :, 0:64], 0.0, r=["E"], w=["E"])
        b = nb()
        for hsl in range(24):
            slot = ws_get()
            wv = wslab[:, slot, :].bitcast(F32)
            for m in range(2):
                ci = hsl * 2 + m
                for kc in range(8):
                    P.op("pe", "matmul", ps[b][:, ci * 5:(ci + 1) * 5],
                         lhsT=wv[:, kc * 256 + m * 128: kc * 256 + (m + 1) * 128],
                         rhs=cs[:, kc, :], start=(kc == 0), stop=(kc == 7),
                         r=[("wslab", slot), "cs"], w=bk(b))
            ws_done()
        for bb in range(5):
            P.op("dve", "tensor_tensor", out=modT[:, :, bb],
                 in0=ps[b][:, 0:240].rearrange("p (c b) -> p c b", b=5)[:, :, bb], in1=bada[:, :],
                 op=ALU.add, r=bk(b) + ["bada"], w=["modT"])
        for which in range(2):
            sc0 = 8 if which == 0 else 32
            for bb in range(5):
                P.op("dve", "tensor_scalar", out=A12[:, which, :, bb], in0=modT[:, sc0:sc0 + 8, bb],
                     scalar1=1.0, scalar2=32.0, op0=ALU.add, op1=ALU.mult, r=["modT"], w=["A12"])
                P.op("dve", "tensor_tensor", out=A12[:, which, :, bb], in0=A12[:, which, :, bb],
                     in1=nrm[:, which * 8:(which + 1) * 8], op=ALU.mult, r=["A12", "nrm"], w=["A12"])

    def norm_mod(which, T, groups):
        b = nb()
        for c in range(8):
            r_ = nrot("sq", 3)
            P.op("act", "activation", out=sq[:, r_, 0:T], in_=cur["xT"][:, c, 0:T], func=AF.Square,
                 r=[("xT", cur["par"], c)], w=[("sq", r_)])
            P.op("pe", "matmul", ps[b][:, 0:T], lhsT=ones_bf[:, :], rhs=sq[:, r_, 0:T],
                 start=(c == 0), stop=(c == 7), r=[("sq", r_), "ones_bf"], w=bk(b))
        P.op("act", "activation", out=rstd[:, 0:T], in_=ps[b][:, 0:T], func=AF.Sqrt, bias=float(D * EPS),
             scale=1.0, r=bk(b), w=["rstd"])
        P.op("dve", "reciprocal", out=rstd[:, 0:T], in_=rstd[:, 0:T], r=["rstd"], w=["rstd"])
        shc = 0 if which == 0 else 24
        for c in range(8):
            r_ = nrot("tmpn", 2)
            for (c0, n, bb) in groups:
                P.op("dve", "scalar_tensor_tensor", out=tmpn[:, r_, c0:c0 + n], in0=cur["xT"][:, c, c0:c0 + n],
                     scalar=A12[:, which, c, bb:bb + 1], in1=rstd[:, c0:c0 + n], op0=ALU.mult, op1=ALU.mult,
                     r=[("xT", cur["par"], c), "A12", "rstd"], w=[("tmpn", r_)])
            for (c0, n, bb) in groups:
                P.op("act", "activation", out=hT[:, c, c0:c0 + n], in_=tmpn[:, r_, c0:c0 + n],
                     func=AF.Identity, bias=modT[:, shc + c, bb:bb + 1], scale=1.0,
                     r=[("tmpn", r_), "modT"], w=[("hT", c)])

    def proj_fm(b, slot, nkc, ms, col0, rhs_t, rkeys, T):
        for kc in range(nkc):
            P.op("pe", "matmul", ps[b][:, 0:T], lhsT=wslab[:, slot, kc * ms + col0: kc * ms + col0 + 128],
                 rhs=rhs_t(kc), start=(kc == 0), stop=(kc == nkc - 1),
                 r=[("wslab", slot)] + rkeys(kc), w=bk(b))

    def headnorm(b, T, gcol, out_ap, out_keys):
        r_ = nrot("sq", 3)
        P.op("act", "activation", out=sq[:, r_, 0:T], in_=ps[b][:, 0:T], func=AF.Square, r=bk(b), w=[("sq", r_)])
        b2 = nb()
        P.op("pe", "matmul", ps[b2][:, 0:T], lhsT=blk64[:, :], rhs=sq[:, r_, 0:T], start=True, stop=True,
             r=[("sq", r_), "blk64"], w=bk(b2))
        r2 = nrot("rs", 2)
        P.op("act", "activation", out=rs[:, r2, 0:T], in_=ps[b2][:, 0:T], func=AF.Sqrt, bias=float(64 * EPS),
             scale=1.0, r=bk(b2), w=[("rs", r2)])
        P.op("dve", "reciprocal", out=rs[:, r2, 0:T], in_=rs[:, r2, 0:T], r=[("rs", r2)], w=[("rs", r2)])
        P.op("dve", "scalar_tensor_tensor", out=out_ap, in0=ps[b][:, 0:T], scalar=qkg[:, gcol:gcol + 1],
             in1=rs[:, r2, 0:T], op0=ALU.mult, op1=ALU.mult, r=bk(b) + [("rs", r2), "qkg"], w=out_keys)

    def hT_r(T):
        return (lambda kc: hT[:, kc, 0:T]), (lambda kc: [("hT", kc)])

    def block(l, kind, t0, ntile):
        sample = kind == "sample"
        T = 64 if sample else ntile * 128
        tiles = [] if sample else list(range(t0, t0 + ntile))
        groups = [(bl * 16, 16, 1 + bl) for bl in range(4)] if sample else [(0, T, 0)]
        hr, hk = hT_r(T)
        bi = cur["idx"]
        cur["par"] = bi % 2
        cur["xT"] = xTt[:, bi % 2]

        def xload(i):
            l_, kind_, t0_, n_ = BLOCKS[i]
            par_ = i % 2
            wk = [("xT", par_, c) for c in range(8)]
            if kind_ == "sample":
                if l_ == 1:
                    return False
                P.op("sp", "dma_start", out=xTt[:, par_, :, 0:64], in_=xs_d, w=wk, semkey=("xld", par_))
                return True
            T_ = n_ * 128
            if l_ == 0:
                P.op("sp", "dma_start", out=xTt[:, par_, :, 0:T_], in_=xin[:, :, t0_ * 128: t0_ * 128 + T_], w=wk,
                     semkey=("xld", par_))
            else:
                P.op("sp", "dma_start", out=xTt[:, par_, :, 0:T_],
                     in_=x1s[:, :, (t0_ - 4) * 128: (t0_ - 4) * 128 + T_],
                     r=[("x1s", t) for t in range(t0_, t0_ + n_)], w=wk, semkey=("xld", par_))
            return True

        if bi == 0:
            xload(0)
        if sample and l == 1:
            P.op("pool", "tensor_copy", cur["xT"][:, :, 0:64], xs_keep[:, :, :], r=["xs_keep"],
                 w=[("xT", cur["par"], c) for c in range(8)])
        if bi + 1 < len(BLOCKS):
            xload(bi + 1)
        cur["idx"] = bi + 1
        norm_mod(0, T, groups)

        if kind == "kv":
            slabs = {1: ws_get()}
        else:
            slabs = {0: ws_get()}
            for c in range(4):
                b = nb()
                proj_fm(b, slabs[0], 8, 512, c * 128, hr, hk, T)
                headnorm(b, T, 0, qT[:, c, 0:T], [("qT", c)])
            ws_done()
            slabs[1] = ws_get()
        for c in range(4):
            b = nb()
            proj_fm(b, slabs[1], 8, 512, c * 128, hr, hk, T)
            r_ = nrot("kst", 2)
            headnorm(b, T, 1, kst[:, r_, 0:T], [("kst", r_)])
            if sample:
                P.op("act", "activation", out=kTs[:, c, :], in_=kst[:, r_, 0:64], func=AF.Copy,
                     r=[("kst", r_)], w=["kTs"])
                P.op("sp", "dma_start", out=nksT_d[l, :, c, :], in_=kst[:, r_, 0:64], r=[("kst", r_)],
                     semkey=("kst", r_))
            else:
                for ti, t in enumerate(tiles):
                    sl = t % 8
                    P.op("act", "activation", out=kT[:, c, sl * 128:(sl + 1) * 128],
                         in_=kst[:, r_, ti * 128:(ti + 1) * 128], func=AF.Copy, r=[("kst", r_)], w=[("kT", sl)])
                    if t >= KEEP_T0:
                        P.op("sp", "dma_start", out=nkT_d[l, :, c, (t - KEEP_T0) * 128:(t - KEEP_T0 + 1) * 128],
                             in_=kst[:, r_, ti * 128:(ti + 1) * 128], r=[("kst", r_)], semkey=("kst", r_))
        ws_done()
        slot = ws_get()
        if sample:
            for bl in range(4):
                b = nb()
                for kc in range(8):
                    P.op("pe", "matmul", ps[b][0:16, 0:512], lhsT=hT[:, kc, bl * 16:(bl + 1) * 16],
                         rhs=wslab[:, slot, kc * 512:(kc + 1) * 512], start=(kc == 0), stop=(kc == 7),
                         r=[("wslab", slot), ("hT", kc)], w=bk(b))
                r_ = nrot("gl", 2)
                P.op("act", "activation", out=gl[0:16, r_, :], in_=ps[b][0:16, 0:512], func=AF.Copy, r=bk(b),
                     w=[("gl", r_)])
                P.op("dve", "tensor_copy", vts[:, bl, :], gl[0:16, r_, :], r=[("gl", r_)], w=["vts"])
                P.op("sp", "dma_start", out=nvs_d[l, :, bl, :], in_=gl[0:16, r_, :], r=[("gl", r_)],
                     semkey=("gl", r_))
        else:
            for ti, t in enumerate(tiles):
                b = nb()
                for kc in range(8):
                    P.op("pe", "matmul", ps[b][:, 0:512], lhsT=hT[:, kc, ti * 128:(ti + 1) * 128],
                         rhs=wslab[:, slot, kc * 512:(kc + 1) * 512], start=(kc == 0), stop=(kc == 7),
                         r=[("wslab", slot), ("hT", kc)], w=bk(b))
                sl = t % 8
                if t >= KEEP_T0:
                    r_ = nrot("gl", 2)
                    P.op("act", "activation", out=gl[:, r_, :], in_=ps[b][:, 0:512], func=AF.Copy, r=bk(b),
                         w=[("gl", r_)])
                    P.op("dve", "tensor_scalar_mul", vt[:, sl, :], gl[:, r_, :], vtok[:, t:t + 1],
                         r=[("gl", r_), "vtok"], w=[("vt", sl)])
                    P.op("sp", "dma_start", out=nv_d[l, t - KEEP_T0], in_=gl[:, r_, :], r=[("gl", r_)],
                         semkey=("gl", r_))
                else:
                    P.op("dve", "tensor_scalar_mul", vt[:, sl, :], ps[b][:, 0:512], vtok[:, t:t + 1],
                         r=bk(b) + ["vtok"], w=[("vt", sl)])
        ws_done()
        if kind == "kv":
            return

        ck(1)
        slot = ws_get()
        for c in range(4):
            b = nb()
            proj_fm(b, slot, 8, 512, c * 128, hr, hk, T)
            P.op("act", "activation", out=ubT[:, c, 0:T], in_=ps[b][:, 0:T], func=AF.Gelu_apprx_tanh, r=bk(b),
                 w=[("ubT", c)])
        ws_done()
        ck(2)
        slot = ws_get()
        tl = [(0, 64)] if sample else [(ti, 128) for ti in range(ntile)]
        for ti, M in tl:
            b = nb()
            for kc in range(8):
                P.op("pe", "matmul", ps[b][0:M, 0:512], lhsT=hT[:, kc, ti * 128: ti * 128 + M],
                     rhs=wslab[:, slot, kc * 512:(kc + 1) * 512], start=(kc == 0), stop=(kc == 7),
                     r=[("wslab", slot), ("hT", kc)], w=bk(b))
            r_ = nrot("gl", 2)
            P.op("act", "activation", out=gl[0:M, r_, :], in_=ps[b][0:M, 0:512], func=AF.Gelu_apprx_tanh, r=bk(b),
                 w=[("gl", r_)])
            r2 = nrot("tmpn", 2)
            P.op("dve", "tensor_tensor", out=tmpn[0:M, r2, :], in0=gl[0:M, r_, :], in1=gl[0:M, r_, :], op=ALU.mult,
                 r=[("gl", r_)], w=[("tmpn", r2)])
            r3 = nrot("ssv", 4)
            P.op("dve", "reduce_sum", out=ssv[0:M, r3:r3 + 1], in_=tmpn[0:M, r2, :], axis=AX.X,
                 r=[("tmpn", r2)], w=[("ssv", r3)])
            P.op("act", "activation", out=ssv[0:M, r3:r3 + 1], in_=ssv[0:M, r3:r3 + 1], func=AF.Sqrt,
                 bias=float(EPS), scale=1.0 / 512.0, r=[("ssv", r3)], w=[("ssv", r3)])
            P.op("dve", "reciprocal", out=ssv[0:M, r3:r3 + 1], in_=ssv[0:M, r3:r3 + 1], r=[("ssv", r3)],
                 w=[("ssv", r3)])
            if sample:
                P.op("dve", "scalar_tensor_tensor", out=tmpn[0:64, r2, :], in0=gl[0:64, r_, :],
                     scalar=ssv[0:64, r3:r3 + 1], in1=vg[0:64, :], op0=ALU.mult, op1=ALU.mult,
                     r=[("gl", r_), ("ssv", r3), "vg"], w=[("tmpn", r2)])
                P.op("act", "activation", out=vbns[:, :], in_=tmpn[0:64, r2, :], func=AF.Copy, r=[("tmpn", r2)],
                     w=["vbns"])
                P.op("sp", "dma_start", out=nbs_d[l], in_=tmpn[0:64, r2, :], r=[("tmpn", r2)], semkey=("tmpn", r2))
            else:
                P.op("dve", "scalar_tensor_tensor", out=vbn[:, ti, :], in0=gl[:, r_, :], scalar=ssv[:, r3:r3 + 1],
                     in1=vg[:, :], op0=ALU.mult, op1=ALU.mult, r=[("gl", r_), ("ssv", r3), "vg"], w=[("vbn", ti)])
        ws_done()
        ck(3)
        for s in range(4):
            slot = ws_get()
            for m in range(4):
                c = s * 4 + m
                b = nb()
                proj_fm(b, slot, 8, 512, m * 128, hr, hk, T)
                P.op("act", "activation", out=SA[:, c, 0:T], in_=ps[b][:, 0:T], func=AF.Sigmoid, r=bk(b),
                     w=[("SA", c)])
            ws_done()

        ck(4)
        if sample:
            items = [(bl, hp) for bl in range(4) for hp in range(4)]

            def s1(it, st):
                bl, hp = it
                cr = 0
                for j in range(5):
                    for hh in range(2):
                        hs = slice(hh * 64, (hh + 1) * 64)
                        b = st * 4 + hh
                        if j < 4:
                            P.op("pe", "matmul", ps[b][:, j * 16:(j + 1) * 16],
                                 lhsT=ckb[hs, cr, hp, j * 128:(j + 1) * 128], rhs=qT[hs, hp, bl * 16:(bl + 1) * 16],
                                 start=True, stop=True, r=[("ckb", cr), ("qT", hp)], w=[("ps", b)])
                        else:
                            P.op("pe", "matmul", ps[b][0:16, 64:80],
                                 lhsT=kTs[hs, hp, bl * 16:(bl + 1) * 16], rhs=qT[hs, hp, bl * 16:(bl + 1) * 16],
                                 start=True, stop=True, r=["kTs", ("qT", hp)], w=[("ps", b)])
                for hh in range(2):
                    b = st * 4 + hh
                    h = 2 * hp + hh
                    P.op("act", "activation", out=Pts[:, st, hh, 0:48], in_=ps[b][:, 0:48], func=AF.Exp,
                         r=[("ps", b)], w=[("Pts", st, hh)])
                    P.op("act", "activation", out=exs[:, st, hh, 0:16], in_=ps[b][:, 48:64], func=AF.Exp,
                         r=[("ps", b)], w=[("exs", st, hh)])
                    P.op("act", "activation", out=exs[0:16, st, hh, 16:32], in_=ps[b][0:16, 64:80], func=AF.Exp,
                         r=[("ps", b)], w=[("exs", st, hh)])
                    P.op("dve", "tensor_tensor", out=Pts[:, st, hh, 48:64], in0=exs[:, st, hh, 0:16],
                         in1=E[:, 0, h, 0:16], op=ALU.mult, r=[("exs", st, hh), "E"], w=[("Pts", st, hh)])
                    P.op("dve", "tensor_tensor", out=Pts[0:16, st, hh, 64:80], in0=exs[0:16, st, hh, 16:32],
                         in1=E[0:16, 1, h, 0:16], op=ALU.mult, r=[("exs", st, hh), "E"], w=[("Pts", st, hh)])

            def s2(it, st):
                bl, hp = it
                cr = 0
                bd, bo = st * 4 + 2, st * 4 + 3
                for hh in range(2):
                    for j in range(5):
                        if j < 4:
                            P.op("pe", "matmul", ps[bd][:, hh * 16:(hh + 1) * 16], lhsT=ones_bf[:, :],
                                 rhs=Pts[:, st, hh, j * 16:(j + 1) * 16], start=(j == 0), stop=False,
                                 r=[("Pts", st, hh), "ones_bf"], w=[("ps", bd)])
                        else:
                            P.op("pe", "matmul", ps[bd][:, hh * 16:(hh + 1) * 16], lhsT=ones_bf[0:16, :],
                                 rhs=Pts[0:16, st, hh, 64:80], start=False, stop=True,
                                 r=[("Pts", st, hh), "ones_bf"], w=[("ps", bd)])
                for hh in range(2):
                    hs = slice(hh * 64, (hh + 1) * 64)
                    fc = (2 * hp + hh) * 64
                    for j in range(5):
                        if j < 4:
                            P.op("pe", "matmul", ps[bo][hs, 0:16], lhsT=cvb[:, cr, j, fc:fc + 64],
                                 rhs=Pts[:, st, hh, j * 16:(j + 1) * 16], start=(j == 0), stop=False,
                                 r=[("Pts", st, hh), ("cvb", cr)], w=[("ps", bo)])
                        else:
                            P.op("pe", "matmul", ps[bo][hs, 0:16], lhsT=vts[0:16, bl, fc:fc + 64],
                                 rhs=Pts[0:16, st, hh, 64:80], start=False, stop=True,
                                 r=[("Pts", st, hh), "vts"], w=[("ps", bo)])
                P.op("dve", "reciprocal", out=rdens[:, st, :], in_=ps[bd][:, 0:32], r=[("ps", bd)],
                     w=[("rdens", st)])
                for hh in range(2):
                    hs = slice(hh * 64, (hh + 1) * 64)
                    P.op("dve", "tensor_tensor", out=oaT[hs, hp, bl * 16:(bl + 1) * 16], in0=ps[bo][hs, 0:16],
                         in1=rdens[hs, st, hh * 16:(hh + 1) * 16], op=ALU.mult, r=[("ps", bo), ("rdens", st)],
                         w=[("oaT", hp)])
        else:
            items = [(qi, hp) for qi in range(ntile) for hp in range(4)]

            def sview(st, hh, j):
                if j < 4:
                    return st * 4 + hh, slice(j * 128, (j + 1) * 128)
                return st * 4 + 2 + hh, slice(0, 128)

            def s1(it, st):
                qi, hp = it
                qt = t0 + qi
                for j in range(5):
                    sl = (qt - 4 + j) % 8
                    for hh in range(2):
                        hs = slice(hh * 64, (hh + 1) * 64)
                        b, cs_ = sview(st, hh, j)
                        P.op("pe", "matmul", ps[b][:, cs_],
                             lhsT=kT[hs, hp, sl * 128:(sl + 1) * 128], rhs=qT[hs, hp, qi * 128:(qi + 1) * 128],
                             start=True, stop=True, r=[("kT", sl), ("qT", hp)], w=[("ps", b)])
                pk = [("Pt", st, 0), ("Pt", st, 1)]
                rk2 = [("ps", st * 4), ("ps", st * 4 + 1)]
                rk4 = [("ps", st * 4 + 2), ("ps", st * 4 + 3)]
                two = "p (h c) -> p h c"
                P.op("act", "activation", out=Pt[64:128, st, :, 0, :],
                     in_=psd[st * 2][64:128, :].rearrange(two, h=2)[:, :, 0:128], func=AF.Exp, r=rk2, w=pk)
                P.op("act", "activation", out=Pt[0:64, st, :, 0, 0:64],
                     in_=psd[st * 2][0:64, :].rearrange(two, h=2)[:, :, 0:64], func=AF.Exp, r=rk2, w=pk)
                P.op("act", "activation", out=Pt[:, st, :, 1:4, :].rearrange("p h j q -> p h (j q)"),
                     in_=psd[st * 2][:, :].rearrange(two, h=2)[:, :, 128:512], func=AF.Exp, r=rk2, w=pk)
                P.op("act", "activation", out=Pt[:, st, :, 4, :],
                     in_=psd[st * 2 + 1][:, :].rearrange(two, h=2)[:, :, 0:128], func=AF.Exp, r=rk4, w=pk)
                for jj in (3, 4):
                    P.op("dve", "tensor_tensor", out=Pt[:, st, :, jj, :], in0=Pt[:, st, :, jj, :],
                         in1=E[:, jj - 3, 2 * hp:2 * hp + 2, :], op=ALU.mult, r=pk + ["E"], w=pk)

            def s2(it, st):
                qi, hp = it
                qt = t0 + qi
                bd, bo = st * 4 + 2, st * 4 + 3
                for hh in range(2):
                    for j in range(5):
                        kt = qt - 4 + j
                        lw = vones[:, kt, :] if kt < 9 else ones_bf[:, :]
                        P.op("pe", "matmul", ps[bd][:, 256 + hh * 128:256 + (hh + 1) * 128], lhsT=lw,
                             rhs=Pt[:, st, hh, j, :], start=(j == 0), stop=(j == 4),
                             r=[("Pt", st, hh), "vones", "ones_bf"], w=[("ps", bd)])
                for hh in range(2):
                    hs = slice(hh * 64, (hh + 1) * 64)
                    fc = (2 * hp + hh) * 64
                    for j in range(5):
                        sl = (qt - 4 + j) % 8
                        P.op("pe", "matmul", ps[bo][hs, 256:384], lhsT=vt[:, sl, fc:fc + 64],
                             rhs=Pt[:, st, hh, j, :], start=(j == 0), stop=(j == 4),
                             r=[("Pt", st, hh), ("vt", sl)], w=[("ps", bo)])
                P.op("dve", "tensor_scalar_max", rden[:, st, :], ps[bd][:, 256:512], 1e-30, r=[("ps", bd)],
                     w=[("rden", st)])
                P.op("dve", "reciprocal", out=rden[:, st, :], in_=rden[:, st, :], r=[("rden", st)],
                     w=[("rden", st)])
                for hh in range(2):
                    hs = slice(hh * 64, (hh + 1) * 64)
                    P.op("dve", "tensor_tensor", out=oaT[hs, hp, qi * 128:(qi + 1) * 128], in0=ps[bo][hs, 256:384],
                         in1=rden[hs, st, hh * 128:(hh + 1) * 128], op=ALU.mult, r=[("ps", bo), ("rden", st)],
                         w=[("oaT", hp)])

        if sample:
            groups_it = [[(bl, hp) for hp in range(4)] for bl in range(4)]
        else:
            groups_it = [items]
        for its in groups_it:
            if sample:
                bl = its[0][0]
                P.op("pool", "dma_start", out=ckb[:, 0, :, :], in_=ckT_d[l, :, :, bl, :], w=[("ckb", 0)],
                     semkey=("ckb", 0))
                P.op("pool", "dma_start", out=cvb[:, 0, :, :], in_=cv_d[l, bl], w=[("cvb", 0)],
                     semkey=("cvb", 0))
            for i in range(len(its) + 1):
                if i < len(its):
                    s1(its[i], i % 2)
                if i >= 1:
                    s2(its[i - 1], (i - 1) % 2)

        ck(5)
        if sample:
            b = nb()
            for g in range(4):
                P.op("pe", "matmul", ps[b][:, g * 64:(g + 1) * 64], lhsT=vbns[0:64, g * 128:(g + 1) * 128],
                     rhs=Wblk[0:64, g, :], start=True, stop=False, r=["vbns", "Wblk"], w=bk(b))
                P.op("pe", "matmul", ps[b][:, g * 64:(g + 1) * 64], lhsT=ones_row[0:1, :],
                     rhs=bsps[0:1, g * 64:(g + 1) * 64], start=False, stop=True, r=["ones_row", "bsps"], w=bk(b))
            P.op("dve", "tensor_tensor", out=obT[:, :, 0:64], in0=ps[b][:, 0:256].rearrange("p (g t) -> p g t", g=4),
                 in1=ubT[:, :, 0:64], op=ALU.mult, r=bk(b) + [("ubT", c) for c in range(4)],
                 w=[("obT", c) for c in range(4)])
        else:
            for ti in range(ntile):
                b = nb()
                for g in range(4):
                    P.op("pe", "matmul", ps[b][:, g * 128:(g + 1) * 128], lhsT=vbn[:, ti, g * 128:(g + 1) * 128],
                         rhs=WcT[:, g, :], start=True, stop=False, r=[("vbn", ti), "WcT"], w=bk(b))
                    P.op("pe", "matmul", ps[b][:, g * 128:(g + 1) * 128], lhsT=ones_row[0:1, :],
                         rhs=bsp[0:1, g * 128:(g + 1) * 128], start=False, stop=True, r=["ones_row", "bsp"], w=bk(b))
                P.op("dve", "tensor_tensor", out=obT[:, :, ti * 128:(ti + 1) * 128],
                     in0=ps[b][:, 0:512].rearrange("p (g t) -> p g t", g=4), in1=ubT[:, :, ti * 128:(ti + 1) * 128],
                     op=ALU.mult, r=bk(b) + [("ubT", c) for c in range(4)], w=[("obT", c) for c in range(4)])

        ck(6)
        sa_, sb_ = ws_get(), ws_get()
        for c in range(8):
            ba, bb_ = nb(), nb()
            proj_fm(ba, sa_, 4, 1024, c * 128, lambda kc: oaT[:, kc, 0:T], lambda kc: [("oaT", kc)], T)
            proj_fm(bb_, sb_, 4, 1024, c * 128, lambda kc: obT[:, kc, 0:T], lambda kc: [("obT", kc)], T)
            P.op("dve", "tensor_tensor", out=tmpn[:, 0, 0:T], in0=ps[ba][:, 0:T], in1=SA[:, c, 0:T], op=ALU.mult,
                 r=bk(ba) + [("SA", c)], w=[("tmpn", 0)])
            P.op("dve", "tensor_tensor", out=tmpn[:, 1, 0:T], in0=ps[bb_][:, 0:T], in1=SA[:, 8 + c, 0:T], op=ALU.mult,
                 r=bk(bb_) + [("SA", 8 + c)], w=[("tmpn", 1)])
            P.op("pool", "tensor_tensor", out=SA[:, 16 + c, 0:T], in0=tmpn[:, 0, 0:T], in1=tmpn[:, 1, 0:T], op=ALU.add,
                 r=[("tmpn", 0), ("tmpn", 1)], w=[("SA", 16 + c)])
        ws_done()
        ws_done()
        ck(7)
        for s in range(2):
            slot = ws_get()
            for m in range(4):
                c = s * 4 + m
                b = nb()
                proj_fm(b, slot, 8, 512, m * 128, lambda kc: SA[:, 16 + kc, 0:T], lambda kc: [("SA", 16 + kc)], T)
                for (c0, n, bb) in groups:
                    P.op("dve", "scalar_tensor_tensor", out=cur["xT"][:, c, c0:c0 + n], in0=ps[b][:, c0:c0 + n],
                         scalar=modT[:, 16 + c, bb:bb + 1], in1=cur["xT"][:, c, c0:c0 + n], op0=ALU.mult, op1=ALU.add,
                         r=bk(b) + [("xT", cur["par"], c), "modT"], w=[("xT", cur["par"], c)])
            ws_done()

        ck(8)
        norm_mod(1, T, groups)
        GW = 72 if sample else T + 2
        for jj in range(11):
            slot = ws_get()
            for sub in range(2):
                j = 2 * jj + sub
                bg, bu = nb(), nb()
                proj_fm(bg, slot, 8, 512, sub * 128, hr, hk, T)
                proj_fm(bu, slot, 8, 512, 256 + sub * 128, hr, hk, T)
                r_ = nrot("gb", 3)
                if sample:
                    g3 = gb[:, r_, 0:72].rearrange("p (b t) -> p b t", t=18)
                    P.op("act", "activation", out=g3[:, :, 2:18],
                         in_=ps[bg][:, 0:64].rearrange("p (b t) -> p b t", t=16), func=AF.Copy, r=bk(bg),
                         w=[("gb", r_)])
                    P.op("pool", "tensor_copy", g3[:, :, 0:2], halos[:, j, :, :], r=["halos"], w=[("gb", r_)])
                    P.op("pool", "tensor_copy", ncs_st[:, j, :, :], g3[:, :, 16:18], r=[("gb", r_)], w=["ncs_st"])
                    views = [g3[:, :, k:k + 16] for k in range(3)]
                    r2 = nrot("gc", 2)
                    gcv = gc[:, r2, 0:64].rearrange("p (b t) -> p b t", t=16)
                else:
                    P.op("act", "activation", out=gb[:, r_, 2:2 + T], in_=ps[bg][:, 0:T], func=AF.Copy, r=bk(bg),
                         w=[("gb", r_)])
                    P.op("pool", "tensor_copy", gb[:, r_, 0:2], halo[:, j, :], r=["halo"], w=[("gb", r_)])
                    if t0 == 8:
                        P.op("pool", "tensor_scalar_mul", gb[:, r_, 128:130], gb[:, r_, 128:130], flag[:, 0:1],
                             r=[("gb", r_), "flag"], w=[("gb", r_)])
                    P.op("pool", "tensor_copy", halo[:, j, :], gb[:, r_, T:T + 2], r=[("gb", r_)], w=["halo"])
                    views = [gb[:, r_, k:k + T] for k in range(3)]
                    r2 = nrot("gc", 2)
                    gcv = gc[:, r2, 0:T]
                P.op("pool", "tensor_scalar", out=gcv, in0=views[0], scalar1=cw[:, j, 0:1], scalar2=cb[:, j:j + 1],
                     op0=ALU.mult, op1=ALU.add, r=[("gb", r_), "cw", "cb"], w=[("gc", r2)])
                for k in (1, 2):
                    P.op("dve", "scalar_tensor_tensor", out=gcv, in0=views[k], scalar=cw[:, j, k:k + 1], in1=gcv,
                         op0=ALU.mult, op1=ALU.add, r=[("gb", r_), ("gc", r2), "cw"], w=[("gc", r2)])
                r3 = nrot("ge", 2)
                P.op("act", "activation", out=ge[:, r3, 0:T], in_=gc[:, r2, 0:T], func=AF.Gelu_apprx_tanh,
                     r=[("gc", r2)], w=[("ge", r3)])
                P.op("dve", "tensor_tensor", out=SA[:, j, 0:T], in0=ps[bu][:, 0:T], in1=ge[:, r3, 0:T], op=ALU.mult,
                     r=bk(bu) + [("ge", r3)], w=[("SA", j)])
            ws_done()
        for c in range(8):
            slot = ws_get()
            b = nb()
            proj_fm(b, slot, 22, 128, 0, lambda kc: SA[:, kc, 0:T], lambda kc: [("SA", kc)], T)
            for (c0, n, bb) in groups:
                P.op("dve", "scalar_tensor_tensor", out=cur["xT"][:, c, c0:c0 + n], in0=ps[b][:, c0:c0 + n],
                     scalar=modT[:, 40 + c, bb:bb + 1], in1=cur["xT"][:, c, c0:c0 + n], op0=ALU.mult, op1=ALU.add,
                     r=bk(b) + [("xT", cur["par"], c), "modT"], w=[("xT", cur["par"], c)])
            ws_done()

        ck(9)
        xk = [("xT", cur["par"], c) for c in range(8)]
        if sample:
            if l == 0:
                P.op("pool", "tensor_copy", xs_keep[:, :, :], cur["xT"][:, :, 0:64], r=xk, w=["xs_keep"])
            else:
                P.op("sp", "dma_start", out=ysT_d, in_=cur["xT"][:, :, 0:64], r=xk, semkey="xst")
            P.op("sp", "dma_start", out=ncs_d[l], in_=ncs_st[:, :, :, :], r=["ncs_st"], semkey="ncs")
        elif l == 0:
            P.op("sp", "dma_start", out=x1s[:, :, (t0 - 4) * 128:(t0 - 4) * 128 + T], in_=cur["xT"][:, :, 0:T], r=xk,
                 w=[("x1s", t) for t in tiles], semkey="xst")
        else:
            lo = max(t0, OUT_T0)
            c0 = (lo - t0) * 128
            P.op("sp", "dma_start", out=yT_d[:, :, (lo - OUT_T0) * 128:(lo - OUT_T0) * 128 + T - c0],
                 in_=cur["xT"][:, :, c0:T], r=xk, semkey="xst")

    xs_keep = sb("xs_keep", (128, 8, 64))

    BLOCKS = []
    for l in range(2):
        BLOCKS.append((l, "kv", 0 if l == 0 else 4, 4))
        for (t0, n) in (L0_BLOCKS if l == 0 else L1_BLOCKS):
            BLOCKS.append((l, "full", t0, n))
        BLOCKS.append((l, "sample", 0, 0))

    try:
        for l in range(2):
            if stop is not None and stop == -1:
                raise _Stop()
            layer_setup(l)
            if stop is not None and stop == 0:
                raise _Stop()
            if l == 1:
                P.op("pool", "memset", halo[:, :, :], 0.0, r=["halo"], w=["halo"])
            for (l_, kind, t0, n) in [b for b in BLOCKS if b[0] == l]:
                if kind == "sample":
                    P.op("sp", "dma_start", out=ncv_d[l], in_=halo[:, :, :], r=["halo"], semkey="ncv")
                block(l_, kind, t0, n)
                nblk[0] += 1
                if stop is not None and nblk[0] >= stop:
                    raise _Stop()
        assert ws["consumed"] == len(ws["seq"]), (ws["consumed"], len(ws["seq"]))
    except _Stop:
        dbg = dout("dbgx", (128, 8, 512))
        P.op("sp", "dma_start", out=dbg, in_=cur["xT"][:, :, :], r=[("xT", cur["par"], c) for c in range(8)], semkey="dbg")
        dbg2 = dout("dbgh", (128, 8, 512))
        P.op("pool", "dma_start", out=dbg2, in_=tmpn[:, :, :].rearrange("p a b -> p (a b)"), r=[("tmpn", 0), ("tmpn", 1)], semkey="dbg") if False else None

    P.emit(nc, stack)
    stack.close()
    return nc


def _fm(a):
    t, f = a.shape
    return np.ascontiguousarray(a.reshape(t, f // 128, 128).transpose(2, 1, 0))


def _slab(wm):
    k, m = wm.shape
    return np.ascontiguousarray(wm.reshape(k // 128, 128, m).transpose(1, 0, 2)).reshape(128, (k // 128) * m)


_NC_CACHE = {}


def kernel(x_prompt, x_sample, cache_attn_k, cache_attn_v, cache_ffn_conv, c_prompt, c_sample,
           norm1_g, norm2_g, w_ada, b_ada, w_in, q_norm_g, k_norm_g, rel_bias, v_norm_g,
           w_spatial, b_spatial, w_out_a, w_out_b, w_out, w_ffn_in, ffn_conv_w, ffn_conv_b, w_ffn_out):
    f = lambda a: np.asarray(a, dtype=np.float32)
    x_prompt, x_sample, cache_attn_k, cache_attn_v, cache_ffn_conv = map(f, (x_prompt, x_sample, cache_attn_k, cache_attn_v, cache_ffn_conv))
    c_prompt, c_sample, norm1_g, norm2_g, w_ada, b_ada, w_in = map(f, (c_prompt, c_sample, norm1_g, norm2_g, w_ada, b_ada, w_in))
    q_norm_g, k_norm_g, rel_bias, v_norm_g, w_spatial, b_spatial = map(f, (q_norm_g, k_norm_g, rel_bias, v_norm_g, w_spatial, b_spatial))
    w_out_a, w_out_b, w_out, w_ffn_in, ffn_conv_w, ffn_conv_b, w_ffn_out = map(f, (w_out_a, w_out_b, w_out, w_ffn_in, ffn_conv_w, ffn_conv_b, w_ffn_out))

    in_maps = _prep(x_prompt, x_sample, cache_attn_k, cache_attn_v, cache_ffn_conv, c_prompt, c_sample,
                    norm1_g, norm2_g, w_ada, b_ada, w_in, q_norm_g, k_norm_g, rel_bias, v_norm_g,
                    w_spatial, b_spatial, w_out_a, w_out_b, w_out, w_ffn_in, ffn_conv_w, ffn_conv_b, w_ffn_out)
    if "nc" not in _NC_CACHE:
        _NC_CACHE["nc"] = build_nc()
    nc = _NC_CACHE["nc"]
    res = run_bass_kernel_spmd(nc, in_maps, core_ids=list(range(NCORES)))
    return _post(res.results)


def _prep(x_prompt, x_sample, cache_attn_k, cache_attn_v, cache_ffn_conv, c_prompt, c_sample,
          norm1_g, norm2_g, w_ada, b_ada, w_in, q_norm_g, k_norm_g, rel_bias, v_norm_g,
          w_spatial, b_spatial, w_out_a, w_out_b, w_out, w_ffn_in, ffn_conv_w, ffn_conv_b, w_ffn_out):

    wall = np.empty((2, 24, 128, 4096), np.float32)
    wfo = np.empty((2, 8, 128, 2816), np.float32)
    wada = np.empty((2, 12, 128, 4096), np.float32)
    for l in range(2):
        for s in range(9):
            wall[l, s] = _slab(w_in[l][:, s * 512:(s + 1) * 512])
        wall[l, 9] = _slab(w_out_a[l])
        wall[l, 10] = _slab(w_out_b[l])
        for s in range(2):
            wall[l, 11 + s] = _slab(w_out[l][:, s * 512:(s + 1) * 512])
        for jj in range(11):
            idx = np.concatenate([np.arange(2 * jj * 128, (2 * jj + 2) * 128), DFF + np.arange(2 * jj * 128, (2 * jj + 2) * 128)])
            wall[l, 13 + jj] = _slab(w_ffn_in[l][:, idx])
        for c in range(8):
            wfo[l, c] = _slab(w_ffn_out[l][:, c * 128:(c + 1) * 128])
        for s in range(12):
            wada[l, s] = _slab(w_ada[l][:, s * 512:(s + 1) * 512])
    col = lambda v: np.ascontiguousarray(v.reshape(-1, 128).T)
    nrm = np.stack([np.concatenate([col(norm1_g[l]), col(norm2_g[l])], 1) for l in range(2)])
    bada = np.stack([col(b_ada[l]) for l in range(2)])
    qkg = np.stack([np.stack([np.tile(q_norm_g[l], 2), np.tile(k_norm_g[l], 2)], 1) for l in range(2)])
    vg = np.stack([np.broadcast_to(v_norm_g[l][None, :], (128, 512)) for l in range(2)]).copy()
    bsp = b_spatial.reshape(2, 1, 512).copy()
    bsps = np.stack([np.tile(b_spatial[l][:, None, :16], (1, 4, 1)).reshape(1, 256) for l in range(2)])
    wspT = np.ascontiguousarray(w_spatial.transpose(0, 3, 1, 2))
    wspb = np.zeros((2, 64, 4, 64), np.float32)
    for bl in range(4):
        wspb[:, bl * 16:(bl + 1) * 16, :, bl * 16:(bl + 1) * 16] = wspT[:, 0:16, :, 0:16]
    s_i = np.arange(128)[:, None]
    t_i = np.arange(128)[None, :]
    tril = np.broadcast_to((s_i <= t_i).astype(np.float32)[:, None, :], (128, 4, 128)).copy()
    trilb = np.zeros((64, 4, 64), np.float32)
    for bl in range(4):
        trilb[bl * 16:(bl + 1) * 16, :, bl * 16:(bl + 1) * 16] = tril[0:16, :, 0:16]
    cw = np.ascontiguousarray(ffn_conv_w.reshape(2, 3, 22, 128).transpose(0, 3, 2, 1))
    cb = np.ascontiguousarray(ffn_conv_b.reshape(2, 22, 128).transpose(0, 2, 1))
    ki = np.arange(128)[:, None]
    qi = np.arange(128)[None, :]
    idx1 = np.clip(128 + qi - ki, -128, 128) + 128
    idx0 = np.clip(qi - ki, -128, 128) + 128
    Tb = np.stack([np.stack([rel_bias[l][:, idx1].transpose(1, 0, 2), rel_bias[l][:, idx0].transpose(1, 0, 2)], 1)
                   for l in range(2)])
    b256 = np.stack([np.broadcast_to(rel_bias[l][None, :, 256], (128, 8)) for l in range(2)]).copy()

    shared = dict(wall=wall, wfo=wfo, wada=wada, nrm=nrm, bada=bada, qkg=qkg, vg=vg, bsp=bsp, bsps=bsps, wspT=wspT,
                  wspb=wspb, tril=tril, trilb=trilb, cw=cw, cb=cb, Tb=np.ascontiguousarray(Tb), b256=b256)
    shared = {k: np.ascontiguousarray(v, dtype=np.float32) for k, v in shared.items()}

    in_maps = []
    for core in range(NCORES):
        b, seg = core // 4, core % 4
        s0 = seg * SEG
        a = s0 - HALO
        xw = np.zeros((W, D), np.float32)
        lo = max(a, 0)
        xw[lo - a:] = x_prompt[b, lo:s0 + SEG]
        valid = (np.arange(W) + a >= 0).astype(np.float32)
        m = dict(shared)
        m["xin"] = _fm(xw)
        m["xs"] = _fm(x_sample[4 * core:4 * core + 4].reshape(64, D))
        m["vtok"] = np.ascontiguousarray(valid.reshape(NT, 128).T)
        m["flag"] = np.full((128, 1), 1.0 if a >= 0 else 0.0, np.float32)
        cc = np.concatenate([c_prompt[b:b + 1], c_sample[4 * core:4 * core + 4]], 0)
        m["cT"] = np.ascontiguousarray(cc.reshape(5, 8, 128).transpose(2, 1, 0))
        ck = cache_attn_k[:, 4 * core:4 * core + 4]
        m["ckT"] = np.ascontiguousarray(ck.reshape(2, 4, 512, 4, 2, 64).transpose(0, 4, 5, 3, 1, 2)).reshape(2, 128, 4, 4, 512)
        cvv = cache_attn_v[:, 4 * core:4 * core + 4].reshape(2, 4, 4, 128, 512)
        m["cv"] = np.ascontiguousarray(cvv.transpose(0, 1, 3, 2, 4))
        cc2 = cache_ffn_conv[:, 4 * core:4 * core + 4].reshape(2, 4, 2, 22, 128)
        m["cconv"] = np.ascontiguousarray(cc2.transpose(0, 4, 3, 1, 2))
        in_maps.append(m)
    return in_maps


def _post(R):

    y_prompt = np.empty((2, 8192, D), np.float32)
    y_sample = np.empty((32, 16, D), np.float32)
    nkp = np.empty((2, 2, 512, 8, 64), np.float32)
    nvp = np.empty((2, 2, 512, 8, 64), np.float32)
    ncp = np.empty((2, 2, 2, DFF), np.float32)
    nks = np.empty((2, 32, 16, 8, 64), np.float32)
    nvs = np.empty((2, 32, 16, 8, 64), np.float32)
    nbs = np.empty((2, 32, 16, 4, 128), np.float32)
    ncs = np.empty((2, 32, 2, DFF), np.float32)
    for core in range(NCORES):
        r = R[core]
        b, seg = core // 4, core % 4
        y_prompt[b, seg * SEG:(seg + 1) * SEG] = r["yT"].transpose(2, 1, 0).reshape(SEG, D)
        y_sample[4 * core:4 * core + 4] = r["ysT"].transpose(2, 1, 0).reshape(4, 16, D)
        sl = slice(4 * core, 4 * core + 4)
        for l in range(2):
            if seg == 3:
                nkp[l, b] = r["nkT"][l].reshape(2, 64, 4, 512).transpose(3, 2, 0, 1).reshape(512, 8, 64)
                nvp[l, b] = r["nv"][l].reshape(512, 8, 64)
                ncp[l, b] = r["ncv"][l].transpose(2, 1, 0).reshape(2, DFF)
            nks[l, sl] = r["nksT"][l].reshape(2, 64, 4, 4, 16).transpose(3, 4, 2, 0, 1).reshape(4, 16, 8, 64)
            nvs[l, sl] = r["nvs"][l].transpose(1, 0, 2).reshape(4, 16, 8, 64)
            nbs[l, sl] = r["nbs"][l].reshape(4, 16, 4, 128)
            ncs[l, sl] = r["ncs"][l].transpose(2, 3, 1, 0).reshape(4, 2, DFF)
    return (y_prompt, y_sample, nkp, nvp, ncp, nks, nvs, nbs, ncs)
```

```python
import contextlib
import numpy as np
import concourse.bass as bass
import concourse.mybir as mybir
from concourse.bass_utils import run_bass_kernel_spmd

F32 = mybir.dt.float32
BF16 = mybir.dt.bfloat16
AF = mybir.ActivationFunctionType
ALU = mybir.AluOpType
AX = mybir.AxisListType

NCORES = 8
D = 1024
SEG = 2048
HALO = 1152
W = SEG + HALO
NT = W // 128
DFF = 2816
EPS = 1e-6
NB = 3
L0_BLOCKS = [(4, 4), (8, 4), (12, 4), (16, 3), (19, 3), (22, 3)]
L1_BLOCKS = [(8, 4), (12, 4), (16, 3), (19, 3), (22, 3)]
OUT_T0 = 9
KEEP_T0 = 21
_DBG = {}


class Prog:
    STREAMS = ("sp", "act", "dve", "pool", "pe")

    def __init__(self):
        self.ops = []
        self.keyw = {}
        self.keyr = {}
        self.dcnt = {}

    def op(self, stream, method, *args, r=(), w=(), semkey=None, **kw):
        idx = len(self.ops)
        dom = ("d", semkey) if semkey is not None else ("s", stream)
        deps = {}

        def add(d, i):
            if d[0] == "d":
                i = self.dcnt[d]
            if deps.get(d, -1) < i:
                deps[d] = i

        for k in r:
            for d, i in self.keyw.get(k, {}).items():
                add(d, i)
        skip_same = dom[0] == "d" or stream == "pe"
        for k in w:
            for d, i in self.keyr.get(k, {}).items():
                if d == dom and (skip_same or i == idx):
                    continue
                add(d, i)
            for d, i in self.keyw.get(k, {}).items():
                if d == dom and skip_same:
                    continue
                add(d, i)
        for k in r:
            self.keyr.setdefault(k, {})[dom] = idx
        for k in w:
            if self.keyr.get(k):
                self.keyw[k] = {dom: idx}
                self.keyr[k] = {}
            else:
                self.keyw.setdefault(k, {})[dom] = idx
        if dom[0] == "d":
            self.dcnt[dom] = self.dcnt.get(dom, 0) + 16
        self.ops.append(dict(stream=stream, method=method, args=args, kw=kw, dom=dom,
                             deps=list(deps.items())))
        return idx

    def emit(self, nc, stack):
        ops = self.ops
        needs = set()
        for o in ops:
            for d, i in o["deps"]:
                if d[0] == "s":
                    needs.add(i)
        cnt = {}
        sems = {}
        issuer = {}
        for i, o in enumerate(ops):
            d = o["dom"]
            if d[0] == "d":
                cnt[d] = cnt.get(d, 0) + 16
                o["done"] = cnt[d]
                assert issuer.setdefault(d, o["stream"]) == o["stream"], d
            elif i in needs:
                cnt[d] = cnt.get(d, 0) + 1
                o["done"] = cnt[d]
            else:
                o["done"] = None
        for d in cnt:
            sems[d] = stack.enter_context(nc.semaphore("s%d" % len(sems)))
        block = stack.enter_context(nc.Block())
        self.nsem = len(sems)

        def run(stream, eng):
            waited = {}
            for o in ops:
                if o["stream"] != stream:
                    continue
                for d, i in o["deps"]:
                    v = i if d[0] == "d" else ops[i]["done"]
                    if waited.get(d, 0) < v:
                        eng.wait_ge(sems[d], v)
                        waited[d] = v
                ins = getattr(eng, o["method"])(*o["args"], **o["kw"])
                if o["done"] is not None:
                    d = o["dom"]
                    ins.then_inc(sems[d], 16 if d[0] == "d" else 1)
            for d, v in cnt.items():
                if d[0] == "d" and issuer[d] == stream and waited.get(d, 0) < v:
                    eng.wait_ge(sems[d], v)

        @block.sync
        def _(e):
            run("sp", e)

        @block.scalar
        def _(e):
            run("act", e)

        @block.vector
        def _(e):
            run("dve", e)

        @block.gpsimd
        def _(e):
            run("pool", e)

        @block.tensor
        def _(e):
            run("pe", e)


def build_nc(stop=None):
    class _Stop(Exception):
        pass

    nblk = [0]

    def ck(st):
        if stop is not None and nblk[0] + st / 10.0 >= stop - 1e-9 and stop > 0:
            raise _Stop()

    nc = bass.Bass("TRN2", target_bir_lowering=False)
    P = Prog()
    stack = contextlib.ExitStack()

    def din(name, shape):
        return nc.dram_tensor(name, list(shape), F32, kind="ExternalInput").ap()

    def dout(name, shape):
        return nc.dram_tensor(name, list(shape), F32, kind="ExternalOutput").ap()

    def dint(name, shape, dt):
        return nc.dram_tensor(name, list(shape), dt, kind="Internal").ap()

    xin = din("xin", (128, 8, W))
    xs_d = din("xs", (128, 8, 64))
    vtok_d = din("vtok", (128, NT))
    flag_d = din("flag", (128, 1))
    cT_d = din("cT", (128, 8, 5))
    ckT_d = din("ckT", (2, 128, 4, 4, 512))
    cv_d = din("cv", (2, 4, 128, 4, 512))
    cconv_d = din("cconv", (2, 128, 22, 4, 2))
    wall_d = din("wall", (2, 24, 128, 4096))
    wfo_d = din("wfo", (2, 8, 128, 2816))
    wada_d = din("wada", (2, 12, 128, 4096))
    nrm_d = din("nrm", (2, 128, 16))
    bada_d = din("bada", (2, 128, 48))
    qkg_d = din("qkg", (2, 128, 2))
    vg_d = din("vg", (2, 128, 512))
    bsp_d = din("bsp", (2, 1, 512))
    bsps_d = din("bsps", (2, 1, 256))
    wspT_d = din("wspT", (2, 128, 4, 128))
    wspb_d = din("wspb", (2, 64, 4, 64))
    tril_d = din("tril", (128, 4, 128))
    trilb_d = din("trilb", (64, 4, 64))
    cw_d = din("cw", (2, 128, 22, 3))
    cb_d = din("cb", (2, 128, 22))
    Tb_d = din("Tb", (2, 128, 2, 8, 128))
    b256_d = din("b256", (2, 128, 8))

    yT_d = dout("yT", (128, 8, SEG))
    ysT_d = dout("ysT", (128, 8, 64))
    nkT_d = dout("nkT", (2, 128, 4, 512))
    nv_d = dout("nv", (2, 4, 128, 512))
    ncv_d = dout("ncv", (2, 128, 22, 2))
    nksT_d = dout("nksT", (2, 128, 4, 64))
    nvs_d = dout("nvs", (2, 16, 4, 512))
    nbs_d = dout("nbs", (2, 64, 512))
    ncs_d = dout("ncs", (2, 128, 22, 4, 2))

    wsc = dint("wsc", (2, 24, 128, 4096), BF16)
    wsc_fo = dint("wscfo", (2, 8, 128, 2816), BF16)
    x1s = dint("x1s", (128, 8, 21 * 128), F32)

    def sb(name, shape, dt=F32):
        return stack.enter_context(nc.sbuf_tensor("sb_" + name, list(shape), dt))

    xTt = sb("xT", (128, 2, 8, 512))
    cur = {"xT": xTt[:, 0], "par": 0, "idx": 0}
    hT = sb("hT", (128, 8, 512), BF16)
    sq = sb("sq", (128, 3, 512), BF16)
    rstd = sb("rstd", (128, 512))
    tmpn = sb("tmpn", (128, 2, 512))
    qT = sb("qT", (128, 4, 512), BF16)
    kst = sb("kst", (128, 2, 512))
    rs = sb("rs", (128, 2, 512))
    kT = sb("kT", (128, 4, 1024), BF16)
    vt = sb("vt", (128, 8, 512), BF16)
    ubT = sb("ubT", (128, 4, 512), BF16)
    vbn = sb("vbn", (128, 4, 512), BF16)
    gl = sb("gl", (128, 2, 512))
    ssv = sb("ssv", (128, 4))
    SA = sb("SA", (128, 24, 512), BF16)
    oaT = sb("oaT", (128, 4, 512), BF16)
    obT = sb("obT", (128, 4, 512), BF16)
    Pt = sb("Pt", (128, 2, 2, 5, 128), BF16)
    rden = sb("rden", (128, 2, 256))
    gb = sb("gb", (128, 3, 516))
    gc = sb("gc", (128, 2, 512))
    ge = sb("ge", (128, 2, 512))
    halo = sb("halo", (128, 22, 2))
    halos = sb("halos", (128, 22, 4, 2))
    ncs_st = sb("ncs_st", (128, 22, 4, 2))
    wslab = sb("wslab", (128, NB, 4096), BF16)
    ones_bf = sb("ones_bf", (128, 128), BF16)
    blk64 = sb("blk64", (128, 128), BF16)
    ones_row = sb("ones_row", (1, 128), BF16)
    vones = sb("vones", (128, 9, 128), BF16)
    vtok = sb("vtok", (128, NT))
    flag = sb("flag", (128, 1))
    cTs = sb("cTs", (128, 8, 5))
    cs = sb("cs", (128, 8, 5), BF16)
    E = sb("E", (128, 2, 8, 128))
    negb = sb("negb", (128, 8))
    modTt = sb("modT", (128, 2, 48, 5))
    A12t = sb("A12", (128, 2, 2, 8, 5))
    nrmt = sb("nrm", (128, 2, 16))
    badat = sb("bada", (128, 2, 48))
    qkg = sb("qkg", (128, 2))
    vg = sb("vg", (128, 512))
    bsp = sb("bsp", (1, 512), BF16)
    bsp_f = sb("bsp_f", (1, 768))
    bsps = sb("bsps", (1, 256), BF16)
    WcT = sb("WcT", (128, 4, 128), BF16)
    Wblk = sb("Wblk", (64, 4, 64), BF16)
    cw = sb("cw", (128, 22, 3))
    cb = sb("cb", (128, 22))
    kTs = sb("kTs", (128, 4, 64), BF16)
    vts = sb("vts", (16, 4, 512), BF16)
    vbns = sb("vbns", (64, 512), BF16)
    ckb = sb("ckb", (128, 1, 4, 512), BF16)
    cvb = sb("cvb", (128, 1, 4, 512), BF16)
    Pts = sb("Pts", (128, 2, 2, 80), BF16)
    exs = sb("exs", (128, 2, 2, 32))
    rdens = sb("rdens", (128, 2, 32))

    psd = [stack.enter_context(nc.psum_tensor("ps%d" % i, [128, 1024], F32)) for i in range(4)]
    ps = [psd[i // 2][:, (i % 2) * 512:(i % 2 + 1) * 512] for i in range(8)]

    def bk(i):
        return [("ps", i)]

    bank_ctr = [0]

    def nb():
        b = bank_ctr[0] % 8
        bank_ctr[0] += 1
        return b

    rot = {}

    def nrot(name, n):
        v = rot.get(name, 0)
        rot[name] = v + 1
        return v % n

    ws = dict(seq=[], issued=0, consumed=0)

    casted = set()

    def ws_issue(upto):
        while ws["issued"] < min(upto, len(ws["seq"])):
            kind_, l_, s_ = ws["seq"][ws["issued"]]
            slot = ws["issued"] % NB
            wk, sk = [("wslab", slot)], ("w", slot)
            if kind_ == "ada":
                P.op("pool", "dma_start", out=wslab[:, slot, :], in_=wada_d[l_, s_], w=wk, semkey=sk)
            else:
                ncols = 4096 if kind_ == "wall" else 2816
                src32 = wall_d[l_, s_] if kind_ == "wall" else wfo_d[l_, s_]
                scr = wsc[l_, s_] if kind_ == "wall" else wsc_fo[l_, s_]
                key = ("wsc", kind_, l_, s_)
                if key not in casted:
                    casted.add(key)
                    P.op("pool", "dma_start", out=wslab[:, slot, 0:ncols], in_=src32, w=wk, semkey=sk)
                    P.op("sp", "dma_start", out=scr, in_=wslab[:, slot, 0:ncols], r=wk, w=[key],
                         semkey=("wst", slot))
                else:
                    P.op("pool", "dma_start", out=wslab[:, slot, 0:ncols], in_=scr, r=[key], w=wk, semkey=sk)
            ws["issued"] += 1

    def ws_get():
        assert ws["consumed"] < len(ws["seq"])
        ws_issue(ws["consumed"] + 1)
        slot = ws["consumed"] % NB
        ws["consumed"] += 1
        return slot

    def ws_done():
        ws_issue(ws["consumed"] + NB)

    def seq_ada(l):
        for s_ in range(12):
            ws["seq"].append(("ada", l, s_))

    def seq_block(l, kind):
        if kind == "kv":
            for s_ in (1, 2):
                ws["seq"].append(("wall", l, s_))
            return
        for s_ in range(24):
            ws["seq"].append(("wall", l, s_))
        for s_ in range(8):
            ws["seq"].append(("fo", l, s_))

    seq_ada(0)
    for l in range(2):
        seq_block(l, "kv")
        for _ in (L0_BLOCKS if l == 0 else L1_BLOCKS):
            seq_block(l, "full")
        if l == 0:
            seq_ada(1)
        seq_block(l, "sample")

    P.op("pool", "memset", ones_bf[:, :], 1.0, w=["ones_bf"])
    P.op("pool", "memset", blk64[:, :], 0.0, w=["blk64"])
    P.op("pool", "memset", blk64[0:64, 0:64], 1.0, w=["blk64"])
    P.op("pool", "memset", blk64[64:128, 64:128], 1.0, w=["blk64"])
    P.op("pool", "memset", ones_row[:, :], 1.0, w=["ones_row"])
    P.op("pool", "memset", Pt[:, :, :, :, :].rearrange("p a b c d -> p (a b c d)"), 0.0, w=[("Pt", s, h) for s in range(2) for h in range(2)])
    P.op("pool", "memset", halo[:, :, :], 0.0, w=["halo"])
    P.op("sp", "dma_start", out=vtok[:, :], in_=vtok_d, w=["vtok"], semkey="c0")
    P.op("sp", "dma_start", out=flag[:, :], in_=flag_d, w=["flag"], semkey="c0")
    P.op("sp", "dma_start", out=cTs[:, :, :], in_=cT_d, w=["cTs"], semkey="c0")
    P.op("act", "activation", out=cs[:, :, :], in_=cTs[:, :, :], func=AF.Silu, r=["cTs"], w=["cs"])
    for t in range(9):
        P.op("act", "activation", out=vones[:, t, :], in_=ones_bf[:, :], func=AF.Copy,
             scale=vtok[:, t:t + 1], r=["ones_bf", "vtok"], w=["vones"])
    def layer_setup(l):
        for dst, src, key in ((qkg, qkg_d[l], "qkg"),
                              (vg, vg_d[l], "vg"), (cw, cw_d[l], "cw"), (cb, cb_d[l], "cb"),
                              (negb, b256_d[l], "negb")):
            full = tuple(slice(None) for _ in dst.shape)
            P.op("sp", "dma_start", out=dst[full], in_=src, w=[key], semkey="ls")
        P.op("sp", "dma_start", out=E[:, :, :, :], in_=Tb_d[l], w=["E"], semkey="ls")
        P.op("sp", "dma_start", out=halos[:, :, :, :], in_=cconv_d[l], w=["halos"], semkey="ls")
        P.op("sp", "dma_start", out=bsp_f[:, 0:512], in_=bsp_d[l], w=["bsp_f"], semkey="ls")
        P.op("sp", "dma_start", out=bsp_f[:, 512:768], in_=bsps_d[l], w=["bsp_f"], semkey="ls")
        P.op("sp", "dma_start", out=gc[:, 0, :], in_=wspT_d[l].rearrange("p g t -> p (g t)"),
             w=[("gc", 0)], semkey="ls3")
        P.op("sp", "dma_start", out=gc[:, 1, :], in_=tril_d.rearrange("p g t -> p (g t)"),
             w=[("gc", 1)], semkey="ls3")
        P.op("sp", "dma_start", out=ge[0:64, 0, 0:256], in_=wspb_d[l].rearrange("p g t -> p (g t)"),
             w=[("ge", 0)], semkey="ls3")
        P.op("sp", "dma_start", out=ge[0:64, 1, 0:256], in_=trilb_d.rearrange("p g t -> p (g t)"),
             w=[("ge", 1)], semkey="ls3")
        P.op("dve", "tensor_tensor", out=WcT[:, :, :].rearrange("p g t -> p (g t)"), in0=gc[:, 0, :],
             in1=gc[:, 1, :], op=ALU.mult, r=[("gc", 0), ("gc", 1)], w=["WcT"])
        P.op("dve", "tensor_tensor", out=Wblk[:, :, :].rearrange("p g t -> p (g t)"), in0=ge[0:64, 0, 0:256],
             in1=ge[0:64, 1, 0:256], op=ALU.mult, r=[("ge", 0), ("ge", 1)], w=["Wblk"])
        P.op("dve", "tensor_copy", bsp[:, :], bsp_f[:, 0:512], r=["bsp_f"], w=["bsp"])
        P.op("dve", "tensor_copy", bsps[:, :], bsp_f[:, 512:768], r=["bsp_f"], w=["bsps"])
        P.op("dve", "tensor_scalar_mul", qkg[:, 1:2], qkg[:, 1:2], 8.0, r=["qkg"], w=["qkg"])
        P.op("dve", "tensor_scalar_mul", negb[:, :], negb[:, :], -1.0, r=["negb"], w=["negb"])
        for t in range(2):
            for h in range(8):
                P.op("act", "activation", out=E[:, t, h, :], in_=E[:, t, h, :], func=AF.Exp,
                     bias=negb[:, h:h + 1], scale=1.0, r=["E", "negb"], w=["E"])
        P.op("pool", "memset", E[64:128, 1, :, 0:64], 0.0, r=["E"], w=["E"])

    def ada_setup(l):
        modT, A12, nrm, bada = modTt[:, l], A12t[:, l], nrmt[:, l], badat[:, l]
        P.op("sp", "dma_start", out=nrm, in_=nrm_d[l], w=[("nrm", l)], semkey=("lsa", l))
        P.op("sp", "dma_start", out=bada, in_=bada_d[l], w=[("bada", l)], semkey=("lsa", l))
        b = nb()
        for s_ in range(12):
            slot = ws_get()
            for m in range(4):
                ci = s_ * 4 + m
                for kc in range(8):
                    P.op("pe", "matmul", ps[b][:, ci * 5:(ci + 1) * 5],
                         lhsT=wslab[:, slot, kc * 512 + m * 128: kc * 512 + (m + 1) * 128],
                         rhs=cs[:, kc, :], start=(kc == 0), stop=(kc == 7),
                         r=[("wslab", slot), "cs"], w=bk(b))
            ws_done()
        for bb in range(5):
            P.op("dve", "tensor_tensor", out=modT[:, :, bb],
                 in0=ps[b][:, 0:240].rearrange("p (c b) -> p c b", b=5)[:, :, bb], in1=bada,
                 op=ALU.add, r=bk(b) + [("bada", l)], w=[("modT", l)])
        for which in range(2):
            sc0 = 8 if which == 0 else 32
            for bb in range(5):
                P.op("dve", "tensor_scalar", out=A12[:, which, :, bb], in0=modT[:, sc0:sc0 + 8, bb],
                     scalar1=1.0, scalar2=32.0, op0=ALU.add, op1=ALU.mult, r=[("modT", l)], w=[("A12", l)])
                P.op("dve", "tensor_tensor", out=A12[:, which, :, bb], in0=A12[:, which, :, bb],
                     in1=nrm[:, which * 8:(which + 1) * 8], op=ALU.mult, r=[("A12", l), ("nrm", l)],
                     w=[("A12", l)])

    def norm_mod(which, T, groups):
        b = nb()
        for c in range(8):
            r_ = nrot("sq", 3)
            P.op("act", "activation", out=sq[:, r_, 0:T], in_=cur["xT"][:, c, 0:T], func=AF.Square,
                 r=[("xT", cur["par"], c)], w=[("sq", r_)])
            P.op("pe", "matmul", ps[b][:, 0:T], lhsT=ones_bf[:, :], rhs=sq[:, r_, 0:T],
                 start=(c == 0), stop=(c == 7), r=[("sq", r_), "ones_bf"], w=bk(b))
        P.op("act", "activation", out=rstd[:, 0:T], in_=ps[b][:, 0:T], func=AF.Sqrt, bias=float(D * EPS),
             scale=1.0, r=bk(b), w=["rstd"])
        P.op("dve", "reciprocal", out=rstd[:, 0:T], in_=rstd[:, 0:T], r=["rstd"], w=["rstd"])
        shc = 0 if which == 0 else 24
        for c in range(8):
            r_ = nrot("tmpn", 2)
            for (c0, n, bb) in groups:
                P.op("dve", "scalar_tensor_tensor", out=tmpn[:, r_, c0:c0 + n], in0=cur["xT"][:, c, c0:c0 + n],
                     scalar=A12t[:, cur["l"], which, c, bb:bb + 1], in1=rstd[:, c0:c0 + n], op0=ALU.mult, op1=ALU.mult,
                     r=[("xT", cur["par"], c), ("A12", cur["l"]), "rstd"], w=[("tmpn", r_)])
            for (c0, n, bb) in groups:
                P.op("act", "activation", out=hT[:, c, c0:c0 + n], in_=tmpn[:, r_, c0:c0 + n],
                     func=AF.Identity, bias=modTt[:, cur["l"], shc + c, bb:bb + 1], scale=1.0,
                     r=[("tmpn", r_), ("modT", cur["l"])], w=[("hT", c)])

    def proj_fm(b, slot, nkc, ms, col0, rhs_t, rkeys, T):
        for kc in range(nkc):
            P.op("pe", "matmul", ps[b][:, 0:T], lhsT=wslab[:, slot, kc * ms + col0: kc * ms + col0 + 128],
                 rhs=rhs_t(kc), start=(kc == 0), stop=(kc == nkc - 1),
                 r=[("wslab", slot)] + rkeys(kc), w=bk(b))

    def headnorm(b, T, gcol, out_ap, out_keys):
        r_ = nrot("sq", 3)
        P.op("act", "activation", out=sq[:, r_, 0:T], in_=ps[b][:, 0:T], func=AF.Square, r=bk(b), w=[("sq", r_)])
        b2 = nb()
        P.op("pe", "matmul", ps[b2][:, 0:T], lhsT=blk64[:, :], rhs=sq[:, r_, 0:T], start=True, stop=True,
             r=[("sq", r_), "blk64"], w=bk(b2))
        r2 = nrot("rs", 2)
        P.op("act", "activation", out=rs[:, r2, 0:T], in_=ps[b2][:, 0:T], func=AF.Sqrt, bias=float(64 * EPS),
             scale=1.0, r=bk(b2), w=[("rs", r2)])
        P.op("dve", "reciprocal", out=rs[:, r2, 0:T], in_=rs[:, r2, 0:T], r=[("rs", r2)], w=[("rs", r2)])
        P.op("dve", "scalar_tensor_tensor", out=out_ap, in0=ps[b][:, 0:T], scalar=qkg[:, gcol:gcol + 1],
             in1=rs[:, r2, 0:T], op0=ALU.mult, op1=ALU.mult, r=bk(b) + [("rs", r2), "qkg"], w=out_keys)

    def hT_r(T):
        return (lambda kc: hT[:, kc, 0:T]), (lambda kc: [("hT", kc)])

    def block(l, kind, t0, ntile):
        sample = kind == "sample"
        T = 64 if sample else ntile * 128
        tiles = [] if sample else list(range(t0, t0 + ntile))
        groups = [(bl * 16, 16, 1 + bl) for bl in range(4)] if sample else [(0, T, 0)]
        hr, hk = hT_r(T)
        bi = cur["idx"]
        cur["l"] = l
        cur["par"] = bi % 2
        cur["xT"] = xTt[:, bi % 2]

        def xload(i):
            l_, kind_, t0_, n_ = BLOCKS[i]
            par_ = i % 2
            wk = [("xT", par_, c) for c in range(8)]
            if kind_ == "sample":
                if l_ == 1:
                    return False
                P.op("sp", "dma_start", out=xTt[:, par_, :, 0:64], in_=xs_d, w=wk, semkey=("xld", par_))
                return True
            T_ = n_ * 128
            if l_ == 0:
                P.op("sp", "dma_start", out=xTt[:, par_, :, 0:T_], in_=xin[:, :, t0_ * 128: t0_ * 128 + T_], w=wk,
                     semkey=("xld", par_))
            else:
                P.op("sp", "dma_start", out=xTt[:, par_, :, 0:T_],
                     in_=x1s[:, :, (t0_ - 4) * 128: (t0_ - 4) * 128 + T_],
                     r=[("x1s", t) for t in range(t0_, t0_ + n_)], w=wk, semkey=("xld", par_))
            return True

        if bi == 0:
            xload(0)
        if sample and l == 1:
            P.op("pool", "tensor_copy", cur["xT"][:, :, 0:64], xs_keep[:, :, :], r=["xs_keep"],
                 w=[("xT", cur["par"], c) for c in range(8)])
        if bi + 1 < len(BLOCKS):
            xload(bi + 1)
        cur["idx"] = bi + 1
        norm_mod(0, T, groups)

        if kind == "kv":
            slabs = {1: ws_get()}
        else:
            slabs = {0: ws_get()}
            for c in range(4):
                b = nb()
                proj_fm(b, slabs[0], 8, 512, c * 128, hr, hk, T)
                headnorm(b, T, 0, qT[:, c, 0:T], [("qT", c)])
            ws_done()
            slabs[1] = ws_get()
        for c in range(4):
            b = nb()
            proj_fm(b, slabs[1], 8, 512, c * 128, hr, hk, T)
            r_ = nrot("kst", 2)
            headnorm(b, T, 1, kst[:, r_, 0:T], [("kst", r_)])
            if sample:
                P.op("act", "activation", out=kTs[:, c, :], in_=kst[:, r_, 0:64], func=AF.Copy,
                     r=[("kst", r_)], w=["kTs"])
                P.op("sp", "dma_start", out=nksT_d[l, :, c, :], in_=kst[:, r_, 0:64], r=[("kst", r_)],
                     semkey=("kst", r_))
            else:
                for ti, t in enumerate(tiles):
                    sl = t % 8
                    P.op("act", "activation", out=kT[:, c, sl * 128:(sl + 1) * 128],
                         in_=kst[:, r_, ti * 128:(ti + 1) * 128], func=AF.Copy, r=[("kst", r_)], w=[("kT", sl)])
                    if t >= KEEP_T0:
                        P.op("sp", "dma_start", out=nkT_d[l, :, c, (t - KEEP_T0) * 128:(t - KEEP_T0 + 1) * 128],
                             in_=kst[:, r_, ti * 128:(ti + 1) * 128], r=[("kst", r_)], semkey=("kst", r_))
        ws_done()
        slot = ws_get()
        if sample:
            for bl in range(4):
                b = nb()
                for kc in range(8):
                    P.op("pe", "matmul", ps[b][0:16, 0:512], lhsT=hT[:, kc, bl * 16:(bl + 1) * 16],
                         rhs=wslab[:, slot, kc * 512:(kc + 1) * 512], start=(kc == 0), stop=(kc == 7),
                         r=[("wslab", slot), ("hT", kc)], w=bk(b))
                r_ = nrot("gl", 2)
                P.op("act", "activation", out=gl[0:16, r_, :], in_=ps[b][0:16, 0:512], func=AF.Copy, r=bk(b),
                     w=[("gl", r_)])
                P.op("dve", "tensor_copy", vts[:, bl, :], gl[0:16, r_, :], r=[("gl", r_)], w=["vts"])
                P.op("sp", "dma_start", out=nvs_d[l, :, bl, :], in_=gl[0:16, r_, :], r=[("gl", r_)],
                     semkey=("gl", r_))
        else:
            for ti, t in enumerate(tiles):
                b = nb()
                for kc in range(8):
                    P.op("pe", "matmul", ps[b][:, 0:512], lhsT=hT[:, kc, ti * 128:(ti + 1) * 128],
                         rhs=wslab[:, slot, kc * 512:(kc + 1) * 512], start=(kc == 0), stop=(kc == 7),
                         r=[("wslab", slot), ("hT", kc)], w=bk(b))
                sl = t % 8
                if t >= KEEP_T0:
                    r_ = nrot("gl", 2)
                    P.op("act", "activation", out=gl[:, r_, :], in_=ps[b][:, 0:512], func=AF.Copy, r=bk(b),
                         w=[("gl", r_)])
                    P.op("dve", "tensor_scalar_mul", vt[:, sl, :], gl[:, r_, :], vtok[:, t:t + 1],
                         r=[("gl", r_), "vtok"], w=[("vt", sl)])
                    P.op("sp", "dma_start", out=nv_d[l, t - KEEP_T0], in_=gl[:, r_, :], r=[("gl", r_)],
                         semkey=("gl", r_))
                else:
                    P.op("dve", "tensor_scalar_mul", vt[:, sl, :], ps[b][:, 0:512], vtok[:, t:t + 1],
                         r=bk(b) + ["vtok"], w=[("vt", sl)])
        ws_done()
        if kind == "kv":
            return

        ck(1)
        slot = ws_get()
        for c in range(4):
            b = nb()
            proj_fm(b, slot, 8, 512, c * 128, hr, hk, T)
            P.op("act", "activation", out=ubT[:, c, 0:T], in_=ps[b][:, 0:T], func=AF.Gelu_apprx_tanh, r=bk(b),
                 w=[("ubT", c)])
        ws_done()
        ck(2)
        slot = ws_get()
        tl = [(0, 64)] if sample else [(ti, 128) for ti in range(ntile)]
        for ti, M in tl:
            b = nb()
            for kc in range(8):
                P.op("pe", "matmul", ps[b][0:M, 0:512], lhsT=hT[:, kc, ti * 128: ti * 128 + M],
                     rhs=wslab[:, slot, kc * 512:(kc + 1) * 512], start=(kc == 0), stop=(kc == 7),
                     r=[("wslab", slot), ("hT", kc)], w=bk(b))
            r_ = nrot("gl", 2)
            P.op("act", "activation", out=gl[0:M, r_, :], in_=ps[b][0:M, 0:512], func=AF.Gelu_apprx_tanh, r=bk(b),
                 w=[("gl", r_)])
            r2 = nrot("tmpn", 2)
            P.op("dve", "tensor_tensor", out=tmpn[0:M, r2, :], in0=gl[0:M, r_, :], in1=gl[0:M, r_, :], op=ALU.mult,
                 r=[("gl", r_)], w=[("tmpn", r2)])
            r3 = nrot("ssv", 4)
            P.op("dve", "reduce_sum", out=ssv[0:M, r3:r3 + 1], in_=tmpn[0:M, r2, :], axis=AX.X,
                 r=[("tmpn", r2)], w=[("ssv", r3)])
            P.op("act", "activation", out=ssv[0:M, r3:r3 + 1], in_=ssv[0:M, r3:r3 + 1], func=AF.Sqrt,
                 bias=float(EPS), scale=1.0 / 512.0, r=[("ssv", r3)], w=[("ssv", r3)])
            P.op("dve", "reciprocal", out=ssv[0:M, r3:r3 + 1], in_=ssv[0:M, r3:r3 + 1], r=[("ssv", r3)],
                 w=[("ssv", r3)])
            if sample:
                P.op("dve", "scalar_tensor_tensor", out=tmpn[0:64, r2, :], in0=gl[0:64, r_, :],
                     scalar=ssv[0:64, r3:r3 + 1], in1=vg[0:64, :], op0=ALU.mult, op1=ALU.mult,
                     r=[("gl", r_), ("ssv", r3), "vg"], w=[("tmpn", r2)])
                P.op("act", "activation", out=vbns[:, :], in_=tmpn[0:64, r2, :], func=AF.Copy, r=[("tmpn", r2)],
                     w=["vbns"])
                P.op("sp", "dma_start", out=nbs_d[l], in_=tmpn[0:64, r2, :], r=[("tmpn", r2)], semkey=("tmpn", r2))
            else:
                P.op("dve", "scalar_tensor_tensor", out=vbn[:, ti, :], in0=gl[:, r_, :], scalar=ssv[:, r3:r3 + 1],
                     in1=vg[:, :], op0=ALU.mult, op1=ALU.mult, r=[("gl", r_), ("ssv", r3), "vg"], w=[("vbn", ti)])
        ws_done()
        ck(3)
        for s in range(4):
            slot = ws_get()
            for m in range(4):
                c = s * 4 + m
                b = nb()
                proj_fm(b, slot, 8, 512, m * 128, hr, hk, T)
                P.op("act", "activation", out=SA[:, c, 0:T], in_=ps[b][:, 0:T], func=AF.Sigmoid, r=bk(b),
                     w=[("SA", c)])
            ws_done()

        ck(4)
        if sample:
            items = [(bl, hp) for bl in range(4) for hp in range(4)]

            def s1(it, st):
                bl, hp = it
                cr = 0
                for j in range(5):
                    for hh in range(2):
                        hs = slice(hh * 64, (hh + 1) * 64)
                        b = st * 4 + hh
                        if j < 4:
                            P.op("pe", "matmul", ps[b][:, j * 16:(j + 1) * 16],
                                 lhsT=ckb[hs, cr, hp, j * 128:(j + 1) * 128], rhs=qT[hs, hp, bl * 16:(bl + 1) * 16],
                                 start=True, stop=True, r=[("ckb", cr), ("qT", hp)], w=[("ps", b)])
                        else:
                            P.op("pe", "matmul", ps[b][0:16, 64:80],
                                 lhsT=kTs[hs, hp, bl * 16:(bl + 1) * 16], rhs=qT[hs, hp, bl * 16:(bl + 1) * 16],
                                 start=True, stop=True, r=["kTs", ("qT", hp)], w=[("ps", b)])
                for hh in range(2):
                    b = st * 4 + hh
                    h = 2 * hp + hh
                    P.op("act", "activation", out=Pts[:, st, hh, 0:48], in_=ps[b][:, 0:48], func=AF.Exp,
                         r=[("ps", b)], w=[("Pts", st, hh)])
                    P.op("act", "activation", out=exs[:, st, hh, 0:16], in_=ps[b][:, 48:64], func=AF.Exp,
                         r=[("ps", b)], w=[("exs", st, hh)])
                    P.op("act", "activation", out=exs[0:16, st, hh, 16:32], in_=ps[b][0:16, 64:80], func=AF.Exp,
                         r=[("ps", b)], w=[("exs", st, hh)])
                    P.op("dve", "tensor_tensor", out=Pts[:, st, hh, 48:64], in0=exs[:, st, hh, 0:16],
                         in1=E[:, 0, h, 0:16], op=ALU.mult, r=[("exs", st, hh), "E"], w=[("Pts", st, hh)])
                    P.op("dve", "tensor_tensor", out=Pts[0:16, st, hh, 64:80], in0=exs[0:16, st, hh, 16:32],
                         in1=E[0:16, 1, h, 0:16], op=ALU.mult, r=[("exs", st, hh), "E"], w=[("Pts", st, hh)])

            def s2(it, st):
                bl, hp = it
                cr = 0
                bd, bo = st * 4 + 2, st * 4 + 3
                for hh in range(2):
                    for j in range(5):
                        if j < 4:
                            P.op("pe", "matmul", ps[bd][:, hh * 16:(hh + 1) * 16], lhsT=ones_bf[:, :],
                                 rhs=Pts[:, st, hh, j * 16:(j + 1) * 16], start=(j == 0), stop=False,
                                 r=[("Pts", st, hh), "ones_bf"], w=[("ps", bd)])
                        else:
                            P.op("pe", "matmul", ps[bd][:, hh * 16:(hh + 1) * 16], lhsT=ones_bf[0:16, :],
                                 rhs=Pts[0:16, st, hh, 64:80], start=False, stop=True,
                                 r=[("Pts", st, hh), "ones_bf"], w=[("ps", bd)])
                for hh in range(2):
                    hs = slice(hh * 64, (hh + 1) * 64)
                    fc = (2 * hp + hh) * 64
                    for j in range(5):
                        if j < 4:
                            P.op("pe", "matmul", ps[bo][hs, 0:16], lhsT=cvb[:, cr, j, fc:fc + 64],
                                 rhs=Pts[:, st, hh, j * 16:(j + 1) * 16], start=(j == 0), stop=False,
                                 r=[("Pts", st, hh), ("cvb", cr)], w=[("ps", bo)])
                        else:
                            P.op("pe", "matmul", ps[bo][hs, 0:16], lhsT=vts[0:16, bl, fc:fc + 64],
                                 rhs=Pts[0:16, st, hh, 64:80], start=False, stop=True,
                                 r=[("Pts", st, hh), "vts"], w=[("ps", bo)])
                P.op("dve", "reciprocal", out=rdens[:, st, :], in_=ps[bd][:, 0:32], r=[("ps", bd)],
                     w=[("rdens", st)])
                for hh in range(2):
                    hs = slice(hh * 64, (hh + 1) * 64)
                    P.op("dve", "tensor_tensor", out=oaT[hs, hp, bl * 16:(bl + 1) * 16], in0=ps[bo][hs, 0:16],
                         in1=rdens[hs, st, hh * 16:(hh + 1) * 16], op=ALU.mult, r=[("ps", bo), ("rdens", st)],
                         w=[("oaT", hp)])
        else:
            items = [(qi, hp) for qi in range(ntile) for hp in range(4)]

            def sview(st, hh, j):
                if j < 4:
                    return st * 4 + hh, slice(j * 128, (j + 1) * 128)
                return st * 4 + 2 + hh, slice(0, 128)

            def s1(it, st):
                qi, hp = it
                qt = t0 + qi
                for j in range(5):
                    sl = (qt - 4 + j) % 8
                    for hh in range(2):
                        hs = slice(hh * 64, (hh + 1) * 64)
                        b, cs_ = sview(st, hh, j)
                        P.op("pe", "matmul", ps[b][:, cs_],
                             lhsT=kT[hs, hp, sl * 128:(sl + 1) * 128], rhs=qT[hs, hp, qi * 128:(qi + 1) * 128],
                             start=True, stop=True, r=[("kT", sl), ("qT", hp)], w=[("ps", b)])
                pk = [("Pt", st, 0), ("Pt", st, 1)]
                rk2 = [("ps", st * 4), ("ps", st * 4 + 1)]
                rk4 = [("ps", st * 4 + 2), ("ps", st * 4 + 3)]
                two = "p (h c) -> p h c"
                P.op("act", "activation", out=Pt[64:128, st, :, 0, :],
                     in_=psd[st * 2][64:128, :].rearrange(two, h=2)[:, :, 0:128], func=AF.Exp, r=rk2, w=pk)
                P.op("act", "activation", out=Pt[0:64, st, :, 0, 0:64],
                     in_=psd[st * 2][0:64, :].rearrange(two, h=2)[:, :, 0:64], func=AF.Exp, r=rk2, w=pk)
                P.op("act", "activation", out=Pt[:, st, :, 1:4, :].rearrange("p h j q -> p h (j q)"),
                     in_=psd[st * 2][:, :].rearrange(two, h=2)[:, :, 128:512], func=AF.Exp, r=rk2, w=pk)
                P.op("act", "activation", out=Pt[:, st, :, 4, :],
                     in_=psd[st * 2 + 1][:, :].rearrange(two, h=2)[:, :, 0:128], func=AF.Exp, r=rk4, w=pk)
                for jj in (3, 4):
                    P.op("dve", "tensor_tensor", out=Pt[:, st, :, jj, :], in0=Pt[:, st, :, jj, :],
                         in1=E[:, jj - 3, 2 * hp:2 * hp + 2, :], op=ALU.mult, r=pk + ["E"], w=pk)

            def s2(it, st):
                qi, hp = it
                qt = t0 + qi
                bd, bo = st * 4 + 2, st * 4 + 3
                for hh in range(2):
                    for j in range(5):
                        kt = qt - 4 + j
                        lw = vones[:, kt, :] if kt < 9 else ones_bf[:, :]
                        P.op("pe", "matmul", ps[bd][:, 256 + hh * 128:256 + (hh + 1) * 128], lhsT=lw,
                             rhs=Pt[:, st, hh, j, :], start=(j == 0), stop=(j == 4),
                             r=[("Pt", st, hh), "vones", "ones_bf"], w=[("ps", bd)])
                for hh in range(2):
                    hs = slice(hh * 64, (hh + 1) * 64)
                    fc = (2 * hp + hh) * 64
                    for j in range(5):
                        sl = (qt - 4 + j) % 8
                        P.op("pe", "matmul", ps[bo][hs, 256:384], lhsT=vt[:, sl, fc:fc + 64],
                             rhs=Pt[:, st, hh, j, :], start=(j == 0), stop=(j == 4),
                             r=[("Pt", st, hh), ("vt", sl)], w=[("ps", bo)])
                P.op("dve", "tensor_scalar_max", rden[:, st, :], ps[bd][:, 256:512], 1e-30, r=[("ps", bd)],
                     w=[("rden", st)])
                P.op("dve", "reciprocal", out=rden[:, st, :], in_=rden[:, st, :], r=[("rden", st)],
                     w=[("rden", st)])
                for hh in range(2):
                    hs = slice(hh * 64, (hh + 1) * 64)
                    P.op("dve", "tensor_tensor", out=oaT[hs, hp, qi * 128:(qi + 1) * 128], in0=ps[bo][hs, 256:384],
                         in1=rden[hs, st, hh * 128:(hh + 1) * 128], op=ALU.mult, r=[("ps", bo), ("rden", st)],
                         w=[("oaT", hp)])

        if sample:
            groups_it = [[(bl, hp) for hp in range(4)] for bl in range(4)]
        else:
            groups_it = [items]
        for its in groups_it:
            if sample:
                bl = its[0][0]
                P.op("pool", "dma_start", out=ckb[:, 0, :, :], in_=ckT_d[l, :, :, bl, :], w=[("ckb", 0)],
                     semkey=("ckb", 0))
                P.op("pool", "dma_start", out=cvb[:, 0, :, :], in_=cv_d[l, bl], w=[("cvb", 0)],
                     semkey=("cvb", 0))
            for i in range(len(its) + 1):
                if i < len(its):
                    s1(its[i], i % 2)
                if i >= 1:
                    s2(its[i - 1], (i - 1) % 2)

        ck(5)
        if sample:
            b = nb()
            for g in range(4):
                P.op("pe", "matmul", ps[b][:, g * 64:(g + 1) * 64], lhsT=vbns[0:64, g * 128:(g + 1) * 128],
                     rhs=Wblk[0:64, g, :], start=True, stop=False, r=["vbns", "Wblk"], w=bk(b))
                P.op("pe", "matmul", ps[b][:, g * 64:(g + 1) * 64], lhsT=ones_row[0:1, :],
                     rhs=bsps[0:1, g * 64:(g + 1) * 64], start=False, stop=True, r=["ones_row", "bsps"], w=bk(b))
            P.op("dve", "tensor_tensor", out=obT[:, :, 0:64], in0=ps[b][:, 0:256].rearrange("p (g t) -> p g t", g=4),
                 in1=ubT[:, :, 0:64], op=ALU.mult, r=bk(b) + [("ubT", c) for c in range(4)],
                 w=[("obT", c) for c in range(4)])
        else:
            for ti in range(ntile):
                b = nb()
                for g in range(4):
                    P.op("pe", "matmul", ps[b][:, g * 128:(g + 1) * 128], lhsT=vbn[:, ti, g * 128:(g + 1) * 128],
                         rhs=WcT[:, g, :], start=True, stop=False, r=[("vbn", ti), "WcT"], w=bk(b))
                    P.op("pe", "matmul", ps[b][:, g * 128:(g + 1) * 128], lhsT=ones_row[0:1, :],
                         rhs=bsp[0:1, g * 128:(g + 1) * 128], start=False, stop=True, r=["ones_row", "bsp"], w=bk(b))
                P.op("dve", "tensor_tensor", out=obT[:, :, ti * 128:(ti + 1) * 128],
                     in0=ps[b][:, 0:512].rearrange("p (g t) -> p g t", g=4), in1=ubT[:, :, ti * 128:(ti + 1) * 128],
                     op=ALU.mult, r=bk(b) + [("ubT", c) for c in range(4)], w=[("obT", c) for c in range(4)])

        ck(6)
        sa_, sb_ = ws_get(), ws_get()
        for c in range(8):
            ba, bb_ = nb(), nb()
            proj_fm(ba, sa_, 4, 1024, c * 128, lambda kc: oaT[:, kc, 0:T], lambda kc: [("oaT", kc)], T)
            proj_fm(bb_, sb_, 4, 1024, c * 128, lambda kc: obT[:, kc, 0:T], lambda kc: [("obT", kc)], T)
            P.op("dve", "tensor_tensor", out=tmpn[:, 0, 0:T], in0=ps[ba][:, 0:T], in1=SA[:, c, 0:T], op=ALU.mult,
                 r=bk(ba) + [("SA", c)], w=[("tmpn", 0)])
            P.op("dve", "tensor_tensor", out=tmpn[:, 1, 0:T], in0=ps[bb_][:, 0:T], in1=SA[:, 8 + c, 0:T], op=ALU.mult,
                 r=bk(bb_) + [("SA", 8 + c)], w=[("tmpn", 1)])
            P.op("dve", "tensor_tensor", out=SA[:, 16 + c, 0:T], in0=tmpn[:, 0, 0:T], in1=tmpn[:, 1, 0:T], op=ALU.add,
                 r=[("tmpn", 0), ("tmpn", 1)], w=[("SA", 16 + c)])
        ws_done()
        ws_done()
        ck(7)
        for s in range(2):
            slot = ws_get()
            for m in range(4):
                c = s * 4 + m
                b = nb()
                proj_fm(b, slot, 8, 512, m * 128, lambda kc: SA[:, 16 + kc, 0:T], lambda kc: [("SA", 16 + kc)], T)
                for (c0, n, bb) in groups:
                    P.op("dve", "scalar_tensor_tensor", out=cur["xT"][:, c, c0:c0 + n], in0=ps[b][:, c0:c0 + n],
                         scalar=modTt[:, cur["l"], 16 + c, bb:bb + 1], in1=cur["xT"][:, c, c0:c0 + n], op0=ALU.mult, op1=ALU.add,
                         r=bk(b) + [("xT", cur["par"], c), ("modT", cur["l"])], w=[("xT", cur["par"], c)])
            ws_done()

        ck(8)
        norm_mod(1, T, groups)
        GW = 72 if sample else T + 2
        for jj in range(11):
            slot = ws_get()
            for sub in range(2):
                j = 2 * jj + sub
                bg, bu = nb(), nb()
                proj_fm(bg, slot, 8, 512, sub * 128, hr, hk, T)
                proj_fm(bu, slot, 8, 512, 256 + sub * 128, hr, hk, T)
                r_ = nrot("gb", 3)
                if sample:
                    g3 = gb[:, r_, 0:72].rearrange("p (b t) -> p b t", t=18)
                    P.op("act", "activation", out=g3[:, :, 2:18],
                         in_=ps[bg][:, 0:64].rearrange("p (b t) -> p b t", t=16), func=AF.Copy, r=bk(bg),
                         w=[("gb", r_)])
                    P.op("dve", "tensor_copy", g3[:, :, 0:2], halos[:, j, :, :], r=["halos"], w=[("gb", r_)])
                    P.op("pool", "tensor_copy", ncs_st[:, j, :, :], g3[:, :, 16:18], r=[("gb", r_)], w=["ncs_st"])
                    views = [g3[:, :, k:k + 16] for k in range(3)]
                    r2 = nrot("gc", 2)
                    gcv = gc[:, r2, 0:64].rearrange("p (b t) -> p b t", t=16)
                else:
                    P.op("act", "activation", out=gb[:, r_, 2:2 + T], in_=ps[bg][:, 0:T], func=AF.Copy, r=bk(bg),
                         w=[("gb", r_)])
                    P.op("dve", "tensor_copy", gb[:, r_, 0:2], halo[:, j, :], r=["halo"], w=[("gb", r_)])
                    if t0 == 8:
                        P.op("pool", "tensor_scalar_mul", gb[:, r_, 128:130], gb[:, r_, 128:130], flag[:, 0:1],
                             r=[("gb", r_), "flag"], w=[("gb", r_)])
                    P.op("pool", "tensor_copy", halo[:, j, :], gb[:, r_, T:T + 2], r=[("gb", r_)], w=["halo"])
                    views = [gb[:, r_, k:k + T] for k in range(3)]
                    r2 = nrot("gc", 2)
                    gcv = gc[:, r2, 0:T]
                P.op("act", "activation", out=gcv, in_=views[0], func=AF.Identity, scale=cw[:, j, 0:1],
                     bias=cb[:, j:j + 1], r=[("gb", r_), "cw", "cb"], w=[("gc", r2)])
                for k in (1, 2):
                    P.op("dve", "scalar_tensor_tensor", out=gcv, in0=views[k], scalar=cw[:, j, k:k + 1], in1=gcv,
                         op0=ALU.mult, op1=ALU.add, r=[("gb", r_), ("gc", r2), "cw"], w=[("gc", r2)])
                r3 = nrot("ge", 2)
                P.op("act", "activation", out=ge[:, r3, 0:T], in_=gc[:, r2, 0:T], func=AF.Gelu_apprx_tanh,
                     r=[("gc", r2)], w=[("ge", r3)])
                P.op("dve", "tensor_tensor", out=SA[:, j, 0:T], in0=ps[bu][:, 0:T], in1=ge[:, r3, 0:T], op=ALU.mult,
                     r=bk(bu) + [("ge", r3)], w=[("SA", j)])
            ws_done()
        for c in range(8):
            slot = ws_get()
            b = nb()
            proj_fm(b, slot, 22, 128, 0, lambda kc: SA[:, kc, 0:T], lambda kc: [("SA", kc)], T)
            for (c0, n, bb) in groups:
                P.op("dve", "scalar_tensor_tensor", out=cur["xT"][:, c, c0:c0 + n], in0=ps[b][:, c0:c0 + n],
                     scalar=modTt[:, cur["l"], 40 + c, bb:bb + 1], in1=cur["xT"][:, c, c0:c0 + n], op0=ALU.mult, op1=ALU.add,
                     r=bk(b) + [("xT", cur["par"], c), ("modT", cur["l"])], w=[("xT", cur["par"], c)])
            ws_done()

        ck(9)
        xk = [("xT", cur["par"], c) for c in range(8)]
        if sample:
            if l == 0:
                P.op("pool", "tensor_copy", xs_keep[:, :, :], cur["xT"][:, :, 0:64], r=xk, w=["xs_keep"])
            else:
                P.op("sp", "dma_start", out=ysT_d, in_=cur["xT"][:, :, 0:64], r=xk, semkey="xst")
            P.op("sp", "dma_start", out=ncs_d[l], in_=ncs_st[:, :, :, :], r=["ncs_st"], semkey="ncs")
        elif l == 0:
            P.op("sp", "dma_start", out=x1s[:, :, (t0 - 4) * 128:(t0 - 4) * 128 + T], in_=cur["xT"][:, :, 0:T], r=xk,
                 w=[("x1s", t) for t in tiles], semkey="xst")
        else:
            lo = max(t0, OUT_T0)
            c0 = (lo - t0) * 128
            P.op("sp", "dma_start", out=yT_d[:, :, (lo - OUT_T0) * 128:(lo - OUT_T0) * 128 + T - c0],
                 in_=cur["xT"][:, :, c0:T], r=xk, semkey="xst")

    xs_keep = sb("xs_keep", (128, 8, 64))

    BLOCKS = []
    for l in range(2):
        BLOCKS.append((l, "kv", 0 if l == 0 else 4, 4))
        for (t0, n) in (L0_BLOCKS if l == 0 else L1_BLOCKS):
            BLOCKS.append((l, "full", t0, n))
        BLOCKS.append((l, "sample", 0, 0))

    try:
        for l in range(2):
            if stop is not None and stop == -1:
                raise _Stop()
            if l == 0:
                ada_setup(0)
            layer_setup(l)
            if stop is not None and stop == 0:
                raise _Stop()
            if l == 1:
                P.op("pool", "memset", halo[:, :, :], 0.0, r=["halo"], w=["halo"])
            for (l_, kind, t0, n) in [b for b in BLOCKS if b[0] == l]:
                if kind == "sample":
                    P.op("sp", "dma_start", out=ncv_d[l], in_=halo[:, :, :], r=["halo"], semkey="ncv")
                    if l == 0:
                        ada_setup(1)
                block(l_, kind, t0, n)
                nblk[0] += 1
                if stop is not None and nblk[0] >= stop:
                    raise _Stop()
        assert ws["consumed"] == len(ws["seq"]), (ws["consumed"], len(ws["seq"]))
    except _Stop:
        dbg = dout("dbgx", (128, 8, 512))
        P.op("sp", "dma_start", out=dbg, in_=cur["xT"][:, :, :], r=[("xT", cur["par"], c) for c in range(8)], semkey="dbg")
        dbg2 = dout("dbgh", (128, 8, 512))
        P.op("pool", "dma_start", out=dbg2, in_=tmpn[:, :, :].rearrange("p a b -> p (a b)"), r=[("tmpn", 0), ("tmpn", 1)], semkey="dbg") if False else None

    P.emit(nc, stack)
    stack.close()
    return nc


def _fm(a):
    t, f = a.shape
    return np.ascontiguousarray(a.reshape(t, f // 128, 128).transpose(2, 1, 0))


def _slab(wm):
    k, m = wm.shape
    return np.ascontiguousarray(wm.reshape(k // 128, 128, m).transpose(1, 0, 2)).reshape(128, (k // 128) * m)


_NC_CACHE = {}


def kernel(x_prompt, x_sample, cache_attn_k, cache_attn_v, cache_ffn_conv, c_prompt, c_sample,
           norm1_g, norm2_g, w_ada, b_ada, w_in, q_norm_g, k_norm_g, rel_bias, v_norm_g,
           w_spatial, b_spatial, w_out_a, w_out_b, w_out, w_ffn_in, ffn_conv_w, ffn_conv_b, w_ffn_out):
    f = lambda a: np.asarray(a, dtype=np.float32)
    x_prompt, x_sample, cache_attn_k, cache_attn_v, cache_ffn_conv = map(f, (x_prompt, x_sample, cache_attn_k, cache_attn_v, cache_ffn_conv))
    c_prompt, c_sample, norm1_g, norm2_g, w_ada, b_ada, w_in = map(f, (c_prompt, c_sample, norm1_g, norm2_g, w_ada, b_ada, w_in))
    q_norm_g, k_norm_g, rel_bias, v_norm_g, w_spatial, b_spatial = map(f, (q_norm_g, k_norm_g, rel_bias, v_norm_g, w_spatial, b_spatial))
    w_out_a, w_out_b, w_out, w_ffn_in, ffn_conv_w, ffn_conv_b, w_ffn_out = map(f, (w_out_a, w_out_b, w_out, w_ffn_in, ffn_conv_w, ffn_conv_b, w_ffn_out))

    in_maps = _prep(x_prompt, x_sample, cache_attn_k, cache_attn_v, cache_ffn_conv, c_prompt, c_sample,
                    norm1_g, norm2_g, w_ada, b_ada, w_in, q_norm_g, k_norm_g, rel_bias, v_norm_g,
                    w_spatial, b_spatial, w_out_a, w_out_b, w_out, w_ffn_in, ffn_conv_w, ffn_conv_b, w_ffn_out)
    if "nc" not in _NC_CACHE:
        _NC_CACHE["nc"] = build_nc()
    nc = _NC_CACHE["nc"]
    res = run_bass_kernel_spmd(nc, in_maps, core_ids=list(range(NCORES)))
    return _post(res.results)


def _prep(x_prompt, x_sample, cache_attn_k, cache_attn_v, cache_ffn_conv, c_prompt, c_sample,
          norm1_g, norm2_g, w_ada, b_ada, w_in, q_norm_g, k_norm_g, rel_bias, v_norm_g,
          w_spatial, b_spatial, w_out_a, w_out_b, w_out, w_ffn_in, ffn_conv_w, ffn_conv_b, w_ffn_out):

    wall = np.empty((2, 24, 128, 4096), np.float32)
    wfo = np.empty((2, 8, 128, 2816), np.float32)
    wada = np.empty((2, 12, 128, 4096), np.float32)
    for l in range(2):
        for s in range(9):
            wall[l, s] = _slab(w_in[l][:, s * 512:(s + 1) * 512])
        wall[l, 9] = _slab(w_out_a[l])
        wall[l, 10] = _slab(w_out_b[l])
        for s in range(2):
            wall[l, 11 + s] = _slab(w_out[l][:, s * 512:(s + 1) * 512])
        for jj in range(11):
            idx = np.concatenate([np.arange(2 * jj * 128, (2 * jj + 2) * 128), DFF + np.arange(2 * jj * 128, (2 * jj + 2) * 128)])
            wall[l, 13 + jj] = _slab(w_ffn_in[l][:, idx])
        for c in range(8):
            wfo[l, c] = _slab(w_ffn_out[l][:, c * 128:(c + 1) * 128])
        for s in range(12):
            wada[l, s] = _slab(w_ada[l][:, s * 512:(s + 1) * 512])
    col = lambda v: np.ascontiguousarray(v.reshape(-1, 128).T)
    nrm = np.stack([np.concatenate([col(norm1_g[l]), col(norm2_g[l])], 1) for l in range(2)])
    bada = np.stack([col(b_ada[l]) for l in range(2)])
    qkg = np.stack([np.stack([np.tile(q_norm_g[l], 2), np.tile(k_norm_g[l], 2)], 1) for l in range(2)])
    vg = np.stack([np.broadcast_to(v_norm_g[l][None, :], (128, 512)) for l in range(2)]).copy()
    bsp = b_spatial.reshape(2, 1, 512).copy()
    bsps = np.stack([np.tile(b_spatial[l][:, None, :16], (1, 4, 1)).reshape(1, 256) for l in range(2)])
    wspT = np.ascontiguousarray(w_spatial.transpose(0, 3, 1, 2))
    wspb = np.zeros((2, 64, 4, 64), np.float32)
    for bl in range(4):
        wspb[:, bl * 16:(bl + 1) * 16, :, bl * 16:(bl + 1) * 16] = wspT[:, 0:16, :, 0:16]
    s_i = np.arange(128)[:, None]
    t_i = np.arange(128)[None, :]
    tril = np.broadcast_to((s_i <= t_i).astype(np.float32)[:, None, :], (128, 4, 128)).copy()
    trilb = np.zeros((64, 4, 64), np.float32)
    for bl in range(4):
        trilb[bl * 16:(bl + 1) * 16, :, bl * 16:(bl + 1) * 16] = tril[0:16, :, 0:16]
    cw = np.ascontiguousarray(ffn_conv_w.reshape(2, 3, 22, 128).transpose(0, 3, 2, 1))
    cb = np.ascontiguousarray(ffn_conv_b.reshape(2, 22, 128).transpose(0, 2, 1))
    ki = np.arange(128)[:, None]
    qi = np.arange(128)[None, :]
    idx1 = np.clip(128 + qi - ki, -128, 128) + 128
    idx0 = np.clip(qi - ki, -128, 128) + 128
    Tb = np.stack([np.stack([rel_bias[l][:, idx1].transpose(1, 0, 2), rel_bias[l][:, idx0].transpose(1, 0, 2)], 1)
                   for l in range(2)])
    b256 = np.stack([np.broadcast_to(rel_bias[l][None, :, 256], (128, 8)) for l in range(2)]).copy()

    shared = dict(wall=wall, wfo=wfo, wada=wada, nrm=nrm, bada=bada, qkg=qkg, vg=vg, bsp=bsp, bsps=bsps, wspT=wspT,
                  wspb=wspb, tril=tril, trilb=trilb, cw=cw, cb=cb, Tb=np.ascontiguousarray(Tb), b256=b256)
    shared = {k: np.ascontiguousarray(v, dtype=np.float32) for k, v in shared.items()}

    in_maps = []
    for core in range(NCORES):
        b, seg = core // 4, core % 4
        s0 = seg * SEG
        a = s0 - HALO
        xw = np.zeros((W, D), np.float32)
        lo = max(a, 0)
        xw[lo - a:] = x_prompt[b, lo:s0 + SEG]
        valid = (np.arange(W) + a >= 0).astype(np.float32)
        m = dict(shared)
        m["xin"] = _fm(xw)
        m["xs"] = _fm(x_sample[4 * core:4 * core + 4].reshape(64, D))
        m["vtok"] = np.ascontiguousarray(valid.reshape(NT, 128).T)
        m["flag"] = np.full((128, 1), 1.0 if a >= 0 else 0.0, np.float32)
        cc = np.concatenate([c_prompt[b:b + 1], c_sample[4 * core:4 * core + 4]], 0)
        m["cT"] = np.ascontiguousarray(cc.reshape(5, 8, 128).transpose(2, 1, 0))
        ck = cache_attn_k[:, 4 * core:4 * core + 4]
        m["ckT"] = np.ascontiguousarray(ck.reshape(2, 4, 512, 4, 2, 64).transpose(0, 4, 5, 3, 1, 2)).reshape(2, 128, 4, 4, 512)
        cvv = cache_attn_v[:, 4 * core:4 * core + 4].reshape(2, 4, 4, 128, 512)
        m["cv"] = np.ascontiguousarray(cvv.transpose(0, 1, 3, 2, 4))
        cc2 = cache_ffn_conv[:, 4 * core:4 * core + 4].reshape(2, 4, 2, 22, 128)
        m["cconv"] = np.ascontiguousarray(cc2.transpose(0, 4, 3, 1, 2))
        in_maps.append(m)
    return in_maps


def _post(R):

    y_prompt = np.empty((2, 8192, D), np.float32)
    y_sample = np.empty((32, 16, D), np.float32)
    nkp = np.empty((2, 2, 512, 8, 64), np.float32)
    nvp = np.empty((2, 2, 512, 8, 64), np.float32)
    ncp = np.empty((2, 2, 2, DFF), np.float32)
    nks = np.empty((2, 32, 16, 8, 64), np.float32)
    nvs = np.empty((2, 32, 16, 8, 64), np.float32)
    nbs = np.empty((2, 32, 16, 4, 128), np.float32)
    ncs = np.empty((2, 32, 2, DFF), np.float32)
    for core in range(NCORES):
        r = R[core]
        b, seg = core // 4, core % 4
        y_prompt[b, seg * SEG:(seg + 1) * SEG] = r["yT"].transpose(2, 1, 0).reshape(SEG, D)
        y_sample[4 * core:4 * core + 4] = r["ysT"].transpose(2, 1, 0).reshape(4, 16, D)
        sl = slice(4 * core, 4 * core + 4)
        for l in range(2):
            if seg == 3:
                nkp[l, b] = r["nkT"][l].reshape(2, 64, 4, 512).transpose(3, 2, 0, 1).reshape(512, 8, 64)
                nvp[l, b] = r["nv"][l].reshape(512, 8, 64)
                ncp[l, b] = r["ncv"][l].transpose(2, 1, 0).reshape(2, DFF)
            nks[l, sl] = r["nksT"][l].reshape(2, 64, 4, 4, 16).transpose(3, 4, 2, 0, 1).reshape(4, 16, 8, 64)
            nvs[l, sl] = r["nvs"][l].transpose(1, 0, 2).reshape(4, 16, 8, 64)
            nbs[l, sl] = r["nbs"][l].reshape(4, 16, 4, 128)
            ncs[l, sl] = r["ncs"][l].transpose(2, 3, 1, 0).reshape(4, 2, DFF)
    return (y_prompt, y_sample, nkp, nvp, ncp, nks, nvs, nbs, ncs)
```

```python
import contextlib
import numpy as np
import concourse.bass as bass
import concourse.mybir as mybir
from concourse.bass_utils import run_bass_kernel_spmd

F32 = mybir.dt.float32
BF16 = mybir.dt.bfloat16
AF = mybir.ActivationFunctionType
ALU = mybir.AluOpType
AX = mybir.AxisListType

NCORES = 8
D = 1024
SEG = 2048
HALO = 1152
W = SEG + HALO
NT = W // 128
DFF = 2816
EPS = 1e-6
NB = 3
L0_BLOCKS = [(4, 4), (8, 4), (12, 4), (16, 3), (19, 3), (22, 3)]
L1_BLOCKS = [(8, 4), (12, 4), (16, 3), (19, 3), (22, 3)]
OUT_T0 = 9
KEEP_T0 = 21
_DBG = {}


class Prog:
    STREAMS = ("sp", "act", "dve", "pool", "pe")

    def __init__(self):
        self.ops = []
        self.keyw = {}
        self.keyr = {}
        self.dcnt = {}

    def op(self, stream, method, *args, r=(), w=(), semkey=None, **kw):
        idx = len(self.ops)
        dom = ("d", semkey) if semkey is not None else ("s", stream)
        deps = {}

        def add(d, i):
            if d[0] == "d":
                i = self.dcnt[d]
            if deps.get(d, -1) < i:
                deps[d] = i

        for k in r:
            for d, i in self.keyw.get(k, {}).items():
                add(d, i)
        skip_same = dom[0] == "d" or stream == "pe"
        for k in w:
            for d, i in self.keyr.get(k, {}).items():
                if d == dom and (skip_same or i == idx):
                    continue
                add(d, i)
            for d, i in self.keyw.get(k, {}).items():
                if d == dom and skip_same:
                    continue
                add(d, i)
        for k in r:
            self.keyr.setdefault(k, {})[dom] = idx
        for k in w:
            if self.keyr.get(k):
                self.keyw[k] = {dom: idx}
                self.keyr[k] = {}
            else:
                self.keyw.setdefault(k, {})[dom] = idx
        if dom[0] == "d":
            self.dcnt[dom] = self.dcnt.get(dom, 0) + 16
        self.ops.append(dict(stream=stream, method=method, args=args, kw=kw, dom=dom,
                             deps=list(deps.items()), stage=_DBG.get("stage", "")))
        return idx

    def emit(self, nc, stack):
        ops = self.ops
        needs = set()
        for o in ops:
            for d, i in o["deps"]:
                if d[0] == "s":
                    needs.add(i)
        cnt = {}
        sems = {}
        issuer = {}
        for i, o in enumerate(ops):
            d = o["dom"]
            if d[0] == "d":
                cnt[d] = cnt.get(d, 0) + 16
                o["done"] = cnt[d]
                assert issuer.setdefault(d, o["stream"]) == o["stream"], d
            elif i in needs:
                cnt[d] = cnt.get(d, 0) + 1
                o["done"] = cnt[d]
            else:
                o["done"] = None
        for d in cnt:
            sems[d] = stack.enter_context(nc.semaphore("s%d" % len(sems)))
        block = stack.enter_context(nc.Block())
        self.nsem = len(sems)

        def run(stream, eng):
            waited = {}
            for o in ops:
                if o["stream"] != stream:
                    continue
                for d, i in o["deps"]:
                    v = i if d[0] == "d" else ops[i]["done"]
                    if waited.get(d, 0) < v:
                        eng.wait_ge(sems[d], v)
                        waited[d] = v
                        _DBG.setdefault("waits", {}).setdefault(stream, []).append((o["stage"], d))
                ins = getattr(eng, o["method"])(*o["args"], **o["kw"])
                if o["done"] is not None:
                    d = o["dom"]
                    ins.then_inc(sems[d], 16 if d[0] == "d" else 1)
            for d, v in cnt.items():
                if d[0] == "d" and issuer[d] == stream and waited.get(d, 0) < v:
                    eng.wait_ge(sems[d], v)

        @block.sync
        def _(e):
            run("sp", e)

        @block.scalar
        def _(e):
            run("act", e)

        @block.vector
        def _(e):
            run("dve", e)

        @block.gpsimd
        def _(e):
            run("pool", e)

        @block.tensor
        def _(e):
            run("pe", e)


def build_nc(stop=None):
    class _Stop(Exception):
        pass

    nblk = [0]

    def ck(st):
        _DBG["stage"] = st + (10 if _DBG.get("stage", 0) >= 10 else 0)
        if stop is not None and nblk[0] + st / 10.0 >= stop - 1e-9 and stop > 0:
            raise _Stop()

    nc = bass.Bass("TRN2", target_bir_lowering=False)
    P = Prog()
    stack = contextlib.ExitStack()

    def din(name, shape):
        return nc.dram_tensor(name, list(shape), F32, kind="ExternalInput").ap()

    def dout(name, shape):
        return nc.dram_tensor(name, list(shape), F32, kind="ExternalOutput").ap()

    def dint(name, shape, dt):
        return nc.dram_tensor(name, list(shape), dt, kind="Internal").ap()

    xin = din("xin", (128, 8, W))
    xs_d = din("xs", (128, 8, 64))
    vtok_d = din("vtok", (128, NT))
    flag_d = din("flag", (128, 1))
    cT_d = din("cT", (128, 8, 5))
    ckT_d = din("ckT", (2, 128, 4, 4, 512))
    cv_d = din("cv", (2, 4, 128, 4, 512))
    cconv_d = din("cconv", (2, 128, 22, 4, 2))
    wall_d = din("wall", (2, 24, 128, 4096))
    wfo_d = din("wfo", (2, 8, 128, 2816))
    wada_d = din("wada", (2, 12, 128, 4096))
    nrm_d = din("nrm", (2, 128, 16))
    bada_d = din("bada", (2, 128, 48))
    qkg_d = din("qkg", (2, 128, 2))
    vg_d = din("vg", (2, 128, 512))
    bsp_d = din("bsp", (2, 1, 512))
    bsps_d = din("bsps", (2, 1, 256))
    wspT_d = din("wspT", (2, 128, 4, 128))
    wspb_d = din("wspb", (2, 64, 4, 64))
    tril_d = din("tril", (128, 4, 128))
    trilb_d = din("trilb", (64, 4, 64))
    cw_d = din("cw", (2, 128, 22, 3))
    cb_d = din("cb", (2, 128, 22))
    Tb_d = din("Tb", (2, 128, 2, 8, 128))
    b256_d = din("b256", (2, 128, 8))

    yT_d = dout("yT", (128, 8, SEG))
    ysT_d = dout("ysT", (128, 8, 64))
    nkT_d = dout("nkT", (2, 128, 4, 512))
    nv_d = dout("nv", (2, 4, 128, 512))
    ncv_d = dout("ncv", (2, 128, 22, 2))
    nksT_d = dout("nksT", (2, 128, 4, 64))
    nvs_d = dout("nvs", (2, 16, 4, 512))
    nbs_d = dout("nbs", (2, 64, 512))
    ncs_d = dout("ncs", (2, 128, 22, 4, 2))

    wsc = dint("wsc", (2, 24, 128, 4096), BF16)
    wsc_fo = dint("wscfo", (2, 8, 128, 2816), BF16)
    x1s = dint("x1s", (128, 8, 21 * 128), F32)

    def sb(name, shape, dt=F32):
        return stack.enter_context(nc.sbuf_tensor("sb_" + name, list(shape), dt))

    xTt = sb("xT", (128, 2, 8, 512))
    cur = {"xT": xTt[:, 0], "par": 0, "idx": 0}
    hT = sb("hT", (128, 8, 512), BF16)
    sq = sb("sq", (128, 3, 512), BF16)
    rstd = sb("rstd", (128, 512))
    tmpn = sb("tmpn", (128, 2, 512))
    qT = sb("qT", (128, 4, 512), BF16)
    kst = sb("kst", (128, 2, 512))
    rs = sb("rs", (128, 2, 512))
    kT = sb("kT", (128, 4, 1024), BF16)
    vt = sb("vt", (128, 8, 512), BF16)
    ubT = sb("ubT", (128, 4, 512), BF16)
    vbn = sb("vbn", (128, 4, 512), BF16)
    gl = sb("gl", (128, 2, 512))
    ssv = sb("ssv", (128, 4))
    SA = sb("SA", (128, 24, 512), BF16)
    oaT = sb("oaT", (128, 4, 512), BF16)
    obT = sb("obT", (128, 4, 512), BF16)
    Pt = sb("Pt", (128, 2, 2, 5, 128), BF16)
    rden = sb("rden", (128, 2, 256))
    gb = sb("gb", (128, 3, 516))
    gc = sb("gc", (128, 2, 512))
    ge = sb("ge", (128, 2, 512))
    halo = sb("halo", (128, 22, 2))
    halos = sb("halos", (128, 22, 4, 2))
    ncs_st = sb("ncs_st", (128, 22, 4, 2))
    wslab = sb("wslab", (128, NB, 4096), BF16)
    ones_bf = sb("ones_bf", (128, 128), BF16)
    blk64 = sb("blk64", (128, 128), BF16)
    ones_row = sb("ones_row", (1, 128), BF16)
    vones = sb("vones", (128, 9, 128), BF16)
    vtok = sb("vtok", (128, NT))
    flag = sb("flag", (128, 1))
    cTs = sb("cTs", (128, 8, 5))
    cs = sb("cs", (128, 8, 5), BF16)
    E = sb("E", (128, 2, 8, 128))
    negb = sb("negb", (128, 8))
    modTt = sb("modT", (128, 2, 48, 5))
    A12t = sb("A12", (128, 2, 2, 8, 5))
    nrmt = sb("nrm", (128, 2, 16))
    badat = sb("bada", (128, 2, 48))
    qkg = sb("qkg", (128, 2))
    vg = sb("vg", (128, 512))
    bsp = sb("bsp", (1, 512), BF16)
    bsp_f = sb("bsp_f", (1, 768))
    bsps = sb("bsps", (1, 256), BF16)
    WcT = sb("WcT", (128, 4, 128), BF16)
    Wblk = sb("Wblk", (64, 4, 64), BF16)
    cw = sb("cw", (128, 22, 3))
    cb = sb("cb", (128, 22))
    kTs = sb("kTs", (128, 4, 64), BF16)
    vts = sb("vts", (16, 4, 512), BF16)
    vbns = sb("vbns", (64, 512), BF16)
    ckb = sb("ckb", (128, 1, 4, 512), BF16)
    cvb = sb("cvb", (128, 1, 4, 512), BF16)
    Pts = sb("Pts", (128, 2, 2, 80), BF16)
    exs = sb("exs", (128, 2, 2, 32))
    rdens = sb("rdens", (128, 2, 32))

    psd = [stack.enter_context(nc.psum_tensor("ps%d" % i, [128, 1024], F32)) for i in range(4)]
    ps = [psd[i // 2][:, (i % 2) * 512:(i % 2 + 1) * 512] for i in range(8)]

    def bk(i):
        return [("ps", i)]

    bank_ctr = [0]

    reserved = set()

    def nb():
        while True:
            b = bank_ctr[0] % 8
            bank_ctr[0] += 1
            if b not in reserved:
                return b

    rot = {}

    def nrot(name, n):
        v = rot.get(name, 0)
        rot[name] = v + 1
        return v % n

    ws = dict(seq=[], issued=0, consumed=0)

    casted = set()

    def ws_issue(upto):
        while ws["issued"] < min(upto, len(ws["seq"])):
            kind_, l_, s_ = ws["seq"][ws["issued"]]
            slot = ws["issued"] % NB
            wk, sk = [("wslab", slot)], ("w", slot)
            if kind_ == "ada":
                P.op("pool", "dma_start", out=wslab[:, slot, :], in_=wada_d[l_, s_], w=wk, semkey=sk)
            else:
                ncols = 4096 if kind_ == "wall" else 2816
                src32 = wall_d[l_, s_] if kind_ == "wall" else wfo_d[l_, s_]
                scr = wsc[l_, s_] if kind_ == "wall" else wsc_fo[l_, s_]
                key = ("wsc", kind_, l_, s_)
                if key not in casted:
                    casted.add(key)
                    P.op("pool", "dma_start", out=wslab[:, slot, 0:ncols], in_=src32, w=wk, semkey=sk)
                    P.op("sp", "dma_start", out=scr, in_=wslab[:, slot, 0:ncols], r=wk, w=[key],
                         semkey=("wst", slot))
                else:
                    P.op("pool", "dma_start", out=wslab[:, slot, 0:ncols], in_=scr, r=[key], w=wk, semkey=sk)
            ws["issued"] += 1

    def ws_get():
        assert ws["consumed"] < len(ws["seq"])
        ws_issue(ws["consumed"] + 1)
        slot = ws["consumed"] % NB
        ws["consumed"] += 1
        return slot

    def ws_done():
        ws_issue(ws["consumed"] + NB)

    def seq_ada(l):
        for s_ in range(12):
            ws["seq"].append(("ada", l, s_))

    def seq_block(l, kind):
        if kind == "kv":
            for s_ in (1, 2):
                ws["seq"].append(("wall", l, s_))
            return
        for s_ in range(24):
            ws["seq"].append(("wall", l, s_))
        for s_ in range(8):
            ws["seq"].append(("fo", l, s_))

    seq_ada(0)
    for l in range(2):
        seq_block(l, "kv")
        for _ in (L0_BLOCKS if l == 0 else L1_BLOCKS):
            seq_block(l, "full")
        if l == 0:
            seq_ada(1)
        seq_block(l, "sample")

    P.op("pool", "memset", ones_bf[:, :], 1.0, w=["ones_bf"])
    P.op("pool", "memset", blk64[:, :], 0.0, w=["blk64"])
    P.op("pool", "memset", blk64[0:64, 0:64], 1.0, w=["blk64"])
    P.op("pool", "memset", blk64[64:128, 64:128], 1.0, w=["blk64"])
    P.op("pool", "memset", ones_row[:, :], 1.0, w=["ones_row"])
    P.op("pool", "memset", Pt[:, :, :, :, :].rearrange("p a b c d -> p (a b c d)"), 0.0, w=[("Pt", s, h) for s in range(2) for h in range(2)])
    P.op("pool", "memset", halo[:, :, :], 0.0, w=["halo"])
    P.op("sp", "dma_start", out=vtok[:, :], in_=vtok_d, w=["vtok"], semkey="c0")
    P.op("sp", "dma_start", out=flag[:, :], in_=flag_d, w=["flag"], semkey="c0")
    P.op("sp", "dma_start", out=cTs[:, :, :], in_=cT_d, w=["cTs"], semkey="c0")
    P.op("act", "activation", out=cs[:, :, :], in_=cTs[:, :, :], func=AF.Silu, r=["cTs"], w=["cs"])
    for t in range(9):
        P.op("act", "activation", out=vones[:, t, :], in_=ones_bf[:, :], func=AF.Copy,
             scale=vtok[:, t:t + 1], r=["ones_bf", "vtok"], w=["vones"])
    def layer_setup(l):
        for dst, src, key in ((qkg, qkg_d[l], "qkg"),
                              (vg, vg_d[l], "vg"), (cw, cw_d[l], "cw"), (cb, cb_d[l], "cb"),
                              (negb, b256_d[l], "negb")):
            full = tuple(slice(None) for _ in dst.shape)
            P.op("sp", "dma_start", out=dst[full], in_=src, w=[key], semkey="ls")
        P.op("sp", "dma_start", out=E[:, :, :, :], in_=Tb_d[l], w=["E"], semkey="ls")
        P.op("sp", "dma_start", out=halos[:, :, :, :], in_=cconv_d[l], w=["halos"], semkey="ls")
        P.op("sp", "dma_start", out=bsp_f[:, 0:512], in_=bsp_d[l], w=["bsp_f"], semkey="ls")
        P.op("sp", "dma_start", out=bsp_f[:, 512:768], in_=bsps_d[l], w=["bsp_f"], semkey="ls")
        P.op("sp", "dma_start", out=gc[:, 0, :], in_=wspT_d[l].rearrange("p g t -> p (g t)"),
             w=[("gc", 0)], semkey="ls3")
        P.op("sp", "dma_start", out=gc[:, 1, :], in_=tril_d.rearrange("p g t -> p (g t)"),
             w=[("gc", 1)], semkey="ls3")
        P.op("sp", "dma_start", out=ge[0:64, 0, 0:256], in_=wspb_d[l].rearrange("p g t -> p (g t)"),
             w=[("ge", 0)], semkey="ls3")
        P.op("sp", "dma_start", out=ge[0:64, 1, 0:256], in_=trilb_d.rearrange("p g t -> p (g t)"),
             w=[("ge", 1)], semkey="ls3")
        P.op("dve", "tensor_tensor", out=WcT[:, :, :].rearrange("p g t -> p (g t)"), in0=gc[:, 0, :],
             in1=gc[:, 1, :], op=ALU.mult, r=[("gc", 0), ("gc", 1)], w=["WcT"])
        P.op("dve", "tensor_tensor", out=Wblk[:, :, :].rearrange("p g t -> p (g t)"), in0=ge[0:64, 0, 0:256],
             in1=ge[0:64, 1, 0:256], op=ALU.mult, r=[("ge", 0), ("ge", 1)], w=["Wblk"])
        P.op("dve", "tensor_copy", bsp[:, :], bsp_f[:, 0:512], r=["bsp_f"], w=["bsp"])
        P.op("dve", "tensor_copy", bsps[:, :], bsp_f[:, 512:768], r=["bsp_f"], w=["bsps"])
        P.op("dve", "tensor_scalar_mul", qkg[:, 1:2], qkg[:, 1:2], 8.0, r=["qkg"], w=["qkg"])
        P.op("dve", "tensor_scalar_mul", negb[:, :], negb[:, :], -1.0, r=["negb"], w=["negb"])
        for t in range(2):
            for h in range(8):
                P.op("act", "activation", out=E[:, t, h, :], in_=E[:, t, h, :], func=AF.Exp,
                     bias=negb[:, h:h + 1], scale=1.0, r=["E", "negb"], w=["E"])
        P.op("pool", "memset", E[64:128, 1, :, 0:64], 0.0, r=["E"], w=["E"])

    def ada_setup(l):
        modT, A12, nrm, bada = modTt[:, l], A12t[:, l], nrmt[:, l], badat[:, l]
        P.op("sp", "dma_start", out=nrm, in_=nrm_d[l], w=[("nrm", l)], semkey=("lsa", l))
        P.op("sp", "dma_start", out=bada, in_=bada_d[l], w=[("bada", l)], semkey=("lsa", l))
        b = nb()
        for s_ in range(12):
            slot = ws_get()
            for m in range(4):
                ci = s_ * 4 + m
                for kc in range(8):
                    P.op("pe", "matmul", ps[b][:, ci * 5:(ci + 1) * 5],
                         lhsT=wslab[:, slot, kc * 512 + m * 128: kc * 512 + (m + 1) * 128],
                         rhs=cs[:, kc, :], start=(kc == 0), stop=(kc == 7),
                         r=[("wslab", slot), "cs"], w=bk(b))
            ws_done()
        for bb in range(5):
            P.op("dve", "tensor_tensor", out=modT[:, :, bb],
                 in0=ps[b][:, 0:240].rearrange("p (c b) -> p c b", b=5)[:, :, bb], in1=bada,
                 op=ALU.add, r=bk(b) + [("bada", l)], w=[("modT", l)])
        for which in range(2):
            sc0 = 8 if which == 0 else 32
            for bb in range(5):
                P.op("dve", "tensor_scalar", out=A12[:, which, :, bb], in0=modT[:, sc0:sc0 + 8, bb],
                     scalar1=1.0, scalar2=32.0, op0=ALU.add, op1=ALU.mult, r=[("modT", l)], w=[("A12", l)])
                P.op("dve", "tensor_tensor", out=A12[:, which, :, bb], in0=A12[:, which, :, bb],
                     in1=nrm[:, which * 8:(which + 1) * 8], op=ALU.mult, r=[("A12", l), ("nrm", l)],
                     w=[("A12", l)])

    def norm_sq(b, c, T):
        r_ = nrot("sq", 3)
        P.op("act", "activation", out=sq[:, r_, 0:T], in_=cur["xT"][:, c, 0:T], func=AF.Square,
             r=[("xT", cur["par"], c)], w=[("sq", r_)])
        return r_

    def norm_ss(b, c, r_, T):
        P.op("pe", "matmul", ps[b][:, 0:T], lhsT=ones_bf[:, :], rhs=sq[:, r_, 0:T],
             start=(c == 0), stop=(c == 7), r=[("sq", r_), "ones_bf"], w=bk(b))

    def norm_fin(b, T):
        P.op("act", "activation", out=rstd[:, 0:T], in_=ps[b][:, 0:T], func=AF.Sqrt, bias=float(D * EPS),
             scale=1.0, r=bk(b), w=["rstd"])
        P.op("dve", "reciprocal", out=rstd[:, 0:T], in_=rstd[:, 0:T], r=["rstd"], w=["rstd"])

    def norm_apply(which, T, groups):
        shc = 0 if which == 0 else 24
        for c in range(8):
            r_ = nrot("tmpn", 2)
            for (c0, n, bb) in groups:
                P.op("dve", "scalar_tensor_tensor", out=tmpn[:, r_, c0:c0 + n], in0=cur["xT"][:, c, c0:c0 + n],
                     scalar=A12t[:, cur["l"], which, c, bb:bb + 1], in1=rstd[:, c0:c0 + n], op0=ALU.mult, op1=ALU.mult,
                     r=[("xT", cur["par"], c), ("A12", cur["l"]), "rstd"], w=[("tmpn", r_)])
            for (c0, n, bb) in groups:
                P.op("act", "activation", out=hT[:, c, c0:c0 + n], in_=tmpn[:, r_, c0:c0 + n],
                     func=AF.Identity, bias=modTt[:, cur["l"], shc + c, bb:bb + 1], scale=1.0,
                     r=[("tmpn", r_), ("modT", cur["l"])], w=[("hT", c)])

    def norm_mod(which, T, groups):
        b = nb()
        for c in range(8):
            r_ = norm_sq(b, c, T)
            norm_ss(b, c, r_, T)
        norm_fin(b, T)
        norm_apply(which, T, groups)

    pro_done = set()

    def blk_geom(i):
        l_, kind_, t0_, n_ = BLOCKS[i]
        if kind_ == "sample":
            return l_, 64, [(bl * 16, 16, 1 + bl) for bl in range(4)]
        return l_, n_ * 128, [(0, n_ * 128, 0)]

    def pro_ok(i):
        return i < len(BLOCKS) and not (BLOCKS[i][1] == "sample" and BLOCKS[i][0] == 1)

    class _NextCtx:
        def __init__(self, i):
            self.i = i

        def __enter__(self):
            self.save = dict(cur)
            cur["l"] = BLOCKS[self.i][0]
            cur["par"] = self.i % 2
            cur["xT"] = xTt[:, self.i % 2]

        def __exit__(self, *a):
            cur.update(self.save)

    def proj_fm(b, slot, nkc, ms, col0, rhs_t, rkeys, T):
        for kc in range(nkc):
            P.op("pe", "matmul", ps[b][:, 0:T], lhsT=wslab[:, slot, kc * ms + col0: kc * ms + col0 + 128],
                 rhs=rhs_t(kc), start=(kc == 0), stop=(kc == nkc - 1),
                 r=[("wslab", slot)] + rkeys(kc), w=bk(b))

    def headnorm(b, T, gcol, out_ap, out_keys):
        r_ = nrot("sq", 3)
        P.op("act", "activation", out=sq[:, r_, 0:T], in_=ps[b][:, 0:T], func=AF.Square, r=bk(b), w=[("sq", r_)])
        b2 = nb()
        P.op("pe", "matmul", ps[b2][:, 0:T], lhsT=blk64[:, :], rhs=sq[:, r_, 0:T], start=True, stop=True,
             r=[("sq", r_), "blk64"], w=bk(b2))
        r2 = nrot("rs", 2)
        P.op("act", "activation", out=rs[:, r2, 0:T], in_=ps[b2][:, 0:T], func=AF.Sqrt, bias=float(64 * EPS),
             scale=1.0, r=bk(b2), w=[("rs", r2)])
        P.op("dve", "reciprocal", out=rs[:, r2, 0:T], in_=rs[:, r2, 0:T], r=[("rs", r2)], w=[("rs", r2)])
        P.op("dve", "scalar_tensor_tensor", out=out_ap, in0=ps[b][:, 0:T], scalar=qkg[:, gcol:gcol + 1],
             in1=rs[:, r2, 0:T], op0=ALU.mult, op1=ALU.mult, r=bk(b) + [("rs", r2), "qkg"], w=out_keys)

    def hT_r(T):
        return (lambda kc: hT[:, kc, 0:T]), (lambda kc: [("hT", kc)])

    def block(l, kind, t0, ntile):
        sample = kind == "sample"
        T = 64 if sample else ntile * 128
        tiles = [] if sample else list(range(t0, t0 + ntile))
        groups = [(bl * 16, 16, 1 + bl) for bl in range(4)] if sample else [(0, T, 0)]
        hr, hk = hT_r(T)
        bi = cur["idx"]
        cur["l"] = l
        _DBG["stage"] = 0 if kind != "sample" else 10
        cur["par"] = bi % 2
        cur["xT"] = xTt[:, bi % 2]

        def xload(i):
            l_, kind_, t0_, n_ = BLOCKS[i]
            par_ = i % 2
            wk = [("xT", par_, c) for c in range(8)]
            if kind_ == "sample":
                if l_ == 1:
                    return False
                P.op("sp", "dma_start", out=xTt[:, par_, :, 0:64], in_=xs_d, w=wk, semkey=("xld", par_))
                return True
            T_ = n_ * 128
            if l_ == 0:
                P.op("sp", "dma_start", out=xTt[:, par_, :, 0:T_], in_=xin[:, :, t0_ * 128: t0_ * 128 + T_], w=wk,
                     semkey=("xld", par_))
            else:
                P.op("sp", "dma_start", out=xTt[:, par_, :, 0:T_],
                     in_=x1s[:, :, (t0_ - 4) * 128: (t0_ - 4) * 128 + T_],
                     r=[("x1s", t) for t in range(t0_, t0_ + n_)], w=wk, semkey=("xld", par_))
            return True

        if bi == 0:
            xload(0)
        if sample and l == 1:
            P.op("pool", "tensor_copy", cur["xT"][:, :, 0:64], xs_keep[:, :, :], r=["xs_keep"],
                 w=[("xT", cur["par"], c) for c in range(8)])
        if bi + 1 < len(BLOCKS):
            xload(bi + 1)
        cur["idx"] = bi + 1
        if bi not in pro_done:
            norm_mod(0, T, groups)

        if kind == "kv":
            slabs = {1: ws_get()}
        else:
            slabs = {0: ws_get()}
            for c in range(4):
                b = nb()
                proj_fm(b, slabs[0], 8, 512, c * 128, hr, hk, T)
                headnorm(b, T, 0, qT[:, c, 0:T], [("qT", c)])
            ws_done()
            slabs[1] = ws_get()
        for c in range(4):
            b = nb()
            proj_fm(b, slabs[1], 8, 512, c * 128, hr, hk, T)
            r_ = nrot("kst", 2)
            headnorm(b, T, 1, kst[:, r_, 0:T], [("kst", r_)])
            if sample:
                P.op("act", "activation", out=kTs[:, c, :], in_=kst[:, r_, 0:64], func=AF.Copy,
                     r=[("kst", r_)], w=["kTs"])
                P.op("sp", "dma_start", out=nksT_d[l, :, c, :], in_=kst[:, r_, 0:64], r=[("kst", r_)],
                     semkey=("kst", r_))
            else:
                for ti, t in enumerate(tiles):
                    sl = t % 8
                    P.op("act", "activation", out=kT[:, c, sl * 128:(sl + 1) * 128],
                         in_=kst[:, r_, ti * 128:(ti + 1) * 128], func=AF.Copy, r=[("kst", r_)], w=[("kT", sl)])
                    if t >= KEEP_T0:
                        P.op("sp", "dma_start", out=nkT_d[l, :, c, (t - KEEP_T0) * 128:(t - KEEP_T0 + 1) * 128],
                             in_=kst[:, r_, ti * 128:(ti + 1) * 128], r=[("kst", r_)], semkey=("kst", r_))
        ws_done()
        slot = ws_get()
        if sample:
            for bl in range(4):
                b = nb()
                for kc in range(8):
                    P.op("pe", "matmul", ps[b][0:16, 0:512], lhsT=hT[:, kc, bl * 16:(bl + 1) * 16],
                         rhs=wslab[:, slot, kc * 512:(kc + 1) * 512], start=(kc == 0), stop=(kc == 7),
                         r=[("wslab", slot), ("hT", kc)], w=bk(b))
                r_ = nrot("gl", 2)
                P.op("act", "activation", out=gl[0:16, r_, :], in_=ps[b][0:16, 0:512], func=AF.Copy, r=bk(b),
                     w=[("gl", r_)])
                P.op("dve", "tensor_copy", vts[:, bl, :], gl[0:16, r_, :], r=[("gl", r_)], w=["vts"])
                P.op("sp", "dma_start", out=nvs_d[l, :, bl, :], in_=gl[0:16, r_, :], r=[("gl", r_)],
                     semkey=("gl", r_))
        else:
            for ti, t in enumerate(tiles):
                b = nb()
                for kc in range(8):
                    P.op("pe", "matmul", ps[b][:, 0:512], lhsT=hT[:, kc, ti * 128:(ti + 1) * 128],
                         rhs=wslab[:, slot, kc * 512:(kc + 1) * 512], start=(kc == 0), stop=(kc == 7),
                         r=[("wslab", slot), ("hT", kc)], w=bk(b))
                sl = t % 8
                if t >= KEEP_T0:
                    r_ = nrot("gl", 2)
                    P.op("act", "activation", out=gl[:, r_, :], in_=ps[b][:, 0:512], func=AF.Copy, r=bk(b),
                         w=[("gl", r_)])
                    P.op("dve", "tensor_scalar_mul", vt[:, sl, :], gl[:, r_, :], vtok[:, t:t + 1],
                         r=[("gl", r_), "vtok"], w=[("vt", sl)])
                    P.op("sp", "dma_start", out=nv_d[l, t - KEEP_T0], in_=gl[:, r_, :], r=[("gl", r_)],
                         semkey=("gl", r_))
                else:
                    P.op("dve", "tensor_scalar_mul", vt[:, sl, :], ps[b][:, 0:512], vtok[:, t:t + 1],
                         r=bk(b) + ["vtok"], w=[("vt", sl)])
        ws_done()
        if kind == "kv":
            if pro_ok(bi + 1):
                with _NextCtx(bi + 1):
                    l2, T2, g2 = blk_geom(bi + 1)
                    norm_mod(0, T2, g2)
                pro_done.add(bi + 1)
            return

        ck(1)
        slot = ws_get()
        for c in range(4):
            b = nb()
            proj_fm(b, slot, 8, 512, c * 128, hr, hk, T)
            P.op("act", "activation", out=ubT[:, c, 0:T], in_=ps[b][:, 0:T], func=AF.Gelu_apprx_tanh, r=bk(b),
                 w=[("ubT", c)])
        ws_done()
        ck(2)
        slot = ws_get()
        tl = [(0, 64)] if sample else [(ti, 128) for ti in range(ntile)]
        for ti, M in tl:
            b = nb()
            for kc in range(8):
                P.op("pe", "matmul", ps[b][0:M, 0:512], lhsT=hT[:, kc, ti * 128: ti * 128 + M],
                     rhs=wslab[:, slot, kc * 512:(kc + 1) * 512], start=(kc == 0), stop=(kc == 7),
                     r=[("wslab", slot), ("hT", kc)], w=bk(b))
            r_ = nrot("gl", 2)
            P.op("act", "activation", out=gl[0:M, r_, :], in_=ps[b][0:M, 0:512], func=AF.Gelu_apprx_tanh, r=bk(b),
                 w=[("gl", r_)])
            r2 = nrot("tmpn", 2)
            P.op("dve", "tensor_tensor", out=tmpn[0:M, r2, :], in0=gl[0:M, r_, :], in1=gl[0:M, r_, :], op=ALU.mult,
                 r=[("gl", r_)], w=[("tmpn", r2)])
            r3 = nrot("ssv", 4)
            P.op("dve", "reduce_sum", out=ssv[0:M, r3:r3 + 1], in_=tmpn[0:M, r2, :], axis=AX.X,
                 r=[("tmpn", r2)], w=[("ssv", r3)])
            P.op("act", "activation", out=ssv[0:M, r3:r3 + 1], in_=ssv[0:M, r3:r3 + 1], func=AF.Sqrt,
                 bias=float(EPS), scale=1.0 / 512.0, r=[("ssv", r3)], w=[("ssv", r3)])
            P.op("dve", "reciprocal", out=ssv[0:M, r3:r3 + 1], in_=ssv[0:M, r3:r3 + 1], r=[("ssv", r3)],
                 w=[("ssv", r3)])
            if sample:
                P.op("dve", "scalar_tensor_tensor", out=tmpn[0:64, r2, :], in0=gl[0:64, r_, :],
                     scalar=ssv[0:64, r3:r3 + 1], in1=vg[0:64, :], op0=ALU.mult, op1=ALU.mult,
                     r=[("gl", r_), ("ssv", r3), "vg"], w=[("tmpn", r2)])
                P.op("act", "activation", out=vbns[:, :], in_=tmpn[0:64, r2, :], func=AF.Copy, r=[("tmpn", r2)],
                     w=["vbns"])
                P.op("sp", "dma_start", out=nbs_d[l], in_=tmpn[0:64, r2, :], r=[("tmpn", r2)], semkey=("tmpn", r2))
            else:
                P.op("dve", "scalar_tensor_tensor", out=vbn[:, ti, :], in0=gl[:, r_, :], scalar=ssv[:, r3:r3 + 1],
                     in1=vg[:, :], op0=ALU.mult, op1=ALU.mult, r=[("gl", r_), ("ssv", r3), "vg"], w=[("vbn", ti)])
        ws_done()
        ck(3)
        for s in range(4):
            slot = ws_get()
            for m in range(4):
                c = s * 4 + m
                b = nb()
                proj_fm(b, slot, 8, 512, m * 128, hr, hk, T)
                P.op("act", "activation", out=SA[:, c, 0:T], in_=ps[b][:, 0:T], func=AF.Sigmoid, r=bk(b),
                     w=[("SA", c)])
            ws_done()

        ck(4)
        if sample:
            items = [(bl, hp) for bl in range(4) for hp in range(4)]

            def s1(it, st):
                bl, hp = it
                cr = 0
                for j in range(5):
                    for hh in range(2):
                        hs = slice(hh * 64, (hh + 1) * 64)
                        b = st * 4 + hh
                        if j < 4:
                            P.op("pe", "matmul", ps[b][:, j * 16:(j + 1) * 16],
                                 lhsT=ckb[hs, cr, hp, j * 128:(j + 1) * 128], rhs=qT[hs, hp, bl * 16:(bl + 1) * 16],
                                 start=True, stop=True, r=[("ckb", cr), ("qT", hp)], w=[("ps", b)])
                        else:
                            P.op("pe", "matmul", ps[b][0:16, 64:80],
                                 lhsT=kTs[hs, hp, bl * 16:(bl + 1) * 16], rhs=qT[hs, hp, bl * 16:(bl + 1) * 16],
                                 start=True, stop=True, r=["kTs", ("qT", hp)], w=[("ps", b)])
                for hh in range(2):
                    b = st * 4 + hh
                    h = 2 * hp + hh
                    P.op("act", "activation", out=Pts[:, st, hh, 0:48], in_=ps[b][:, 0:48], func=AF.Exp,
                         r=[("ps", b)], w=[("Pts", st, hh)])
                    P.op("act", "activation", out=exs[:, st, hh, 0:16], in_=ps[b][:, 48:64], func=AF.Exp,
                         r=[("ps", b)], w=[("exs", st, hh)])
                    P.op("act", "activation", out=exs[0:16, st, hh, 16:32], in_=ps[b][0:16, 64:80], func=AF.Exp,
                         r=[("ps", b)], w=[("exs", st, hh)])
                    P.op("dve", "tensor_tensor", out=Pts[:, st, hh, 48:64], in0=exs[:, st, hh, 0:16],
                         in1=E[:, 0, h, 0:16], op=ALU.mult, r=[("exs", st, hh), "E"], w=[("Pts", st, hh)])
                    P.op("dve", "tensor_tensor", out=Pts[0:16, st, hh, 64:80], in0=exs[0:16, st, hh, 16:32],
                         in1=E[0:16, 1, h, 0:16], op=ALU.mult, r=[("exs", st, hh), "E"], w=[("Pts", st, hh)])

            def s2(it, st):
                bl, hp = it
                cr = 0
                bd, bo = st * 4 + 2, st * 4 + 3
                for hh in range(2):
                    for j in range(5):
                        if j < 4:
                            P.op("pe", "matmul", ps[bd][:, hh * 16:(hh + 1) * 16], lhsT=ones_bf[:, :],
                                 rhs=Pts[:, st, hh, j * 16:(j + 1) * 16], start=(j == 0), stop=False,
                                 r=[("Pts", st, hh), "ones_bf"], w=[("ps", bd)])
                        else:
                            P.op("pe", "matmul", ps[bd][:, hh * 16:(hh + 1) * 16], lhsT=ones_bf[0:16, :],
                                 rhs=Pts[0:16, st, hh, 64:80], start=False, stop=True,
                                 r=[("Pts", st, hh), "ones_bf"], w=[("ps", bd)])
                for hh in range(2):
                    hs = slice(hh * 64, (hh + 1) * 64)
                    fc = (2 * hp + hh) * 64
                    for j in range(5):
                        if j < 4:
                            P.op("pe", "matmul", ps[bo][hs, 0:16], lhsT=cvb[:, cr, j, fc:fc + 64],
                                 rhs=Pts[:, st, hh, j * 16:(j + 1) * 16], start=(j == 0), stop=False,
                                 r=[("Pts", st, hh), ("cvb", cr)], w=[("ps", bo)])
                        else:
                            P.op("pe", "matmul", ps[bo][hs, 0:16], lhsT=vts[0:16, bl, fc:fc + 64],
                                 rhs=Pts[0:16, st, hh, 64:80], start=False, stop=True,
                                 r=[("Pts", st, hh), "vts"], w=[("ps", bo)])
                P.op("dve", "reciprocal", out=rdens[:, st, :], in_=ps[bd][:, 0:32], r=[("ps", bd)],
                     w=[("rdens", st)])
                for hh in range(2):
                    hs = slice(hh * 64, (hh + 1) * 64)
                    P.op("dve", "tensor_tensor", out=oaT[hs, hp, bl * 16:(bl + 1) * 16], in0=ps[bo][hs, 0:16],
                         in1=rdens[hs, st, hh * 16:(hh + 1) * 16], op=ALU.mult, r=[("ps", bo), ("rdens", st)],
                         w=[("oaT", hp)])
        else:
            items = [(qi, hp) for qi in range(ntile) for hp in range(4)]

            def sview(st, hh, j):
                if j < 4:
                    return st * 4 + hh, slice(j * 128, (j + 1) * 128)
                return st * 4 + 2 + hh, slice(0, 128)

            def s1(it, st):
                qi, hp = it
                qt = t0 + qi
                for j in range(5):
                    sl = (qt - 4 + j) % 8
                    for hh in range(2):
                        hs = slice(hh * 64, (hh + 1) * 64)
                        b, cs_ = sview(st, hh, j)
                        P.op("pe", "matmul", ps[b][:, cs_],
                             lhsT=kT[hs, hp, sl * 128:(sl + 1) * 128], rhs=qT[hs, hp, qi * 128:(qi + 1) * 128],
                             start=True, stop=True, r=[("kT", sl), ("qT", hp)], w=[("ps", b)])
                pk = [("Pt", st, 0), ("Pt", st, 1)]
                rk2 = [("ps", st * 4), ("ps", st * 4 + 1)]
                rk4 = [("ps", st * 4 + 2), ("ps", st * 4 + 3)]
                two = "p (h c) -> p h c"
                P.op("act", "activation", out=Pt[64:128, st, :, 0, :],
                     in_=psd[st * 2][64:128, :].rearrange(two, h=2)[:, :, 0:128], func=AF.Exp, r=rk2, w=pk)
                P.op("act", "activation", out=Pt[0:64, st, :, 0, 0:64],
                     in_=psd[st * 2][0:64, :].rearrange(two, h=2)[:, :, 0:64], func=AF.Exp, r=rk2, w=pk)
                P.op("act", "activation", out=Pt[:, st, :, 1:4, :].rearrange("p h j q -> p h (j q)"),
                     in_=psd[st * 2][:, :].rearrange(two, h=2)[:, :, 128:512], func=AF.Exp, r=rk2, w=pk)
                P.op("act", "activation", out=Pt[:, st, :, 4, :],
                     in_=psd[st * 2 + 1][:, :].rearrange(two, h=2)[:, :, 0:128], func=AF.Exp, r=rk4, w=pk)
                for jj in (3, 4):
                    P.op("dve", "tensor_tensor", out=Pt[:, st, :, jj, :], in0=Pt[:, st, :, jj, :],
                         in1=E[:, jj - 3, 2 * hp:2 * hp + 2, :], op=ALU.mult, r=pk + ["E"], w=pk)

            def s2(it, st):
                qi, hp = it
                qt = t0 + qi
                bd, bo = st * 4 + 2, st * 4 + 3
                for hh in range(2):
                    for j in range(5):
                        kt = qt - 4 + j
                        lw = vones[:, kt, :] if kt < 9 else ones_bf[:, :]
                        P.op("pe", "matmul", ps[bd][:, 256 + hh * 128:256 + (hh + 1) * 128], lhsT=lw,
                             rhs=Pt[:, st, hh, j, :], start=(j == 0), stop=(j == 4),
                             r=[("Pt", st, hh), "vones", "ones_bf"], w=[("ps", bd)])
                for hh in range(2):
                    hs = slice(hh * 64, (hh + 1) * 64)
                    fc = (2 * hp + hh) * 64
                    for j in range(5):
                        sl = (qt - 4 + j) % 8
                        P.op("pe", "matmul", ps[bo][hs, 256:384], lhsT=vt[:, sl, fc:fc + 64],
                             rhs=Pt[:, st, hh, j, :], start=(j == 0), stop=(j == 4),
                             r=[("Pt", st, hh), ("vt", sl)], w=[("ps", bo)])
                P.op("dve", "tensor_scalar_max", rden[:, st, :], ps[bd][:, 256:512], 1e-30, r=[("ps", bd)],
                     w=[("rden", st)])
                P.op("dve", "reciprocal", out=rden[:, st, :], in_=rden[:, st, :], r=[("rden", st)],
                     w=[("rden", st)])
                for hh in range(2):
                    hs = slice(hh * 64, (hh + 1) * 64)
                    P.op("dve", "tensor_tensor", out=oaT[hs, hp, qi * 128:(qi + 1) * 128], in0=ps[bo][hs, 256:384],
                         in1=rden[hs, st, hh * 128:(hh + 1) * 128], op=ALU.mult, r=[("ps", bo), ("rden", st)],
                         w=[("oaT", hp)])

        if sample:
            groups_it = [[(bl, hp) for hp in range(4)] for bl in range(4)]
        else:
            groups_it = [items]
        for its in groups_it:
            if sample:
                bl = its[0][0]
                P.op("pool", "dma_start", out=ckb[:, 0, :, :], in_=ckT_d[l, :, :, bl, :], w=[("ckb", 0)],
                     semkey=("ckb", 0))
                P.op("pool", "dma_start", out=cvb[:, 0, :, :], in_=cv_d[l, bl], w=[("cvb", 0)],
                     semkey=("cvb", 0))
            for i in range(len(its) + 1):
                if i < len(its):
                    s1(its[i], i % 2)
                if i >= 1:
                    s2(its[i - 1], (i - 1) % 2)

        ck(5)
        if sample:
            b = nb()
            for g in range(4):
                P.op("pe", "matmul", ps[b][:, g * 64:(g + 1) * 64], lhsT=vbns[0:64, g * 128:(g + 1) * 128],
                     rhs=Wblk[0:64, g, :], start=True, stop=False, r=["vbns", "Wblk"], w=bk(b))
                P.op("pe", "matmul", ps[b][:, g * 64:(g + 1) * 64], lhsT=ones_row[0:1, :],
                     rhs=bsps[0:1, g * 64:(g + 1) * 64], start=False, stop=True, r=["ones_row", "bsps"], w=bk(b))
            P.op("dve", "tensor_tensor", out=obT[:, :, 0:64], in0=ps[b][:, 0:256].rearrange("p (g t) -> p g t", g=4),
                 in1=ubT[:, :, 0:64], op=ALU.mult, r=bk(b) + [("ubT", c) for c in range(4)],
                 w=[("obT", c) for c in range(4)])
        else:
            for ti in range(ntile):
                b = nb()
                for g in range(4):
                    P.op("pe", "matmul", ps[b][:, g * 128:(g + 1) * 128], lhsT=vbn[:, ti, g * 128:(g + 1) * 128],
                         rhs=WcT[:, g, :], start=True, stop=False, r=[("vbn", ti), "WcT"], w=bk(b))
                    P.op("pe", "matmul", ps[b][:, g * 128:(g + 1) * 128], lhsT=ones_row[0:1, :],
                         rhs=bsp[0:1, g * 128:(g + 1) * 128], start=False, stop=True, r=["ones_row", "bsp"], w=bk(b))
                P.op("dve", "tensor_tensor", out=obT[:, :, ti * 128:(ti + 1) * 128],
                     in0=ps[b][:, 0:512].rearrange("p (g t) -> p g t", g=4), in1=ubT[:, :, ti * 128:(ti + 1) * 128],
                     op=ALU.mult, r=bk(b) + [("ubT", c) for c in range(4)], w=[("obT", c) for c in range(4)])

        ck(6)
        sa_, sb_ = ws_get(), ws_get()
        for c in range(8):
            ba, bb_ = nb(), nb()
            proj_fm(ba, sa_, 4, 1024, c * 128, lambda kc: oaT[:, kc, 0:T], lambda kc: [("oaT", kc)], T)
            proj_fm(bb_, sb_, 4, 1024, c * 128, lambda kc: obT[:, kc, 0:T], lambda kc: [("obT", kc)], T)
            P.op("dve", "tensor_tensor", out=tmpn[:, 0, 0:T], in0=ps[ba][:, 0:T], in1=SA[:, c, 0:T], op=ALU.mult,
                 r=bk(ba) + [("SA", c)], w=[("tmpn", 0)])
            P.op("dve", "tensor_tensor", out=tmpn[:, 1, 0:T], in0=ps[bb_][:, 0:T], in1=SA[:, 8 + c, 0:T], op=ALU.mult,
                 r=bk(bb_) + [("SA", 8 + c)], w=[("tmpn", 1)])
            P.op("dve", "tensor_tensor", out=SA[:, 16 + c, 0:T], in0=tmpn[:, 0, 0:T], in1=tmpn[:, 1, 0:T], op=ALU.add,
                 r=[("tmpn", 0), ("tmpn", 1)], w=[("SA", 16 + c)])
        ws_done()
        ws_done()
        ck(7)
        for s in range(2):
            slot = ws_get()
            for m in range(4):
                c = s * 4 + m
                b = nb()
                proj_fm(b, slot, 8, 512, m * 128, lambda kc: SA[:, 16 + kc, 0:T], lambda kc: [("SA", 16 + kc)], T)
                for (c0, n, bb) in groups:
                    P.op("dve", "scalar_tensor_tensor", out=cur["xT"][:, c, c0:c0 + n], in0=ps[b][:, c0:c0 + n],
                         scalar=modTt[:, cur["l"], 16 + c, bb:bb + 1], in1=cur["xT"][:, c, c0:c0 + n], op0=ALU.mult, op1=ALU.add,
                         r=bk(b) + [("xT", cur["par"], c), ("modT", cur["l"])], w=[("xT", cur["par"], c)])
            ws_done()

        ck(8)
        norm_mod(1, T, groups)
        GW = 72 if sample else T + 2
        nxt = bi + 1 if pro_ok(bi + 1) else None
        if nxt is not None:
            l2, T2, g2 = blk_geom(nxt)
            bss = nb()
            reserved.add(bss)
            sqr = {}
        for jj in range(11):
            if nxt is not None:
                with _NextCtx(nxt):
                    if 1 <= jj <= 8:
                        sqr[jj - 1] = norm_sq(bss, jj - 1, T2)
                    if 2 <= jj <= 9:
                        norm_ss(bss, jj - 2, sqr[jj - 2], T2)
            slot = ws_get()
            for sub in range(2):
                j = 2 * jj + sub
                bg, bu = nb(), nb()
                proj_fm(bg, slot, 8, 512, sub * 128, hr, hk, T)
                proj_fm(bu, slot, 8, 512, 256 + sub * 128, hr, hk, T)
                r_ = nrot("gb", 3)
                if sample:
                    g3 = gb[:, r_, 0:72].rearrange("p (b t) -> p b t", t=18)
                    P.op("act", "activation", out=g3[:, :, 2:18],
                         in_=ps[bg][:, 0:64].rearrange("p (b t) -> p b t", t=16), func=AF.Copy, r=bk(bg),
                         w=[("gb", r_)])
                    P.op("dve", "tensor_copy", g3[:, :, 0:2], halos[:, j, :, :], r=["halos"], w=[("gb", r_)])
                    P.op("pool", "tensor_copy", ncs_st[:, j, :, :], g3[:, :, 16:18], r=[("gb", r_)], w=["ncs_st"])
                    views = [g3[:, :, k:k + 16] for k in range(3)]
                    r2 = nrot("gc", 2)
                    gcv = gc[:, r2, 0:64].rearrange("p (b t) -> p b t", t=16)
                else:
                    P.op("act", "activation", out=gb[:, r_, 2:2 + T], in_=ps[bg][:, 0:T], func=AF.Copy, r=bk(bg),
                         w=[("gb", r_)])
                    P.op("dve", "tensor_copy", gb[:, r_, 0:2], halo[:, j, :], r=["halo"], w=[("gb", r_)])
                    if t0 == 8:
                        P.op("pool", "tensor_scalar_mul", gb[:, r_, 128:130], gb[:, r_, 128:130], flag[:, 0:1],
                             r=[("gb", r_), "flag"], w=[("gb", r_)])
                    P.op("pool", "tensor_copy", halo[:, j, :], gb[:, r_, T:T + 2], r=[("gb", r_)], w=["halo"])
                    views = [gb[:, r_, k:k + T] for k in range(3)]
                    r2 = nrot("gc", 2)
                    gcv = gc[:, r2, 0:T]
                P.op("act", "activation", out=gcv, in_=views[0], func=AF.Identity, scale=cw[:, j, 0:1],
                     bias=cb[:, j:j + 1], r=[("gb", r_), "cw", "cb"], w=[("gc", r2)])
                for k in (1, 2):
                    P.op("dve", "scalar_tensor_tensor", out=gcv, in0=views[k], scalar=cw[:, j, k:k + 1], in1=gcv,
                         op0=ALU.mult, op1=ALU.add, r=[("gb", r_), ("gc", r2), "cw"], w=[("gc", r2)])
                r3 = nrot("ge", 2)
                P.op("act", "activation", out=ge[:, r3, 0:T], in_=gc[:, r2, 0:T], func=AF.Gelu_apprx_tanh,
                     r=[("gc", r2)], w=[("ge", r3)])
                P.op("dve", "tensor_tensor", out=SA[:, j, 0:T], in0=ps[bu][:, 0:T], in1=ge[:, r3, 0:T], op=ALU.mult,
                     r=bk(bu) + [("ge", r3)], w=[("SA", j)])
            ws_done()
        if nxt is not None:
            with _NextCtx(nxt):
                norm_fin(bss, T2)
                norm_apply(0, T2, g2)
            reserved.discard(bss)
            pro_done.add(nxt)
        for c in range(8):
            slot = ws_get()
            b = nb()
            proj_fm(b, slot, 22, 128, 0, lambda kc: SA[:, kc, 0:T], lambda kc: [("SA", kc)], T)
            for (c0, n, bb) in groups:
                P.op("dve", "scalar_tensor_tensor", out=cur["xT"][:, c, c0:c0 + n], in0=ps[b][:, c0:c0 + n],
                     scalar=modTt[:, cur["l"], 40 + c, bb:bb + 1], in1=cur["xT"][:, c, c0:c0 + n], op0=ALU.mult, op1=ALU.add,
                     r=bk(b) + [("xT", cur["par"], c), ("modT", cur["l"])], w=[("xT", cur["par"], c)])
            ws_done()

        ck(9)
        xk = [("xT", cur["par"], c) for c in range(8)]
        if sample:
            if l == 0:
                P.op("pool", "tensor_copy", xs_keep[:, :, :], cur["xT"][:, :, 0:64], r=xk, w=["xs_keep"])
            else:
                P.op("sp", "dma_start", out=ysT_d, in_=cur["xT"][:, :, 0:64], r=xk, semkey="xst")
            P.op("sp", "dma_start", out=ncs_d[l], in_=ncs_st[:, :, :, :], r=["ncs_st"], semkey="ncs")
        elif l == 0:
            P.op("sp", "dma_start", out=x1s[:, :, (t0 - 4) * 128:(t0 - 4) * 128 + T], in_=cur["xT"][:, :, 0:T], r=xk,
                 w=[("x1s", t) for t in tiles], semkey="xst")
        else:
            lo = max(t0, OUT_T0)
            c0 = (lo - t0) * 128
            P.op("sp", "dma_start", out=yT_d[:, :, (lo - OUT_T0) * 128:(lo - OUT_T0) * 128 + T - c0],
                 in_=cur["xT"][:, :, c0:T], r=xk, semkey="xst")

    xs_keep = sb("xs_keep", (128, 8, 64))

    BLOCKS = []
    for l in range(2):
        BLOCKS.append((l, "kv", 0 if l == 0 else 4, 4))
        for (t0, n) in (L0_BLOCKS if l == 0 else L1_BLOCKS):
            BLOCKS.append((l, "full", t0, n))
        BLOCKS.append((l, "sample", 0, 0))

    try:
        for l in range(2):
            if stop is not None and stop == -1:
                raise _Stop()
            if l == 0:
                ada_setup(0)
            layer_setup(l)
            if stop is not None and stop == 0:
                raise _Stop()
            if l == 1:
                P.op("pool", "memset", halo[:, :, :], 0.0, r=["halo"], w=["halo"])
            for (l_, kind, t0, n) in [b for b in BLOCKS if b[0] == l]:
                if kind == "sample":
                    P.op("sp", "dma_start", out=ncv_d[l], in_=halo[:, :, :], r=["halo"], semkey="ncv")
                    if l == 0:
                        ada_setup(1)
                block(l_, kind, t0, n)
                nblk[0] += 1
                if stop is not None and nblk[0] >= stop:
                    raise _Stop()
        assert ws["consumed"] == len(ws["seq"]), (ws["consumed"], len(ws["seq"]))
    except _Stop:
        dbg = dout("dbgx", (128, 8, 512))
        P.op("sp", "dma_start", out=dbg, in_=cur["xT"][:, :, :], r=[("xT", cur["par"], c) for c in range(8)], semkey="dbg")
        dbg2 = dout("dbgh", (128, 8, 512))
        P.op("pool", "dma_start", out=dbg2, in_=tmpn[:, :, :].rearrange("p a b -> p (a b)"), r=[("tmpn", 0), ("tmpn", 1)], semkey="dbg") if False else None

    P.emit(nc, stack)
    stack.close()
    return nc


def _fm(a):
    t, f = a.shape
    return np.ascontiguousarray(a.reshape(t, f // 128, 128).transpose(2, 1, 0))


def _slab(wm):
    k, m = wm.shape
    return np.ascontiguousarray(wm.reshape(k // 128, 128, m).transpose(1, 0, 2)).reshape(128, (k // 128) * m)


_NC_CACHE = {}


def kernel(x_prompt, x_sample, cache_attn_k, cache_attn_v, cache_ffn_conv, c_prompt, c_sample,
           norm1_g, norm2_g, w_ada, b_ada, w_in, q_norm_g, k_norm_g, rel_bias, v_norm_g,
           w_spatial, b_spatial, w_out_a, w_out_b, w_out, w_ffn_in, ffn_conv_w, ffn_conv_b, w_ffn_out):
    f = lambda a: np.asarray(a, dtype=np.float32)
    x_prompt, x_sample, cache_attn_k, cache_attn_v, cache_ffn_conv = map(f, (x_prompt, x_sample, cache_attn_k, cache_attn_v, cache_ffn_conv))
    c_prompt, c_sample, norm1_g, norm2_g, w_ada, b_ada, w_in = map(f, (c_prompt, c_sample, norm1_g, norm2_g, w_ada, b_ada, w_in))
    q_norm_g, k_norm_g, rel_bias, v_norm_g, w_spatial, b_spatial = map(f, (q_norm_g, k_norm_g, rel_bias, v_norm_g, w_spatial, b_spatial))
    w_out_a, w_out_b, w_out, w_ffn_in, ffn_conv_w, ffn_conv_b, w_ffn_out = map(f, (w_out_a, w_out_b, w_out, w_ffn_in, ffn_conv_w, ffn_conv_b, w_ffn_out))

    in_maps = _prep(x_prompt, x_sample, cache_attn_k, cache_attn_v, cache_ffn_conv, c_prompt, c_sample,
                    norm1_g, norm2_g, w_ada, b_ada, w_in, q_norm_g, k_norm_g, rel_bias, v_norm_g,
                    w_spatial, b_spatial, w_out_a, w_out_b, w_out, w_ffn_in, ffn_conv_w, ffn_conv_b, w_ffn_out)
    if "nc" not in _NC_CACHE:
        _NC_CACHE["nc"] = build_nc()
    nc = _NC_CACHE["nc"]
    res = run_bass_kernel_spmd(nc, in_maps, core_ids=list(range(NCORES)))
    return _post(res.results)


def _prep(x_prompt, x_sample, cache_attn_k, cache_attn_v, cache_ffn_conv, c_prompt, c_sample,
          norm1_g, norm2_g, w_ada, b_ada, w_in, q_norm_g, k_norm_g, rel_bias, v_norm_g,
          w_spatial, b_spatial, w_out_a, w_out_b, w_out, w_ffn_in, ffn_conv_w, ffn_conv_b, w_ffn_out):

    wall = np.empty((2, 24, 128, 4096), np.float32)
    wfo = np.empty((2, 8, 128, 2816), np.float32)
    wada = np.empty((2, 12, 128, 4096), np.float32)
    for l in range(2):
        for s in range(9):
            wall[l, s] = _slab(w_in[l][:, s * 512:(s + 1) * 512])
        wall[l, 9] = _slab(w_out_a[l])
        wall[l, 10] = _slab(w_out_b[l])
        for s in range(2):
            wall[l, 11 + s] = _slab(w_out[l][:, s * 512:(s + 1) * 512])
        for jj in range(11):
            idx = np.concatenate([np.arange(2 * jj * 128, (2 * jj + 2) * 128), DFF + np.arange(2 * jj * 128, (2 * jj + 2) * 128)])
            wall[l, 13 + jj] = _slab(w_ffn_in[l][:, idx])
        for c in range(8):
            wfo[l, c] = _slab(w_ffn_out[l][:, c * 128:(c + 1) * 128])
        for s in range(12):
            wada[l, s] = _slab(w_ada[l][:, s * 512:(s + 1) * 512])
    col = lambda v: np.ascontiguousarray(v.reshape(-1, 128).T)
    nrm = np.stack([np.concatenate([col(norm1_g[l]), col(norm2_g[l])], 1) for l in range(2)])
    bada = np.stack([col(b_ada[l]) for l in range(2)])
    qkg = np.stack([np.stack([np.tile(q_norm_g[l], 2), np.tile(k_norm_g[l], 2)], 1) for l in range(2)])
    vg = np.stack([np.broadcast_to(v_norm_g[l][None, :], (128, 512)) for l in range(2)]).copy()
    bsp = b_spatial.reshape(2, 1, 512).copy()
    bsps = np.stack([np.tile(b_spatial[l][:, None, :16], (1, 4, 1)).reshape(1, 256) for l in range(2)])
    wspT = np.ascontiguousarray(w_spatial.transpose(0, 3, 1, 2))
    wspb = np.zeros((2, 64, 4, 64), np.float32)
    for bl in range(4):
        wspb[:, bl * 16:(bl + 1) * 16, :, bl * 16:(bl + 1) * 16] = wspT[:, 0:16, :, 0:16]
    s_i = np.arange(128)[:, None]
    t_i = np.arange(128)[None, :]
    tril = np.broadcast_to((s_i <= t_i).astype(np.float32)[:, None, :], (128, 4, 128)).copy()
    trilb = np.zeros((64, 4, 64), np.float32)
    for bl in range(4):
        trilb[bl * 16:(bl + 1) * 16, :, bl * 16:(bl + 1) * 16] = tril[0:16, :, 0:16]
    cw = np.ascontiguousarray(ffn_conv_w.reshape(2, 3, 22, 128).transpose(0, 3, 2, 1))
    cb = np.ascontiguousarray(ffn_conv_b.reshape(2, 22, 128).transpose(0, 2, 1))
    ki = np.arange(128)[:, None]
    qi = np.arange(128)[None, :]
    idx1 = np.clip(128 + qi - ki, -128, 128) + 128
    idx0 = np.clip(qi - ki, -128, 128) + 128
    Tb = np.stack([np.stack([rel_bias[l][:, idx1].transpose(1, 0, 2), rel_bias[l][:, idx0].transpose(1, 0, 2)], 1)
                   for l in range(2)])
    b256 = np.stack([np.broadcast_to(rel_bias[l][None, :, 256], (128, 8)) for l in range(2)]).copy()

    shared = dict(wall=wall, wfo=wfo, wada=wada, nrm=nrm, bada=bada, qkg=qkg, vg=vg, bsp=bsp, bsps=bsps, wspT=wspT,
                  wspb=wspb, tril=tril, trilb=trilb, cw=cw, cb=cb, Tb=np.ascontiguousarray(Tb), b256=b256)
    shared = {k: np.ascontiguousarray(v, dtype=np.float32) for k, v in shared.items()}

    in_maps = []
    for core in range(NCORES):
        b, seg = core // 4, core % 4
        s0 = seg * SEG
        a = s0 - HALO
        xw = np.zeros((W, D), np.float32)
        lo = max(a, 0)
        xw[lo - a:] = x_prompt[b, lo:s0 + SEG]
        valid = (np.arange(W) + a >= 0).astype(np.float32)
        m = dict(shared)
        m["xin"] = _fm(xw)
        m["xs"] = _fm(x_sample[4 * core:4 * core + 4].reshape(64, D))
        m["vtok"] = np.ascontiguousarray(valid.reshape(NT, 128).T)
        m["flag"] = np.full((128, 1), 1.0 if a >= 0 else 0.0, np.float32)
        cc = np.concatenate([c_prompt[b:b + 1], c_sample[4 * core:4 * core + 4]], 0)
        m["cT"] = np.ascontiguousarray(cc.reshape(5, 8, 128).transpose(2, 1, 0))
        ck = cache_attn_k[:, 4 * core:4 * core + 4]
        m["ckT"] = np.ascontiguousarray(ck.reshape(2, 4, 512, 4, 2, 64).transpose(0, 4, 5, 3, 1, 2)).reshape(2, 128, 4, 4, 512)
        cvv = cache_attn_v[:, 4 * core:4 * core + 4].reshape(2, 4, 4, 128, 512)
        m["cv"] = np.ascontiguousarray(cvv.transpose(0, 1, 3, 2, 4))
        cc2 = cache_ffn_conv[:, 4 * core:4 * core + 4].reshape(2, 4, 2, 22, 128)
        m["cconv"] = np.ascontiguousarray(cc2.transpose(0, 4, 3, 1, 2))
        in_maps.append(m)
    return in_maps


def _post(R):

    y_prompt = np.empty((2, 8192, D), np.float32)
    y_sample = np.empty((32, 16, D), np.float32)
    nkp = np.empty((2, 2, 512, 8, 64), np.float32)
    nvp = np.empty((2, 2, 512, 8, 64), np.float32)
    ncp = np.empty((2, 2, 2, DFF), np.float32)
    nks = np.empty((2, 32, 16, 8, 64), np.float32)
    nvs = np.empty((2, 32, 16, 8, 64), np.float32)
    nbs = np.empty((2, 32, 16, 4, 128), np.float32)
    ncs = np.empty((2, 32, 2, DFF), np.float32)
    for core in range(NCORES):
        r = R[core]
        b, seg = core // 4, core % 4
        y_prompt[b, seg * SEG:(seg + 1) * SEG] = r["yT"].transpose(2, 1, 0).reshape(SEG, D)
        y_sample[4 * core:4 * core + 4] = r["ysT"].transpose(2, 1, 0).reshape(4, 16, D)
        sl = slice(4 * core, 4 * core + 4)
        for l in range(2):
            if seg == 3:
                nkp[l, b] = r["nkT"][l].reshape(2, 64, 4, 512).transpose(3, 2, 0, 1).reshape(512, 8, 64)
                nvp[l, b] = r["nv"][l].reshape(512, 8, 64)
                ncp[l, b] = r["ncv"][l].transpose(2, 1, 0).reshape(2, DFF)
            nks[l, sl] = r["nksT"][l].reshape(2, 64, 4, 4, 16).transpose(3, 4, 2, 0, 1).reshape(4, 16, 8, 64)
            nvs[l, sl] = r["nvs"][l].transpose(1, 0, 2).reshape(4, 16, 8, 64)
            nbs[l, sl] = r["nbs"][l].reshape(4, 16, 4, 128)
            ncs[l, sl] = r["ncs"][l].transpose(2, 3, 1, 0).reshape(4, 2, DFF)
    return (y_prompt, y_sample, nkp, nvp, ncp, nks, nvs, nbs, ncs)
```

```python
import contextlib
import numpy as np
import concourse.bass as bass
import concourse.mybir as mybir
from concourse.bass_utils import run_bass_kernel_spmd

F32 = mybir.dt.float32
BF16 = mybir.dt.bfloat16
AF = mybir.ActivationFunctionType
ALU = mybir.AluOpType
AX = mybir.AxisListType

NCORES = 8
D = 1024
SEG = 2048
HALO = 1152
W = SEG + HALO
NT = W // 128
DFF = 2816
EPS = 1e-6
NB = 3
L0_BLOCKS = [(4, 4), (8, 4), (12, 4), (16, 3), (19, 3), (22, 3)]
L1_BLOCKS = [(8, 4), (12, 4), (16, 3), (19, 3), (22, 3)]
OUT_T0 = 9
KEEP_T0 = 21
_DBG = {}


class Prog:
    STREAMS = ("sp", "act", "dve", "pool", "pe")

    def __init__(self):
        self.ops = []
        self.keyw = {}
        self.keyr = {}
        self.dcnt = {}

    def op(self, stream, method, *args, r=(), w=(), semkey=None, **kw):
        idx = len(self.ops)
        dom = ("d", semkey) if semkey is not None else ("s", stream)
        deps = {}

        def add(d, i):
            if d[0] == "d":
                i = self.dcnt[d]
            if deps.get(d, -1) < i:
                deps[d] = i

        for k in r:
            for d, i in self.keyw.get(k, {}).items():
                add(d, i)
        skip_same = dom[0] == "d" or stream == "pe"
        for k in w:
            for d, i in self.keyr.get(k, {}).items():
                if d == dom and (skip_same or i == idx):
                    continue
                add(d, i)
            for d, i in self.keyw.get(k, {}).items():
                if d == dom and skip_same:
                    continue
                add(d, i)
        for k in r:
            self.keyr.setdefault(k, {})[dom] = idx
        for k in w:
            if self.keyr.get(k):
                self.keyw[k] = {dom: idx}
                self.keyr[k] = {}
            else:
                self.keyw.setdefault(k, {})[dom] = idx
        if dom[0] == "d":
            self.dcnt[dom] = self.dcnt.get(dom, 0) + 16
        self.ops.append(dict(stream=stream, method=method, args=args, kw=kw, dom=dom,
                             deps=list(deps.items()), stage=_DBG.get("stage", "")))
        return idx

    def emit(self, nc, stack):
        ops = self.ops
        needs = set()
        for o in ops:
            for d, i in o["deps"]:
                if d[0] == "s":
                    needs.add(i)
        cnt = {}
        sems = {}
        issuer = {}
        for i, o in enumerate(ops):
            d = o["dom"]
            if d[0] == "d":
                cnt[d] = cnt.get(d, 0) + 16
                o["done"] = cnt[d]
                assert issuer.setdefault(d, o["stream"]) == o["stream"], d
            elif i in needs:
                cnt[d] = cnt.get(d, 0) + 1
                o["done"] = cnt[d]
            else:
                o["done"] = None
        for d in cnt:
            sems[d] = stack.enter_context(nc.semaphore("s%d" % len(sems)))
        block = stack.enter_context(nc.Block())
        self.nsem = len(sems)

        def run(stream, eng):
            waited = {}
            for o in ops:
                if o["stream"] != stream:
                    continue
                for d, i in o["deps"]:
                    v = i if d[0] == "d" else ops[i]["done"]
                    if waited.get(d, 0) < v:
                        eng.wait_ge(sems[d], v)
                        waited[d] = v
                        _DBG.setdefault("waits", {}).setdefault(stream, []).append((o["stage"], d))
                ins = getattr(eng, o["method"])(*o["args"], **o["kw"])
                if o["done"] is not None:
                    d = o["dom"]
                    ins.then_inc(sems[d], 16 if d[0] == "d" else 1)
            for d, v in cnt.items():
                if d[0] == "d" and issuer[d] == stream and waited.get(d, 0) < v:
                    eng.wait_ge(sems[d], v)

        @block.sync
        def _(e):
            run("sp", e)

        @block.scalar
        def _(e):
            run("act", e)

        @block.vector
        def _(e):
            run("dve", e)

        @block.gpsimd
        def _(e):
            run("pool", e)

        @block.tensor
        def _(e):
            run("pe", e)


def build_nc(stop=None):
    class _Stop(Exception):
        pass

    nblk = [0]

    def ck(st):
        _DBG["stage"] = st + (10 if _DBG.get("stage", 0) >= 10 else 0)
        if stop is not None and nblk[0] + st / 10.0 >= stop - 1e-9 and stop > 0:
            raise _Stop()

    nc = bass.Bass("TRN2", target_bir_lowering=False)
    P = Prog()
    stack = contextlib.ExitStack()

    def din(name, shape):
        return nc.dram_tensor(name, list(shape), F32, kind="ExternalInput").ap()

    def dout(name, shape):
        return nc.dram_tensor(name, list(shape), F32, kind="ExternalOutput").ap()

    def dint(name, shape, dt):
        return nc.dram_tensor(name, list(shape), dt, kind="Internal").ap()

    xin = din("xin", (128, 8, W))
    xs_d = din("xs", (128, 8, 64))
    vtok_d = din("vtok", (128, NT))
    flag_d = din("flag", (128, 1))
    cT_d = din("cT", (128, 8, 5))
    ckT_d = din("ckT", (2, 128, 4, 4, 512))
    cv_d = din("cv", (2, 4, 128, 4, 512))
    cconv_d = din("cconv", (2, 128, 22, 4, 2))
    wall_d = din("wall", (2, 24, 128, 4096))
    wfo_d = din("wfo", (2, 8, 128, 2816))
    wada_d = din("wada", (2, 12, 128, 4096))
    nrm_d = din("nrm", (2, 128, 16))
    bada_d = din("bada", (2, 128, 48))
    qkg_d = din("qkg", (2, 128, 2))
    vg_d = din("vg", (2, 128, 512))
    bsp_d = din("bsp", (2, 1, 512))
    bsps_d = din("bsps", (2, 1, 256))
    wspT_d = din("wspT", (2, 128, 4, 128))
    wspb_d = din("wspb", (2, 64, 4, 64))
    tril_d = din("tril", (128, 4, 128))
    trilb_d = din("trilb", (64, 4, 64))
    cw_d = din("cw", (2, 128, 22, 3))
    cb_d = din("cb", (2, 128, 22))
    Tb_d = din("Tb", (2, 128, 2, 8, 128))
    b256_d = din("b256", (2, 128, 8))

    yT_d = dout("yT", (128, 8, SEG))
    ysT_d = dout("ysT", (128, 8, 64))
    nkT_d = dout("nkT", (2, 128, 4, 512))
    nv_d = dout("nv", (2, 4, 128, 512))
    ncv_d = dout("ncv", (2, 128, 22, 2))
    nksT_d = dout("nksT", (2, 128, 4, 64))
    nvs_d = dout("nvs", (2, 16, 4, 512))
    nbs_d = dout("nbs", (2, 64, 512))
    ncs_d = dout("ncs", (2, 128, 22, 4, 2))

    wsc = dint("wsc", (2, 24, 128, 4096), BF16)
    wsc_fo = dint("wscfo", (2, 8, 128, 2816), BF16)
    x1s = dint("x1s", (128, 8, 21 * 128), F32)
    xs1 = dint("xs1", (128, 8, 64), F32)

    def sb(name, shape, dt=F32):
        return stack.enter_context(nc.sbuf_tensor("sb_" + name, list(shape), dt))

    xTt = sb("xT", (128, 2, 8, 512))
    cur = {"xT": xTt[:, 0], "par": 0, "idx": 0}
    hT = sb("hT", (128, 8, 512), BF16)
    sq = sb("sq", (128, 3, 512), BF16)
    rstd = sb("rstd", (128, 512))
    tmpn = sb("tmpn", (128, 2, 512))
    qz = sb("qz", (128, 4, 4, 2, 128), BF16)
    kst = sb("kst", (128, 2, 512))
    rs = sb("rs", (128, 2, 512))
    kT = sb("kT", (128, 4, 1024), BF16)
    vt = sb("vt", (128, 8, 512), BF16)
    ubT = sb("ubT", (128, 4, 512), BF16)
    vbn = sb("vbn", (128, 4, 512), BF16)
    gl = sb("gl", (128, 2, 512))
    ssv = sb("ssv", (128, 4))
    SA = sb("SA", (128, 24, 512), BF16)
    oaT = sb("oaT", (128, 4, 512), BF16)
    obT = sb("obT", (128, 4, 512), BF16)
    Pt = sb("Pt", (128, 2, 5, 256), BF16)
    rden = sb("rden", (128, 2, 256))
    gb = sb("gb", (128, 3, 516))
    gc = sb("gc", (128, 2, 512))
    ge = sb("ge", (128, 2, 512))
    halo = sb("halo", (128, 22, 2))
    halos = sb("halos", (128, 22, 4, 2))
    ncs_st = sb("ncs_st", (128, 22, 4, 2))
    wslab = sb("wslab", (128, NB, 4096), BF16)
    ones_bf = sb("ones_bf", (128, 128), BF16)
    blk64 = sb("blk64", (128, 128), BF16)
    ones_row = sb("ones_row", (1, 128), BF16)
    vones = sb("vones", (128, 9, 128), BF16)
    vtok = sb("vtok", (128, NT))
    flag = sb("flag", (128, 1))
    cTs = sb("cTs", (128, 8, 5))
    cs = sb("cs", (128, 8, 5), BF16)
    E = sb("E", (128, 2, 8, 128))
    negb = sb("negb", (128, 8))
    modTt = sb("modT", (128, 2, 48, 5))
    A12t = sb("A12", (128, 2, 2, 8, 5))
    nrmt = sb("nrm", (128, 2, 16))
    badat = sb("bada", (128, 2, 48))
    qkg = sb("qkg", (128, 2))
    vg = sb("vg", (128, 512))
    bsp = sb("bsp", (1, 512), BF16)
    bsps = sb("bsps", (1, 256), BF16)
    WcT = sb("WcT", (128, 4, 128), BF16)
    Wblk = sb("Wblk", (64, 4, 64), BF16)
    cw = sb("cw", (128, 22, 3))
    cb = sb("cb", (128, 22))
    kTs = sb("kTs", (128, 4, 64), BF16)
    vts = sb("vts", (16, 4, 512), BF16)
    vbns = sb("vbns", (64, 512), BF16)
    ckb = sb("ckb", (128, 1, 4, 512), BF16)
    cvb = sb("cvb", (128, 1, 4, 512), BF16)
    Pts = sb("Pts", (128, 2, 2, 80), BF16)
    exs = sb("exs", (128, 2, 2, 32))
    rdens = sb("rdens", (128, 2, 32))

    psd = [stack.enter_context(nc.psum_tensor("ps%d" % i, [128, 1024], F32)) for i in range(4)]
    ps = [psd[i // 2][:, (i % 2) * 512:(i % 2 + 1) * 512] for i in range(8)]

    def bk(i):
        return [("ps", i)]

    bank_ctr = [0]

    reserved = set()

    def nb():
        while True:
            b = bank_ctr[0] % 8
            bank_ctr[0] += 1
            if b not in reserved:
                return b

    rot = {}

    def nrot(name, n):
        v = rot.get(name, 0)
        rot[name] = v + 1
        return v % n

    ws = dict(seq=[], issued=0, consumed=0)

    casted = set()

    def ws_issue(upto):
        while ws["issued"] < min(upto, len(ws["seq"])):
            kind_, l_, s_ = ws["seq"][ws["issued"]]
            slot = ws["issued"] % NB
            wk, sk = [("wslab", slot)], ("w", slot)
            if kind_ == "ada":
                P.op("pool", "dma_start", out=wslab[:, slot, :], in_=wada_d[l_, s_], w=wk, semkey=sk)
            else:
                ncols = 4096 if kind_ == "wall" else 2816
                src32 = wall_d[l_, s_] if kind_ == "wall" else wfo_d[l_, s_]
                scr = wsc[l_, s_] if kind_ == "wall" else wsc_fo[l_, s_]
                key = ("wsc", kind_, l_, s_)
                if key not in casted:
                    casted.add(key)
                    P.op("pool", "dma_start", out=wslab[:, slot, 0:ncols], in_=src32, w=wk, semkey=sk)
                    P.op("sp", "dma_start", out=scr, in_=wslab[:, slot, 0:ncols], r=wk, w=[key],
                         semkey=("wst", slot))
                else:
                    P.op("pool", "dma_start", out=wslab[:, slot, 0:ncols], in_=scr, r=[key], w=wk, semkey=sk)
            ws["issued"] += 1

    def ws_get():
        assert ws["consumed"] < len(ws["seq"])
        ws_issue(ws["consumed"] + 1)
        slot = ws["consumed"] % NB
        ws["consumed"] += 1
        return slot

    def ws_done():
        ws_issue(ws["consumed"] + NB)

    def seq_ada(l):
        for s_ in range(12):
            ws["seq"].append(("ada", l, s_))

    def seq_block(l, kind):
        if kind == "kv":
            for s_ in (1, 2):
                ws["seq"].append(("wall", l, s_))
            return
        for s_ in range(24):
            ws["seq"].append(("wall", l, s_))
        for s_ in range(8):
            ws["seq"].append(("fo", l, s_))

    seq_ada(0)
    for l in range(2):
        seq_block(l, "kv")
        for _ in (L0_BLOCKS if l == 0 else L1_BLOCKS):
            seq_block(l, "full")
        if l == 0:
            seq_ada(1)
        seq_block(l, "sample")

    P.op("pool", "memset", ones_bf[:, :], 1.0, w=["ones_bf"])
    P.op("pool", "memset", blk64[:, :], 0.0, w=["blk64"])
    P.op("pool", "memset", blk64[0:64, 0:64], 1.0, w=["blk64"])
    P.op("pool", "memset", blk64[64:128, 64:128], 1.0, w=["blk64"])
    P.op("pool", "memset", ones_row[:, :], 1.0, w=["ones_row"])
    P.op("pool", "memset", Pt[:, :, :, :].rearrange("p a b c -> p (a b c)"), 0.0, w=[("Pt", 0), ("Pt", 1)])
    P.op("pool", "memset", qz[:, :, :, :, :].rearrange("p a b c d -> p (a b c d)"), 0.0, w=[("qT", c) for c in range(4)])
    P.op("pool", "memset", halo[:, :, :], 0.0, w=["halo"])
    P.op("sp", "dma_start", out=vtok[:, :], in_=vtok_d, w=["vtok"], semkey="c0")
    P.op("sp", "dma_start", out=flag[:, :], in_=flag_d, w=["flag"], semkey="c0")
    P.op("sp", "dma_start", out=cTs[:, :, :], in_=cT_d, w=["cTs"], semkey="c0")
    P.op("act", "activation", out=cs[:, :, :], in_=cTs[:, :, :], func=AF.Silu, r=["cTs"], w=["cs"])
    for t in range(9):
        P.op("act", "activation", out=vones[:, t, :], in_=ones_bf[:, :], func=AF.Copy,
             scale=vtok[:, t:t + 1], r=["ones_bf", "vtok"], w=["vones"])
    def layer_setup(l):
        for dst, src, key in ((qkg, qkg_d[l], "qkg"),
                              (vg, vg_d[l], "vg"), (cw, cw_d[l], "cw"), (cb, cb_d[l], "cb"),
                              (negb, b256_d[l], "negb")):
            full = tuple(slice(None) for _ in dst.shape)
            P.op("sp", "dma_start", out=dst[full], in_=src, w=[key], semkey="ls")
        P.op("sp", "dma_start", out=E[:, :, :, :], in_=Tb_d[l], w=["E"], semkey="ls")
        P.op("sp", "dma_start", out=halos[:, :, :, :], in_=cconv_d[l], w=["halos"], semkey="ls")
        P.op("sp", "dma_start", out=gl[0:1, 0, :], in_=bsp_d[l], w=[("gl", 0)], semkey="ls")
        P.op("sp", "dma_start", out=gl[0:1, 1, 0:256], in_=bsps_d[l], w=[("gl", 1)], semkey="ls")
        P.op("sp", "dma_start", out=gc[:, 0, :], in_=wspT_d[l].rearrange("p g t -> p (g t)"),
             w=[("gc", 0)], semkey="ls3")
        P.op("sp", "dma_start", out=gc[:, 1, :], in_=tril_d.rearrange("p g t -> p (g t)"),
             w=[("gc", 1)], semkey="ls3")
        P.op("sp", "dma_start", out=ge[0:64, 0, 0:256], in_=wspb_d[l].rearrange("p g t -> p (g t)"),
             w=[("ge", 0)], semkey="ls3")
        P.op("sp", "dma_start", out=ge[0:64, 1, 0:256], in_=trilb_d.rearrange("p g t -> p (g t)"),
             w=[("ge", 1)], semkey="ls3")
        P.op("dve", "tensor_tensor", out=WcT[:, :, :].rearrange("p g t -> p (g t)"), in0=gc[:, 0, :],
             in1=gc[:, 1, :], op=ALU.mult, r=[("gc", 0), ("gc", 1)], w=["WcT"])
        P.op("dve", "tensor_tensor", out=Wblk[:, :, :].rearrange("p g t -> p (g t)"), in0=ge[0:64, 0, 0:256],
             in1=ge[0:64, 1, 0:256], op=ALU.mult, r=[("ge", 0), ("ge", 1)], w=["Wblk"])
        P.op("dve", "tensor_copy", bsp[:, :], gl[0:1, 0, :], r=[("gl", 0)], w=["bsp"])
        P.op("dve", "tensor_copy", bsps[:, :], gl[0:1, 1, 0:256], r=[("gl", 1)], w=["bsps"])
        P.op("dve", "tensor_scalar_mul", qkg[:, 1:2], qkg[:, 1:2], 8.0, r=["qkg"], w=["qkg"])
        P.op("dve", "tensor_scalar_mul", negb[:, :], negb[:, :], -1.0, r=["negb"], w=["negb"])
        for t in range(2):
            for h in range(8):
                P.op("act", "activation", out=E[:, t, h, :], in_=E[:, t, h, :], func=AF.Exp,
                     bias=negb[:, h:h + 1], scale=1.0, r=["E", "negb"], w=["E"])
        P.op("pool", "memset", E[64:128, 1, :, 0:64], 0.0, r=["E"], w=["E"])

    def ada_setup(l):
        modT, A12, nrm, bada = modTt[:, l], A12t[:, l], nrmt[:, l], badat[:, l]
        P.op("sp", "dma_start", out=nrm, in_=nrm_d[l], w=[("nrm", l)], semkey=("lsa", l))
        P.op("sp", "dma_start", out=bada, in_=bada_d[l], w=[("bada", l)], semkey=("lsa", l))
        b = nb()
        for s_ in range(12):
            slot = ws_get()
            for m in range(4):
                ci = s_ * 4 + m
                for kc in range(8):
                    P.op("pe", "matmul", ps[b][:, ci * 5:(ci + 1) * 5],
                         lhsT=wslab[:, slot, kc * 512 + m * 128: kc * 512 + (m + 1) * 128],
                         rhs=cs[:, kc, :], start=(kc == 0), stop=(kc == 7),
                         r=[("wslab", slot), "cs"], w=bk(b))
            ws_done()
        for bb in range(5):
            P.op("dve", "tensor_tensor", out=modT[:, :, bb],
                 in0=ps[b][:, 0:240].rearrange("p (c b) -> p c b", b=5)[:, :, bb], in1=bada,
                 op=ALU.add, r=bk(b) + [("bada", l)], w=[("modT", l)])
        for which in range(2):
            sc0 = 8 if which == 0 else 32
            for bb in range(5):
                P.op("dve", "tensor_scalar", out=A12[:, which, :, bb], in0=modT[:, sc0:sc0 + 8, bb],
                     scalar1=1.0, scalar2=32.0, op0=ALU.add, op1=ALU.mult, r=[("modT", l)], w=[("A12", l)])
                P.op("dve", "tensor_tensor", out=A12[:, which, :, bb], in0=A12[:, which, :, bb],
                     in1=nrm[:, which * 8:(which + 1) * 8], op=ALU.mult, r=[("A12", l), ("nrm", l)],
                     w=[("A12", l)])

    def norm_sq(b, c, T):
        r_ = nrot("sq", 3)
        P.op("act", "activation", out=sq[:, r_, 0:T], in_=cur["xT"][:, c, 0:T], func=AF.Square,
             r=[("xT", cur["par"], c)], w=[("sq", r_)])
        return r_

    def norm_ss(b, c, r_, T):
        P.op("pe", "matmul", ps[b][:, 0:T], lhsT=ones_bf[:, :], rhs=sq[:, r_, 0:T],
             start=(c == 0), stop=(c == 7), r=[("sq", r_), "ones_bf"], w=bk(b))

    def norm_fin(b, T):
        P.op("act", "activation", out=rstd[:, 0:T], in_=ps[b][:, 0:T], func=AF.Sqrt, bias=float(D * EPS),
             scale=1.0, r=bk(b), w=["rstd"])
        P.op("dve", "reciprocal", out=rstd[:, 0:T], in_=rstd[:, 0:T], r=["rstd"], w=["rstd"])

    def norm_apply(which, T, groups):
        shc = 0 if which == 0 else 24
        for c in range(8):
            r_ = nrot("tmpn", 2)
            for (c0, n, bb) in groups:
                P.op("dve", "scalar_tensor_tensor", out=tmpn[:, r_, c0:c0 + n], in0=cur["xT"][:, c, c0:c0 + n],
                     scalar=A12t[:, cur["l"], which, c, bb:bb + 1], in1=rstd[:, c0:c0 + n], op0=ALU.mult, op1=ALU.mult,
                     r=[("xT", cur["par"], c), ("A12", cur["l"]), "rstd"], w=[("tmpn", r_)])
            for (c0, n, bb) in groups:
                P.op("act", "activation", out=hT[:, c, c0:c0 + n], in_=tmpn[:, r_, c0:c0 + n],
                     func=AF.Identity, bias=modTt[:, cur["l"], shc + c, bb:bb + 1], scale=1.0,
                     r=[("tmpn", r_), ("modT", cur["l"])], w=[("hT", c)])

    def norm_mod(which, T, groups):
        b = nb()
        for c in range(8):
            r_ = norm_sq(b, c, T)
            norm_ss(b, c, r_, T)
        norm_fin(b, T)
        norm_apply(which, T, groups)

    pro_done = set()

    def blk_geom(i):
        l_, kind_, t0_, n_ = BLOCKS[i]
        if kind_ == "sample":
            return l_, 64, [(bl * 16, 16, 1 + bl) for bl in range(4)]
        return l_, n_ * 128, [(0, n_ * 128, 0)]

    def pro_ok(i):
        return i < len(BLOCKS)

    class _NextCtx:
        def __init__(self, i):
            self.i = i

        def __enter__(self):
            self.save = dict(cur)
            cur["l"] = BLOCKS[self.i][0]
            cur["par"] = self.i % 2
            cur["xT"] = xTt[:, self.i % 2]

        def __exit__(self, *a):
            cur.update(self.save)

    def proj_fm(b, slot, nkc, ms, col0, rhs_t, rkeys, T):
        for kc in range(nkc):
            P.op("pe", "matmul", ps[b][:, 0:T], lhsT=wslab[:, slot, kc * ms + col0: kc * ms + col0 + 128],
                 rhs=rhs_t(kc), start=(kc == 0), stop=(kc == nkc - 1),
                 r=[("wslab", slot)] + rkeys(kc), w=bk(b))

    def headnorm(b, T, gcol, out_ap, out_keys, qchunk=None):
        r_ = nrot("sq", 3)
        P.op("act", "activation", out=sq[:, r_, 0:T], in_=ps[b][:, 0:T], func=AF.Square, r=bk(b), w=[("sq", r_)])
        b2 = nb()
        P.op("pe", "matmul", ps[b2][:, 0:T], lhsT=blk64[:, :], rhs=sq[:, r_, 0:T], start=True, stop=True,
             r=[("sq", r_), "blk64"], w=bk(b2))
        r2 = nrot("rs", 2)
        P.op("act", "activation", out=rs[:, r2, 0:T], in_=ps[b2][:, 0:T], func=AF.Sqrt, bias=float(64 * EPS),
             scale=1.0, r=bk(b2), w=[("rs", r2)])
        P.op("dve", "reciprocal", out=rs[:, r2, 0:T], in_=rs[:, r2, 0:T], r=[("rs", r2)], w=[("rs", r2)])
        if qchunk is not None:
            nq = max(T // 128, 1)
            w_ = min(T, 128)
            for hh in range(2):
                hs = slice(hh * 64, (hh + 1) * 64)
                P.op("dve", "scalar_tensor_tensor", out=qz[hs, qchunk, 0:nq, hh, 0:w_],
                     in0=ps[b][hs, 0:T].rearrange("p (t q) -> p t q", t=nq), scalar=qkg[hs, gcol:gcol + 1],
                     in1=rs[hs, r2, 0:T].rearrange("p (t q) -> p t q", t=nq), op0=ALU.mult, op1=ALU.mult,
                     r=bk(b) + [("rs", r2), "qkg"], w=out_keys)
            return
        P.op("dve", "scalar_tensor_tensor", out=out_ap, in0=ps[b][:, 0:T], scalar=qkg[:, gcol:gcol + 1],
             in1=rs[:, r2, 0:T], op0=ALU.mult, op1=ALU.mult, r=bk(b) + [("rs", r2), "qkg"], w=out_keys)

    def hT_r(T):
        return (lambda kc: hT[:, kc, 0:T]), (lambda kc: [("hT", kc)])

    def block(l, kind, t0, ntile):
        sample = kind == "sample"
        T = 64 if sample else ntile * 128
        tiles = [] if sample else list(range(t0, t0 + ntile))
        groups = [(bl * 16, 16, 1 + bl) for bl in range(4)] if sample else [(0, T, 0)]
        hr, hk = hT_r(T)
        bi = cur["idx"]
        cur["l"] = l
        _DBG["stage"] = 0 if kind != "sample" else 10
        cur["par"] = bi % 2
        cur["xT"] = xTt[:, bi % 2]

        def xload(i):
            l_, kind_, t0_, n_ = BLOCKS[i]
            par_ = i % 2
            wk = [("xT", par_, c) for c in range(8)]
            if kind_ == "sample":
                if l_ == 1:
                    P.op("sp", "dma_start", out=xTt[:, par_, :, 0:64], in_=xs1, r=["xs1"], w=wk,
                         semkey=("xld", par_))
                else:
                    P.op("sp", "dma_start", out=xTt[:, par_, :, 0:64], in_=xs_d, w=wk, semkey=("xld", par_))
                return True
            T_ = n_ * 128
            if l_ == 0:
                P.op("sp", "dma_start", out=xTt[:, par_, :, 0:T_], in_=xin[:, :, t0_ * 128: t0_ * 128 + T_], w=wk,
                     semkey=("xld", par_))
            else:
                P.op("sp", "dma_start", out=xTt[:, par_, :, 0:T_],
                     in_=x1s[:, :, (t0_ - 4) * 128: (t0_ - 4) * 128 + T_],
                     r=[("x1s", t) for t in range(t0_, t0_ + n_)], w=wk, semkey=("xld", par_))
            return True

        if bi == 0:
            xload(0)
        if bi + 1 < len(BLOCKS):
            xload(bi + 1)
        cur["idx"] = bi + 1
        if bi not in pro_done:
            norm_mod(0, T, groups)

        if kind == "kv":
            slabs = {1: ws_get()}
        else:
            slabs = {0: ws_get()}
            for c in range(4):
                b = nb()
                proj_fm(b, slabs[0], 8, 512, c * 128, hr, hk, T)
                headnorm(b, T, 0, None, [("qT", c)], qchunk=c)
            ws_done()
            slabs[1] = ws_get()
        for c in range(4):
            b = nb()
            proj_fm(b, slabs[1], 8, 512, c * 128, hr, hk, T)
            r_ = nrot("kst", 2)
            headnorm(b, T, 1, kst[:, r_, 0:T], [("kst", r_)])
            if sample:
                P.op("act", "activation", out=kTs[:, c, :], in_=kst[:, r_, 0:64], func=AF.Copy,
                     r=[("kst", r_)], w=["kTs"])
                P.op("sp", "dma_start", out=nksT_d[l, :, c, :], in_=kst[:, r_, 0:64], r=[("kst", r_)],
                     semkey=("kst", r_))
            else:
                for ti, t in enumerate(tiles):
                    sl = t % 8
                    P.op("act", "activation", out=kT[:, c, sl * 128:(sl + 1) * 128],
                         in_=kst[:, r_, ti * 128:(ti + 1) * 128], func=AF.Copy, r=[("kst", r_)], w=[("kT", sl)])
                    if t >= KEEP_T0:
                        P.op("sp", "dma_start", out=nkT_d[l, :, c, (t - KEEP_T0) * 128:(t - KEEP_T0 + 1) * 128],
                             in_=kst[:, r_, ti * 128:(ti + 1) * 128], r=[("kst", r_)], semkey=("kst", r_))
        ws_done()
        slot = ws_get()
        if sample:
            for bl in range(4):
                b = nb()
                for kc in range(8):
                    P.op("pe", "matmul", ps[b][0:16, 0:512], lhsT=hT[:, kc, bl * 16:(bl + 1) * 16],
                         rhs=wslab[:, slot, kc * 512:(kc + 1) * 512], start=(kc == 0), stop=(kc == 7),
                         r=[("wslab", slot), ("hT", kc)], w=bk(b))
                r_ = nrot("gl", 2)
                P.op("act", "activation", out=gl[0:16, r_, :], in_=ps[b][0:16, 0:512], func=AF.Copy, r=bk(b),
                     w=[("gl", r_)])
                P.op("dve", "tensor_copy", vts[:, bl, :], gl[0:16, r_, :], r=[("gl", r_)], w=["vts"])
                P.op("sp", "dma_start", out=nvs_d[l, :, bl, :], in_=gl[0:16, r_, :], r=[("gl", r_)],
                     semkey=("gl", r_))
        else:
            for ti, t in enumerate(tiles):
                b = nb()
                for kc in range(8):
                    P.op("pe", "matmul", ps[b][:, 0:512], lhsT=hT[:, kc, ti * 128:(ti + 1) * 128],
                         rhs=wslab[:, slot, kc * 512:(kc + 1) * 512], start=(kc == 0), stop=(kc == 7),
                         r=[("wslab", slot), ("hT", kc)], w=bk(b))
                sl = t % 8
                if t >= KEEP_T0:
                    r_ = nrot("gl", 2)
                    P.op("act", "activation", out=gl[:, r_, :], in_=ps[b][:, 0:512], func=AF.Copy, r=bk(b),
                         w=[("gl", r_)])
                    P.op("dve", "tensor_scalar_mul", vt[:, sl, :], gl[:, r_, :], vtok[:, t:t + 1],
                         r=[("gl", r_), "vtok"], w=[("vt", sl)])
                    P.op("sp", "dma_start", out=nv_d[l, t - KEEP_T0], in_=gl[:, r_, :], r=[("gl", r_)],
                         semkey=("gl", r_))
                else:
                    P.op("dve", "tensor_scalar_mul", vt[:, sl, :], ps[b][:, 0:512], vtok[:, t:t + 1],
                         r=bk(b) + ["vtok"], w=[("vt", sl)])
        ws_done()
        if kind == "kv":
            if pro_ok(bi + 1):
                with _NextCtx(bi + 1):
                    l2, T2, g2 = blk_geom(bi + 1)
                    norm_mod(0, T2, g2)
                pro_done.add(bi + 1)
            return

        ck(1)
        slot = ws_get()
        for c in range(4):
            b = nb()
            proj_fm(b, slot, 8, 512, c * 128, hr, hk, T)
            P.op("act", "activation", out=ubT[:, c, 0:T], in_=ps[b][:, 0:T], func=AF.Gelu_apprx_tanh, r=bk(b),
                 w=[("ubT", c)])
        ws_done()
        ck(2)
        slot = ws_get()
        tl = [(0, 64)] if sample else [(ti, 128) for ti in range(ntile)]
        for ti, M in tl:
            b = nb()
            for kc in range(8):
                P.op("pe", "matmul", ps[b][0:M, 0:512], lhsT=hT[:, kc, ti * 128: ti * 128 + M],
                     rhs=wslab[:, slot, kc * 512:(kc + 1) * 512], start=(kc == 0), stop=(kc == 7),
                     r=[("wslab", slot), ("hT", kc)], w=bk(b))
            r_ = nrot("gl", 2)
            P.op("act", "activation", out=gl[0:M, r_, :], in_=ps[b][0:M, 0:512], func=AF.Gelu_apprx_tanh, r=bk(b),
                 w=[("gl", r_)])
            r2 = nrot("tmpn", 2)
            P.op("dve", "tensor_tensor", out=tmpn[0:M, r2, :], in0=gl[0:M, r_, :], in1=gl[0:M, r_, :], op=ALU.mult,
                 r=[("gl", r_)], w=[("tmpn", r2)])
            r3 = nrot("ssv", 4)
            P.op("dve", "reduce_sum", out=ssv[0:M, r3:r3 + 1], in_=tmpn[0:M, r2, :], axis=AX.X,
                 r=[("tmpn", r2)], w=[("ssv", r3)])
            P.op("act", "activation", out=ssv[0:M, r3:r3 + 1], in_=ssv[0:M, r3:r3 + 1], func=AF.Sqrt,
                 bias=float(EPS), scale=1.0 / 512.0, r=[("ssv", r3)], w=[("ssv", r3)])
            P.op("dve", "reciprocal", out=ssv[0:M, r3:r3 + 1], in_=ssv[0:M, r3:r3 + 1], r=[("ssv", r3)],
                 w=[("ssv", r3)])
            if sample:
                P.op("dve", "scalar_tensor_tensor", out=tmpn[0:64, r2, :], in0=gl[0:64, r_, :],
                     scalar=ssv[0:64, r3:r3 + 1], in1=vg[0:64, :], op0=ALU.mult, op1=ALU.mult,
                     r=[("gl", r_), ("ssv", r3), "vg"], w=[("tmpn", r2)])
                P.op("act", "activation", out=vbns[:, :], in_=tmpn[0:64, r2, :], func=AF.Copy, r=[("tmpn", r2)],
                     w=["vbns"])
                P.op("sp", "dma_start", out=nbs_d[l], in_=tmpn[0:64, r2, :], r=[("tmpn", r2)], semkey=("tmpn", r2))
            else:
                P.op("dve", "scalar_tensor_tensor", out=vbn[:, ti, :], in0=gl[:, r_, :], scalar=ssv[:, r3:r3 + 1],
                     in1=vg[:, :], op0=ALU.mult, op1=ALU.mult, r=[("gl", r_), ("ssv", r3), "vg"], w=[("vbn", ti)])
        ws_done()
        ck(3)
        for s in range(4):
            slot = ws_get()
            for m in range(4):
                c = s * 4 + m
                b = nb()
                proj_fm(b, slot, 8, 512, m * 128, hr, hk, T)
                P.op("act", "activation", out=SA[:, c, 0:T], in_=ps[b][:, 0:T], func=AF.Sigmoid, r=bk(b),
                     w=[("SA", c)])
            ws_done()

        ck(4)
        if sample:
            items = [(bl, hp) for bl in range(4) for hp in range(4)]

            def s1(it, st):
                bl, hp = it
                cr = 0
                for j in range(5):
                    for hh in range(2):
                        hs = slice(hh * 64, (hh + 1) * 64)
                        b = st * 4 + hh
                        if j < 4:
                            P.op("pe", "matmul", ps[b][:, j * 16:(j + 1) * 16],
                                 lhsT=ckb[hs, cr, hp, j * 128:(j + 1) * 128], rhs=qz[hs, hp, 0, hh, bl * 16:(bl + 1) * 16],
                                 start=True, stop=True, r=[("ckb", cr), ("qT", hp)], w=[("ps", b)])
                        else:
                            P.op("pe", "matmul", ps[b][0:16, 64:80],
                                 lhsT=kTs[hs, hp, bl * 16:(bl + 1) * 16], rhs=qz[hs, hp, 0, hh, bl * 16:(bl + 1) * 16],
                                 start=True, stop=True, r=["kTs", ("qT", hp)], w=[("ps", b)])
                for hh in range(2):
                    b = st * 4 + hh
                    h = 2 * hp + hh
                    P.op("act", "activation", out=Pts[:, st, hh, 0:48], in_=ps[b][:, 0:48], func=AF.Exp,
                         r=[("ps", b)], w=[("Pts", st, hh)])
                    P.op("act", "activation", out=exs[:, st, hh, 0:16], in_=ps[b][:, 48:64], func=AF.Exp,
                         r=[("ps", b)], w=[("exs", st, hh)])
                    P.op("act", "activation", out=exs[0:16, st, hh, 16:32], in_=ps[b][0:16, 64:80], func=AF.Exp,
                         r=[("ps", b)], w=[("exs", st, hh)])
                    P.op("dve", "tensor_tensor", out=Pts[:, st, hh, 48:64], in0=exs[:, st, hh, 0:16],
                         in1=E[:, 0, h, 0:16], op=ALU.mult, r=[("exs", st, hh), "E"], w=[("Pts", st, hh)])
                    P.op("dve", "tensor_tensor", out=Pts[0:16, st, hh, 64:80], in0=exs[0:16, st, hh, 16:32],
                         in1=E[0:16, 1, h, 0:16], op=ALU.mult, r=[("exs", st, hh), "E"], w=[("Pts", st, hh)])

            def s2(it, st):
                bl, hp = it
                cr = 0
                bd, bo = st * 4 + 2, st * 4 + 3
                for hh in range(2):
                    for j in range(5):
                        if j < 4:
                            P.op("pe", "matmul", ps[bd][:, hh * 16:(hh + 1) * 16], lhsT=ones_bf[:, :],
                                 rhs=Pts[:, st, hh, j * 16:(j + 1) * 16], start=(j == 0), stop=False,
                                 r=[("Pts", st, hh), "ones_bf"], w=[("ps", bd)])
                        else:
                            P.op("pe", "matmul", ps[bd][:, hh * 16:(hh + 1) * 16], lhsT=ones_bf[0:16, :],
                                 rhs=Pts[0:16, st, hh, 64:80], start=False, stop=True,
                                 r=[("Pts", st, hh), "ones_bf"], w=[("ps", bd)])
                for hh in range(2):
                    hs = slice(hh * 64, (hh + 1) * 64)
                    fc = (2 * hp + hh) * 64
                    for j in range(5):
                        if j < 4:
                            P.op("pe", "matmul", ps[bo][hs, 0:16], lhsT=cvb[:, cr, j, fc:fc + 64],
                                 rhs=Pts[:, st, hh, j * 16:(j + 1) * 16], start=(j == 0), stop=False,
                                 r=[("Pts", st, hh), ("cvb", cr)], w=[("ps", bo)])
                        else:
                            P.op("pe", "matmul", ps[bo][hs, 0:16], lhsT=vts[0:16, bl, fc:fc + 64],
                                 rhs=Pts[0:16, st, hh, 64:80], start=False, stop=True,
                                 r=[("Pts", st, hh), "vts"], w=[("ps", bo)])
                P.op("dve", "reciprocal", out=rdens[:, st, :], in_=ps[bd][:, 0:32], r=[("ps", bd)],
                     w=[("rdens", st)])
                for hh in range(2):
                    hs = slice(hh * 64, (hh + 1) * 64)
                    P.op("dve", "tensor_tensor", out=oaT[hs, hp, bl * 16:(bl + 1) * 16], in0=ps[bo][hs, 0:16],
                         in1=rdens[hs, st, hh * 16:(hh + 1) * 16], op=ALU.mult, r=[("ps", bo), ("rdens", st)],
                         w=[("oaT", hp)])
        else:
            items = [(qi, hp) for qi in range(ntile) for hp in range(4)]

            def s1(it, st):
                qi, hp = it
                qt = t0 + qi
                bS, b4 = st * 4, st * 4 + 2
                for j in range(5):
                    sl = (qt - 4 + j) % 8
                    if j < 4:
                        out_ = psd[st * 2][:, j * 256:(j + 1) * 256]
                        wkey = ("ps", bS + j // 2)
                    else:
                        out_ = ps[b4][:, 0:256]
                        wkey = ("ps", b4)
                    P.op("pe", "matmul", out_, lhsT=kT[:, hp, sl * 128:(sl + 1) * 128],
                         rhs=qz[:, hp, qi, :, :].rearrange("p h q -> p (h q)"), start=True, stop=True,
                         r=[("kT", sl), ("qT", hp)], w=[wkey])
                pk = [("Pt", st)]
                rk2 = [("ps", bS), ("ps", bS + 1)]
                two = "p (h q) -> p h q"
                P.op("act", "activation", out=Pt[64:128, st, 0, :], in_=psd[st * 2][64:128, 0:256], func=AF.Exp,
                     r=[("ps", bS)], w=pk)
                P.op("act", "activation", out=Pt[0:64, st, 0, :].rearrange(two, h=2)[:, :, 0:64],
                     in_=psd[st * 2][0:64, 0:256].rearrange(two, h=2)[:, :, 0:64], func=AF.Exp, r=[("ps", bS)], w=pk)
                P.op("act", "activation", out=Pt[:, st, 1:4, :].rearrange("p j c -> p (j c)"),
                     in_=psd[st * 2][:, 256:1024], func=AF.Exp, r=rk2, w=pk)
                P.op("act", "activation", out=Pt[:, st, 4, :], in_=ps[b4][:, 0:256], func=AF.Exp,
                     r=[("ps", b4)], w=pk)
                for jj in (3, 4):
                    P.op("dve", "tensor_tensor", out=Pt[:, st, jj, :], in0=Pt[:, st, jj, :],
                         in1=E[:, jj - 3, 2 * hp:2 * hp + 2, :].rearrange("p h q -> p (h q)"), op=ALU.mult,
                         r=pk + ["E"], w=pk)

            def s2(it, st):
                qi, hp = it
                qt = t0 + qi
                bd, bo = st * 4 + 2, st * 4 + 3
                for j in range(5):
                    kt = qt - 4 + j
                    lw = vones[:, kt, :] if kt < 9 else ones_bf[:, :]
                    P.op("pe", "matmul", ps[bd][:, 256:512], lhsT=lw, rhs=Pt[:, st, j, :], start=(j == 0),
                         stop=(j == 4), r=[("Pt", st), "vones", "ones_bf"], w=[("ps", bd)])
                for hh in range(2):
                    hs = slice(hh * 64, (hh + 1) * 64)
                    fc = (2 * hp + hh) * 64
                    for j in range(5):
                        sl = (qt - 4 + j) % 8
                        P.op("pe", "matmul", ps[bo][hs, 0:128], lhsT=vt[:, sl, fc:fc + 64],
                             rhs=Pt[:, st, j, hh * 128:(hh + 1) * 128], start=(j == 0), stop=(j == 4),
                             r=[("Pt", st), ("vt", sl)], w=[("ps", bo)])
                P.op("dve", "tensor_scalar_max", rden[:, st, :], ps[bd][:, 256:512], 1e-30, r=[("ps", bd)],
                     w=[("rden", st)])
                P.op("dve", "reciprocal", out=rden[:, st, :], in_=rden[:, st, :], r=[("rden", st)],
                     w=[("rden", st)])
                for hh in range(2):
                    hs = slice(hh * 64, (hh + 1) * 64)
                    P.op("dve", "tensor_tensor", out=oaT[hs, hp, qi * 128:(qi + 1) * 128], in0=ps[bo][hs, 0:128],
                         in1=rden[hs, st, hh * 128:(hh + 1) * 128], op=ALU.mult, r=[("ps", bo), ("rden", st)],
                         w=[("oaT", hp)])

        if sample:
            groups_it = [[(bl, hp) for hp in range(4)] for bl in range(4)]
        else:
            groups_it = [items]
        for its in groups_it:
            if sample:
                bl = its[0][0]
                P.op("pool", "dma_start", out=ckb[:, 0, :, :], in_=ckT_d[l, :, :, bl, :], w=[("ckb", 0)],
                     semkey=("ckb", 0))
                P.op("pool", "dma_start", out=cvb[:, 0, :, :], in_=cv_d[l, bl], w=[("cvb", 0)],
                     semkey=("cvb", 0))
            for i in range(len(its) + 1):
                if i < len(its):
                    s1(its[i], i % 2)
                if i >= 1:
                    s2(its[i - 1], (i - 1) % 2)

        ck(5)
        if sample:
            b = nb()
            for g in range(4):
                P.op("pe", "matmul", ps[b][:, g * 64:(g + 1) * 64], lhsT=vbns[0:64, g * 128:(g + 1) * 128],
                     rhs=Wblk[0:64, g, :], start=True, stop=False, r=["vbns", "Wblk"], w=bk(b))
                P.op("pe", "matmul", ps[b][:, g * 64:(g + 1) * 64], lhsT=ones_row[0:1, :],
                     rhs=bsps[0:1, g * 64:(g + 1) * 64], start=False, stop=True, r=["ones_row", "bsps"], w=bk(b))
            P.op("dve", "tensor_tensor", out=obT[:, :, 0:64], in0=ps[b][:, 0:256].rearrange("p (g t) -> p g t", g=4),
                 in1=ubT[:, :, 0:64], op=ALU.mult, r=bk(b) + [("ubT", c) for c in range(4)],
                 w=[("obT", c) for c in range(4)])
        else:
            for ti in range(ntile):
                b = nb()
                for g in range(4):
                    P.op("pe", "matmul", ps[b][:, g * 128:(g + 1) * 128], lhsT=vbn[:, ti, g * 128:(g + 1) * 128],
                         rhs=WcT[:, g, :], start=True, stop=False, r=[("vbn", ti), "WcT"], w=bk(b))
                    P.op("pe", "matmul", ps[b][:, g * 128:(g + 1) * 128], lhsT=ones_row[0:1, :],
                         rhs=bsp[0:1, g * 128:(g + 1) * 128], start=False, stop=True, r=["ones_row", "bsp"], w=bk(b))
                P.op("dve", "tensor_tensor", out=obT[:, :, ti * 128:(ti + 1) * 128],
                     in0=ps[b][:, 0:512].rearrange("p (g t) -> p g t", g=4), in1=ubT[:, :, ti * 128:(ti + 1) * 128],
                     op=ALU.mult, r=bk(b) + [("ubT", c) for c in range(4)], w=[("obT", c) for c in range(4)])

        ck(6)
        sa_, sb_ = ws_get(), ws_get()
        for c in range(8):
            ba, bb_ = nb(), nb()
            proj_fm(ba, sa_, 4, 1024, c * 128, lambda kc: oaT[:, kc, 0:T], lambda kc: [("oaT", kc)], T)
            proj_fm(bb_, sb_, 4, 1024, c * 128, lambda kc: obT[:, kc, 0:T], lambda kc: [("obT", kc)], T)
            P.op("dve", "tensor_tensor", out=tmpn[:, 0, 0:T], in0=ps[ba][:, 0:T], in1=SA[:, c, 0:T], op=ALU.mult,
                 r=bk(ba) + [("SA", c)], w=[("tmpn", 0)])
            P.op("dve", "tensor_tensor", out=tmpn[:, 1, 0:T], in0=ps[bb_][:, 0:T], in1=SA[:, 8 + c, 0:T], op=ALU.mult,
                 r=bk(bb_) + [("SA", 8 + c)], w=[("tmpn", 1)])
            P.op("dve", "tensor_tensor", out=SA[:, 16 + c, 0:T], in0=tmpn[:, 0, 0:T], in1=tmpn[:, 1, 0:T], op=ALU.add,
                 r=[("tmpn", 0), ("tmpn", 1)], w=[("SA", 16 + c)])
        ws_done()
        ws_done()
        ck(7)
        for s in range(2):
            slot = ws_get()
            for m in range(4):
                c = s * 4 + m
                b = nb()
                proj_fm(b, slot, 8, 512, m * 128, lambda kc: SA[:, 16 + kc, 0:T], lambda kc: [("SA", 16 + kc)], T)
                for (c0, n, bb) in groups:
                    P.op("dve", "scalar_tensor_tensor", out=cur["xT"][:, c, c0:c0 + n], in0=ps[b][:, c0:c0 + n],
                         scalar=modTt[:, cur["l"], 16 + c, bb:bb + 1], in1=cur["xT"][:, c, c0:c0 + n], op0=ALU.mult, op1=ALU.add,
                         r=bk(b) + [("xT", cur["par"], c), ("modT", cur["l"])], w=[("xT", cur["par"], c)])
            ws_done()

        ck(8)
        norm_mod(1, T, groups)
        GW = 72 if sample else T + 2
        nxt = bi + 1 if pro_ok(bi + 1) else None
        if nxt is not None:
            l2, T2, g2 = blk_geom(nxt)
            bss = nb()
            reserved.add(bss)
            sqr = {}
        for jj in range(11):
            if nxt is not None:
                with _NextCtx(nxt):
                    if 1 <= jj <= 8:
                        sqr[jj - 1] = norm_sq(bss, jj - 1, T2)
                    if 2 <= jj <= 9:
                        norm_ss(bss, jj - 2, sqr[jj - 2], T2)
            slot = ws_get()
            for sub in range(2):
                j = 2 * jj + sub
                bg, bu = nb(), nb()
                proj_fm(bg, slot, 8, 512, sub * 128, hr, hk, T)
                proj_fm(bu, slot, 8, 512, 256 + sub * 128, hr, hk, T)
                r_ = nrot("gb", 3)
                if sample:
                    g3 = gb[:, r_, 0:72].rearrange("p (b t) -> p b t", t=18)
                    P.op("act", "activation", out=g3[:, :, 2:18],
                         in_=ps[bg][:, 0:64].rearrange("p (b t) -> p b t", t=16), func=AF.Copy, r=bk(bg),
                         w=[("gb", r_)])
                    P.op("dve", "tensor_copy", g3[:, :, 0:2], halos[:, j, :, :], r=["halos"], w=[("gb", r_)])
                    P.op("pool", "tensor_copy", ncs_st[:, j, :, :], g3[:, :, 16:18], r=[("gb", r_)], w=["ncs_st"])
                    views = [g3[:, :, k:k + 16] for k in range(3)]
                    r2 = nrot("gc", 2)
                    gcv = gc[:, r2, 0:64].rearrange("p (b t) -> p b t", t=16)
                else:
                    P.op("act", "activation", out=gb[:, r_, 2:2 + T], in_=ps[bg][:, 0:T], func=AF.Copy, r=bk(bg),
                         w=[("gb", r_)])
                    P.op("dve", "tensor_copy", gb[:, r_, 0:2], halo[:, j, :], r=["halo"], w=[("gb", r_)])
                    if t0 == 8:
                        P.op("pool", "tensor_scalar_mul", gb[:, r_, 128:130], gb[:, r_, 128:130], flag[:, 0:1],
                             r=[("gb", r_), "flag"], w=[("gb", r_)])
                    P.op("pool", "tensor_copy", halo[:, j, :], gb[:, r_, T:T + 2], r=[("gb", r_)], w=["halo"])
                    views = [gb[:, r_, k:k + T] for k in range(3)]
                    r2 = nrot("gc", 2)
                    gcv = gc[:, r2, 0:T]
                P.op("act", "activation", out=gcv, in_=views[0], func=AF.Identity, scale=cw[:, j, 0:1],
                     bias=cb[:, j:j + 1], r=[("gb", r_), "cw", "cb"], w=[("gc", r2)])
                for k in (1, 2):
                    P.op("dve", "scalar_tensor_tensor", out=gcv, in0=views[k], scalar=cw[:, j, k:k + 1], in1=gcv,
                         op0=ALU.mult, op1=ALU.add, r=[("gb", r_), ("gc", r2), "cw"], w=[("gc", r2)])
                r3 = nrot("ge", 2)
                P.op("act", "activation", out=ge[:, r3, 0:T], in_=gc[:, r2, 0:T], func=AF.Gelu_apprx_tanh,
                     r=[("gc", r2)], w=[("ge", r3)])
                P.op("dve", "tensor_tensor", out=SA[:, j, 0:T], in0=ps[bu][:, 0:T], in1=ge[:, r3, 0:T], op=ALU.mult,
                     r=bk(bu) + [("ge", r3)], w=[("SA", j)])
            ws_done()
        if nxt is not None:
            with _NextCtx(nxt):
                norm_fin(bss, T2)
                norm_apply(0, T2, g2)
            reserved.discard(bss)
            pro_done.add(nxt)
        for c in range(8):
            slot = ws_get()
            b = nb()
            proj_fm(b, slot, 22, 128, 0, lambda kc: SA[:, kc, 0:T], lambda kc: [("SA", kc)], T)
            for (c0, n, bb) in groups:
                P.op("dve", "scalar_tensor_tensor", out=cur["xT"][:, c, c0:c0 + n], in0=ps[b][:, c0:c0 + n],
                     scalar=modTt[:, cur["l"], 40 + c, bb:bb + 1], in1=cur["xT"][:, c, c0:c0 + n], op0=ALU.mult, op1=ALU.add,
                     r=bk(b) + [("xT", cur["par"], c), ("modT", cur["l"])], w=[("xT", cur["par"], c)])
            ws_done()

        ck(9)
        xk = [("xT", cur["par"], c) for c in range(8)]
        if sample:
            if l == 0:
                P.op("sp", "dma_start", out=xs1, in_=cur["xT"][:, :, 0:64], r=xk, w=["xs1"], semkey="xst")
            else:
                P.op("sp", "dma_start", out=ysT_d, in_=cur["xT"][:, :, 0:64], r=xk, semkey="xst")
            P.op("sp", "dma_start", out=ncs_d[l], in_=ncs_st[:, :, :, :], r=["ncs_st"], semkey="ncs")
        elif l == 0:
            P.op("sp", "dma_start", out=x1s[:, :, (t0 - 4) * 128:(t0 - 4) * 128 + T], in_=cur["xT"][:, :, 0:T], r=xk,
                 w=[("x1s", t) for t in tiles], semkey="xst")
        else:
            lo = max(t0, OUT_T0)
            c0 = (lo - t0) * 128
            P.op("sp", "dma_start", out=yT_d[:, :, (lo - OUT_T0) * 128:(lo - OUT_T0) * 128 + T - c0],
                 in_=cur["xT"][:, :, c0:T], r=xk, semkey="xst")


    BLOCKS = []
    for l in range(2):
        BLOCKS.append((l, "kv", 0 if l == 0 else 4, 4))
        for (t0, n) in (L0_BLOCKS if l == 0 else L1_BLOCKS):
            BLOCKS.append((l, "full", t0, n))
        BLOCKS.append((l, "sample", 0, 0))

    try:
        for l in range(2):
            if stop is not None and stop == -1:
                raise _Stop()
            if l == 0:
                ada_setup(0)
            layer_setup(l)
            if stop is not None and stop == 0:
                raise _Stop()
            if l == 1:
                P.op("pool", "memset", halo[:, :, :], 0.0, r=["halo"], w=["halo"])
            for (l_, kind, t0, n) in [b for b in BLOCKS if b[0] == l]:
                if kind == "sample":
                    P.op("sp", "dma_start", out=ncv_d[l], in_=halo[:, :, :], r=["halo"], semkey="ncv")
                    if l == 0:
                        ada_setup(1)
                block(l_, kind, t0, n)
                nblk[0] += 1
                if stop is not None and nblk[0] >= stop:
                    raise _Stop()
        assert ws["consumed"] == len(ws["seq"]), (ws["consumed"], len(ws["seq"]))
    except _Stop:
        dbg = dout("dbgx", (128, 8, 512))
        P.op("sp", "dma_start", out=dbg, in_=cur["xT"][:, :, :], r=[("xT", cur["par"], c) for c in range(8)], semkey="dbg")
        dbg2 = dout("dbgh", (128, 8, 512))
        P.op("pool", "dma_start", out=dbg2, in_=tmpn[:, :, :].rearrange("p a b -> p (a b)"), r=[("tmpn", 0), ("tmpn", 1)], semkey="dbg") if False else None

    P.emit(nc, stack)
    stack.close()
    return nc


def _fm(a):
    t, f = a.shape
    return np.ascontiguousarray(a.reshape(t, f // 128, 128).transpose(2, 1, 0))


def _slab(wm):
    k, m = wm.shape
    return np.ascontiguousarray(wm.reshape(k // 128, 128, m).transpose(1, 0, 2)).reshape(128, (k // 128) * m)


_NC_CACHE = {}


def kernel(x_prompt, x_sample, cache_attn_k, cache_attn_v, cache_ffn_conv, c_prompt, c_sample,
           norm1_g, norm2_g, w_ada, b_ada, w_in, q_norm_g, k_norm_g, rel_bias, v_norm_g,
           w_spatial, b_spatial, w_out_a, w_out_b, w_out, w_ffn_in, ffn_conv_w, ffn_conv_b, w_ffn_out):
    f = lambda a: np.asarray(a, dtype=np.float32)
    x_prompt, x_sample, cache_attn_k, cache_attn_v, cache_ffn_conv = map(f, (x_prompt, x_sample, cache_attn_k, cache_attn_v, cache_ffn_conv))
    c_prompt, c_sample, norm1_g, norm2_g, w_ada, b_ada, w_in = map(f, (c_prompt, c_sample, norm1_g, norm2_g, w_ada, b_ada, w_in))
    q_norm_g, k_norm_g, rel_bias, v_norm_g, w_spatial, b_spatial = map(f, (q_norm_g, k_norm_g, rel_bias, v_norm_g, w_spatial, b_spatial))
    w_out_a, w_out_b, w_out, w_ffn_in, ffn_conv_w, ffn_conv_b, w_ffn_out = map(f, (w_out_a, w_out_b, w_out, w_ffn_in, ffn_conv_w, ffn_conv_b, w_ffn_out))

    in_maps = _prep(x_prompt, x_sample, cache_attn_k, cache_attn_v, cache_ffn_conv, c_prompt, c_sample,
                    norm1_g, norm2_g, w_ada, b_ada, w_in, q_norm_g, k_norm_g, rel_bias, v_norm_g,
                    w_spatial, b_spatial, w_out_a, w_out_b, w_out, w_ffn_in, ffn_conv_w, ffn_conv_b, w_ffn_out)
    if "nc" not in _NC_CACHE:
        _NC_CACHE["nc"] = build_nc()
    nc = _NC_CACHE["nc"]
    res = run_bass_kernel_spmd(nc, in_maps, core_ids=list(range(NCORES)))
    return _post(res.results)


def _prep(x_prompt, x_sample, cache_attn_k, cache_attn_v, cache_ffn_conv, c_prompt, c_sample,
          norm1_g, norm2_g, w_ada, b_ada, w_in, q_norm_g, k_norm_g, rel_bias, v_norm_g,
          w_spatial, b_spatial, w_out_a, w_out_b, w_out, w_ffn_in, ffn_conv_w, ffn_conv_b, w_ffn_out):

    wall = np.empty((2, 24, 128, 4096), np.float32)
    wfo = np.empty((2, 8, 128, 2816), np.float32)
    wada = np.empty((2, 12, 128, 4096), np.float32)
    for l in range(2):
        for s in range(9):
            wall[l, s] = _slab(w_in[l][:, s * 512:(s + 1) * 512])
        wall[l, 9] = _slab(w_out_a[l])
        wall[l, 10] = _slab(w_out_b[l])
        for s in range(2):
            wall[l, 11 + s] = _slab(w_out[l][:, s * 512:(s + 1) * 512])
        for jj in range(11):
            idx = np.concatenate([np.arange(2 * jj * 128, (2 * jj + 2) * 128), DFF + np.arange(2 * jj * 128, (2 * jj + 2) * 128)])
            wall[l, 13 + jj] = _slab(w_ffn_in[l][:, idx])
        for c in range(8):
            wfo[l, c] = _slab(w_ffn_out[l][:, c * 128:(c + 1) * 128])
        for s in range(12):
            wada[l, s] = _slab(w_ada[l][:, s * 512:(s + 1) * 512])
    col = lambda v: np.ascontiguousarray(v.reshape(-1, 128).T)
    nrm = np.stack([np.concatenate([col(norm1_g[l]), col(norm2_g[l])], 1) for l in range(2)])
    bada = np.stack([col(b_ada[l]) for l in range(2)])
    qkg = np.stack([np.stack([np.tile(q_norm_g[l], 2), np.tile(k_norm_g[l], 2)], 1) for l in range(2)])
    vg = np.stack([np.broadcast_to(v_norm_g[l][None, :], (128, 512)) for l in range(2)]).copy()
    bsp = b_spatial.reshape(2, 1, 512).copy()
    bsps = np.stack([np.tile(b_spatial[l][:, None, :16], (1, 4, 1)).reshape(1, 256) for l in range(2)])
    wspT = np.ascontiguousarray(w_spatial.transpose(0, 3, 1, 2))
    wspb = np.zeros((2, 64, 4, 64), np.float32)
    for bl in range(4):
        wspb[:, bl * 16:(bl + 1) * 16, :, bl * 16:(bl + 1) * 16] = wspT[:, 0:16, :, 0:16]
    s_i = np.arange(128)[:, None]
    t_i = np.arange(128)[None, :]
    tril = np.broadcast_to((s_i <= t_i).astype(np.float32)[:, None, :], (128, 4, 128)).copy()
    trilb = np.zeros((64, 4, 64), np.float32)
    for bl in range(4):
        trilb[bl * 16:(bl + 1) * 16, :, bl * 16:(bl + 1) * 16] = tril[0:16, :, 0:16]
    cw = np.ascontiguousarray(ffn_conv_w.reshape(2, 3, 22, 128).transpose(0, 3, 2, 1))
    cb = np.ascontiguousarray(ffn_conv_b.reshape(2, 22, 128).transpose(0, 2, 1))
    ki = np.arange(128)[:, None]
    qi = np.arange(128)[None, :]
    idx1 = np.clip(128 + qi - ki, -128, 128) + 128
    idx0 = np.clip(qi - ki, -128, 128) + 128
    Tb = np.stack([np.stack([rel_bias[l][:, idx1].transpose(1, 0, 2), rel_bias[l][:, idx0].transpose(1, 0, 2)], 1)
                   for l in range(2)])
    b256 = np.stack([np.broadcast_to(rel_bias[l][None, :, 256], (128, 8)) for l in range(2)]).copy()

    shared = dict(wall=wall, wfo=wfo, wada=wada, nrm=nrm, bada=bada, qkg=qkg, vg=vg, bsp=bsp, bsps=bsps, wspT=wspT,
                  wspb=wspb, tril=tril, trilb=trilb, cw=cw, cb=cb, Tb=np.ascontiguousarray(Tb), b256=b256)
    shared = {k: np.ascontiguousarray(v, dtype=np.float32) for k, v in shared.items()}

    in_maps = []
    for core in range(NCORES):
        b, seg = core // 4, core % 4
        s0 = seg * SEG
        a = s0 - HALO
        xw = np.zeros((W, D), np.float32)
        lo = max(a, 0)
        xw[lo - a:] = x_prompt[b, lo:s0 + SEG]
        valid = (np.arange(W) + a >= 0).astype(np.float32)
        m = dict(shared)
        m["xin"] = _fm(xw)
        m["xs"] = _fm(x_sample[4 * core:4 * core + 4].reshape(64, D))
        m["vtok"] = np.ascontiguousarray(valid.reshape(NT, 128).T)
        m["flag"] = np.full((128, 1), 1.0 if a >= 0 else 0.0, np.float32)
        cc = np.concatenate([c_prompt[b:b + 1], c_sample[4 * core:4 * core + 4]], 0)
        m["cT"] = np.ascontiguousarray(cc.reshape(5, 8, 128).transpose(2, 1, 0))
        ck = cache_attn_k[:, 4 * core:4 * core + 4]
        m["ckT"] = np.ascontiguousarray(ck.reshape(2, 4, 512, 4, 2, 64).transpose(0, 4, 5, 3, 1, 2)).reshape(2, 128, 4, 4, 512)
        cvv = cache_attn_v[:, 4 * core:4 * core + 4].reshape(2, 4, 4, 128, 512)
        m["cv"] = np.ascontiguousarray(cvv.transpose(0, 1, 3, 2, 4))
        cc2 = cache_ffn_conv[:, 4 * core:4 * core + 4].reshape(2, 4, 2, 22, 128)
        m["cconv"] = np.ascontiguousarray(cc2.transpose(0, 4, 3, 1, 2))
        in_maps.append(m)
    return in_maps


def _post(R):

    y_prompt = np.empty((2, 8192, D), np.float32)
    y_sample = np.empty((32, 16, D), np.float32)
    nkp = np.empty((2, 2, 512, 8, 64), np.float32)
    nvp = np.empty((2, 2, 512, 8, 64), np.float32)
    ncp = np.empty((2, 2, 2, DFF), np.float32)
    nks = np.empty((2, 32, 16, 8, 64), np.float32)
    nvs = np.empty((2, 32, 16, 8, 64), np.float32)
    nbs = np.empty((2, 32, 16, 4, 128), np.float32)
    ncs = np.empty((2, 32, 2, DFF), np.float32)
    for core in range(NCORES):
        r = R[core]
        b, seg = core // 4, core % 4
        y_prompt[b, seg * SEG:(seg + 1) * SEG] = r["yT"].transpose(2, 1, 0).reshape(SEG, D)
        y_sample[4 * core:4 * core + 4] = r["ysT"].transpose(2, 1, 0).reshape(4, 16, D)
        sl = slice(4 * core, 4 * core + 4)
        for l in range(2):
            if seg == 3:
                nkp[l, b] = r["nkT"][l].reshape(2, 64, 4, 512).transpose(3, 2, 0, 1).reshape(512, 8, 64)
                nvp[l, b] = r["nv"][l].reshape(512, 8, 64)
                ncp[l, b] = r["ncv"][l].transpose(2, 1, 0).reshape(2, DFF)
            nks[l, sl] = r["nksT"][l].reshape(2, 64, 4, 4, 16).transpose(3, 4, 2, 0, 1).reshape(4, 16, 8, 64)
            nvs[l, sl] = r["nvs"][l].transpose(1, 0, 2).reshape(4, 16, 8, 64)
            nbs[l, sl] = r["nbs"][l].reshape(4, 16, 4, 128)
            ncs[l, sl] = r["ncs"][l].transpose(2, 3, 1, 0).reshape(4, 2, DFF)
    return (y_prompt, y_sample, nkp, nvp, ncp, nks, nvs, nbs, ncs)
```

```python
import contextlib
import numpy as np
import concourse.bass as bass
import concourse.mybir as mybir
from concourse.bass_utils import run_bass_kernel_spmd

F32 = mybir.dt.float32
BF16 = mybir.dt.bfloat16
AF = mybir.ActivationFunctionType
ALU = mybir.AluOpType
AX = mybir.AxisListType

NCORES = 8
D = 1024
SEG = 2048
HALO = 1152
W = SEG + HALO
NT = W // 128
DFF = 2816
EPS = 1e-6
NB = 4
L0_BLOCKS = [(4, 4), (8, 4), (12, 4), (16, 3), (19, 3), (22, 3)]
L1_BLOCKS = [(8, 4), (12, 4), (16, 3), (19, 3), (22, 3)]
OUT_T0 = 9
KEEP_T0 = 21
_DBG = {}


class Prog:
    STREAMS = ("sp", "act", "dve", "pool", "pe")

    def __init__(self):
        self.ops = []
        self.keyw = {}
        self.keyr = {}
        self.dcnt = {}

    def op(self, stream, method, *args, r=(), w=(), semkey=None, **kw):
        idx = len(self.ops)
        dom = ("d", semkey) if semkey is not None else ("s", stream)
        deps = {}

        def add(d, i):
            if d[0] == "d":
                i = self.dcnt[d]
            if deps.get(d, -1) < i:
                deps[d] = i

        for k in r:
            for d, i in self.keyw.get(k, {}).items():
                add(d, i)
        skip_same = dom[0] == "d" or stream == "pe"
        for k in w:
            for d, i in self.keyr.get(k, {}).items():
                if d == dom and (skip_same or i == idx):
                    continue
                add(d, i)
            for d, i in self.keyw.get(k, {}).items():
                if d == dom and skip_same:
                    continue
                add(d, i)
        for k in r:
            self.keyr.setdefault(k, {})[dom] = idx
        for k in w:
            if self.keyr.get(k):
                self.keyw[k] = {dom: idx}
                self.keyr[k] = {}
            else:
                self.keyw.setdefault(k, {})[dom] = idx
        if dom[0] == "d":
            self.dcnt[dom] = self.dcnt.get(dom, 0) + 16
        self.ops.append(dict(stream=stream, method=method, args=args, kw=kw, dom=dom,
                             deps=list(deps.items()), stage=_DBG.get("stage", "")))
        return idx

    def emit(self, nc, stack):
        ops = self.ops
        needs = set()
        for o in ops:
            for d, i in o["deps"]:
                if d[0] == "s":
                    needs.add(i)
        cnt = {}
        sems = {}
        issuer = {}
        for i, o in enumerate(ops):
            d = o["dom"]
            if d[0] == "d":
                cnt[d] = cnt.get(d, 0) + 16
                o["done"] = cnt[d]
                assert issuer.setdefault(d, o["stream"]) == o["stream"], d
            elif i in needs:
                cnt[d] = cnt.get(d, 0) + 1
                o["done"] = cnt[d]
            else:
                o["done"] = None
        for d in cnt:
            sems[d] = stack.enter_context(nc.semaphore("s%d" % len(sems)))
        block = stack.enter_context(nc.Block())
        self.nsem = len(sems)

        def run(stream, eng):
            waited = {}
            for o in ops:
                if o["stream"] != stream:
                    continue
                for d, i in o["deps"]:
                    v = i if d[0] == "d" else ops[i]["done"]
                    if waited.get(d, 0) < v:
                        eng.wait_ge(sems[d], v)
                        waited[d] = v
                        _DBG.setdefault("waits", {}).setdefault(stream, []).append((o["stage"], d, (ops[i]["method"], ops[i]["stage"], str(ops[i]["kw"].get("out", ops[i]["args"][:1]))[:90]) if d[0] == "s" else None, o["method"]))
                ins = getattr(eng, o["method"])(*o["args"], **o["kw"])
                if o["done"] is not None:
                    d = o["dom"]
                    ins.then_inc(sems[d], 16 if d[0] == "d" else 1)
            for d, v in cnt.items():
                if d[0] == "d" and issuer[d] == stream and waited.get(d, 0) < v:
                    eng.wait_ge(sems[d], v)

        @block.sync
        def _(e):
            run("sp", e)

        @block.scalar
        def _(e):
            run("act", e)

        @block.vector
        def _(e):
            run("dve", e)

        @block.gpsimd
        def _(e):
            run("pool", e)

        @block.tensor
        def _(e):
            run("pe", e)


def build_nc(stop=None):
    class _Stop(Exception):
        pass

    nblk = [0]

    def ck(st):
        _DBG["stage"] = st + (10 if _DBG.get("stage", 0) >= 10 else 0)
        if stop is not None and nblk[0] + st / 10.0 >= stop - 1e-9 and stop > 0:
            raise _Stop()

    nc = bass.Bass("TRN2", target_bir_lowering=False)
    P = Prog()
    stack = contextlib.ExitStack()

    def din(name, shape):
        return nc.dram_tensor(name, list(shape), F32, kind="ExternalInput").ap()

    def dout(name, shape):
        return nc.dram_tensor(name, list(shape), F32, kind="ExternalOutput").ap()

    def dint(name, shape, dt):
        return nc.dram_tensor(name, list(shape), dt, kind="Internal").ap()

    xin = din("xin", (128, 8, W))
    xs_d = din("xs", (128, 8, 64))
    vtok_d = din("vtok", (128, NT))
    flag_d = din("flag", (128, 1))
    cT_d = din("cT", (128, 8, 5))
    ckT_d = din("ckT", (2, 128, 4, 4, 512))
    cv_d = din("cv", (2, 4, 128, 4, 512))
    cconv_d = din("cconv", (2, 128, 22, 4, 2))
    wall_d = din("wall", (2, 24, 128, 4096))
    wfo_d = din("wfo", (2, 8, 128, 2816))
    wada_d = din("wada", (2, 12, 128, 4096))
    nrm_d = din("nrm", (2, 128, 16))
    bada_d = din("bada", (2, 128, 48))
    qkg_d = din("qkg", (2, 128, 2))
    vg_d = din("vg", (2, 128, 512))
    bsp_d = din("bsp", (2, 1, 512))
    bsps_d = din("bsps", (2, 1, 256))
    wspT_d = din("wspT", (2, 128, 4, 128))
    wspb_d = din("wspb", (2, 64, 4, 64))
    tril_d = din("tril", (128, 4, 128))
    trilb_d = din("trilb", (64, 4, 64))
    cw_d = din("cw", (2, 128, 22, 3))
    cb_d = din("cb", (2, 128, 22))
    Tb_d = din("Tb", (2, 128, 2, 8, 128))
    b256_d = din("b256", (2, 128, 8))

    yT_d = dout("yT", (128, 8, SEG))
    ysT_d = dout("ysT", (128, 8, 64))
    nkT_d = dout("nkT", (2, 128, 4, 512))
    nv_d = dout("nv", (2, 4, 128, 512))
    ncv_d = dout("ncv", (2, 128, 22, 2))
    nksT_d = dout("nksT", (2, 128, 4, 64))
    nvs_d = dout("nvs", (2, 16, 4, 512))
    nbs_d = dout("nbs", (2, 64, 512))
    ncs_d = dout("ncs", (2, 128, 22, 4, 2))

    wsc = dint("wsc", (2, 24, 128, 4096), BF16)
    wsc_fo = dint("wscfo", (2, 8, 128, 2816), BF16)
    x1s = dint("x1s", (128, 8, 21 * 128), F32)
    xs1 = dint("xs1", (128, 8, 64), F32)

    def sb(name, shape, dt=F32):
        return stack.enter_context(nc.sbuf_tensor("sb_" + name, list(shape), dt))

    xTt = sb("xT", (128, 2, 8, 512))
    cur = {"xT": xTt[:, 0], "par": 0, "idx": 0}
    hT = sb("hT", (128, 8, 512), BF16)
    sq = sb("sq", (128, 3, 512), BF16)
    rstd = sb("rstd", (128, 512))
    tmpn = sb("tmpn", (128, 2, 512))
    qz = sb("qz", (128, 4, 4, 2, 128), BF16)
    kst = sb("kst", (128, 2, 512))
    rs = sb("rs", (128, 2, 512))
    kT = sb("kT", (128, 4, 1024), BF16)
    vt = sb("vt", (128, 8, 512), BF16)
    ubT = sb("ubT", (128, 4, 512), BF16)
    vbn = sb("vbn", (128, 4, 512), BF16)
    gl = sb("gl", (128, 2, 512))
    ssv = sb("ssv", (128, 4))
    SA = sb("SA", (128, 24, 512), BF16)
    oaT = sb("oaT", (128, 4, 512), BF16)
    obT = sb("obT", (128, 4, 512), BF16)
    Pt = sb("Pt", (128, 2, 5, 256), BF16)
    rden = sb("rden", (128, 2, 256))
    gb = sb("gb", (128, 3, 516))
    gc = sb("gc", (128, 2, 512))
    ge = sb("ge", (128, 2, 512))
    halo = sb("halo", (128, 22, 2))
    halos = sb("halos", (128, 22, 4, 2))
    ncs_st = sb("ncs_st", (128, 22, 4, 2))
    wslab = sb("wslab", (128, NB, 4096), BF16)
    ones_bf = sb("ones_bf", (128, 128), BF16)
    blk64 = sb("blk64", (128, 128), BF16)
    ones_row = sb("ones_row", (1, 128), BF16)
    vones = sb("vones", (128, 9, 128), BF16)
    vtok = sb("vtok", (128, NT))
    flag = sb("flag", (128, 1))
    cTs = sb("cTs", (128, 8, 5))
    cs = sb("cs", (128, 8, 5), BF16)
    E = sb("E", (128, 2, 8, 128), BF16)
    negb = sb("negb", (128, 8))
    modTt = sb("modT", (128, 2, 48, 5))
    A12t = sb("A12", (128, 2, 2, 8, 5))
    nrmt = sb("nrm", (128, 2, 16))
    badat = sb("bada", (128, 2, 48))
    qkg = sb("qkg", (128, 2))
    vg = sb("vg", (128, 512))
    bsp = sb("bsp", (1, 512), BF16)
    bsps = sb("bsps", (1, 256), BF16)
    WcT = sb("WcT", (128, 4, 128), BF16)
    Wblk = sb("Wblk", (64, 4, 64), BF16)
    cw = sb("cw", (128, 22, 3))
    cb = sb("cb", (128, 22))
    kTs = sb("kTs", (128, 4, 64), BF16)
    vbns = sb("vbns", (64, 512), BF16)
    ckb = sb("ckb", (128, 1, 4, 512), BF16)
    cvb = sb("cvb", (128, 1, 4, 512), BF16)
    Pts = sb("Pts", (128, 2, 2, 80), BF16)
    exs = sb("exs", (128, 2, 2, 32))
    rdens = sb("rdens", (128, 2, 32))

    psd = [stack.enter_context(nc.psum_tensor("ps%d" % i, [128, 1024], F32)) for i in range(4)]
    ps = [psd[i // 2][:, (i % 2) * 512:(i % 2 + 1) * 512] for i in range(8)]

    def bk(i):
        return [("ps", i)]

    bank_ctr = [0]

    reserved = set()

    def nb():
        while True:
            b = bank_ctr[0] % 8
            bank_ctr[0] += 1
            if b not in reserved:
                return b

    rot = {}

    def nrot(name, n):
        v = rot.get(name, 0)
        rot[name] = v + 1
        return v % n

    ws = dict(seq=[], issued=0, consumed=0)

    casted = set()

    def ws_issue(upto):
        while ws["issued"] < min(upto, len(ws["seq"])):
            kind_, l_, s_ = ws["seq"][ws["issued"]]
            slot = ws["issued"] % NB
            wk, sk = [("wslab", slot)], ("w", slot)
            if kind_ == "ada":
                P.op("pool", "dma_start", out=wslab[:, slot, :], in_=wada_d[l_, s_], w=wk, semkey=sk)
            else:
                ncols = 4096 if kind_ == "wall" else 2816
                src32 = wall_d[l_, s_] if kind_ == "wall" else wfo_d[l_, s_]
                scr = wsc[l_, s_] if kind_ == "wall" else wsc_fo[l_, s_]
                key = ("wsc", kind_, l_, s_)
                if key not in casted:
                    casted.add(key)
                    P.op("pool", "dma_start", out=wslab[:, slot, 0:ncols], in_=src32, w=wk, semkey=sk)
                    P.op("sp", "dma_start", out=scr, in_=wslab[:, slot, 0:ncols], r=wk, w=[key],
                         semkey=("wst", slot))
                else:
                    P.op("pool", "dma_start", out=wslab[:, slot, 0:ncols], in_=scr, r=[key], w=wk, semkey=sk)
            ws["issued"] += 1

    def ws_get():
        assert ws["consumed"] < len(ws["seq"])
        ws_issue(ws["consumed"] + 1)
        slot = ws["consumed"] % NB
        ws["consumed"] += 1
        return slot

    def ws_done():
        ws_issue(ws["consumed"] + NB)

    def seq_ada(l):
        for s_ in range(12):
            ws["seq"].append(("ada", l, s_))

    def seq_block(l, kind):
        if kind == "kv":
            for s_ in (1, 2):
                ws["seq"].append(("wall", l, s_))
            return
        for s_ in range(24):
            ws["seq"].append(("wall", l, s_))
        for s_ in range(8):
            ws["seq"].append(("fo", l, s_))

    seq_ada(0)
    for l in range(2):
        seq_block(l, "kv")
        for _ in (L0_BLOCKS if l == 0 else L1_BLOCKS):
            seq_block(l, "full")
        if l == 0:
            seq_ada(1)
        seq_block(l, "sample")

    P.op("pool", "memset", ones_bf[:, :], 1.0, w=["ones_bf"])
    P.op("pool", "memset", blk64[:, :], 0.0, w=["blk64"])
    P.op("pool", "memset", blk64[0:64, 0:64], 1.0, w=["blk64"])
    P.op("pool", "memset", blk64[64:128, 64:128], 1.0, w=["blk64"])
    P.op("pool", "memset", ones_row[:, :], 1.0, w=["ones_row"])
    P.op("pool", "memset", Pt[:, :, :, :].rearrange("p a b c -> p (a b c)"), 0.0, w=[("Pt", 0), ("Pt", 1)])
    P.op("pool", "memset", qz[:, :, :, :, :].rearrange("p a b c d -> p (a b c d)"), 0.0, w=[("qT", c) for c in range(4)])
    P.op("pool", "memset", halo[:, :, :], 0.0, w=["halo"])
    P.op("sp", "dma_start", out=vtok[:, :], in_=vtok_d, w=["vtok"], semkey="c0")
    P.op("sp", "dma_start", out=flag[:, :], in_=flag_d, w=["flag"], semkey="c0")
    P.op("sp", "dma_start", out=cTs[:, :, :], in_=cT_d, w=["cTs"], semkey="c0")
    P.op("act", "activation", out=cs[:, :, :], in_=cTs[:, :, :], func=AF.Silu, r=["cTs"], w=["cs"])
    for t in range(9):
        P.op("act", "activation", out=vones[:, t, :], in_=ones_bf[:, :], func=AF.Copy,
             scale=vtok[:, t:t + 1], r=["ones_bf", "vtok"], w=["vones"])
    def layer_setup(l):
        for dst, src, key in ((qkg, qkg_d[l], "qkg"),
                              (vg, vg_d[l], "vg"), (cw, cw_d[l], "cw"), (cb, cb_d[l], "cb"),
                              (negb, b256_d[l], "negb")):
            full = tuple(slice(None) for _ in dst.shape)
            P.op("sp", "dma_start", out=dst[full], in_=src, w=[key], semkey="ls")
        P.op("sp", "dma_start", out=tmpn[:, 0, :].rearrange("p (h q) -> p h q", h=4), in_=Tb_d[l][:, 0, 0:4, :],
             w=[("tmpn", 0)], semkey="ls")
        P.op("sp", "dma_start", out=tmpn[:, 1, :].rearrange("p (h q) -> p h q", h=4), in_=Tb_d[l][:, 0, 4:8, :],
             w=[("tmpn", 1)], semkey="ls")
        P.op("sp", "dma_start", out=rs[:, 0, :].rearrange("p (h q) -> p h q", h=4), in_=Tb_d[l][:, 1, 0:4, :],
             w=[("rs", 0)], semkey="ls")
        P.op("sp", "dma_start", out=rs[:, 1, :].rearrange("p (h q) -> p h q", h=4), in_=Tb_d[l][:, 1, 4:8, :],
             w=[("rs", 1)], semkey="ls")
        P.op("sp", "dma_start", out=halos[:, :, :, :], in_=cconv_d[l], w=["halos"], semkey="ls")
        P.op("sp", "dma_start", out=gl[0:1, 0, :], in_=bsp_d[l], w=[("gl", 0)], semkey="ls")
        P.op("sp", "dma_start", out=gl[0:1, 1, 0:256], in_=bsps_d[l], w=[("gl", 1)], semkey="ls")
        P.op("sp", "dma_start", out=gc[:, 0, :], in_=wspT_d[l].rearrange("p g t -> p (g t)"),
             w=[("gc", 0)], semkey="ls3")
        P.op("sp", "dma_start", out=gc[:, 1, :], in_=tril_d.rearrange("p g t -> p (g t)"),
             w=[("gc", 1)], semkey="ls3")
        P.op("sp", "dma_start", out=ge[0:64, 0, 0:256], in_=wspb_d[l].rearrange("p g t -> p (g t)"),
             w=[("ge", 0)], semkey="ls3")
        P.op("sp", "dma_start", out=ge[0:64, 1, 0:256], in_=trilb_d.rearrange("p g t -> p (g t)"),
             w=[("ge", 1)], semkey="ls3")
        P.op("dve", "tensor_tensor", out=WcT[:, :, :].rearrange("p g t -> p (g t)"), in0=gc[:, 0, :],
             in1=gc[:, 1, :], op=ALU.mult, r=[("gc", 0), ("gc", 1)], w=["WcT"])
        P.op("dve", "tensor_tensor", out=Wblk[:, :, :].rearrange("p g t -> p (g t)"), in0=ge[0:64, 0, 0:256],
             in1=ge[0:64, 1, 0:256], op=ALU.mult, r=[("ge", 0), ("ge", 1)], w=["Wblk"])
        P.op("dve", "tensor_copy", bsp[:, :], gl[0:1, 0, :], r=[("gl", 0)], w=["bsp"])
        P.op("dve", "tensor_copy", bsps[:, :], gl[0:1, 1, 0:256], r=[("gl", 1)], w=["bsps"])
        P.op("dve", "tensor_scalar_mul", qkg[:, 1:2], qkg[:, 1:2], 8.0, r=["qkg"], w=["qkg"])
        P.op("dve", "tensor_scalar_mul", negb[:, :], negb[:, :], -1.0, r=["negb"], w=["negb"])
        for t in range(2):
            for h in range(8):
                stg = (tmpn if t == 0 else rs)
                skey = ("tmpn" if t == 0 else "rs", h // 4)
                P.op("act", "activation", out=E[:, t, h, :], in_=stg[:, h // 4, (h % 4) * 128:(h % 4 + 1) * 128],
                     func=AF.Exp, bias=negb[:, h:h + 1], scale=1.0, r=[skey, "negb"], w=["E"])
        P.op("pool", "memset", E[64:128, 1, :, 0:64], 0.0, r=["E"], w=["E"])

    def ada_setup(l):
        modT, A12, nrm, bada = modTt[:, l], A12t[:, l], nrmt[:, l], badat[:, l]
        P.op("sp", "dma_start", out=nrm, in_=nrm_d[l], w=[("nrm", l)], semkey=("lsa", l))
        P.op("sp", "dma_start", out=bada, in_=bada_d[l], w=[("bada", l)], semkey=("lsa", l))
        b = nb()
        for s_ in range(12):
            slot = ws_get()
            for m in range(4):
                ci = s_ * 4 + m
                for kc in range(8):
                    P.op("pe", "matmul", ps[b][:, ci * 5:(ci + 1) * 5],
                         lhsT=wslab[:, slot, kc * 512 + m * 128: kc * 512 + (m + 1) * 128],
                         rhs=cs[:, kc, :], start=(kc == 0), stop=(kc == 7),
                         r=[("wslab", slot), "cs"], w=bk(b))
            ws_done()
        for bb in range(5):
            P.op("dve", "tensor_tensor", out=modT[:, :, bb],
                 in0=ps[b][:, 0:240].rearrange("p (c b) -> p c b", b=5)[:, :, bb], in1=bada,
                 op=ALU.add, r=bk(b) + [("bada", l)], w=[("modT", l)])
        for which in range(2):
            sc0 = 8 if which == 0 else 32
            for bb in range(5):
                P.op("dve", "tensor_scalar", out=A12[:, which, :, bb], in0=modT[:, sc0:sc0 + 8, bb],
                     scalar1=1.0, scalar2=32.0, op0=ALU.add, op1=ALU.mult, r=[("modT", l)], w=[("A12", l)])
                P.op("dve", "tensor_tensor", out=A12[:, which, :, bb], in0=A12[:, which, :, bb],
                     in1=nrm[:, which * 8:(which + 1) * 8], op=ALU.mult, r=[("A12", l), ("nrm", l)],
                     w=[("A12", l)])

    def norm_sq(b, c, T):
        r_ = nrot("sq", 3)
        P.op("act", "activation", out=sq[:, r_, 0:T], in_=cur["xT"][:, c, 0:T], func=AF.Square,
             r=[("xT", cur["par"], c)], w=[("sq", r_)])
        return r_

    def norm_ss(b, c, r_, T):
        P.op("pe", "matmul", ps[b][:, 0:T], lhsT=ones_bf[:, :], rhs=sq[:, r_, 0:T],
             start=(c == 0), stop=(c == 7), r=[("sq", r_), "ones_bf"], w=bk(b))

    def norm_fin(b, T):
        P.op("act", "activation", out=rstd[:, 0:T], in_=ps[b][:, 0:T], func=AF.Sqrt, bias=float(D * EPS),
             scale=1.0, r=bk(b), w=["rstd"])
        P.op("dve", "reciprocal", out=rstd[:, 0:T], in_=rstd[:, 0:T], r=["rstd"], w=["rstd"])

    def norm_apply(which, T, groups):
        shc = 0 if which == 0 else 24
        for c in range(8):
            r_ = nrot("tmpn", 2)
            for (c0, n, bb) in groups:
                P.op("dve", "scalar_tensor_tensor", out=tmpn[:, r_, c0:c0 + n], in0=cur["xT"][:, c, c0:c0 + n],
                     scalar=A12t[:, cur["l"], which, c, bb:bb + 1], in1=rstd[:, c0:c0 + n], op0=ALU.mult, op1=ALU.mult,
                     r=[("xT", cur["par"], c), ("A12", cur["l"]), "rstd"], w=[("tmpn", r_)])
            for (c0, n, bb) in groups:
                P.op("act", "activation", out=hT[:, c, c0:c0 + n], in_=tmpn[:, r_, c0:c0 + n],
                     func=AF.Identity, bias=modTt[:, cur["l"], shc + c, bb:bb + 1], scale=1.0,
                     r=[("tmpn", r_), ("modT", cur["l"])], w=[("hT", c)])

    def norm_mod(which, T, groups):
        b = nb()
        for c in range(8):
            r_ = norm_sq(b, c, T)
            norm_ss(b, c, r_, T)
        norm_fin(b, T)
        norm_apply(which, T, groups)

    pro_done = set()

    def blk_geom(i):
        l_, kind_, t0_, n_ = BLOCKS[i]
        if kind_ == "sample":
            return l_, 64, [(bl * 16, 16, 1 + bl) for bl in range(4)]
        return l_, n_ * 128, [(0, n_ * 128, 0)]

    def pro_ok(i):
        return i < len(BLOCKS)

    class _NextCtx:
        def __init__(self, i):
            self.i = i

        def __enter__(self):
            self.save = dict(cur)
            cur["l"] = BLOCKS[self.i][0]
            cur["par"] = self.i % 2
            cur["xT"] = xTt[:, self.i % 2]

        def __exit__(self, *a):
            cur.update(self.save)

    def proj_fm(b, slot, nkc, ms, col0, rhs_t, rkeys, T):
        for kc in range(nkc):
            P.op("pe", "matmul", ps[b][:, 0:T], lhsT=wslab[:, slot, kc * ms + col0: kc * ms + col0 + 128],
                 rhs=rhs_t(kc), start=(kc == 0), stop=(kc == nkc - 1),
                 r=[("wslab", slot)] + rkeys(kc), w=bk(b))

    def headnorm(b, T, gcol, out_ap, out_keys, qchunk=None):
        r_ = nrot("sq", 3)
        P.op("act", "activation", out=sq[:, r_, 0:T], in_=ps[b][:, 0:T], func=AF.Square, r=bk(b), w=[("sq", r_)])
        b2 = nb()
        P.op("pe", "matmul", ps[b2][:, 0:T], lhsT=blk64[:, :], rhs=sq[:, r_, 0:T], start=True, stop=True,
             r=[("sq", r_), "blk64"], w=bk(b2))
        r2 = nrot("rs", 2)
        P.op("act", "activation", out=rs[:, r2, 0:T], in_=ps[b2][:, 0:T], func=AF.Sqrt, bias=float(64 * EPS),
             scale=1.0, r=bk(b2), w=[("rs", r2)])
        P.op("dve", "reciprocal", out=rs[:, r2, 0:T], in_=rs[:, r2, 0:T], r=[("rs", r2)], w=[("rs", r2)])
        if qchunk is not None:
            nq = max(T // 128, 1)
            w_ = min(T, 128)
            for hh in range(2):
                hs = slice(hh * 64, (hh + 1) * 64)
                P.op("dve", "scalar_tensor_tensor", out=qz[hs, qchunk, 0:nq, hh, 0:w_],
                     in0=ps[b][hs, 0:T].rearrange("p (t q) -> p t q", t=nq), scalar=qkg[hs, gcol:gcol + 1],
                     in1=rs[hs, r2, 0:T].rearrange("p (t q) -> p t q", t=nq), op0=ALU.mult, op1=ALU.mult,
                     r=bk(b) + [("rs", r2), "qkg"], w=out_keys)
            return
        P.op("dve", "scalar_tensor_tensor", out=out_ap, in0=ps[b][:, 0:T], scalar=qkg[:, gcol:gcol + 1],
             in1=rs[:, r2, 0:T], op0=ALU.mult, op1=ALU.mult, r=bk(b) + [("rs", r2), "qkg"], w=out_keys)

    def hT_r(T):
        return (lambda kc: hT[:, kc, 0:T]), (lambda kc: [("hT", kc)])

    def block(l, kind, t0, ntile):
        sample = kind == "sample"
        T = 64 if sample else ntile * 128
        tiles = [] if sample else list(range(t0, t0 + ntile))
        groups = [(bl * 16, 16, 1 + bl) for bl in range(4)] if sample else [(0, T, 0)]
        hr, hk = hT_r(T)
        bi = cur["idx"]
        cur["l"] = l
        _DBG["stage"] = 0 if kind != "sample" else 10
        cur["par"] = bi % 2
        cur["xT"] = xTt[:, bi % 2]

        def xload(i):
            l_, kind_, t0_, n_ = BLOCKS[i]
            par_ = i % 2
            wk = [("xT", par_, c) for c in range(8)]
            if kind_ == "sample":
                if l_ == 1:
                    P.op("sp", "dma_start", out=xTt[:, par_, :, 0:64], in_=xs1, r=["xs1"], w=wk,
                         semkey=("xld", par_))
                else:
                    P.op("sp", "dma_start", out=xTt[:, par_, :, 0:64], in_=xs_d, w=wk, semkey=("xld", par_))
                return True
            T_ = n_ * 128
            if l_ == 0:
                P.op("sp", "dma_start", out=xTt[:, par_, :, 0:T_], in_=xin[:, :, t0_ * 128: t0_ * 128 + T_], w=wk,
                     semkey=("xld", par_))
            else:
                P.op("sp", "dma_start", out=xTt[:, par_, :, 0:T_],
                     in_=x1s[:, :, (t0_ - 4) * 128: (t0_ - 4) * 128 + T_],
                     r=[("x1s", t) for t in range(t0_, t0_ + n_)], w=wk, semkey=("xld", par_))
            return True

        if bi == 0:
            xload(0)
        if bi + 1 < len(BLOCKS):
            xload(bi + 1)
        cur["idx"] = bi + 1
        if bi not in pro_done:
            norm_mod(0, T, groups)

        if kind == "kv":
            slabs = {1: ws_get()}
        else:
            slabs = {0: ws_get()}
            for c in range(4):
                b = nb()
                proj_fm(b, slabs[0], 8, 512, c * 128, hr, hk, T)
                headnorm(b, T, 0, None, [("qT", c)], qchunk=c)
            ws_done()
            slabs[1] = ws_get()
        for c in range(4):
            b = nb()
            proj_fm(b, slabs[1], 8, 512, c * 128, hr, hk, T)
            r_ = nrot("kst", 2)
            headnorm(b, T, 1, kst[:, r_, 0:T], [("kst", r_)])
            if sample:
                P.op("act", "activation", out=kTs[:, c, :], in_=kst[:, r_, 0:64], func=AF.Copy,
                     r=[("kst", r_)], w=["kTs"])
                P.op("sp", "dma_start", out=nksT_d[l, :, c, :], in_=kst[:, r_, 0:64], r=[("kst", r_)],
                     semkey=("kst", r_))
            else:
                for ti, t in enumerate(tiles):
                    sl = t % 8
                    P.op("act", "activation", out=kT[:, c, sl * 128:(sl + 1) * 128],
                         in_=kst[:, r_, ti * 128:(ti + 1) * 128], func=AF.Copy, r=[("kst", r_)], w=[("kT", sl)])
                    if t >= KEEP_T0:
                        P.op("sp", "dma_start", out=nkT_d[l, :, c, (t - KEEP_T0) * 128:(t - KEEP_T0 + 1) * 128],
                             in_=kst[:, r_, ti * 128:(ti + 1) * 128], r=[("kst", r_)], semkey=("kst", r_))
        ws_done()
        slot = ws_get()
        if sample:
            for bl in range(4):
                b = nb()
                for kc in range(8):
                    P.op("pe", "matmul", ps[b][0:16, 0:512], lhsT=hT[:, kc, bl * 16:(bl + 1) * 16],
                         rhs=wslab[:, slot, kc * 512:(kc + 1) * 512], start=(kc == 0), stop=(kc == 7),
                         r=[("wslab", slot), ("hT", kc)], w=bk(b))
                r_ = nrot("gl", 2)
                P.op("act", "activation", out=gl[0:16, r_, :], in_=ps[b][0:16, 0:512], func=AF.Copy, r=bk(b),
                     w=[("gl", r_)])
                P.op("dve", "tensor_copy", vt[0:16, bl, :], gl[0:16, r_, :], r=[("gl", r_)], w=[("vt", bl)])
                P.op("sp", "dma_start", out=nvs_d[l, :, bl, :], in_=gl[0:16, r_, :], r=[("gl", r_)],
                     semkey=("gl", r_))
        else:
            for ti, t in enumerate(tiles):
                b = nb()
                for kc in range(8):
                    P.op("pe", "matmul", ps[b][:, 0:512], lhsT=hT[:, kc, ti * 128:(ti + 1) * 128],
                         rhs=wslab[:, slot, kc * 512:(kc + 1) * 512], start=(kc == 0), stop=(kc == 7),
                         r=[("wslab", slot), ("hT", kc)], w=bk(b))
                sl = t % 8
                if t >= KEEP_T0:
                    r_ = nrot("gl", 2)
                    P.op("act", "activation", out=gl[:, r_, :], in_=ps[b][:, 0:512], func=AF.Copy, r=bk(b),
                         w=[("gl", r_)])
                    P.op("dve", "tensor_scalar_mul", vt[:, sl, :], gl[:, r_, :], vtok[:, t:t + 1],
                         r=[("gl", r_), "vtok"], w=[("vt", sl)])
                    P.op("sp", "dma_start", out=nv_d[l, t - KEEP_T0], in_=gl[:, r_, :], r=[("gl", r_)],
                         semkey=("gl", r_))
                else:
                    P.op("dve", "tensor_scalar_mul", vt[:, sl, :], ps[b][:, 0:512], vtok[:, t:t + 1],
                         r=bk(b) + ["vtok"], w=[("vt", sl)])
        ws_done()
        if kind == "kv":
            if pro_ok(bi + 1):
                with _NextCtx(bi + 1):
                    l2, T2, g2 = blk_geom(bi + 1)
                    norm_mod(0, T2, g2)
                pro_done.add(bi + 1)
            return

        ck(1)
        slot = ws_get()
        for c in range(4):
            b = nb()
            proj_fm(b, slot, 8, 512, c * 128, hr, hk, T)
            P.op("act", "activation", out=ubT[:, c, 0:T], in_=ps[b][:, 0:T], func=AF.Gelu_apprx_tanh, r=bk(b),
                 w=[("ubT", c)])
        ws_done()
        ck(2)
        slot = ws_get()
        tl = [(0, 64)] if sample else [(ti, 128) for ti in range(ntile)]
        for ti, M in tl:
            b = nb()
            for kc in range(8):
                P.op("pe", "matmul", ps[b][0:M, 0:512], lhsT=hT[:, kc, ti * 128: ti * 128 + M],
                     rhs=wslab[:, slot, kc * 512:(kc + 1) * 512], start=(kc == 0), stop=(kc == 7),
                     r=[("wslab", slot), ("hT", kc)], w=bk(b))
            r_ = nrot("gl", 2)
            P.op("act", "activation", out=gl[0:M, r_, :], in_=ps[b][0:M, 0:512], func=AF.Gelu_apprx_tanh, r=bk(b),
                 w=[("gl", r_)])
            r2 = nrot("tmpn", 2)
            P.op("dve", "tensor_tensor", out=tmpn[0:M, r2, :], in0=gl[0:M, r_, :], in1=gl[0:M, r_, :], op=ALU.mult,
                 r=[("gl", r_)], w=[("tmpn", r2)])
            r3 = nrot("ssv", 4)
            P.op("dve", "reduce_sum", out=ssv[0:M, r3:r3 + 1], in_=tmpn[0:M, r2, :], axis=AX.X,
                 r=[("tmpn", r2)], w=[("ssv", r3)])
            P.op("act", "activation", out=ssv[0:M, r3:r3 + 1], in_=ssv[0:M, r3:r3 + 1], func=AF.Sqrt,
                 bias=float(EPS), scale=1.0 / 512.0, r=[("ssv", r3)], w=[("ssv", r3)])
            P.op("dve", "reciprocal", out=ssv[0:M, r3:r3 + 1], in_=ssv[0:M, r3:r3 + 1], r=[("ssv", r3)],
                 w=[("ssv", r3)])
            if sample:
                P.op("dve", "scalar_tensor_tensor", out=tmpn[0:64, r2, :], in0=gl[0:64, r_, :],
                     scalar=ssv[0:64, r3:r3 + 1], in1=vg[0:64, :], op0=ALU.mult, op1=ALU.mult,
                     r=[("gl", r_), ("ssv", r3), "vg"], w=[("tmpn", r2)])
                P.op("act", "activation", out=vbns[:, :], in_=tmpn[0:64, r2, :], func=AF.Copy, r=[("tmpn", r2)],
                     w=["vbns"])
                P.op("sp", "dma_start", out=nbs_d[l], in_=tmpn[0:64, r2, :], r=[("tmpn", r2)], semkey=("tmpn", r2))
            else:
                P.op("dve", "scalar_tensor_tensor", out=vbn[:, ti, :], in0=gl[:, r_, :], scalar=ssv[:, r3:r3 + 1],
                     in1=vg[:, :], op0=ALU.mult, op1=ALU.mult, r=[("gl", r_), ("ssv", r3), "vg"], w=[("vbn", ti)])
        ws_done()
        ck(3)
        for s in range(4):
            slot = ws_get()
            for m in range(4):
                c = s * 4 + m
                b = nb()
                proj_fm(b, slot, 8, 512, m * 128, hr, hk, T)
                P.op("act", "activation", out=SA[:, c, 0:T], in_=ps[b][:, 0:T], func=AF.Sigmoid, r=bk(b),
                     w=[("SA", c)])
            ws_done()

        ck(4)
        if sample:
            items = [(bl, hp) for bl in range(4) for hp in range(4)]

            def s1(it, st):
                bl, hp = it
                cr = 0
                for j in range(5):
                    for hh in range(2):
                        hs = slice(hh * 64, (hh + 1) * 64)
                        b = st * 4 + hh
                        if j < 4:
                            P.op("pe", "matmul", ps[b][:, j * 16:(j + 1) * 16],
                                 lhsT=ckb[hs, cr, hp, j * 128:(j + 1) * 128], rhs=qz[hs, hp, 0, hh, bl * 16:(bl + 1) * 16],
                                 start=True, stop=True, r=[("ckb", cr), ("qT", hp)], w=[("ps", b)])
                        else:
                            P.op("pe", "matmul", ps[b][0:16, 64:80],
                                 lhsT=kTs[hs, hp, bl * 16:(bl + 1) * 16], rhs=qz[hs, hp, 0, hh, bl * 16:(bl + 1) * 16],
                                 start=True, stop=True, r=["kTs", ("qT", hp)], w=[("ps", b)])
                for hh in range(2):
                    b = st * 4 + hh
                    h = 2 * hp + hh
                    P.op("act", "activation", out=Pts[:, st, hh, 0:48], in_=ps[b][:, 0:48], func=AF.Exp,
                         r=[("ps", b)], w=[("Pts", st, hh)])
                    P.op("act", "activation", out=exs[:, st, hh, 0:16], in_=ps[b][:, 48:64], func=AF.Exp,
                         r=[("ps", b)], w=[("exs", st, hh)])
                    P.op("act", "activation", out=exs[0:16, st, hh, 16:32], in_=ps[b][0:16, 64:80], func=AF.Exp,
                         r=[("ps", b)], w=[("exs", st, hh)])
                    P.op("dve", "tensor_tensor", out=Pts[:, st, hh, 48:64], in0=exs[:, st, hh, 0:16],
                         in1=E[:, 0, h, 0:16], op=ALU.mult, r=[("exs", st, hh), "E"], w=[("Pts", st, hh)])
                    P.op("dve", "tensor_tensor", out=Pts[0:16, st, hh, 64:80], in0=exs[0:16, st, hh, 16:32],
                         in1=E[0:16, 1, h, 0:16], op=ALU.mult, r=[("exs", st, hh), "E"], w=[("Pts", st, hh)])

            def s2(it, st):
                bl, hp = it
                cr = 0
                bd, bo = st * 4 + 2, st * 4 + 3
                for hh in range(2):
                    for j in range(5):
                        if j < 4:
                            P.op("pe", "matmul", ps[bd][:, hh * 16:(hh + 1) * 16], lhsT=ones_bf[:, :],
                                 rhs=Pts[:, st, hh, j * 16:(j + 1) * 16], start=(j == 0), stop=False,
                                 r=[("Pts", st, hh), "ones_bf"], w=[("ps", bd)])
                        else:
                            P.op("pe", "matmul", ps[bd][:, hh * 16:(hh + 1) * 16], lhsT=ones_bf[0:16, :],
                                 rhs=Pts[0:16, st, hh, 64:80], start=False, stop=True,
                                 r=[("Pts", st, hh), "ones_bf"], w=[("ps", bd)])
                for hh in range(2):
                    hs = slice(hh * 64, (hh + 1) * 64)
                    fc = (2 * hp + hh) * 64
                    for j in range(5):
                        if j < 4:
                            P.op("pe", "matmul", ps[bo][hs, 0:16], lhsT=cvb[:, cr, j, fc:fc + 64],
                                 rhs=Pts[:, st, hh, j * 16:(j + 1) * 16], start=(j == 0), stop=False,
                                 r=[("Pts", st, hh), ("cvb", cr)], w=[("ps", bo)])
                        else:
                            P.op("pe", "matmul", ps[bo][hs, 0:16], lhsT=vt[0:16, bl, fc:fc + 64],
                                 rhs=Pts[0:16, st, hh, 64:80], start=False, stop=True,
                                 r=[("Pts", st, hh), ("vt", bl)], w=[("ps", bo)])
                P.op("dve", "reciprocal", out=rdens[:, st, :], in_=ps[bd][:, 0:32], r=[("ps", bd)],
                     w=[("rdens", st)])
                for hh in range(2):
                    hs = slice(hh * 64, (hh + 1) * 64)
                    P.op("dve", "tensor_tensor", out=oaT[hs, hp, bl * 16:(bl + 1) * 16], in0=ps[bo][hs, 0:16],
                         in1=rdens[hs, st, hh * 16:(hh + 1) * 16], op=ALU.mult, r=[("ps", bo), ("rdens", st)],
                         w=[("oaT", hp)])
        else:
            items = [(qi, hp) for qi in range(ntile) for hp in range(4)]

            def s1(it, st):
                qi, hp = it
                qt = t0 + qi
                bS, b4 = st * 4, st * 4 + 2
                for j in range(5):
                    sl = (qt - 4 + j) % 8
                    if j < 4:
                        out_ = psd[st * 2][:, j * 256:(j + 1) * 256]
                        wkey = ("ps", bS + j // 2)
                    else:
                        out_ = ps[b4][:, 0:256]
                        wkey = ("ps", b4)
                    P.op("pe", "matmul", out_, lhsT=kT[:, hp, sl * 128:(sl + 1) * 128],
                         rhs=qz[:, hp, qi, :, :].rearrange("p h q -> p (h q)"), start=True, stop=True,
                         r=[("kT", sl), ("qT", hp)], w=[wkey])
                pk = [("Pt", st)]
                rk2 = [("ps", bS), ("ps", bS + 1)]
                two = "p (h q) -> p h q"
                P.op("act", "activation", out=Pt[64:128, st, 0, :], in_=psd[st * 2][64:128, 0:256], func=AF.Exp,
                     r=[("ps", bS)], w=pk)
                P.op("act", "activation", out=Pt[0:64, st, 0, :].rearrange(two, h=2)[:, :, 0:64],
                     in_=psd[st * 2][0:64, 0:256].rearrange(two, h=2)[:, :, 0:64], func=AF.Exp, r=[("ps", bS)], w=pk)
                P.op("act", "activation", out=Pt[:, st, 1:4, :].rearrange("p j c -> p (j c)"),
                     in_=psd[st * 2][:, 256:1024], func=AF.Exp, r=rk2, w=pk)
                P.op("act", "activation", out=Pt[:, st, 4, :], in_=ps[b4][:, 0:256], func=AF.Exp,
                     r=[("ps", b4)], w=pk)
                for jj in (3, 4):
                    P.op("dve", "tensor_tensor", out=Pt[:, st, jj, :], in0=Pt[:, st, jj, :],
                         in1=E[:, jj - 3, 2 * hp:2 * hp + 2, :].rearrange("p h q -> p (h q)"), op=ALU.mult,
                         r=pk + ["E"], w=pk)

            def s2(it, st):
                qi, hp = it
                qt = t0 + qi
                bd, bo = st * 4 + 2, st * 4 + 3
                for j in range(5):
                    kt = qt - 4 + j
                    lw = vones[:, kt, :] if kt < 9 else ones_bf[:, :]
                    P.op("pe", "matmul", ps[bd][:, 256:512], lhsT=lw, rhs=Pt[:, st, j, :], start=(j == 0),
                         stop=(j == 4), r=[("Pt", st), "vones", "ones_bf"], w=[("ps", bd)])
                for hh in range(2):
                    hs = slice(hh * 64, (hh + 1) * 64)
                    fc = (2 * hp + hh) * 64
                    for j in range(5):
                        sl = (qt - 4 + j) % 8
                        P.op("pe", "matmul", ps[bo][hs, 0:128], lhsT=vt[:, sl, fc:fc + 64],
                             rhs=Pt[:, st, j, hh * 128:(hh + 1) * 128], start=(j == 0), stop=(j == 4),
                             r=[("Pt", st), ("vt", sl)], w=[("ps", bo)])
                P.op("dve", "tensor_scalar_max", rden[:, st, :], ps[bd][:, 256:512], 1e-30, r=[("ps", bd)],
                     w=[("rden", st)])
                P.op("dve", "reciprocal", out=rden[:, st, :], in_=rden[:, st, :], r=[("rden", st)],
                     w=[("rden", st)])
                for hh in range(2):
                    hs = slice(hh * 64, (hh + 1) * 64)
                    P.op("dve", "tensor_tensor", out=oaT[hs, hp, qi * 128:(qi + 1) * 128], in0=ps[bo][hs, 0:128],
                         in1=rden[hs, st, hh * 128:(hh + 1) * 128], op=ALU.mult, r=[("ps", bo), ("rden", st)],
                         w=[("oaT", hp)])

        if sample:
            groups_it = [[(bl, hp) for hp in range(4)] for bl in range(4)]
        else:
            groups_it = [items]
        for its in groups_it:
            if sample:
                bl = its[0][0]
                P.op("pool", "dma_start", out=ckb[:, 0, :, :], in_=ckT_d[l, :, :, bl, :], w=[("ckb", 0)],
                     semkey=("ckb", 0))
                P.op("pool", "dma_start", out=cvb[:, 0, :, :], in_=cv_d[l, bl], w=[("cvb", 0)],
                     semkey=("cvb", 0))
            for i in range(len(its) + 1):
                if i < len(its):
                    s1(its[i], i % 2)
                if i >= 1:
                    s2(its[i - 1], (i - 1) % 2)

        ck(5)
        if sample:
            b = nb()
            for g in range(4):
                P.op("pe", "matmul", ps[b][:, g * 64:(g + 1) * 64], lhsT=vbns[0:64, g * 128:(g + 1) * 128],
                     rhs=Wblk[0:64, g, :], start=True, stop=False, r=["vbns", "Wblk"], w=bk(b))
                P.op("pe", "matmul", ps[b][:, g * 64:(g + 1) * 64], lhsT=ones_row[0:1, :],
                     rhs=bsps[0:1, g * 64:(g + 1) * 64], start=False, stop=True, r=["ones_row", "bsps"], w=bk(b))
            P.op("dve", "tensor_tensor", out=obT[:, :, 0:64], in0=ps[b][:, 0:256].rearrange("p (g t) -> p g t", g=4),
                 in1=ubT[:, :, 0:64], op=ALU.mult, r=bk(b) + [("ubT", c) for c in range(4)],
                 w=[("obT", c) for c in range(4)])
        else:
            for ti in range(ntile):
                b = nb()
                for g in range(4):
                    P.op("pe", "matmul", ps[b][:, g * 128:(g + 1) * 128], lhsT=vbn[:, ti, g * 128:(g + 1) * 128],
                         rhs=WcT[:, g, :], start=True, stop=False, r=[("vbn", ti), "WcT"], w=bk(b))
                    P.op("pe", "matmul", ps[b][:, g * 128:(g + 1) * 128], lhsT=ones_row[0:1, :],
                         rhs=bsp[0:1, g * 128:(g + 1) * 128], start=False, stop=True, r=["ones_row", "bsp"], w=bk(b))
                P.op("dve", "tensor_tensor", out=obT[:, :, ti * 128:(ti + 1) * 128],
                     in0=ps[b][:, 0:512].rearrange("p (g t) -> p g t", g=4), in1=ubT[:, :, ti * 128:(ti + 1) * 128],
                     op=ALU.mult, r=bk(b) + [("ubT", c) for c in range(4)], w=[("obT", c) for c in range(4)])

        ck(6)
        sa_, sb_ = ws_get(), ws_get()
        for c in range(8):
            ba, bb_ = nb(), nb()
            proj_fm(ba, sa_, 4, 1024, c * 128, lambda kc: oaT[:, kc, 0:T], lambda kc: [("oaT", kc)], T)
            proj_fm(bb_, sb_, 4, 1024, c * 128, lambda kc: obT[:, kc, 0:T], lambda kc: [("obT", kc)], T)
            P.op("dve", "tensor_tensor", out=tmpn[:, 0, 0:T], in0=ps[ba][:, 0:T], in1=SA[:, c, 0:T], op=ALU.mult,
                 r=bk(ba) + [("SA", c)], w=[("tmpn", 0)])
            P.op("dve", "tensor_tensor", out=tmpn[:, 1, 0:T], in0=ps[bb_][:, 0:T], in1=SA[:, 8 + c, 0:T], op=ALU.mult,
                 r=bk(bb_) + [("SA", 8 + c)], w=[("tmpn", 1)])
            P.op("dve", "tensor_tensor", out=SA[:, 16 + c, 0:T], in0=tmpn[:, 0, 0:T], in1=tmpn[:, 1, 0:T], op=ALU.add,
                 r=[("tmpn", 0), ("tmpn", 1)], w=[("SA", 16 + c)])
        ws_done()
        ws_done()
        ck(7)
        for s in range(2):
            slot = ws_get()
            for m in range(4):
                c = s * 4 + m
                b = nb()
                proj_fm(b, slot, 8, 512, m * 128, lambda kc: SA[:, 16 + kc, 0:T], lambda kc: [("SA", 16 + kc)], T)
                for (c0, n, bb) in groups:
                    P.op("dve", "scalar_tensor_tensor", out=cur["xT"][:, c, c0:c0 + n], in0=ps[b][:, c0:c0 + n],
                         scalar=modTt[:, cur["l"], 16 + c, bb:bb + 1], in1=cur["xT"][:, c, c0:c0 + n], op0=ALU.mult, op1=ALU.add,
                         r=bk(b) + [("xT", cur["par"], c), ("modT", cur["l"])], w=[("xT", cur["par"], c)])
            ws_done()

        ck(8)
        norm_mod(1, T, groups)
        GW = 72 if sample else T + 2
        nxt = bi + 1 if pro_ok(bi + 1) else None
        if nxt is not None:
            l2, T2, g2 = blk_geom(nxt)
            bss = nb()
            reserved.add(bss)
            sqr = {}
        for jj in range(11):
            if nxt is not None:
                with _NextCtx(nxt):
                    if 1 <= jj <= 8:
                        sqr[jj - 1] = norm_sq(bss, jj - 1, T2)
                    if 2 <= jj <= 9:
                        norm_ss(bss, jj - 2, sqr[jj - 2], T2)
            slot = ws_get()
            for sub in range(2):
                j = 2 * jj + sub
                bg, bu = nb(), nb()
                proj_fm(bg, slot, 8, 512, sub * 128, hr, hk, T)
                proj_fm(bu, slot, 8, 512, 256 + sub * 128, hr, hk, T)
                r_ = nrot("gb", 3)
                if sample:
                    g3 = gb[:, r_, 0:72].rearrange("p (b t) -> p b t", t=18)
                    P.op("act", "activation", out=g3[:, :, 2:18],
                         in_=ps[bg][:, 0:64].rearrange("p (b t) -> p b t", t=16), func=AF.Copy, r=bk(bg),
                         w=[("gb", r_)])
                    P.op("dve", "tensor_copy", g3[:, :, 0:2], halos[:, j, :, :], r=["halos"], w=[("gb", r_)])
                    P.op("pool", "tensor_copy", ncs_st[:, j, :, :], g3[:, :, 16:18], r=[("gb", r_)], w=["ncs_st"])
                    views = [g3[:, :, k:k + 16] for k in range(3)]
                    r2 = nrot("gc", 2)
                    gcv = gc[:, r2, 0:64].rearrange("p (b t) -> p b t", t=16)
                else:
                    P.op("act", "activation", out=gb[:, r_, 2:2 + T], in_=ps[bg][:, 0:T], func=AF.Copy, r=bk(bg),
                         w=[("gb", r_)])
                    P.op("dve", "tensor_copy", gb[:, r_, 0:2], halo[:, j, :], r=["halo"], w=[("gb", r_)])
                    if t0 == 8:
                        P.op("pool", "tensor_scalar_mul", gb[:, r_, 128:130], gb[:, r_, 128:130], flag[:, 0:1],
                             r=[("gb", r_), "flag"], w=[("gb", r_)])
                    P.op("pool", "tensor_copy", halo[:, j, :], gb[:, r_, T:T + 2], r=[("gb", r_)], w=["halo"])
                    views = [gb[:, r_, k:k + T] for k in range(3)]
                    r2 = nrot("gc", 2)
                    gcv = gc[:, r2, 0:T]
                P.op("act", "activation", out=gcv, in_=views[0], func=AF.Identity, scale=cw[:, j, 0:1],
                     bias=cb[:, j:j + 1], r=[("gb", r_), "cw", "cb"], w=[("gc", r2)])
                for k in (1, 2):
                    P.op("dve", "scalar_tensor_tensor", out=gcv, in0=views[k], scalar=cw[:, j, k:k + 1], in1=gcv,
                         op0=ALU.mult, op1=ALU.add, r=[("gb", r_), ("gc", r2), "cw"], w=[("gc", r2)])
                r3 = nrot("ge", 2)
                P.op("act", "activation", out=ge[:, r3, 0:T], in_=gc[:, r2, 0:T], func=AF.Gelu_apprx_tanh,
                     r=[("gc", r2)], w=[("ge", r3)])
                P.op("dve", "tensor_tensor", out=SA[:, j, 0:T], in0=ps[bu][:, 0:T], in1=ge[:, r3, 0:T], op=ALU.mult,
                     r=bk(bu) + [("ge", r3)], w=[("SA", j)])
            ws_done()
        if nxt is not None:
            with _NextCtx(nxt):
                norm_fin(bss, T2)
                norm_apply(0, T2, g2)
            reserved.discard(bss)
            pro_done.add(nxt)
        for c in range(8):
            slot = ws_get()
            b = nb()
            proj_fm(b, slot, 22, 128, 0, lambda kc: SA[:, kc, 0:T], lambda kc: [("SA", kc)], T)
            for (c0, n, bb) in groups:
                P.op("dve", "scalar_tensor_tensor", out=cur["xT"][:, c, c0:c0 + n], in0=ps[b][:, c0:c0 + n],
                     scalar=modTt[:, cur["l"], 40 + c, bb:bb + 1], in1=cur["xT"][:, c, c0:c0 + n], op0=ALU.mult, op1=ALU.add,
                     r=bk(b) + [("xT", cur["par"], c), ("modT", cur["l"])], w=[("xT", cur["par"], c)])
            ws_done()

        ck(9)
        xk = [("xT", cur["par"], c) for c in range(8)]
        if sample:
            if l == 0:
                P.op("sp", "dma_start", out=xs1, in_=cur["xT"][:, :, 0:64], r=xk, w=["xs1"], semkey="xst")
            else:
                P.op("sp", "dma_start", out=ysT_d, in_=cur["xT"][:, :, 0:64], r=xk, semkey="xst")
            P.op("sp", "dma_start", out=ncs_d[l], in_=ncs_st[:, :, :, :], r=["ncs_st"], semkey="ncs")
        elif l == 0:
            P.op("sp", "dma_start", out=x1s[:, :, (t0 - 4) * 128:(t0 - 4) * 128 + T], in_=cur["xT"][:, :, 0:T], r=xk,
                 w=[("x1s", t) for t in tiles], semkey="xst")
        else:
            lo = max(t0, OUT_T0)
            c0 = (lo - t0) * 128
            P.op("sp", "dma_start", out=yT_d[:, :, (lo - OUT_T0) * 128:(lo - OUT_T0) * 128 + T - c0],
                 in_=cur["xT"][:, :, c0:T], r=xk, semkey="xst")


    BLOCKS = []
    for l in range(2):
        BLOCKS.append((l, "kv", 0 if l == 0 else 4, 4))
        for (t0, n) in (L0_BLOCKS if l == 0 else L1_BLOCKS):
            BLOCKS.append((l, "full", t0, n))
        BLOCKS.append((l, "sample", 0, 0))

    try:
        for l in range(2):
            if stop is not None and stop == -1:
                raise _Stop()
            if l == 0:
                ada_setup(0)
            layer_setup(l)
            if stop is not None and stop == 0:
                raise _Stop()
            if l == 1:
                P.op("pool", "memset", halo[:, :, :], 0.0, r=["halo"], w=["halo"])
            for (l_, kind, t0, n) in [b for b in BLOCKS if b[0] == l]:
                if kind == "sample":
                    P.op("sp", "dma_start", out=ncv_d[l], in_=halo[:, :, :], r=["halo"], semkey="ncv")
                    if l == 0:
                        ada_setup(1)
                block(l_, kind, t0, n)
                nblk[0] += 1
                if stop is not None and nblk[0] >= stop:
                    raise _Stop()
        assert ws["consumed"] == len(ws["seq"]), (ws["consumed"], len(ws["seq"]))
    except _Stop:
        dbg = dout("dbgx", (128, 8, 512))
        P.op("sp", "dma_start", out=dbg, in_=cur["xT"][:, :, :], r=[("xT", cur["par"], c) for c in range(8)], semkey="dbg")
        dbg2 = dout("dbgh", (128, 8, 512))
        P.op("pool", "dma_start", out=dbg2, in_=tmpn[:, :, :].rearrange("p a b -> p (a b)"), r=[("tmpn", 0), ("tmpn", 1)], semkey="dbg") if False else None

    P.emit(nc, stack)
    stack.close()
    return nc


def _fm(a):
    t, f = a.shape
    return np.ascontiguousarray(a.reshape(t, f // 128, 128).transpose(2, 1, 0))


def _slab(wm):
    k, m = wm.shape
    return np.ascontiguousarray(wm.reshape(k // 128, 128, m).transpose(1, 0, 2)).reshape(128, (k // 128) * m)


_NC_CACHE = {}


def kernel(x_prompt, x_sample, cache_attn_k, cache_attn_v, cache_ffn_conv, c_prompt, c_sample,
           norm1_g, norm2_g, w_ada, b_ada, w_in, q_norm_g, k_norm_g, rel_bias, v_norm_g,
           w_spatial, b_spatial, w_out_a, w_out_b, w_out, w_ffn_in, ffn_conv_w, ffn_conv_b, w_ffn_out):
    f = lambda a: np.asarray(a, dtype=np.float32)
    x_prompt, x_sample, cache_attn_k, cache_attn_v, cache_ffn_conv = map(f, (x_prompt, x_sample, cache_attn_k, cache_attn_v, cache_ffn_conv))
    c_prompt, c_sample, norm1_g, norm2_g, w_ada, b_ada, w_in = map(f, (c_prompt, c_sample, norm1_g, norm2_g, w_ada, b_ada, w_in))
    q_norm_g, k_norm_g, rel_bias, v_norm_g, w_spatial, b_spatial = map(f, (q_norm_g, k_norm_g, rel_bias, v_norm_g, w_spatial, b_spatial))
    w_out_a, w_out_b, w_out, w_ffn_in, ffn_conv_w, ffn_conv_b, w_ffn_out = map(f, (w_out_a, w_out_b, w_out, w_ffn_in, ffn_conv_w, ffn_conv_b, w_ffn_out))

    in_maps = _prep(x_prompt, x_sample, cache_attn_k, cache_attn_v, cache_ffn_conv, c_prompt, c_sample,
                    norm1_g, norm2_g, w_ada, b_ada, w_in, q_norm_g, k_norm_g, rel_bias, v_norm_g,
                    w_spatial, b_spatial, w_out_a, w_out_b, w_out, w_ffn_in, ffn_conv_w, ffn_conv_b, w_ffn_out)
    if "nc" not in _NC_CACHE:
        _NC_CACHE["nc"] = build_nc()
    nc = _NC_CACHE["nc"]
    res = run_bass_kernel_spmd(nc, in_maps, core_ids=list(range(NCORES)))
    return _post(res.results)


def _prep(x_prompt, x_sample, cache_attn_k, cache_attn_v, cache_ffn_conv, c_prompt, c_sample,
          norm1_g, norm2_g, w_ada, b_ada, w_in, q_norm_g, k_norm_g, rel_bias, v_norm_g,
          w_spatial, b_spatial, w_out_a, w_out_b, w_out, w_ffn_in, ffn_conv_w, ffn_conv_b, w_ffn_out):

    wall = np.empty((2, 24, 128, 4096), np.float32)
    wfo = np.empty((2, 8, 128, 2816), np.float32)
    wada = np.empty((2, 12, 128, 4096), np.float32)
    for l in range(2):
        for s in range(9):
            wall[l, s] = _slab(w_in[l][:, s * 512:(s + 1) * 512])
        wall[l, 9] = _slab(w_out_a[l])
        wall[l, 10] = _slab(w_out_b[l])
        for s in range(2):
            wall[l, 11 + s] = _slab(w_out[l][:, s * 512:(s + 1) * 512])
        for jj in range(11):
            idx = np.concatenate([np.arange(2 * jj * 128, (2 * jj + 2) * 128), DFF + np.arange(2 * jj * 128, (2 * jj + 2) * 128)])
            wall[l, 13 + jj] = _slab(w_ffn_in[l][:, idx])
        for c in range(8):
            wfo[l, c] = _slab(w_ffn_out[l][:, c * 128:(c + 1) * 128])
        for s in range(12):
            wada[l, s] = _slab(w_ada[l][:, s * 512:(s + 1) * 512])
    col = lambda v: np.ascontiguousarray(v.reshape(-1, 128).T)
    nrm = np.stack([np.concatenate([col(norm1_g[l]), col(norm2_g[l])], 1) for l in range(2)])
    bada = np.stack([col(b_ada[l]) for l in range(2)])
    qkg = np.stack([np.stack([np.tile(q_norm_g[l], 2), np.tile(k_norm_g[l], 2)], 1) for l in range(2)])
    vg = np.stack([np.broadcast_to(v_norm_g[l][None, :], (128, 512)) for l in range(2)]).copy()
    bsp = b_spatial.reshape(2, 1, 512).copy()
    bsps = np.stack([np.tile(b_spatial[l][:, None, :16], (1, 4, 1)).reshape(1, 256) for l in range(2)])
    wspT = np.ascontiguousarray(w_spatial.transpose(0, 3, 1, 2))
    wspb = np.zeros((2, 64, 4, 64), np.float32)
    for bl in range(4):
        wspb[:, bl * 16:(bl + 1) * 16, :, bl * 16:(bl + 1) * 16] = wspT[:, 0:16, :, 0:16]
    s_i = np.arange(128)[:, None]
    t_i = np.arange(128)[None, :]
    tril = np.broadcast_to((s_i <= t_i).astype(np.float32)[:, None, :], (128, 4, 128)).copy()
    trilb = np.zeros((64, 4, 64), np.float32)
    for bl in range(4):
        trilb[bl * 16:(bl + 1) * 16, :, bl * 16:(bl + 1) * 16] = tril[0:16, :, 0:16]
    cw = np.ascontiguousarray(ffn_conv_w.reshape(2, 3, 22, 128).transpose(0, 3, 2, 1))
    cb = np.ascontiguousarray(ffn_conv_b.reshape(2, 22, 128).transpose(0, 2, 1))
    ki = np.arange(128)[:, None]
    qi = np.arange(128)[None, :]
    idx1 = np.clip(128 + qi - ki, -128, 128) + 128
    idx0 = np.clip(qi - ki, -128, 128) + 128
    Tb = np.stack([np.stack([rel_bias[l][:, idx1].transpose(1, 0, 2), rel_bias[l][:, idx0].transpose(1, 0, 2)], 1)
                   for l in range(2)])
    b256 = np.stack([np.broadcast_to(rel_bias[l][None, :, 256], (128, 8)) for l in range(2)]).copy()

    shared = dict(wall=wall, wfo=wfo, wada=wada, nrm=nrm, bada=bada, qkg=qkg, vg=vg, bsp=bsp, bsps=bsps, wspT=wspT,
                  wspb=wspb, tril=tril, trilb=trilb, cw=cw, cb=cb, Tb=np.ascontiguousarray(Tb), b256=b256)
    shared = {k: np.ascontiguousarray(v, dtype=np.float32) for k, v in shared.items()}

    in_maps = []
    for core in range(NCORES):
        b, seg = core // 4, core % 4
        s0 = seg * SEG
        a = s0 - HALO
        xw = np.zeros((W, D), np.float32)
        lo = max(a, 0)
        xw[lo - a:] = x_prompt[b, lo:s0 + SEG]
        valid = (np.arange(W) + a >= 0).astype(np.float32)
        m = dict(shared)
        m["xin"] = _fm(xw)
        m["xs"] = _fm(x_sample[4 * core:4 * core + 4].reshape(64, D))
        m["vtok"] = np.ascontiguousarray(valid.reshape(NT, 128).T)
        m["flag"] = np.full((128, 1), 1.0 if a >= 0 else 0.0, np.float32)
        cc = np.concatenate([c_prompt[b:b + 1], c_sample[4 * core:4 * core + 4]], 0)
        m["cT"] = np.ascontiguousarray(cc.reshape(5, 8, 128).transpose(2, 1, 0))
        ck = cache_attn_k[:, 4 * core:4 * core + 4]
        m["ckT"] = np.ascontiguousarray(ck.reshape(2, 4, 512, 4, 2, 64).transpose(0, 4, 5, 3, 1, 2)).reshape(2, 128, 4, 4, 512)
        cvv = cache_attn_v[:, 4 * core:4 * core + 4].reshape(2, 4, 4, 128, 512)
        m["cv"] = np.ascontiguousarray(cvv.transpose(0, 1, 3, 2, 4))
        cc2 = cache_ffn_conv[:, 4 * core:4 * core + 4].reshape(2, 4, 2, 22, 128)
        m["cconv"] = np.ascontiguousarray(cc2.transpose(0, 4, 3, 1, 2))
        in_maps.append(m)
    return in_maps


def _post(R):

    y_prompt = np.empty((2, 8192, D), np.float32)
    y_sample = np.empty((32, 16, D), np.float32)
    nkp = np.empty((2, 2, 512, 8, 64), np.float32)
    nvp = np.empty((2, 2, 512, 8, 64), np.float32)
    ncp = np.empty((2, 2, 2, DFF), np.float32)
    nks = np.empty((2, 32, 16, 8, 64), np.float32)
    nvs = np.empty((2, 32, 16, 8, 64), np.float32)
    nbs = np.empty((2, 32, 16, 4, 128), np.float32)
    ncs = np.empty((2, 32, 2, DFF), np.float32)
    for core in range(NCORES):
        r = R[core]
        b, seg = core // 4, core % 4
        y_prompt[b, seg * SEG:(seg + 1) * SEG] = r["yT"].transpose(2, 1, 0).reshape(SEG, D)
        y_sample[4 * core:4 * core + 4] = r["ysT"].transpose(2, 1, 0).reshape(4, 16, D)
        sl = slice(4 * core, 4 * core + 4)
        for l in range(2):
            if seg == 3:
                nkp[l, b] = r["nkT"][l].reshape(2, 64, 4, 512).transpose(3, 2, 0, 1).reshape(512, 8, 64)
                nvp[l, b] = r["nv"][l].reshape(512, 8, 64)
                ncp[l, b] = r["ncv"][l].transpose(2, 1, 0).reshape(2, DFF)
            nks[l, sl] = r["nksT"][l].reshape(2, 64, 4, 4, 16).transpose(3, 4, 2, 0, 1).reshape(4, 16, 8, 64)
            nvs[l, sl] = r["nvs"][l].transpose(1, 0, 2).reshape(4, 16, 8, 64)
            nbs[l, sl] = r["nbs"][l].reshape(4, 16, 4, 128)
            ncs[l, sl] = r["ncs"][l].transpose(2, 3, 1, 0).reshape(4, 2, DFF)
    return (y_prompt, y_sample, nkp, nvp, ncp, nks, nvs, nbs, ncs)
```

```python
import contextlib
import numpy as np
import concourse.bass as bass
import concourse.mybir as mybir
from concourse.bass_utils import run_bass_kernel_spmd

F32 = mybir.dt.float32
BF16 = mybir.dt.bfloat16
AF = mybir.ActivationFunctionType
ALU = mybir.AluOpType
AX = mybir.AxisListType

NCORES = 8
D = 1024
SEG = 2048
HALO = 1152
W = SEG + HALO
NT = W // 128
DFF = 2816
EPS = 1e-6
NB = 4
L0_BLOCKS = [(4, 4), (8, 4), (12, 4), (16, 3), (19, 3), (22, 3)]
L1_BLOCKS = [(8, 4), (12, 4), (16, 3), (19, 3), (22, 3)]
OUT_T0 = 9
KEEP_T0 = 21
_DBG = {}


class Prog:
    STREAMS = ("sp", "act", "dve", "pool", "pe")

    def __init__(self):
        self.ops = []
        self.keyw = {}
        self.keyr = {}
        self.dcnt = {}

    def op(self, stream, method, *args, r=(), w=(), semkey=None, **kw):
        idx = len(self.ops)
        dom = ("d", semkey) if semkey is not None else ("s", stream)
        deps = {}

        def add(d, i):
            if d[0] == "d":
                i = self.dcnt[d]
            if deps.get(d, -1) < i:
                deps[d] = i

        for k in r:
            for d, i in self.keyw.get(k, {}).items():
                add(d, i)
        skip_same = dom[0] == "d" or stream == "pe"
        for k in w:
            for d, i in self.keyr.get(k, {}).items():
                if d == dom and (skip_same or i == idx):
                    continue
                add(d, i)
            for d, i in self.keyw.get(k, {}).items():
                if d == dom and skip_same:
                    continue
                add(d, i)
        for k in r:
            self.keyr.setdefault(k, {})[dom] = idx
        for k in w:
            if self.keyr.get(k):
                self.keyw[k] = {dom: idx}
                self.keyr[k] = {}
            else:
                self.keyw.setdefault(k, {})[dom] = idx
        if dom[0] == "d":
            self.dcnt[dom] = self.dcnt.get(dom, 0) + 16
        self.ops.append(dict(stream=stream, method=method, args=args, kw=kw, dom=dom,
                             deps=list(deps.items()), stage=_DBG.get("stage", "")))
        return idx

    def emit(self, nc, stack):
        ops = self.ops
        needs = set()
        for o in ops:
            for d, i in o["deps"]:
                if d[0] == "s":
                    needs.add(i)
        cnt = {}
        sems = {}
        issuer = {}
        for i, o in enumerate(ops):
            d = o["dom"]
            if d[0] == "d":
                cnt[d] = cnt.get(d, 0) + 16
                o["done"] = cnt[d]
                assert issuer.setdefault(d, o["stream"]) == o["stream"], d
            elif i in needs:
                cnt[d] = cnt.get(d, 0) + 1
                o["done"] = cnt[d]
            else:
                o["done"] = None
        for d in cnt:
            sems[d] = stack.enter_context(nc.semaphore("s%d" % len(sems)))
        block = stack.enter_context(nc.Block())
        self.nsem = len(sems)

        def run(stream, eng):
            waited = {}
            for o in ops:
                if o["stream"] != stream:
                    continue
                for d, i in o["deps"]:
                    v = i if d[0] == "d" else ops[i]["done"]
                    if waited.get(d, 0) < v:
                        eng.wait_ge(sems[d], v)
                        waited[d] = v
                        _DBG.setdefault("waits", {}).setdefault(stream, []).append((o["stage"], d, (ops[i]["method"], ops[i]["stage"], str(ops[i]["kw"].get("out", ops[i]["args"][:1]))[:90]) if d[0] == "s" else None, o["method"]))
                ins = getattr(eng, o["method"])(*o["args"], **o["kw"])
                if o["done"] is not None:
                    d = o["dom"]
                    ins.then_inc(sems[d], 16 if d[0] == "d" else 1)
            for d, v in cnt.items():
                if d[0] == "d" and issuer[d] == stream and waited.get(d, 0) < v:
                    eng.wait_ge(sems[d], v)

        @block.sync
        def _(e):
            run("sp", e)

        @block.scalar
        def _(e):
            run("act", e)

        @block.vector
        def _(e):
            run("dve", e)

        @block.gpsimd
        def _(e):
            run("pool", e)

        @block.tensor
        def _(e):
            run("pe", e)


def build_nc(stop=None):
    class _Stop(Exception):
        pass

    nblk = [0]

    def ck(st):
        _DBG["stage"] = st + (10 if _DBG.get("stage", 0) >= 10 else 0)
        if stop is not None and nblk[0] + st / 10.0 >= stop - 1e-9 and stop > 0:
            raise _Stop()

    nc = bass.Bass("TRN2", target_bir_lowering=False)
    P = Prog()
    stack = contextlib.ExitStack()

    def din(name, shape):
        return nc.dram_tensor(name, list(shape), F32, kind="ExternalInput").ap()

    def dout(name, shape):
        return nc.dram_tensor(name, list(shape), F32, kind="ExternalOutput").ap()

    def dint(name, shape, dt):
        return nc.dram_tensor(name, list(shape), dt, kind="Internal").ap()

    xin = din("xin", (128, 8, W))
    xs_d = din("xs", (128, 8, 64))
    vtok_d = din("vtok", (128, NT))
    flag_d = din("flag", (128, 1))
    cT_d = din("cT", (128, 8, 5))
    ckT_d = din("ckT", (2, 128, 4, 4, 512))
    cv_d = din("cv", (2, 4, 128, 4, 512))
    cconv_d = din("cconv", (2, 128, 22, 4, 2))
    wall_d = din("wall", (2, 24, 128, 4096))
    wfo_d = din("wfo", (2, 8, 128, 2816))
    wada_d = din("wada", (2, 12, 128, 4096))
    nrm_d = din("nrm", (2, 128, 16))
    bada_d = din("bada", (2, 128, 48))
    qkg_d = din("qkg", (2, 128, 2))
    vg_d = din("vg", (2, 128, 512))
    bsp_d = din("bsp", (2, 1, 512))
    bsps_d = din("bsps", (2, 1, 256))
    wspT_d = din("wspT", (2, 128, 4, 128))
    wspb_d = din("wspb", (2, 64, 4, 64))
    tril_d = din("tril", (128, 4, 128))
    trilb_d = din("trilb", (64, 4, 64))
    cw_d = din("cw", (2, 128, 22, 3))
    cb_d = din("cb", (2, 128, 22))
    Tb_d = din("Tb", (2, 128, 2, 8, 128))
    b256_d = din("b256", (2, 128, 8))

    yT_d = dout("yT", (128, 8, SEG))
    ysT_d = dout("ysT", (128, 8, 64))
    nkT_d = dout("nkT", (2, 128, 4, 512))
    nv_d = dout("nv", (2, 4, 128, 512))
    ncv_d = dout("ncv", (2, 128, 22, 2))
    nksT_d = dout("nksT", (2, 128, 4, 64))
    nvs_d = dout("nvs", (2, 16, 4, 512))
    nbs_d = dout("nbs", (2, 64, 512))
    ncs_d = dout("ncs", (2, 128, 22, 4, 2))

    wsc = dint("wsc", (2, 24, 128, 4096), BF16)
    wsc_fo = dint("wscfo", (2, 8, 128, 2816), BF16)
    x1s = dint("x1s", (128, 8, 21 * 128), F32)
    xs1 = dint("xs1", (128, 8, 64), F32)

    def sb(name, shape, dt=F32):
        return stack.enter_context(nc.sbuf_tensor("sb_" + name, list(shape), dt))

    xTt = sb("xT", (128, 2, 8, 512))
    cur = {"xT": xTt[:, 0], "par": 0, "idx": 0}
    hT = sb("hT", (128, 8, 512), BF16)
    sq = sb("sq", (128, 3, 512), BF16)
    rstd = sb("rstd", (128, 512))
    tmpn = sb("tmpn", (128, 2, 512))
    qz = sb("qz", (128, 4, 4, 2, 128), BF16)
    kst = sb("kst", (128, 2, 512))
    rs = sb("rs", (128, 2, 512))
    kT = sb("kT", (128, 4, 1024), BF16)
    vt = sb("vt", (128, 8, 512), BF16)
    ubT = sb("ubT", (128, 4, 512), BF16)
    vbn = sb("vbn", (128, 4, 512), BF16)
    gl = sb("gl", (128, 2, 512))
    ssv = sb("ssv", (128, 4))
    SA = sb("SA", (128, 24, 512), BF16)
    oaT = sb("oaT", (128, 4, 512), BF16)
    obT = sb("obT", (128, 4, 512), BF16)
    Pt = sb("Pt", (128, 2, 5, 256), BF16)
    rden = sb("rden", (128, 2, 256))
    gb = sb("gb", (128, 3, 516))
    gc = sb("gc", (128, 2, 512))
    ge = sb("ge", (128, 2, 512))
    halo = sb("halo", (128, 22, 2))
    halos = sb("halos", (128, 22, 4, 2))
    ncs_st = sb("ncs_st", (128, 22, 4, 2))
    wslab = sb("wslab", (128, NB, 4096), BF16)
    ones_bf = sb("ones_bf", (128, 128), BF16)
    blk64 = sb("blk64", (128, 128), BF16)
    ones_row = sb("ones_row", (1, 128), BF16)
    vones = sb("vones", (128, 9, 128), BF16)
    vtok = sb("vtok", (128, NT))
    flag = sb("flag", (128, 1))
    cTs = sb("cTs", (128, 8, 5))
    cs = sb("cs", (128, 8, 5), BF16)
    E = sb("E", (128, 2, 8, 128), BF16)
    negb = sb("negb", (128, 8))
    modTt = sb("modT", (128, 2, 48, 5))
    A12t = sb("A12", (128, 2, 2, 8, 5))
    nrmt = sb("nrm", (128, 2, 16))
    badat = sb("bada", (128, 2, 48))
    qkg = sb("qkg", (128, 2))
    vg = sb("vg", (128, 512))
    bsp = sb("bsp", (1, 512), BF16)
    bsps = sb("bsps", (1, 256), BF16)
    WcT = sb("WcT", (128, 4, 128), BF16)
    Wblk = sb("Wblk", (64, 4, 64), BF16)
    cw = sb("cw", (128, 22, 3))
    cb = sb("cb", (128, 22))
    kTs = sb("kTs", (128, 4, 64), BF16)
    vbns = sb("vbns", (64, 512), BF16)
    ckb = sb("ckb", (128, 1, 4, 512), BF16)
    cvb = sb("cvb", (128, 1, 4, 512), BF16)
    Pts = sb("Pts", (128, 2, 2, 80), BF16)
    exs = sb("exs", (128, 2, 2, 32))
    rdens = sb("rdens", (128, 2, 32))

    psd = [stack.enter_context(nc.psum_tensor("ps%d" % i, [128, 1024], F32)) for i in range(4)]
    ps = [psd[i // 2][:, (i % 2) * 512:(i % 2 + 1) * 512] for i in range(8)]

    def bk(i):
        return [("ps", i)]

    bank_ctr = [0]

    reserved = set()

    def nb():
        while True:
            b = bank_ctr[0] % 8
            bank_ctr[0] += 1
            if b not in reserved:
                return b

    rot = {}

    def nrot(name, n):
        v = rot.get(name, 0)
        rot[name] = v + 1
        return v % n

    ws = dict(seq=[], issued=0, consumed=0)

    casted = set()

    def ws_issue(upto):
        while ws["issued"] < min(upto, len(ws["seq"])):
            kind_, l_, s_ = ws["seq"][ws["issued"]]
            slot = ws["issued"] % NB
            wk, sk = [("wslab", slot)], ("w", slot)
            if kind_ == "ada":
                P.op("pool", "dma_start", out=wslab[:, slot, :], in_=wada_d[l_, s_], w=wk, semkey=sk)
            else:
                ncols = 4096 if kind_ == "wall" else 2816
                src32 = wall_d[l_, s_] if kind_ == "wall" else wfo_d[l_, s_]
                scr = wsc[l_, s_] if kind_ == "wall" else wsc_fo[l_, s_]
                key = ("wsc", kind_, l_, s_)
                if key not in casted:
                    casted.add(key)
                    P.op("pool", "dma_start", out=wslab[:, slot, 0:ncols], in_=src32, w=wk, semkey=sk)
                    P.op("sp", "dma_start", out=scr, in_=wslab[:, slot, 0:ncols], r=wk, w=[key],
                         semkey=("wst", slot))
                else:
                    P.op("pool", "dma_start", out=wslab[:, slot, 0:ncols], in_=scr, r=[key], w=wk, semkey=sk)
            ws["issued"] += 1

    def ws_get():
        assert ws["consumed"] < len(ws["seq"])
        ws_issue(ws["consumed"] + 1)
        slot = ws["consumed"] % NB
        ws["consumed"] += 1
        return slot

    def ws_done():
        ws_issue(ws["consumed"] + NB)

    def seq_ada(l):
        for s_ in range(12):
            ws["seq"].append(("ada", l, s_))

    def seq_block(l, kind):
        if kind == "kv":
            for s_ in (1, 2):
                ws["seq"].append(("wall", l, s_))
            return
        for s_ in range(24):
            ws["seq"].append(("wall", l, s_))
        for s_ in range(8):
            ws["seq"].append(("fo", l, s_))

    seq_ada(0)
    for l in range(2):
        seq_block(l, "kv")
        nfull = len(L0_BLOCKS if l == 0 else L1_BLOCKS)
        for k_ in range(nfull):
            if l == 0 and k_ == nfull - 1:
                seq_ada(1)
            seq_block(l, "full")

    P.op("pool", "memset", ones_bf[:, :], 1.0, w=["ones_bf"])
    P.op("pool", "memset", blk64[:, :], 0.0, w=["blk64"])
    P.op("pool", "memset", blk64[0:64, 0:64], 1.0, w=["blk64"])
    P.op("pool", "memset", blk64[64:128, 64:128], 1.0, w=["blk64"])
    P.op("pool", "memset", ones_row[:, :], 1.0, w=["ones_row"])
    P.op("pool", "memset", Pt[:, :, :, :].rearrange("p a b c -> p (a b c)"), 0.0, w=[("Pt", 0), ("Pt", 1)])
    P.op("pool", "memset", qz[:, :, :, :, :].rearrange("p a b c d -> p (a b c d)"), 0.0, w=[("qT", c) for c in range(4)])
    P.op("pool", "memset", halo[:, :, :], 0.0, w=["halo"])
    P.op("sp", "dma_start", out=vtok[:, :], in_=vtok_d, w=["vtok"], semkey="c0")
    P.op("sp", "dma_start", out=flag[:, :], in_=flag_d, w=["flag"], semkey="c0")
    P.op("sp", "dma_start", out=cTs[:, :, :], in_=cT_d, w=["cTs"], semkey="c0")
    P.op("act", "activation", out=cs[:, :, :], in_=cTs[:, :, :], func=AF.Silu, r=["cTs"], w=["cs"])
    for t in range(9):
        P.op("act", "activation", out=vones[:, t, :], in_=ones_bf[:, :], func=AF.Copy,
             scale=vtok[:, t:t + 1], r=["ones_bf", "vtok"], w=["vones"])
    def layer_setup(l):
        for dst, src, key in ((qkg, qkg_d[l], "qkg"),
                              (vg, vg_d[l], "vg"), (cw, cw_d[l], "cw"), (cb, cb_d[l], "cb"),
                              (negb, b256_d[l], "negb")):
            full = tuple(slice(None) for _ in dst.shape)
            P.op("sp", "dma_start", out=dst[full], in_=src, w=[key], semkey="ls")
        P.op("sp", "dma_start", out=tmpn[:, 0, :].rearrange("p (h q) -> p h q", h=4), in_=Tb_d[l][:, 0, 0:4, :],
             w=[("tmpn", 0)], semkey="ls")
        P.op("sp", "dma_start", out=tmpn[:, 1, :].rearrange("p (h q) -> p h q", h=4), in_=Tb_d[l][:, 0, 4:8, :],
             w=[("tmpn", 1)], semkey="ls")
        P.op("sp", "dma_start", out=rs[:, 0, :].rearrange("p (h q) -> p h q", h=4), in_=Tb_d[l][:, 1, 0:4, :],
             w=[("rs", 0)], semkey="ls")
        P.op("sp", "dma_start", out=rs[:, 1, :].rearrange("p (h q) -> p h q", h=4), in_=Tb_d[l][:, 1, 4:8, :],
             w=[("rs", 1)], semkey="ls")
        P.op("sp", "dma_start", out=halos[:, :, :, :], in_=cconv_d[l], w=["halos"], semkey="ls")
        P.op("sp", "dma_start", out=gl[0:1, 0, :], in_=bsp_d[l], w=[("gl", 0)], semkey="ls")
        P.op("sp", "dma_start", out=gl[0:1, 1, 0:256], in_=bsps_d[l], w=[("gl", 1)], semkey="ls")
        P.op("sp", "dma_start", out=gc[:, 0, :], in_=wspT_d[l].rearrange("p g t -> p (g t)"),
             w=[("gc", 0)], semkey="ls3")
        P.op("sp", "dma_start", out=gc[:, 1, :], in_=tril_d.rearrange("p g t -> p (g t)"),
             w=[("gc", 1)], semkey="ls3")
        P.op("sp", "dma_start", out=ge[0:64, 0, 0:256], in_=wspb_d[l].rearrange("p g t -> p (g t)"),
             w=[("ge", 0)], semkey="ls3")
        P.op("sp", "dma_start", out=ge[0:64, 1, 0:256], in_=trilb_d.rearrange("p g t -> p (g t)"),
             w=[("ge", 1)], semkey="ls3")
        P.op("dve", "tensor_tensor", out=WcT[:, :, :].rearrange("p g t -> p (g t)"), in0=gc[:, 0, :],
             in1=gc[:, 1, :], op=ALU.mult, r=[("gc", 0), ("gc", 1)], w=["WcT"])
        P.op("dve", "tensor_tensor", out=Wblk[:, :, :].rearrange("p g t -> p (g t)"), in0=ge[0:64, 0, 0:256],
             in1=ge[0:64, 1, 0:256], op=ALU.mult, r=[("ge", 0), ("ge", 1)], w=["Wblk"])
        P.op("dve", "tensor_copy", bsp[:, :], gl[0:1, 0, :], r=[("gl", 0)], w=["bsp"])
        P.op("dve", "tensor_copy", bsps[:, :], gl[0:1, 1, 0:256], r=[("gl", 1)], w=["bsps"])
        P.op("dve", "tensor_scalar_mul", qkg[:, 1:2], qkg[:, 1:2], 8.0, r=["qkg"], w=["qkg"])
        P.op("dve", "tensor_scalar_mul", negb[:, :], negb[:, :], -1.0, r=["negb"], w=["negb"])
        for t in range(2):
            for h in range(8):
                stg = (tmpn if t == 0 else rs)
                skey = ("tmpn" if t == 0 else "rs", h // 4)
                P.op("act", "activation", out=E[:, t, h, :], in_=stg[:, h // 4, (h % 4) * 128:(h % 4 + 1) * 128],
                     func=AF.Exp, bias=negb[:, h:h + 1], scale=1.0, r=[skey, "negb"], w=["E"])
        P.op("pool", "memset", E[64:128, 1, :, 0:64], 0.0, r=["E"], w=["E"])

    def ada_setup(l):
        modT, A12, nrm, bada = modTt[:, l], A12t[:, l], nrmt[:, l], badat[:, l]
        P.op("sp", "dma_start", out=nrm, in_=nrm_d[l], w=[("nrm", l)], semkey=("lsa", l))
        P.op("sp", "dma_start", out=bada, in_=bada_d[l], w=[("bada", l)], semkey=("lsa", l))
        b = nb()
        for s_ in range(12):
            slot = ws_get()
            for m in range(4):
                ci = s_ * 4 + m
                for kc in range(8):
                    P.op("pe", "matmul", ps[b][:, ci * 5:(ci + 1) * 5],
                         lhsT=wslab[:, slot, kc * 512 + m * 128: kc * 512 + (m + 1) * 128],
                         rhs=cs[:, kc, :], start=(kc == 0), stop=(kc == 7),
                         r=[("wslab", slot), "cs"], w=bk(b))
            ws_done()
        for bb in range(5):
            P.op("dve", "tensor_tensor", out=modT[:, :, bb],
                 in0=ps[b][:, 0:240].rearrange("p (c b) -> p c b", b=5)[:, :, bb], in1=bada,
                 op=ALU.add, r=bk(b) + [("bada", l)], w=[("modT", l)])
        for which in range(2):
            sc0 = 8 if which == 0 else 32
            for bb in range(5):
                P.op("dve", "tensor_scalar", out=A12[:, which, :, bb], in0=modT[:, sc0:sc0 + 8, bb],
                     scalar1=1.0, scalar2=32.0, op0=ALU.add, op1=ALU.mult, r=[("modT", l)], w=[("A12", l)])
                P.op("dve", "tensor_tensor", out=A12[:, which, :, bb], in0=A12[:, which, :, bb],
                     in1=nrm[:, which * 8:(which + 1) * 8], op=ALU.mult, r=[("A12", l), ("nrm", l)],
                     w=[("A12", l)])

    def norm_sq(b, c, T):
        r_ = nrot("sq", 3)
        P.op("act", "activation", out=sq[:, r_, 0:T], in_=cur["xT"][:, c, 0:T], func=AF.Square,
             r=[("xT", cur["par"], c)], w=[("sq", r_)])
        return r_

    def norm_ss(b, c, r_, T):
        P.op("pe", "matmul", ps[b][:, 0:T], lhsT=ones_bf[:, :], rhs=sq[:, r_, 0:T],
             start=(c == 0), stop=(c == 7), r=[("sq", r_), "ones_bf"], w=bk(b))

    def norm_fin(b, T):
        P.op("act", "activation", out=rstd[:, 0:T], in_=ps[b][:, 0:T], func=AF.Sqrt, bias=float(D * EPS),
             scale=1.0, r=bk(b), w=["rstd"])
        P.op("dve", "reciprocal", out=rstd[:, 0:T], in_=rstd[:, 0:T], r=["rstd"], w=["rstd"])

    def norm_apply(which, T, groups):
        shc = 0 if which == 0 else 24
        for c in range(8):
            r_ = nrot("tmpn", 2)
            for (c0, n, bb) in groups:
                P.op("dve", "scalar_tensor_tensor", out=tmpn[:, r_, c0:c0 + n], in0=cur["xT"][:, c, c0:c0 + n],
                     scalar=A12t[:, cur["l"], which, c, bb:bb + 1], in1=rstd[:, c0:c0 + n], op0=ALU.mult, op1=ALU.mult,
                     r=[("xT", cur["par"], c), ("A12", cur["l"]), "rstd"], w=[("tmpn", r_)])
            for (c0, n, bb) in groups:
                P.op("act", "activation", out=hT[:, c, c0:c0 + n], in_=tmpn[:, r_, c0:c0 + n],
                     func=AF.Identity, bias=modTt[:, cur["l"], shc + c, bb:bb + 1], scale=1.0,
                     r=[("tmpn", r_), ("modT", cur["l"])], w=[("hT", c)])

    def norm_mod(which, T, groups):
        b = nb()
        for c in range(8):
            r_ = norm_sq(b, c, T)
            norm_ss(b, c, r_, T)
        norm_fin(b, T)
        norm_apply(which, T, groups)

    pro_done = set()

    def blk_geom(i):
        l_, kind_, t0_, n_ = BLOCKS[i]
        Tp_ = n_ * 128
        if kind_ == "fullms":
            return l_, Tp_ + 64, [(0, Tp_, 0)] + [(Tp_ + bl * 16, 16, 1 + bl) for bl in range(4)]
        return l_, Tp_, [(0, Tp_, 0)]

    def pro_ok(i):
        return i < len(BLOCKS)

    class _NextCtx:
        def __init__(self, i):
            self.i = i

        def __enter__(self):
            self.save = dict(cur)
            cur["l"] = BLOCKS[self.i][0]
            cur["par"] = self.i % 2
            cur["xT"] = xTt[:, self.i % 2]

        def __exit__(self, *a):
            cur.update(self.save)

    def proj_fm(b, slot, nkc, ms, col0, rhs_t, rkeys, T):
        for kc in range(nkc):
            P.op("pe", "matmul", ps[b][:, 0:T], lhsT=wslab[:, slot, kc * ms + col0: kc * ms + col0 + 128],
                 rhs=rhs_t(kc), start=(kc == 0), stop=(kc == nkc - 1),
                 r=[("wslab", slot)] + rkeys(kc), w=bk(b))

    def headnorm(b, T, gcol, out_ap, out_keys, qchunk=None, Tp=None, ms=False):
        r_ = nrot("sq", 3)
        P.op("act", "activation", out=sq[:, r_, 0:T], in_=ps[b][:, 0:T], func=AF.Square, r=bk(b), w=[("sq", r_)])
        b2 = nb()
        P.op("pe", "matmul", ps[b2][:, 0:T], lhsT=blk64[:, :], rhs=sq[:, r_, 0:T], start=True, stop=True,
             r=[("sq", r_), "blk64"], w=bk(b2))
        r2 = nrot("rs", 2)
        P.op("act", "activation", out=rs[:, r2, 0:T], in_=ps[b2][:, 0:T], func=AF.Sqrt, bias=float(64 * EPS),
             scale=1.0, r=bk(b2), w=[("rs", r2)])
        P.op("dve", "reciprocal", out=rs[:, r2, 0:T], in_=rs[:, r2, 0:T], r=[("rs", r2)], w=[("rs", r2)])
        if qchunk is not None:
            nq = Tp // 128
            for hh in range(2):
                hs = slice(hh * 64, (hh + 1) * 64)
                P.op("dve", "scalar_tensor_tensor", out=qz[hs, qchunk, 0:nq, hh, :],
                     in0=ps[b][hs, 0:Tp].rearrange("p (t q) -> p t q", t=nq), scalar=qkg[hs, gcol:gcol + 1],
                     in1=rs[hs, r2, 0:Tp].rearrange("p (t q) -> p t q", t=nq), op0=ALU.mult, op1=ALU.mult,
                     r=bk(b) + [("rs", r2), "qkg"], w=out_keys)
                if ms:
                    P.op("dve", "scalar_tensor_tensor", out=qz[hs, qchunk, 3, hh, 0:64], in0=ps[b][hs, Tp:Tp + 64],
                         scalar=qkg[hs, gcol:gcol + 1], in1=rs[hs, r2, Tp:Tp + 64], op0=ALU.mult, op1=ALU.mult,
                         r=bk(b) + [("rs", r2), "qkg"], w=out_keys)
            return
        P.op("dve", "scalar_tensor_tensor", out=out_ap, in0=ps[b][:, 0:T], scalar=qkg[:, gcol:gcol + 1],
             in1=rs[:, r2, 0:T], op0=ALU.mult, op1=ALU.mult, r=bk(b) + [("rs", r2), "qkg"], w=out_keys)

    def hT_r(T):
        return (lambda kc: hT[:, kc, 0:T]), (lambda kc: [("hT", kc)])

    def block(l, kind, t0, ntile):
        sample = False
        ms = kind == "fullms"
        Tp = ntile * 128
        C0 = Tp
        T = Tp + (64 if ms else 0)
        tiles = list(range(t0, t0 + ntile))
        groups = [(0, Tp, 0)] + ([(C0 + bl * 16, 16, 1 + bl) for bl in range(4)] if ms else [])
        hr, hk = hT_r(T)
        bi = cur["idx"]
        cur["l"] = l
        _DBG["stage"] = 0
        cur["par"] = bi % 2
        cur["xT"] = xTt[:, bi % 2]

        def xload(i):
            l_, kind_, t0_, n_ = BLOCKS[i]
            par_ = i % 2
            wk = [("xT", par_, c) for c in range(8)]
            T_ = n_ * 128
            if kind_ == "fullms":
                if l_ == 1:
                    P.op("sp", "dma_start", out=xTt[:, par_, :, T_:T_ + 64], in_=xs1, r=["xs1"], w=wk,
                         semkey=("xld", par_))
                else:
                    P.op("sp", "dma_start", out=xTt[:, par_, :, T_:T_ + 64], in_=xs_d, w=wk, semkey=("xld", par_))
            if l_ == 0:
                P.op("sp", "dma_start", out=xTt[:, par_, :, 0:T_], in_=xin[:, :, t0_ * 128: t0_ * 128 + T_], w=wk,
                     semkey=("xld", par_))
            else:
                P.op("sp", "dma_start", out=xTt[:, par_, :, 0:T_],
                     in_=x1s[:, :, (t0_ - 4) * 128: (t0_ - 4) * 128 + T_],
                     r=[("x1s", t) for t in range(t0_, t0_ + n_)], w=wk, semkey=("xld", par_))
            return True

        if bi == 0:
            xload(0)
        if bi + 1 < len(BLOCKS):
            xload(bi + 1)
        cur["idx"] = bi + 1
        if bi not in pro_done:
            norm_mod(0, T, groups)

        if kind == "kv":
            slabs = {1: ws_get()}
        else:
            slabs = {0: ws_get()}
            for c in range(4):
                b = nb()
                proj_fm(b, slabs[0], 8, 512, c * 128, hr, hk, T)
                headnorm(b, T, 0, None, [("qT", c)], qchunk=c, Tp=Tp, ms=ms)
            ws_done()
            slabs[1] = ws_get()
        for c in range(4):
            b = nb()
            proj_fm(b, slabs[1], 8, 512, c * 128, hr, hk, T)
            r_ = nrot("kst", 2)
            headnorm(b, T, 1, kst[:, r_, 0:T], [("kst", r_)])
            if ms:
                P.op("act", "activation", out=kTs[:, c, :], in_=kst[:, r_, C0:C0 + 64], func=AF.Copy,
                     r=[("kst", r_)], w=["kTs"])
                P.op("sp", "dma_start", out=nksT_d[l, :, c, :], in_=kst[:, r_, C0:C0 + 64], r=[("kst", r_)],
                     semkey=("kst", r_))
            if True:
                for ti, t in enumerate(tiles):
                    sl = t % 8
                    P.op("act", "activation", out=kT[:, c, sl * 128:(sl + 1) * 128],
                         in_=kst[:, r_, ti * 128:(ti + 1) * 128], func=AF.Copy, r=[("kst", r_)], w=[("kT", sl)])
                    if t >= KEEP_T0:
                        P.op("sp", "dma_start", out=nkT_d[l, :, c, (t - KEEP_T0) * 128:(t - KEEP_T0 + 1) * 128],
                             in_=kst[:, r_, ti * 128:(ti + 1) * 128], r=[("kst", r_)], semkey=("kst", r_))
        ws_done()
        slot = ws_get()
        if ms:
            for bl in range(4):
                b = nb()
                for kc in range(8):
                    P.op("pe", "matmul", ps[b][0:16, 0:512], lhsT=hT[:, kc, C0 + bl * 16:C0 + (bl + 1) * 16],
                         rhs=wslab[:, slot, kc * 512:(kc + 1) * 512], start=(kc == 0), stop=(kc == 7),
                         r=[("wslab", slot), ("hT", kc)], w=bk(b))
                r_ = nrot("gl", 2)
                P.op("act", "activation", out=gl[0:16, r_, :], in_=ps[b][0:16, 0:512], func=AF.Copy, r=bk(b),
                     w=[("gl", r_)])
                P.op("dve", "tensor_copy", SA[0:16, 16 + bl, :], gl[0:16, r_, :], r=[("gl", r_)], w=[("SA", 16 + bl)])
                P.op("sp", "dma_start", out=nvs_d[l, :, bl, :], in_=gl[0:16, r_, :], r=[("gl", r_)],
                     semkey=("gl", r_))
        if True:
            for ti, t in enumerate(tiles):
                b = nb()
                for kc in range(8):
                    P.op("pe", "matmul", ps[b][:, 0:512], lhsT=hT[:, kc, ti * 128:(ti + 1) * 128],
                         rhs=wslab[:, slot, kc * 512:(kc + 1) * 512], start=(kc == 0), stop=(kc == 7),
                         r=[("wslab", slot), ("hT", kc)], w=bk(b))
                sl = t % 8
                if t >= KEEP_T0:
                    r_ = nrot("gl", 2)
                    P.op("act", "activation", out=gl[:, r_, :], in_=ps[b][:, 0:512], func=AF.Copy, r=bk(b),
                         w=[("gl", r_)])
                    P.op("dve", "tensor_scalar_mul", vt[:, sl, :], gl[:, r_, :], vtok[:, t:t + 1],
                         r=[("gl", r_), "vtok"], w=[("vt", sl)])
                    P.op("sp", "dma_start", out=nv_d[l, t - KEEP_T0], in_=gl[:, r_, :], r=[("gl", r_)],
                         semkey=("gl", r_))
                else:
                    P.op("dve", "tensor_scalar_mul", vt[:, sl, :], ps[b][:, 0:512], vtok[:, t:t + 1],
                         r=bk(b) + ["vtok"], w=[("vt", sl)])
        ws_done()
        if kind == "kv":
            if pro_ok(bi + 1):
                with _NextCtx(bi + 1):
                    l2, T2, g2 = blk_geom(bi + 1)
                    norm_mod(0, T2, g2)
                pro_done.add(bi + 1)
            return

        ck(1)
        slot = ws_get()
        for c in range(4):
            b = nb()
            proj_fm(b, slot, 8, 512, c * 128, hr, hk, T)
            P.op("act", "activation", out=ubT[:, c, 0:T], in_=ps[b][:, 0:T], func=AF.Gelu_apprx_tanh, r=bk(b),
                 w=[("ubT", c)])
        ws_done()
        ck(2)
        slot = ws_get()
        tl = [(ti, 128, False) for ti in range(ntile)] + ([(ntile, 64, True)] if ms else [])
        for ti, M, sample in tl:
            b = nb()
            for kc in range(8):
                P.op("pe", "matmul", ps[b][0:M, 0:512], lhsT=hT[:, kc, ti * 128: ti * 128 + M],
                     rhs=wslab[:, slot, kc * 512:(kc + 1) * 512], start=(kc == 0), stop=(kc == 7),
                     r=[("wslab", slot), ("hT", kc)], w=bk(b))
            r_ = nrot("gl", 2)
            P.op("act", "activation", out=gl[0:M, r_, :], in_=ps[b][0:M, 0:512], func=AF.Gelu_apprx_tanh, r=bk(b),
                 w=[("gl", r_)])
            r2 = nrot("tmpn", 2)
            P.op("dve", "tensor_tensor", out=tmpn[0:M, r2, :], in0=gl[0:M, r_, :], in1=gl[0:M, r_, :], op=ALU.mult,
                 r=[("gl", r_)], w=[("tmpn", r2)])
            r3 = nrot("ssv", 4)
            P.op("dve", "reduce_sum", out=ssv[0:M, r3:r3 + 1], in_=tmpn[0:M, r2, :], axis=AX.X,
                 r=[("tmpn", r2)], w=[("ssv", r3)])
            P.op("act", "activation", out=ssv[0:M, r3:r3 + 1], in_=ssv[0:M, r3:r3 + 1], func=AF.Sqrt,
                 bias=float(EPS), scale=1.0 / 512.0, r=[("ssv", r3)], w=[("ssv", r3)])
            P.op("dve", "reciprocal", out=ssv[0:M, r3:r3 + 1], in_=ssv[0:M, r3:r3 + 1], r=[("ssv", r3)],
                 w=[("ssv", r3)])
            if sample:
                P.op("dve", "scalar_tensor_tensor", out=tmpn[0:64, r2, :], in0=gl[0:64, r_, :],
                     scalar=ssv[0:64, r3:r3 + 1], in1=vg[0:64, :], op0=ALU.mult, op1=ALU.mult,
                     r=[("gl", r_), ("ssv", r3), "vg"], w=[("tmpn", r2)])
                P.op("act", "activation", out=vbns[:, :], in_=tmpn[0:64, r2, :], func=AF.Copy, r=[("tmpn", r2)],
                     w=["vbns"])
                P.op("sp", "dma_start", out=nbs_d[l], in_=tmpn[0:64, r2, :], r=[("tmpn", r2)], semkey=("tmpn", r2))
            else:
                P.op("dve", "scalar_tensor_tensor", out=vbn[:, ti, :], in0=gl[:, r_, :], scalar=ssv[:, r3:r3 + 1],
                     in1=vg[:, :], op0=ALU.mult, op1=ALU.mult, r=[("gl", r_), ("ssv", r3), "vg"], w=[("vbn", ti)])
        ws_done()
        sample = False
        ck(3)
        for s in range(4):
            slot = ws_get()
            for m in range(4):
                c = s * 4 + m
                b = nb()
                proj_fm(b, slot, 8, 512, m * 128, hr, hk, T)
                P.op("act", "activation", out=SA[:, c, 0:T], in_=ps[b][:, 0:T], func=AF.Sigmoid, r=bk(b),
                     w=[("SA", c)])
            ws_done()

        ck(4)
        if ms:
            items = [(bl, hp) for bl in range(4) for hp in range(4)]

            def ss1(it, st):
                bl, hp = it
                cr = 0
                for j in range(5):
                    for hh in range(2):
                        hs = slice(hh * 64, (hh + 1) * 64)
                        b = st * 4 + hh
                        if j < 4:
                            P.op("pe", "matmul", ps[b][:, j * 16:(j + 1) * 16],
                                 lhsT=ckb[hs, cr, hp, j * 128:(j + 1) * 128], rhs=qz[hs, hp, 3, hh, bl * 16:(bl + 1) * 16],
                                 start=True, stop=True, r=[("ckb", cr), ("qT", hp)], w=[("ps", b)])
                        else:
                            P.op("pe", "matmul", ps[b][0:16, 64:80],
                                 lhsT=kTs[hs, hp, bl * 16:(bl + 1) * 16], rhs=qz[hs, hp, 3, hh, bl * 16:(bl + 1) * 16],
                                 start=True, stop=True, r=["kTs", ("qT", hp)], w=[("ps", b)])
                for hh in range(2):
                    b = st * 4 + hh
                    h = 2 * hp + hh
                    P.op("act", "activation", out=Pts[:, st, hh, 0:48], in_=ps[b][:, 0:48], func=AF.Exp,
                         r=[("ps", b)], w=[("Pts", st, hh)])
                    P.op("act", "activation", out=exs[:, st, hh, 0:16], in_=ps[b][:, 48:64], func=AF.Exp,
                         r=[("ps", b)], w=[("exs", st, hh)])
                    P.op("act", "activation", out=exs[0:16, st, hh, 16:32], in_=ps[b][0:16, 64:80], func=AF.Exp,
                         r=[("ps", b)], w=[("exs", st, hh)])
                    P.op("dve", "tensor_tensor", out=Pts[:, st, hh, 48:64], in0=exs[:, st, hh, 0:16],
                         in1=E[:, 0, h, 0:16], op=ALU.mult, r=[("exs", st, hh), "E"], w=[("Pts", st, hh)])
                    P.op("dve", "tensor_tensor", out=Pts[0:16, st, hh, 64:80], in0=exs[0:16, st, hh, 16:32],
                         in1=E[0:16, 1, h, 0:16], op=ALU.mult, r=[("exs", st, hh), "E"], w=[("Pts", st, hh)])

            def ss2(it, st):
                bl, hp = it
                cr = 0
                bd, bo = st * 4 + 2, st * 4 + 3
                for hh in range(2):
                    for j in range(5):
                        if j < 4:
                            P.op("pe", "matmul", ps[bd][:, hh * 16:(hh + 1) * 16], lhsT=ones_bf[:, :],
                                 rhs=Pts[:, st, hh, j * 16:(j + 1) * 16], start=(j == 0), stop=False,
                                 r=[("Pts", st, hh), "ones_bf"], w=[("ps", bd)])
                        else:
                            P.op("pe", "matmul", ps[bd][:, hh * 16:(hh + 1) * 16], lhsT=ones_bf[0:16, :],
                                 rhs=Pts[0:16, st, hh, 64:80], start=False, stop=True,
                                 r=[("Pts", st, hh), "ones_bf"], w=[("ps", bd)])
                for hh in range(2):
                    hs = slice(hh * 64, (hh + 1) * 64)
                    fc = (2 * hp + hh) * 64
                    for j in range(5):
                        if j < 4:
                            P.op("pe", "matmul", ps[bo][hs, 0:16], lhsT=cvb[:, cr, j, fc:fc + 64],
                                 rhs=Pts[:, st, hh, j * 16:(j + 1) * 16], start=(j == 0), stop=False,
                                 r=[("Pts", st, hh), ("cvb", cr)], w=[("ps", bo)])
                        else:
                            P.op("pe", "matmul", ps[bo][hs, 0:16], lhsT=SA[0:16, 16 + bl, fc:fc + 64],
                                 rhs=Pts[0:16, st, hh, 64:80], start=False, stop=True,
                                 r=[("Pts", st, hh), ("SA", 16 + bl)], w=[("ps", bo)])
                P.op("dve", "reciprocal", out=rdens[:, st, :], in_=ps[bd][:, 0:32], r=[("ps", bd)],
                     w=[("rdens", st)])
                for hh in range(2):
                    hs = slice(hh * 64, (hh + 1) * 64)
                    P.op("dve", "tensor_tensor", out=oaT[hs, hp, C0 + bl * 16:C0 + (bl + 1) * 16], in0=ps[bo][hs, 0:16],
                         in1=rdens[hs, st, hh * 16:(hh + 1) * 16], op=ALU.mult, r=[("ps", bo), ("rdens", st)],
                         w=[("oaT", hp)])
        if True:
            items = [(qi, hp) for qi in range(ntile) for hp in range(4)]

            def s1(it, st):
                qi, hp = it
                qt = t0 + qi
                bS, b4 = st * 4, st * 4 + 2
                for j in range(5):
                    sl = (qt - 4 + j) % 8
                    if j < 4:
                        out_ = psd[st * 2][:, j * 256:(j + 1) * 256]
                        wkey = ("ps", bS + j // 2)
                    else:
                        out_ = ps[b4][:, 0:256]
                        wkey = ("ps", b4)
                    P.op("pe", "matmul", out_, lhsT=kT[:, hp, sl * 128:(sl + 1) * 128],
                         rhs=qz[:, hp, qi, :, :].rearrange("p h q -> p (h q)"), start=True, stop=True,
                         r=[("kT", sl), ("qT", hp)], w=[wkey])
                pk = [("Pt", st)]
                rk2 = [("ps", bS), ("ps", bS + 1)]
                two = "p (h q) -> p h q"
                P.op("act", "activation", out=Pt[64:128, st, 0, :], in_=psd[st * 2][64:128, 0:256], func=AF.Exp,
                     r=[("ps", bS)], w=pk)
                P.op("act", "activation", out=Pt[0:64, st, 0, :].rearrange(two, h=2)[:, :, 0:64],
                     in_=psd[st * 2][0:64, 0:256].rearrange(two, h=2)[:, :, 0:64], func=AF.Exp, r=[("ps", bS)], w=pk)
                P.op("act", "activation", out=Pt[:, st, 1:4, :].rearrange("p j c -> p (j c)"),
                     in_=psd[st * 2][:, 256:1024], func=AF.Exp, r=rk2, w=pk)
                P.op("act", "activation", out=Pt[:, st, 4, :], in_=ps[b4][:, 0:256], func=AF.Exp,
                     r=[("ps", b4)], w=pk)
                for jj in (3, 4):
                    P.op("dve", "tensor_tensor", out=Pt[:, st, jj, :], in0=Pt[:, st, jj, :],
                         in1=E[:, jj - 3, 2 * hp:2 * hp + 2, :].rearrange("p h q -> p (h q)"), op=ALU.mult,
                         r=pk + ["E"], w=pk)

            def s2(it, st):
                qi, hp = it
                qt = t0 + qi
                bd, bo = st * 4 + 2, st * 4 + 3
                for j in range(5):
                    kt = qt - 4 + j
                    lw = vones[:, kt, :] if kt < 9 else ones_bf[:, :]
                    P.op("pe", "matmul", ps[bd][:, 256:512], lhsT=lw, rhs=Pt[:, st, j, :], start=(j == 0),
                         stop=(j == 4), r=[("Pt", st), "vones", "ones_bf"], w=[("ps", bd)])
                for hh in range(2):
                    hs = slice(hh * 64, (hh + 1) * 64)
                    fc = (2 * hp + hh) * 64
                    for j in range(5):
                        sl = (qt - 4 + j) % 8
                        P.op("pe", "matmul", ps[bo][hs, 0:128], lhsT=vt[:, sl, fc:fc + 64],
                             rhs=Pt[:, st, j, hh * 128:(hh + 1) * 128], start=(j == 0), stop=(j == 4),
                             r=[("Pt", st), ("vt", sl)], w=[("ps", bo)])
                P.op("dve", "tensor_scalar_max", rden[:, st, :], ps[bd][:, 256:512], 1e-30, r=[("ps", bd)],
                     w=[("rden", st)])
                P.op("dve", "reciprocal", out=rden[:, st, :], in_=rden[:, st, :], r=[("rden", st)],
                     w=[("rden", st)])
                for hh in range(2):
                    hs = slice(hh * 64, (hh + 1) * 64)
                    P.op("dve", "tensor_tensor", out=oaT[hs, hp, qi * 128:(qi + 1) * 128], in0=ps[bo][hs, 0:128],
                         in1=rden[hs, st, hh * 128:(hh + 1) * 128], op=ALU.mult, r=[("ps", bo), ("rden", st)],
                         w=[("oaT", hp)])

        for i in range(len(items) + 1):
            if i < len(items):
                s1(items[i], i % 2)
            if i >= 1:
                s2(items[i - 1], (i - 1) % 2)
        if ms:
            for bl in range(4):
                its = [(bl, hp) for hp in range(4)]
                P.op("pool", "dma_start", out=ckb[:, 0, :, :], in_=ckT_d[l, :, :, bl, :], w=[("ckb", 0)],
                     semkey=("ckb", 0))
                P.op("pool", "dma_start", out=cvb[:, 0, :, :], in_=cv_d[l, bl], w=[("cvb", 0)],
                     semkey=("cvb", 0))
                for i in range(len(its) + 1):
                    if i < len(its):
                        ss1(its[i], i % 2)
                    if i >= 1:
                        ss2(its[i - 1], (i - 1) % 2)

        ck(5)
        if ms:
            b = nb()
            for g in range(4):
                P.op("pe", "matmul", ps[b][:, g * 64:(g + 1) * 64], lhsT=vbns[0:64, g * 128:(g + 1) * 128],
                     rhs=Wblk[0:64, g, :], start=True, stop=False, r=["vbns", "Wblk"], w=bk(b))
                P.op("pe", "matmul", ps[b][:, g * 64:(g + 1) * 64], lhsT=ones_row[0:1, :],
                     rhs=bsps[0:1, g * 64:(g + 1) * 64], start=False, stop=True, r=["ones_row", "bsps"], w=bk(b))
            P.op("dve", "tensor_tensor", out=obT[:, :, C0:C0 + 64], in0=ps[b][:, 0:256].rearrange("p (g t) -> p g t", g=4),
                 in1=ubT[:, :, C0:C0 + 64], op=ALU.mult, r=bk(b) + [("ubT", c) for c in range(4)],
                 w=[("obT", c) for c in range(4)])
        if True:
            for ti in range(ntile):
                b = nb()
                for g in range(4):
                    P.op("pe", "matmul", ps[b][:, g * 128:(g + 1) * 128], lhsT=vbn[:, ti, g * 128:(g + 1) * 128],
                         rhs=WcT[:, g, :], start=True, stop=False, r=[("vbn", ti), "WcT"], w=bk(b))
                    P.op("pe", "matmul", ps[b][:, g * 128:(g + 1) * 128], lhsT=ones_row[0:1, :],
                         rhs=bsp[0:1, g * 128:(g + 1) * 128], start=False, stop=True, r=["ones_row", "bsp"], w=bk(b))
                P.op("dve", "tensor_tensor", out=obT[:, :, ti * 128:(ti + 1) * 128],
                     in0=ps[b][:, 0:512].rearrange("p (g t) -> p g t", g=4), in1=ubT[:, :, ti * 128:(ti + 1) * 128],
                     op=ALU.mult, r=bk(b) + [("ubT", c) for c in range(4)], w=[("obT", c) for c in range(4)])

        ck(6)
        sa_, sb_ = ws_get(), ws_get()
        for c in range(8):
            ba, bb_ = nb(), nb()
            proj_fm(ba, sa_, 4, 1024, c * 128, lambda kc: oaT[:, kc, 0:T], lambda kc: [("oaT", kc)], T)
            proj_fm(bb_, sb_, 4, 1024, c * 128, lambda kc: obT[:, kc, 0:T], lambda kc: [("obT", kc)], T)
            P.op("dve", "tensor_tensor", out=tmpn[:, 0, 0:T], in0=ps[ba][:, 0:T], in1=SA[:, c, 0:T], op=ALU.mult,
                 r=bk(ba) + [("SA", c)], w=[("tmpn", 0)])
            P.op("dve", "tensor_tensor", out=tmpn[:, 1, 0:T], in0=ps[bb_][:, 0:T], in1=SA[:, 8 + c, 0:T], op=ALU.mult,
                 r=bk(bb_) + [("SA", 8 + c)], w=[("tmpn", 1)])
            P.op("dve", "tensor_tensor", out=SA[:, 16 + c, 0:T], in0=tmpn[:, 0, 0:T], in1=tmpn[:, 1, 0:T], op=ALU.add,
                 r=[("tmpn", 0), ("tmpn", 1)], w=[("SA", 16 + c)])
        ws_done()
        ws_done()
        ck(7)
        for s in range(2):
            slot = ws_get()
            for m in range(4):
                c = s * 4 + m
                b = nb()
                proj_fm(b, slot, 8, 512, m * 128, lambda kc: SA[:, 16 + kc, 0:T], lambda kc: [("SA", 16 + kc)], T)
                for (c0, n, bb) in groups:
                    P.op("dve", "scalar_tensor_tensor", out=cur["xT"][:, c, c0:c0 + n], in0=ps[b][:, c0:c0 + n],
                         scalar=modTt[:, cur["l"], 16 + c, bb:bb + 1], in1=cur["xT"][:, c, c0:c0 + n], op0=ALU.mult, op1=ALU.add,
                         r=bk(b) + [("xT", cur["par"], c), ("modT", cur["l"])], w=[("xT", cur["par"], c)])
            ws_done()

        ck(8)
        norm_mod(1, T, groups)
        nxt = bi + 1 if pro_ok(bi + 1) else None
        if nxt is not None:
            l2, T2, g2 = blk_geom(nxt)
            bss = nb()
            reserved.add(bss)
            sqr = {}
        for jj in range(11):
            if nxt is not None:
                with _NextCtx(nxt):
                    if 1 <= jj <= 8:
                        sqr[jj - 1] = norm_sq(bss, jj - 1, T2)
                    if 2 <= jj <= 9:
                        norm_ss(bss, jj - 2, sqr[jj - 2], T2)
            slot = ws_get()
            for sub in range(2):
                j = 2 * jj + sub
                bg, bu = nb(), nb()
                proj_fm(bg, slot, 8, 512, sub * 128, hr, hk, T)
                proj_fm(bu, slot, 8, 512, 256 + sub * 128, hr, hk, T)
                r_ = nrot("gb", 3)
                r2 = nrot("gc", 2)
                P.op("act", "activation", out=gb[:, r_, 2:2 + Tp], in_=ps[bg][:, 0:Tp], func=AF.Copy, r=bk(bg),
                     w=[("gb", r_)])
                P.op("dve", "tensor_copy", gb[:, r_, 0:2], halo[:, j, :], r=["halo"], w=[("gb", r_)])
                if t0 == 8:
                    P.op("pool", "tensor_scalar_mul", gb[:, r_, 128:130], gb[:, r_, 128:130], flag[:, 0:1],
                         r=[("gb", r_), "flag"], w=[("gb", r_)])
                P.op("pool", "tensor_copy", halo[:, j, :], gb[:, r_, Tp:Tp + 2], r=[("gb", r_)], w=["halo"])
                convs = [([gb[:, r_, k:k + Tp] for k in range(3)], gc[:, r2, 0:Tp])]
                if ms:
                    g3 = gb[:, r_, Tp + 2:Tp + 74].rearrange("p (b t) -> p b t", t=18)
                    P.op("act", "activation", out=g3[:, :, 2:18],
                         in_=ps[bg][:, C0:C0 + 64].rearrange("p (b t) -> p b t", t=16), func=AF.Copy, r=bk(bg),
                         w=[("gb", r_)])
                    P.op("dve", "tensor_copy", g3[:, :, 0:2], halos[:, j, :, :], r=["halos"], w=[("gb", r_)])
                    P.op("pool", "tensor_copy", ncs_st[:, j, :, :], g3[:, :, 16:18], r=[("gb", r_)], w=["ncs_st"])
                    convs.append(([g3[:, :, k:k + 16] for k in range(3)],
                                  gc[:, r2, C0:C0 + 64].rearrange("p (b t) -> p b t", t=16)))
                for views, gcv in convs:
                    P.op("act", "activation", out=gcv, in_=views[0], func=AF.Identity, scale=cw[:, j, 0:1],
                         bias=cb[:, j:j + 1], r=[("gb", r_), "cw", "cb"], w=[("gc", r2)])
                    for k in (1, 2):
                        P.op("dve", "scalar_tensor_tensor", out=gcv, in0=views[k], scalar=cw[:, j, k:k + 1], in1=gcv,
                             op0=ALU.mult, op1=ALU.add, r=[("gb", r_), ("gc", r2), "cw"], w=[("gc", r2)])
                r3 = nrot("ge", 2)
                P.op("act", "activation", out=ge[:, r3, 0:T], in_=gc[:, r2, 0:T], func=AF.Gelu_apprx_tanh,
                     r=[("gc", r2)], w=[("ge", r3)])
                P.op("dve", "tensor_tensor", out=SA[:, j, 0:T], in0=ps[bu][:, 0:T], in1=ge[:, r3, 0:T], op=ALU.mult,
                     r=bk(bu) + [("ge", r3)], w=[("SA", j)])
            ws_done()
        if nxt is not None:
            with _NextCtx(nxt):
                norm_fin(bss, T2)
                norm_apply(0, T2, g2)
            reserved.discard(bss)
            pro_done.add(nxt)
        for c in range(8):
            slot = ws_get()
            b = nb()
            proj_fm(b, slot, 22, 128, 0, lambda kc: SA[:, kc, 0:T], lambda kc: [("SA", kc)], T)
            for (c0, n, bb) in groups:
                P.op("dve", "scalar_tensor_tensor", out=cur["xT"][:, c, c0:c0 + n], in0=ps[b][:, c0:c0 + n],
                     scalar=modTt[:, cur["l"], 40 + c, bb:bb + 1], in1=cur["xT"][:, c, c0:c0 + n], op0=ALU.mult, op1=ALU.add,
                     r=bk(b) + [("xT", cur["par"], c), ("modT", cur["l"])], w=[("xT", cur["par"], c)])
            ws_done()

        ck(9)
        xk = [("xT", cur["par"], c) for c in range(8)]
        if ms:
            if l == 0:
                P.op("sp", "dma_start", out=xs1, in_=cur["xT"][:, :, C0:C0 + 64], r=xk, w=["xs1"], semkey="xst")
            else:
                P.op("sp", "dma_start", out=ysT_d, in_=cur["xT"][:, :, C0:C0 + 64], r=xk, semkey="xst")
            P.op("sp", "dma_start", out=ncs_d[l], in_=ncs_st[:, :, :, :], r=["ncs_st"], semkey="ncs")
            P.op("sp", "dma_start", out=ncv_d[l], in_=halo[:, :, :], r=["halo"], semkey="ncv")
        if l == 0:
            P.op("sp", "dma_start", out=x1s[:, :, (t0 - 4) * 128:(t0 - 4) * 128 + Tp], in_=cur["xT"][:, :, 0:Tp], r=xk,
                 w=[("x1s", t) for t in tiles], semkey="xst")
        else:
            lo = max(t0, OUT_T0)
            c0 = (lo - t0) * 128
            P.op("sp", "dma_start", out=yT_d[:, :, (lo - OUT_T0) * 128:(lo - OUT_T0) * 128 + Tp - c0],
                 in_=cur["xT"][:, :, c0:Tp], r=xk, semkey="xst")

    BLOCKS = []
    for l in range(2):
        BLOCKS.append((l, "kv", 0 if l == 0 else 4, 4))
        bl_ = (L0_BLOCKS if l == 0 else L1_BLOCKS)
        for k_, (t0, n) in enumerate(bl_):
            BLOCKS.append((l, "fullms" if k_ == len(bl_) - 1 else "full", t0, n))

    try:
        for l in range(2):
            if stop is not None and stop == -1:
                raise _Stop()
            if l == 0:
                ada_setup(0)
            layer_setup(l)
            if stop is not None and stop == 0:
                raise _Stop()
            if l == 1:
                P.op("pool", "memset", halo[:, :, :], 0.0, r=["halo"], w=["halo"])
            for (l_, kind, t0, n) in [b for b in BLOCKS if b[0] == l]:
                if kind == "fullms" and l == 0:
                    ada_setup(1)
                block(l_, kind, t0, n)
                nblk[0] += 1
                if stop is not None and nblk[0] >= stop:
                    raise _Stop()
        assert ws["consumed"] == len(ws["seq"]), (ws["consumed"], len(ws["seq"]))
    except _Stop:
        dbg = dout("dbgx", (128, 8, 512))
        P.op("sp", "dma_start", out=dbg, in_=cur["xT"][:, :, :], r=[("xT", cur["par"], c) for c in range(8)], semkey="dbg")
        dbg2 = dout("dbgh", (128, 8, 512))
        P.op("pool", "dma_start", out=dbg2, in_=tmpn[:, :, :].rearrange("p a b -> p (a b)"), r=[("tmpn", 0), ("tmpn", 1)], semkey="dbg") if False else None

    P.emit(nc, stack)
    stack.close()
    return nc


def _fm(a):
    t, f = a.shape
    return np.ascontiguousarray(a.reshape(t, f // 128, 128).transpose(2, 1, 0))


def _slab(wm):
    k, m = wm.shape
    return np.ascontiguousarray(wm.reshape(k // 128, 128, m).transpose(1, 0, 2)).reshape(128, (k // 128) * m)


_NC_CACHE = {}


def kernel(x_prompt, x_sample, cache_attn_k, cache_attn_v, cache_ffn_conv, c_prompt, c_sample,
           norm1_g, norm2_g, w_ada, b_ada, w_in, q_norm_g, k_norm_g, rel_bias, v_norm_g,
           w_spatial, b_spatial, w_out_a, w_out_b, w_out, w_ffn_in, ffn_conv_w, ffn_conv_b, w_ffn_out):
    f = lambda a: np.asarray(a, dtype=np.float32)
    x_prompt, x_sample, cache_attn_k, cache_attn_v, cache_ffn_conv = map(f, (x_prompt, x_sample, cache_attn_k, cache_attn_v, cache_ffn_conv))
    c_prompt, c_sample, norm1_g, norm2_g, w_ada, b_ada, w_in = map(f, (c_prompt, c_sample, norm1_g, norm2_g, w_ada, b_ada, w_in))
    q_norm_g, k_norm_g, rel_bias, v_norm_g, w_spatial, b_spatial = map(f, (q_norm_g, k_norm_g, rel_bias, v_norm_g, w_spatial, b_spatial))
    w_out_a, w_out_b, w_out, w_ffn_in, ffn_conv_w, ffn_conv_b, w_ffn_out = map(f, (w_out_a, w_out_b, w_out, w_ffn_in, ffn_conv_w, ffn_conv_b, w_ffn_out))

    in_maps = _prep(x_prompt, x_sample, cache_attn_k, cache_attn_v, cache_ffn_conv, c_prompt, c_sample,
                    norm1_g, norm2_g, w_ada, b_ada, w_in, q_norm_g, k_norm_g, rel_bias, v_norm_g,
                    w_spatial, b_spatial, w_out_a, w_out_b, w_out, w_ffn_in, ffn_conv_w, ffn_conv_b, w_ffn_out)
    if "nc" not in _NC_CACHE:
        _NC_CACHE["nc"] = build_nc()
    nc = _NC_CACHE["nc"]
    res = run_bass_kernel_spmd(nc, in_maps, core_ids=list(range(NCORES)))
    return _post(res.results)


def _prep(x_prompt, x_sample, cache_attn_k, cache_attn_v, cache_ffn_conv, c_prompt, c_sample,
          norm1_g, norm2_g, w_ada, b_ada, w_in, q_norm_g, k_norm_g, rel_bias, v_norm_g,
          w_spatial, b_spatial, w_out_a, w_out_b, w_out, w_ffn_in, ffn_conv_w, ffn_conv_b, w_ffn_out):

    wall = np.empty((2, 24, 128, 4096), np.float32)
    wfo = np.empty((2, 8, 128, 2816), np.float32)
    wada = np.empty((2, 12, 128, 4096), np.float32)
    for l in range(2):
        for s in range(9):
            wall[l, s] = _slab(w_in[l][:, s * 512:(s + 1) * 512])
        wall[l, 9] = _slab(w_out_a[l])
        wall[l, 10] = _slab(w_out_b[l])
        for s in range(2):
            wall[l, 11 + s] = _slab(w_out[l][:, s * 512:(s + 1) * 512])
        for jj in range(11):
            idx = np.concatenate([np.arange(2 * jj * 128, (2 * jj + 2) * 128), DFF + np.arange(2 * jj * 128, (2 * jj + 2) * 128)])
            wall[l, 13 + jj] = _slab(w_ffn_in[l][:, idx])
        for c in range(8):
            wfo[l, c] = _slab(w_ffn_out[l][:, c * 128:(c + 1) * 128])
        for s in range(12):
            wada[l, s] = _slab(w_ada[l][:, s * 512:(s + 1) * 512])
    col = lambda v: np.ascontiguousarray(v.reshape(-1, 128).T)
    nrm = np.stack([np.concatenate([col(norm1_g[l]), col(norm2_g[l])], 1) for l in range(2)])
    bada = np.stack([col(b_ada[l]) for l in range(2)])
    qkg = np.stack([np.stack([np.tile(q_norm_g[l], 2), np.tile(k_norm_g[l], 2)], 1) for l in range(2)])
    vg = np.stack([np.broadcast_to(v_norm_g[l][None, :], (128, 512)) for l in range(2)]).copy()
    bsp = b_spatial.reshape(2, 1, 512).copy()
    bsps = np.stack([np.tile(b_spatial[l][:, None, :16], (1, 4, 1)).reshape(1, 256) for l in range(2)])
    wspT = np.ascontiguousarray(w_spatial.transpose(0, 3, 1, 2))
    wspb = np.zeros((2, 64, 4, 64), np.float32)
    for bl in range(4):
        wspb[:, bl * 16:(bl + 1) * 16, :, bl * 16:(bl + 1) * 16] = wspT[:, 0:16, :, 0:16]
    s_i = np.arange(128)[:, None]
    t_i = np.arange(128)[None, :]
    tril = np.broadcast_to((s_i <= t_i).astype(np.float32)[:, None, :], (128, 4, 128)).copy()
    trilb = np.zeros((64, 4, 64), np.float32)
    for bl in range(4):
        trilb[bl * 16:(bl + 1) * 16, :, bl * 16:(bl + 1) * 16] = tril[0:16, :, 0:16]
    cw = np.ascontiguousarray(ffn_conv_w.reshape(2, 3, 22, 128).transpose(0, 3, 2, 1))
    cb = np.ascontiguousarray(ffn_conv_b.reshape(2, 22, 128).transpose(0, 2, 1))
    ki = np.arange(128)[:, None]
    qi = np.arange(128)[None, :]
    idx1 = np.clip(128 + qi - ki, -128, 128) + 128
    idx0 = np.clip(qi - ki, -128, 128) + 128
    Tb = np.stack([np.stack([rel_bias[l][:, idx1].transpose(1, 0, 2), rel_bias[l][:, idx0].transpose(1, 0, 2)], 1)
                   for l in range(2)])
    b256 = np.stack([np.broadcast_to(rel_bias[l][None, :, 256], (128, 8)) for l in range(2)]).copy()

    shared = dict(wall=wall, wfo=wfo, wada=wada, nrm=nrm, bada=bada, qkg=qkg, vg=vg, bsp=bsp, bsps=bsps, wspT=wspT,
                  wspb=wspb, tril=tril, trilb=trilb, cw=cw, cb=cb, Tb=np.ascontiguousarray(Tb), b256=b256)
    shared = {k: np.ascontiguousarray(v, dtype=np.float32) for k, v in shared.items()}

    in_maps = []
    for core in range(NCORES):
        b, seg = core // 4, core % 4
        s0 = seg * SEG
        a = s0 - HALO
        xw = np.zeros((W, D), np.float32)
        lo = max(a, 0)
        xw[lo - a:] = x_prompt[b, lo:s0 + SEG]
        valid = (np.arange(W) + a >= 0).astype(np.float32)
        m = dict(shared)
        m["xin"] = _fm(xw)
        m["xs"] = _fm(x_sample[4 * core:4 * core + 4].reshape(64, D))
        m["vtok"] = np.ascontiguousarray(valid.reshape(NT, 128).T)
        m["flag"] = np.full((128, 1), 1.0 if a >= 0 else 0.0, np.float32)
        cc = np.concatenate([c_prompt[b:b + 1], c_sample[4 * core:4 * core + 4]], 0)
        m["cT"] = np.ascontiguousarray(cc.reshape(5, 8, 128).transpose(2, 1, 0))
        ck = cache_attn_k[:, 4 * core:4 * core + 4]
        m["ckT"] = np.ascontiguousarray(ck.reshape(2, 4, 512, 4, 2, 64).transpose(0, 4, 5, 3, 1, 2)).reshape(2, 128, 4, 4, 512)
        cvv = cache_attn_v[:, 4 * core:4 * core + 4].reshape(2, 4, 4, 128, 512)
        m["cv"] = np.ascontiguousarray(cvv.transpose(0, 1, 3, 2, 4))
        cc2 = cache_ffn_conv[:, 4 * core:4 * core + 4].reshape(2, 4, 2, 22, 128)
        m["cconv"] = np.ascontiguousarray(cc2.transpose(0, 4, 3, 1, 2))
        in_maps.append(m)
    return in_maps


def _post(R):

    y_prompt = np.empty((2, 8192, D), np.float32)
    y_sample = np.empty((32, 16, D), np.float32)
    nkp = np.empty((2, 2, 512, 8, 64), np.float32)
    nvp = np.empty((2, 2, 512, 8, 64), np.float32)
    ncp = np.empty((2, 2, 2, DFF), np.float32)
    nks = np.empty((2, 32, 16, 8, 64), np.float32)
    nvs = np.empty((2, 32, 16, 8, 64), np.float32)
    nbs = np.empty((2, 32, 16, 4, 128), np.float32)
    ncs = np.empty((2, 32, 2, DFF), np.float32)
    for core in range(NCORES):
        r = R[core]
        b, seg = core // 4, core % 4
        y_prompt[b, seg * SEG:(seg + 1) * SEG] = r["yT"].transpose(2, 1, 0).reshape(SEG, D)
        y_sample[4 * core:4 * core + 4] = r["ysT"].transpose(2, 1, 0).reshape(4, 16, D)
        sl = slice(4 * core, 4 * core + 4)
        for l in range(2):
            if seg == 3:
                nkp[l, b] = r["nkT"][l].reshape(2, 64, 4, 512).transpose(3, 2, 0, 1).reshape(512, 8, 64)
                nvp[l, b] = r["nv"][l].reshape(512, 8, 64)
                ncp[l, b] = r["ncv"][l].transpose(2, 1, 0).reshape(2, DFF)
            nks[l, sl] = r["nksT"][l].reshape(2, 64, 4, 4, 16).transpose(3, 4, 2, 0, 1).reshape(4, 16, 8, 64)
            nvs[l, sl] = r["nvs"][l].transpose(1, 0, 2).reshape(4, 16, 8, 64)
            nbs[l, sl] = r["nbs"][l].reshape(4, 16, 4, 128)
            ncs[l, sl] = r["ncs"][l].transpose(2, 3, 1, 0).reshape(4, 2, DFF)
    return (y_prompt, y_sample, nkp, nvp, ncp, nks, nvs, nbs, ncs)
```

```python
import contextlib
import numpy as np
import concourse.bass as bass
import concourse.mybir as mybir
from concourse.bass_utils import run_bass_kernel_spmd

F32 = mybir.dt.float32
BF16 = mybir.dt.bfloat16
AF = mybir.ActivationFunctionType
ALU = mybir.AluOpType
AX = mybir.AxisListType

NCORES = 8
D = 1024
SEG = 2048
HALO = 1152
W = SEG + HALO
NT = W // 128
DFF = 2816
EPS = 1e-6
NB = 4
L0_BLOCKS = [(4, 4), (8, 4), (12, 4), (16, 3), (19, 3), (22, 3)]
L1_BLOCKS = [(8, 4), (12, 4), (16, 3), (19, 3), (22, 3)]
OUT_T0 = 9
KEEP_T0 = 21
_DBG = {}


class Prog:
    STREAMS = ("sp", "act", "dve", "pool", "pe")

    def __init__(self):
        self.ops = []
        self.keyw = {}
        self.keyr = {}
        self.dcnt = {}

    def op(self, stream, method, *args, r=(), w=(), semkey=None, **kw):
        idx = len(self.ops)
        dom = ("d", semkey) if semkey is not None else ("s", stream)
        deps = {}

        def add(d, i):
            if d[0] == "d":
                i = self.dcnt[d]
            if deps.get(d, -1) < i:
                deps[d] = i

        for k in r:
            for d, i in self.keyw.get(k, {}).items():
                add(d, i)
        skip_same = dom[0] == "d" or stream == "pe"
        for k in w:
            for d, i in self.keyr.get(k, {}).items():
                if d == dom and (skip_same or i == idx):
                    continue
                add(d, i)
            for d, i in self.keyw.get(k, {}).items():
                if d == dom and skip_same:
                    continue
                add(d, i)
        for k in r:
            self.keyr.setdefault(k, {})[dom] = idx
        for k in w:
            if self.keyr.get(k):
                self.keyw[k] = {dom: idx}
                self.keyr[k] = {}
            else:
                self.keyw.setdefault(k, {})[dom] = idx
        if dom[0] == "d":
            self.dcnt[dom] = self.dcnt.get(dom, 0) + 16
        self.ops.append(dict(stream=stream, method=method, args=args, kw=kw, dom=dom,
                             deps=list(deps.items()), stage=_DBG.get("stage", "")))
        return idx

    def emit(self, nc, stack):
        ops = self.ops
        needs = set()
        for o in ops:
            for d, i in o["deps"]:
                if d[0] == "s":
                    needs.add(i)
        cnt = {}
        sems = {}
        issuer = {}
        for i, o in enumerate(ops):
            d = o["dom"]
            if d[0] == "d":
                cnt[d] = cnt.get(d, 0) + 16
                o["done"] = cnt[d]
                assert issuer.setdefault(d, o["stream"]) == o["stream"], d
            elif i in needs:
                cnt[d] = cnt.get(d, 0) + 1
                o["done"] = cnt[d]
            else:
                o["done"] = None
        for d in cnt:
            sems[d] = stack.enter_context(nc.semaphore("s%d" % len(sems)))
        block = stack.enter_context(nc.Block())
        self.nsem = len(sems)

        def run(stream, eng):
            waited = {}
            for o in ops:
                if o["stream"] != stream:
                    continue
                for d, i in o["deps"]:
                    v = i if d[0] == "d" else ops[i]["done"]
                    if waited.get(d, 0) < v:
                        eng.wait_ge(sems[d], v)
                        waited[d] = v
                        _DBG.setdefault("waits", {}).setdefault(stream, []).append((o["stage"], d, (ops[i]["method"], ops[i]["stage"], str(ops[i]["kw"].get("out", ops[i]["args"][:1]))[:90]) if d[0] == "s" else None, o["method"]))
                ins = getattr(eng, o["method"])(*o["args"], **o["kw"])
                if o["done"] is not None:
                    d = o["dom"]
                    ins.then_inc(sems[d], 16 if d[0] == "d" else 1)
            for d, v in cnt.items():
                if d[0] == "d" and issuer[d] == stream and waited.get(d, 0) < v:
                    eng.wait_ge(sems[d], v)

        @block.sync
        def _(e):
            run("sp", e)

        @block.scalar
        def _(e):
            run("act", e)

        @block.vector
        def _(e):
            run("dve", e)

        @block.gpsimd
        def _(e):
            run("pool", e)

        @block.tensor
        def _(e):
            run("pe", e)


def build_nc(stop=None):
    class _Stop(Exception):
        pass

    nblk = [0]

    def ck(st):
        _DBG["stage"] = st + (10 if _DBG.get("stage", 0) >= 10 else 0)
        if stop is not None and nblk[0] + st / 10.0 >= stop - 1e-9 and stop > 0:
            raise _Stop()

    nc = bass.Bass("TRN2", target_bir_lowering=False)
    P = Prog()
    stack = contextlib.ExitStack()

    def din(name, shape):
        return nc.dram_tensor(name, list(shape), F32, kind="ExternalInput").ap()

    def dout(name, shape):
        return nc.dram_tensor(name, list(shape), F32, kind="ExternalOutput").ap()

    def dint(name, shape, dt):
        return nc.dram_tensor(name, list(shape), dt, kind="Internal").ap()

    xin = din("xin", (128, 8, W))
    xs_d = din("xs", (128, 8, 64))
    vtok_d = din("vtok", (128, NT))
    flag_d = din("flag", (128, 1))
    cT_d = din("cT", (128, 8, 5))
    ckT_d = din("ckT", (2, 128, 4, 4, 512))
    cv_d = din("cv", (2, 4, 128, 4, 512))
    cconv_d = din("cconv", (2, 128, 22, 4, 2))
    wall_d = din("wall", (2, 24, 128, 4096))
    wfo_d = din("wfo", (2, 8, 128, 2816))
    wada_d = din("wada", (2, 12, 128, 4096))
    nrm_d = din("nrm", (2, 128, 16))
    bada_d = din("bada", (2, 128, 48))
    qkg_d = din("qkg", (2, 128, 2))
    vg_d = din("vg", (2, 128, 512))
    bsp_d = din("bsp", (2, 1, 512))
    bsps_d = din("bsps", (2, 1, 256))
    wspT_d = din("wspT", (2, 128, 4, 128))
    wspb_d = din("wspb", (2, 64, 4, 64))
    tril_d = din("tril", (128, 4, 128))
    trilb_d = din("trilb", (64, 4, 64))
    cw_d = din("cw", (2, 128, 22, 3))
    cb_d = din("cb", (2, 128, 22))
    Tb_d = din("Tb", (2, 128, 2, 8, 128))
    b256_d = din("b256", (2, 128, 8))

    yT_d = dout("yT", (128, 8, SEG))
    ysT_d = dout("ysT", (128, 8, 64))
    nkT_d = dout("nkT", (2, 128, 4, 512))
    nv_d = dout("nv", (2, 4, 128, 512))
    ncv_d = dout("ncv", (2, 128, 22, 2))
    nksT_d = dout("nksT", (2, 128, 4, 64))
    nvs_d = dout("nvs", (2, 16, 4, 512))
    nbs_d = dout("nbs", (2, 64, 512))
    ncs_d = dout("ncs", (2, 128, 22, 4, 2))

    wsc = dint("wsc", (2, 24, 128, 4096), BF16)
    wsc_fo = dint("wscfo", (2, 8, 128, 2816), BF16)
    x1s = dint("x1s", (128, 8, 21 * 128), F32)
    xs1 = dint("xs1", (128, 8, 64), F32)

    def sb(name, shape, dt=F32):
        return stack.enter_context(nc.sbuf_tensor("sb_" + name, list(shape), dt))

    xTt = sb("xT", (128, 2, 8, 512))
    cur = {"xT": xTt[:, 0], "par": 0, "idx": 0}
    hT = sb("hT", (128, 8, 512), BF16)
    sq = sb("sq", (128, 3, 512), BF16)
    rstd = sb("rstd", (128, 512))
    tmpn = sb("tmpn", (128, 2, 512))
    qz = sb("qz", (128, 4, 4, 2, 128), BF16)
    kst = sb("kst", (128, 2, 512))
    rs = sb("rs", (128, 2, 512))
    kT = sb("kT", (128, 4, 1024), BF16)
    vt = sb("vt", (128, 8, 512), BF16)
    ubT = sb("ubT", (128, 4, 512), BF16)
    vbn = sb("vbn", (128, 4, 512), BF16)
    gl = sb("gl", (128, 2, 512))
    ssv = sb("ssv", (128, 4))
    SA = sb("SA", (128, 24, 512), BF16)
    oaT = sb("oaT", (128, 4, 512), BF16)
    obT = sb("obT", (128, 4, 512), BF16)
    Pt = sb("Pt", (128, 2, 5, 256), BF16)
    rden = sb("rden", (128, 2, 256))
    gb = sb("gb", (128, 3, 516))
    gc = sb("gc", (128, 2, 512))
    ge = sb("ge", (128, 2, 512))
    halo = sb("halo", (128, 22, 2))
    halos = sb("halos", (128, 22, 4, 2))
    ncs_st = sb("ncs_st", (128, 22, 4, 2))
    wslab = sb("wslab", (128, NB, 4096), BF16)
    ones_bf = sb("ones_bf", (128, 128), BF16)
    M0 = sb("M0", (128, 256), BF16)
    blk64 = sb("blk64", (128, 128), BF16)
    ones_row = sb("ones_row", (1, 128), BF16)
    vones = sb("vones", (128, 9, 128), BF16)
    vtok = sb("vtok", (128, NT))
    flag = sb("flag", (128, 1))
    cTs = sb("cTs", (128, 8, 5))
    cs = sb("cs", (128, 8, 5), BF16)
    E = sb("E", (128, 2, 8, 128), BF16)
    negb = sb("negb", (128, 8))
    modTt = sb("modT", (128, 2, 48, 5))
    A12t = sb("A12", (128, 2, 2, 8, 5))
    nrmt = sb("nrm", (128, 2, 16))
    badat = sb("bada", (128, 2, 48))
    qkg = sb("qkg", (128, 2))
    vg = sb("vg", (128, 512))
    bsp = sb("bsp", (1, 512), BF16)
    bsps = sb("bsps", (1, 256), BF16)
    WcT = sb("WcT", (128, 4, 128), BF16)
    Wblk = sb("Wblk", (64, 4, 64), BF16)
    cw = sb("cw", (128, 22, 3))
    cb = sb("cb", (128, 22))
    kTs = sb("kTs", (128, 4, 64), BF16)
    vbns = sb("vbns", (64, 512), BF16)
    ckb = sb("ckb", (128, 1, 4, 512), BF16)
    cvb = sb("cvb", (128, 1, 4, 512), BF16)
    Pts = sb("Pts", (128, 2, 2, 80), BF16)
    exs = sb("exs", (128, 2, 2, 32))
    rdens = sb("rdens", (128, 2, 32))

    psd = [stack.enter_context(nc.psum_tensor("ps%d" % i, [128, 1024], F32)) for i in range(4)]
    ps = [psd[i // 2][:, (i % 2) * 512:(i % 2 + 1) * 512] for i in range(8)]

    def bk(i):
        return [("ps", i)]

    bank_ctr = [0]

    reserved = set()

    def nb():
        while True:
            b = bank_ctr[0] % 8
            bank_ctr[0] += 1
            if b not in reserved:
                return b

    rot = {}

    def nrot(name, n):
        v = rot.get(name, 0)
        rot[name] = v + 1
        return v % n

    ws = dict(seq=[], issued=0, consumed=0)

    casted = set()

    def ws_issue(upto):
        while ws["issued"] < min(upto, len(ws["seq"])):
            kind_, l_, s_ = ws["seq"][ws["issued"]]
            slot = ws["issued"] % NB
            wk, sk = [("wslab", slot)], ("w", slot)
            if kind_ == "ada":
                P.op("pool", "dma_start", out=wslab[:, slot, :], in_=wada_d[l_, s_], w=wk, semkey=sk)
            else:
                ncols = 4096 if kind_ == "wall" else 2816
                src32 = wall_d[l_, s_] if kind_ == "wall" else wfo_d[l_, s_]
                scr = wsc[l_, s_] if kind_ == "wall" else wsc_fo[l_, s_]
                key = ("wsc", kind_, l_, s_)
                if key not in casted:
                    casted.add(key)
                    P.op("pool", "dma_start", out=wslab[:, slot, 0:ncols], in_=src32, w=wk, semkey=sk)
                    P.op("sp", "dma_start", out=scr, in_=wslab[:, slot, 0:ncols], r=wk, w=[key],
                         semkey=("wst", slot))
                else:
                    P.op("pool", "dma_start", out=wslab[:, slot, 0:ncols], in_=scr, r=[key], w=wk, semkey=sk)
            ws["issued"] += 1

    def ws_get():
        assert ws["consumed"] < len(ws["seq"])
        ws_issue(ws["consumed"] + 1)
        slot = ws["consumed"] % NB
        ws["consumed"] += 1
        return slot

    def ws_done():
        ws_issue(ws["consumed"] + NB)

    def seq_ada(l):
        for s_ in range(12):
            ws["seq"].append(("ada", l, s_))

    def seq_block(l, kind):
        if kind == "kv":
            for s_ in (1, 2):
                ws["seq"].append(("wall", l, s_))
            return
        for s_ in range(24):
            ws["seq"].append(("wall", l, s_))
        for s_ in range(8):
            ws["seq"].append(("fo", l, s_))

    seq_ada(0)
    for l in range(2):
        seq_block(l, "kv")
        nfull = len(L0_BLOCKS if l == 0 else L1_BLOCKS)
        for k_ in range(nfull):
            if l == 0 and k_ == nfull - 1:
                seq_ada(1)
            seq_block(l, "full")

    P.op("pool", "memset", ones_bf[:, :], 1.0, w=["ones_bf"])
    P.op("pool", "memset", blk64[:, :], 0.0, w=["blk64"])
    P.op("pool", "memset", blk64[0:64, 0:64], 1.0, w=["blk64"])
    P.op("pool", "memset", blk64[64:128, 64:128], 1.0, w=["blk64"])
    P.op("pool", "memset", ones_row[:, :], 1.0, w=["ones_row"])
    P.op("pool", "memset", M0[:, :], 1.0, w=["M0"])
    P.op("pool", "memset", M0[0:64, :].rearrange("p (h q) -> p h q", h=2)[:, :, 64:128], 0.0, w=["M0"])
    P.op("pool", "memset", Pt[:, :, :, :].rearrange("p a b c -> p (a b c)"), 0.0, w=[("Pt", 0), ("Pt", 1)])
    P.op("pool", "memset", qz[:, :, :, :, :].rearrange("p a b c d -> p (a b c d)"), 0.0, w=[("qT", c) for c in range(4)])
    P.op("pool", "memset", halo[:, :, :], 0.0, w=["halo"])
    P.op("sp", "dma_start", out=vtok[:, :], in_=vtok_d, w=["vtok"], semkey="c0")
    P.op("sp", "dma_start", out=flag[:, :], in_=flag_d, w=["flag"], semkey="c0")
    P.op("sp", "dma_start", out=cTs[:, :, :], in_=cT_d, w=["cTs"], semkey="c0")
    P.op("act", "activation", out=cs[:, :, :], in_=cTs[:, :, :], func=AF.Silu, r=["cTs"], w=["cs"])
    for t in range(9):
        P.op("act", "activation", out=vones[:, t, :], in_=ones_bf[:, :], func=AF.Copy,
             scale=vtok[:, t:t + 1], r=["ones_bf", "vtok"], w=["vones"])
    def layer_setup(l):
        for dst, src, key in ((qkg, qkg_d[l], "qkg"),
                              (vg, vg_d[l], "vg"), (cw, cw_d[l], "cw"), (cb, cb_d[l], "cb"),
                              (negb, b256_d[l], "negb")):
            full = tuple(slice(None) for _ in dst.shape)
            P.op("sp", "dma_start", out=dst[full], in_=src, w=[key], semkey="ls")
        P.op("sp", "dma_start", out=tmpn[:, 0, :].rearrange("p (h q) -> p h q", h=4), in_=Tb_d[l][:, 0, 0:4, :],
             w=[("tmpn", 0)], semkey="ls")
        P.op("sp", "dma_start", out=tmpn[:, 1, :].rearrange("p (h q) -> p h q", h=4), in_=Tb_d[l][:, 0, 4:8, :],
             w=[("tmpn", 1)], semkey="ls")
        P.op("sp", "dma_start", out=rs[:, 0, :].rearrange("p (h q) -> p h q", h=4), in_=Tb_d[l][:, 1, 0:4, :],
             w=[("rs", 0)], semkey="ls")
        P.op("sp", "dma_start", out=rs[:, 1, :].rearrange("p (h q) -> p h q", h=4), in_=Tb_d[l][:, 1, 4:8, :],
             w=[("rs", 1)], semkey="ls")
        P.op("sp", "dma_start", out=halos[:, :, :, :], in_=cconv_d[l], w=["halos"], semkey="ls")
        P.op("sp", "dma_start", out=gl[0:1, 0, :], in_=bsp_d[l], w=[("gl", 0)], semkey="ls")
        P.op("sp", "dma_start", out=gl[0:1, 1, 0:256], in_=bsps_d[l], w=[("gl", 1)], semkey="ls")
        P.op("sp", "dma_start", out=gc[:, 0, :], in_=wspT_d[l].rearrange("p g t -> p (g t)"),
             w=[("gc", 0)], semkey="ls3")
        P.op("sp", "dma_start", out=gc[:, 1, :], in_=tril_d.rearrange("p g t -> p (g t)"),
             w=[("gc", 1)], semkey="ls3")
        P.op("sp", "dma_start", out=ge[0:64, 0, 0:256], in_=wspb_d[l].rearrange("p g t -> p (g t)"),
             w=[("ge", 0)], semkey="ls3")
        P.op("sp", "dma_start", out=ge[0:64, 1, 0:256], in_=trilb_d.rearrange("p g t -> p (g t)"),
             w=[("ge", 1)], semkey="ls3")
        P.op("dve", "tensor_tensor", out=WcT[:, :, :].rearrange("p g t -> p (g t)"), in0=gc[:, 0, :],
             in1=gc[:, 1, :], op=ALU.mult, r=[("gc", 0), ("gc", 1)], w=["WcT"])
        P.op("dve", "tensor_tensor", out=Wblk[:, :, :].rearrange("p g t -> p (g t)"), in0=ge[0:64, 0, 0:256],
             in1=ge[0:64, 1, 0:256], op=ALU.mult, r=[("ge", 0), ("ge", 1)], w=["Wblk"])
        P.op("dve", "tensor_copy", bsp[:, :], gl[0:1, 0, :], r=[("gl", 0)], w=["bsp"])
        P.op("dve", "tensor_copy", bsps[:, :], gl[0:1, 1, 0:256], r=[("gl", 1)], w=["bsps"])
        P.op("dve", "tensor_scalar_mul", qkg[:, 1:2], qkg[:, 1:2], 8.0, r=["qkg"], w=["qkg"])
        P.op("dve", "tensor_scalar_mul", negb[:, :], negb[:, :], -1.0, r=["negb"], w=["negb"])
        for t in range(2):
            for h in range(8):
                stg = (tmpn if t == 0 else rs)
                skey = ("tmpn" if t == 0 else "rs", h // 4)
                P.op("act", "activation", out=E[:, t, h, :], in_=stg[:, h // 4, (h % 4) * 128:(h % 4 + 1) * 128],
                     func=AF.Exp, bias=negb[:, h:h + 1], scale=1.0, r=[skey, "negb"], w=["E"])
        P.op("pool", "memset", E[64:128, 1, :, 0:64], 0.0, r=["E"], w=["E"])

    def ada_setup(l):
        modT, A12, nrm, bada = modTt[:, l], A12t[:, l], nrmt[:, l], badat[:, l]
        P.op("sp", "dma_start", out=nrm, in_=nrm_d[l], w=[("nrm", l)], semkey=("lsa", l))
        P.op("sp", "dma_start", out=bada, in_=bada_d[l], w=[("bada", l)], semkey=("lsa", l))
        b = nb()
        for s_ in range(12):
            slot = ws_get()
            for m in range(4):
                ci = s_ * 4 + m
                for kc in range(8):
                    P.op("pe", "matmul", ps[b][:, ci * 5:(ci + 1) * 5],
                         lhsT=wslab[:, slot, kc * 512 + m * 128: kc * 512 + (m + 1) * 128],
                         rhs=cs[:, kc, :], start=(kc == 0), stop=(kc == 7),
                         r=[("wslab", slot), "cs"], w=bk(b))
            ws_done()
        for bb in range(5):
            P.op("dve", "tensor_tensor", out=modT[:, :, bb],
                 in0=ps[b][:, 0:240].rearrange("p (c b) -> p c b", b=5)[:, :, bb], in1=bada,
                 op=ALU.add, r=bk(b) + [("bada", l)], w=[("modT", l)])
        for which in range(2):
            sc0 = 8 if which == 0 else 32
            for bb in range(5):
                P.op("dve", "tensor_scalar", out=A12[:, which, :, bb], in0=modT[:, sc0:sc0 + 8, bb],
                     scalar1=1.0, scalar2=32.0, op0=ALU.add, op1=ALU.mult, r=[("modT", l)], w=[("A12", l)])
                P.op("dve", "tensor_tensor", out=A12[:, which, :, bb], in0=A12[:, which, :, bb],
                     in1=nrm[:, which * 8:(which + 1) * 8], op=ALU.mult, r=[("A12", l), ("nrm", l)],
                     w=[("A12", l)])

    def norm_sq(b, c, T):
        r_ = nrot("sq", 3)
        P.op("act", "activation", out=sq[:, r_, 0:T], in_=cur["xT"][:, c, 0:T], func=AF.Square,
             r=[("xT", cur["par"], c)], w=[("sq", r_)])
        return r_

    def norm_ss(b, c, r_, T):
        P.op("pe", "matmul", ps[b][:, 0:T], lhsT=ones_bf[:, :], rhs=sq[:, r_, 0:T],
             start=(c == 0), stop=(c == 7), r=[("sq", r_), "ones_bf"], w=bk(b))

    def norm_fin(b, T):
        P.op("act", "activation", out=rstd[:, 0:T], in_=ps[b][:, 0:T], func=AF.Sqrt, bias=float(D * EPS),
             scale=1.0, r=bk(b), w=["rstd"])
        P.op("dve", "reciprocal", out=rstd[:, 0:T], in_=rstd[:, 0:T], r=["rstd"], w=["rstd"])

    def norm_apply(which, T, groups):
        shc = 0 if which == 0 else 24
        for c in range(8):
            r_ = nrot("tmpn", 2)
            for (c0, n, bb) in groups:
                P.op("dve", "scalar_tensor_tensor", out=tmpn[:, r_, c0:c0 + n], in0=cur["xT"][:, c, c0:c0 + n],
                     scalar=A12t[:, cur["l"], which, c, bb:bb + 1], in1=rstd[:, c0:c0 + n], op0=ALU.mult, op1=ALU.mult,
                     r=[("xT", cur["par"], c), ("A12", cur["l"]), "rstd"], w=[("tmpn", r_)])
            for (c0, n, bb) in groups:
                P.op("act", "activation", out=hT[:, c, c0:c0 + n], in_=tmpn[:, r_, c0:c0 + n],
                     func=AF.Identity, bias=modTt[:, cur["l"], shc + c, bb:bb + 1], scale=1.0,
                     r=[("tmpn", r_), ("modT", cur["l"])], w=[("hT", c)])

    def norm_mod(which, T, groups):
        b = nb()
        for c in range(8):
            r_ = norm_sq(b, c, T)
            norm_ss(b, c, r_, T)
        norm_fin(b, T)
        norm_apply(which, T, groups)

    pro_done = set()

    def blk_geom(i):
        l_, kind_, t0_, n_ = BLOCKS[i]
        Tp_ = n_ * 128
        if kind_ == "fullms":
            return l_, Tp_ + 64, [(0, Tp_, 0)] + [(Tp_ + bl * 16, 16, 1 + bl) for bl in range(4)]
        return l_, Tp_, [(0, Tp_, 0)]

    def pro_ok(i):
        return i < len(BLOCKS)

    class _NextCtx:
        def __init__(self, i):
            self.i = i

        def __enter__(self):
            self.save = dict(cur)
            cur["l"] = BLOCKS[self.i][0]
            cur["par"] = self.i % 2
            cur["xT"] = xTt[:, self.i % 2]

        def __exit__(self, *a):
            cur.update(self.save)

    def proj_fm(b, slot, nkc, ms, col0, rhs_t, rkeys, T):
        for kc in range(nkc):
            P.op("pe", "matmul", ps[b][:, 0:T], lhsT=wslab[:, slot, kc * ms + col0: kc * ms + col0 + 128],
                 rhs=rhs_t(kc), start=(kc == 0), stop=(kc == nkc - 1),
                 r=[("wslab", slot)] + rkeys(kc), w=bk(b))

    def hn_a(b, T):
        r_ = nrot("sq", 3)
        P.op("act", "activation", out=sq[:, r_, 0:T], in_=ps[b][:, 0:T], func=AF.Square, r=bk(b), w=[("sq", r_)])
        return r_

    def hn_b(b, r_, T, gcol, out_ap, out_keys, qchunk=None, Tp=None, ms=False):
        b2 = nb()
        P.op("pe", "matmul", ps[b2][:, 0:T], lhsT=blk64[:, :], rhs=sq[:, r_, 0:T], start=True, stop=True,
             r=[("sq", r_), "blk64"], w=bk(b2))
        r2 = nrot("rs", 2)
        P.op("act", "activation", out=rs[:, r2, 0:T], in_=ps[b2][:, 0:T], func=AF.Sqrt, bias=float(64 * EPS),
             scale=1.0, r=bk(b2), w=[("rs", r2)])
        P.op("dve", "reciprocal", out=rs[:, r2, 0:T], in_=rs[:, r2, 0:T], r=[("rs", r2)], w=[("rs", r2)])
        if qchunk is not None:
            nq = Tp // 128
            for hh in range(2):
                hs = slice(hh * 64, (hh + 1) * 64)
                P.op("dve", "scalar_tensor_tensor", out=qz[hs, qchunk, 0:nq, hh, :],
                     in0=ps[b][hs, 0:Tp].rearrange("p (t q) -> p t q", t=nq), scalar=qkg[hs, gcol:gcol + 1],
                     in1=rs[hs, r2, 0:Tp].rearrange("p (t q) -> p t q", t=nq), op0=ALU.mult, op1=ALU.mult,
                     r=bk(b) + [("rs", r2), "qkg"], w=out_keys)
                if ms:
                    P.op("dve", "scalar_tensor_tensor", out=qz[hs, qchunk, 3, hh, 0:64], in0=ps[b][hs, Tp:Tp + 64],
                         scalar=qkg[hs, gcol:gcol + 1], in1=rs[hs, r2, Tp:Tp + 64], op0=ALU.mult, op1=ALU.mult,
                         r=bk(b) + [("rs", r2), "qkg"], w=out_keys)
            return
        P.op("dve", "scalar_tensor_tensor", out=out_ap, in0=ps[b][:, 0:T], scalar=qkg[:, gcol:gcol + 1],
             in1=rs[:, r2, 0:T], op0=ALU.mult, op1=ALU.mult, r=bk(b) + [("rs", r2), "qkg"], w=out_keys)

    def hT_r(T):
        return (lambda kc: hT[:, kc, 0:T]), (lambda kc: [("hT", kc)])

    def block(l, kind, t0, ntile):
        sample = False
        ms = kind == "fullms"
        Tp = ntile * 128
        C0 = Tp
        T = Tp + (64 if ms else 0)
        tiles = list(range(t0, t0 + ntile))
        groups = [(0, Tp, 0)] + ([(C0 + bl * 16, 16, 1 + bl) for bl in range(4)] if ms else [])
        hr, hk = hT_r(T)
        bi = cur["idx"]
        cur["l"] = l
        _DBG["stage"] = 0
        cur["par"] = bi % 2
        cur["xT"] = xTt[:, bi % 2]

        def xload(i):
            l_, kind_, t0_, n_ = BLOCKS[i]
            par_ = i % 2
            wk = [("xT", par_, c) for c in range(8)]
            T_ = n_ * 128
            if kind_ == "fullms":
                if l_ == 1:
                    P.op("sp", "dma_start", out=xTt[:, par_, :, T_:T_ + 64], in_=xs1, r=["xs1"], w=wk,
                         semkey=("xld", par_))
                else:
                    P.op("sp", "dma_start", out=xTt[:, par_, :, T_:T_ + 64], in_=xs_d, w=wk, semkey=("xld", par_))
            if l_ == 0:
                P.op("sp", "dma_start", out=xTt[:, par_, :, 0:T_], in_=xin[:, :, t0_ * 128: t0_ * 128 + T_], w=wk,
                     semkey=("xld", par_))
            else:
                P.op("sp", "dma_start", out=xTt[:, par_, :, 0:T_],
                     in_=x1s[:, :, (t0_ - 4) * 128: (t0_ - 4) * 128 + T_],
                     r=[("x1s", t) for t in range(t0_, t0_ + n_)], w=wk, semkey=("xld", par_))
            return True

        if bi == 0:
            xload(0)
        if bi + 1 < len(BLOCKS):
            xload(bi + 1)
        cur["idx"] = bi + 1
        if bi not in pro_done:
            norm_mod(0, T, groups)

        qk = ([("q", c) for c in range(4)] if kind != "kv" else []) + [("k", c) for c in range(4)]
        st_ = {}

        def qk_a(n):
            which, c = qk[n]
            if c == 0:
                if which == "k" and kind != "kv":
                    ws_done()
                st_["slot"] = ws_get()
            b = nb()
            proj_fm(b, st_["slot"], 8, 512, c * 128, hr, hk, T)
            return b, hn_a(b, T)

        def qk_b(n, b, r_sq):
            which, c = qk[n]
            if which == "q":
                hn_b(b, r_sq, T, 0, None, [("qT", c)], qchunk=c, Tp=Tp, ms=ms)
                return
            r_ = nrot("kst", 2)
            hn_b(b, r_sq, T, 1, kst[:, r_, 0:T], [("kst", r_)])
            if ms:
                P.op("act", "activation", out=kTs[:, c, :], in_=kst[:, r_, C0:C0 + 64], func=AF.Copy,
                     r=[("kst", r_)], w=["kTs"])
                P.op("sp", "dma_start", out=nksT_d[l, :, c, :], in_=kst[:, r_, C0:C0 + 64], r=[("kst", r_)],
                     semkey=("kst", r_))
            for ti, t in enumerate(tiles):
                sl = t % 8
                P.op("act", "activation", out=kT[:, c, sl * 128:(sl + 1) * 128],
                     in_=kst[:, r_, ti * 128:(ti + 1) * 128], func=AF.Copy, r=[("kst", r_)], w=[("kT", sl)])
                if t >= KEEP_T0:
                    P.op("sp", "dma_start", out=nkT_d[l, :, c, (t - KEEP_T0) * 128:(t - KEEP_T0 + 1) * 128],
                         in_=kst[:, r_, ti * 128:(ti + 1) * 128], r=[("kst", r_)], semkey=("kst", r_))

        pend = qk_a(0)
        for n in range(len(qk)):
            nxt_ = qk_a(n + 1) if n + 1 < len(qk) else None
            qk_b(n, *pend)
            pend = nxt_
        ws_done()
        slot = ws_get()
        if ms:
            for bl in range(4):
                b = nb()
                for kc in range(8):
                    P.op("pe", "matmul", ps[b][0:16, 0:512], lhsT=hT[:, kc, C0 + bl * 16:C0 + (bl + 1) * 16],
                         rhs=wslab[:, slot, kc * 512:(kc + 1) * 512], start=(kc == 0), stop=(kc == 7),
                         r=[("wslab", slot), ("hT", kc)], w=bk(b))
                r_ = nrot("gl", 2)
                P.op("act", "activation", out=gl[0:16, r_, :], in_=ps[b][0:16, 0:512], func=AF.Copy, r=bk(b),
                     w=[("gl", r_)])
                P.op("dve", "tensor_copy", SA[0:16, 16 + bl, :], gl[0:16, r_, :], r=[("gl", r_)], w=[("SA", 16 + bl)])
                P.op("sp", "dma_start", out=nvs_d[l, :, bl, :], in_=gl[0:16, r_, :], r=[("gl", r_)],
                     semkey=("gl", r_))
        if True:
            for ti, t in enumerate(tiles):
                b = nb()
                for kc in range(8):
                    P.op("pe", "matmul", ps[b][:, 0:512], lhsT=hT[:, kc, ti * 128:(ti + 1) * 128],
                         rhs=wslab[:, slot, kc * 512:(kc + 1) * 512], start=(kc == 0), stop=(kc == 7),
                         r=[("wslab", slot), ("hT", kc)], w=bk(b))
                sl = t % 8
                if t >= KEEP_T0:
                    r_ = nrot("gl", 2)
                    P.op("act", "activation", out=gl[:, r_, :], in_=ps[b][:, 0:512], func=AF.Copy, r=bk(b),
                         w=[("gl", r_)])
                    P.op("dve", "tensor_scalar_mul", vt[:, sl, :], gl[:, r_, :], vtok[:, t:t + 1],
                         r=[("gl", r_), "vtok"], w=[("vt", sl)])
                    P.op("sp", "dma_start", out=nv_d[l, t - KEEP_T0], in_=gl[:, r_, :], r=[("gl", r_)],
                         semkey=("gl", r_))
                else:
                    P.op("dve", "tensor_scalar_mul", vt[:, sl, :], ps[b][:, 0:512], vtok[:, t:t + 1],
                         r=bk(b) + ["vtok"], w=[("vt", sl)])
        ws_done()
        if kind == "kv":
            if pro_ok(bi + 1):
                with _NextCtx(bi + 1):
                    l2, T2, g2 = blk_geom(bi + 1)
                    norm_mod(0, T2, g2)
                pro_done.add(bi + 1)
            return

        ck(1)
        slot = ws_get()
        for c in range(4):
            b = nb()
            proj_fm(b, slot, 8, 512, c * 128, hr, hk, T)
            P.op("act", "activation", out=ubT[:, c, 0:T], in_=ps[b][:, 0:T], func=AF.Gelu_apprx_tanh, r=bk(b),
                 w=[("ubT", c)])
        ws_done()
        ck(2)
        slot = ws_get()
        tl = [(ti, 128, False) for ti in range(ntile)] + ([(ntile, 64, True)] if ms else [])
        for ti, M, sample in tl:
            b = nb()
            for kc in range(8):
                P.op("pe", "matmul", ps[b][0:M, 0:512], lhsT=hT[:, kc, ti * 128: ti * 128 + M],
                     rhs=wslab[:, slot, kc * 512:(kc + 1) * 512], start=(kc == 0), stop=(kc == 7),
                     r=[("wslab", slot), ("hT", kc)], w=bk(b))
            r_ = nrot("gl", 2)
            P.op("act", "activation", out=gl[0:M, r_, :], in_=ps[b][0:M, 0:512], func=AF.Gelu_apprx_tanh, r=bk(b),
                 w=[("gl", r_)])
            r2 = nrot("tmpn", 2)
            P.op("dve", "tensor_tensor", out=tmpn[0:M, r2, :], in0=gl[0:M, r_, :], in1=gl[0:M, r_, :], op=ALU.mult,
                 r=[("gl", r_)], w=[("tmpn", r2)])
            r3 = nrot("ssv", 4)
            P.op("dve", "reduce_sum", out=ssv[0:M, r3:r3 + 1], in_=tmpn[0:M, r2, :], axis=AX.X,
                 r=[("tmpn", r2)], w=[("ssv", r3)])
            P.op("act", "activation", out=ssv[0:M, r3:r3 + 1], in_=ssv[0:M, r3:r3 + 1], func=AF.Sqrt,
                 bias=float(EPS), scale=1.0 / 512.0, r=[("ssv", r3)], w=[("ssv", r3)])
            P.op("dve", "reciprocal", out=ssv[0:M, r3:r3 + 1], in_=ssv[0:M, r3:r3 + 1], r=[("ssv", r3)],
                 w=[("ssv", r3)])
            if sample:
                P.op("dve", "scalar_tensor_tensor", out=tmpn[0:64, r2, :], in0=gl[0:64, r_, :],
                     scalar=ssv[0:64, r3:r3 + 1], in1=vg[0:64, :], op0=ALU.mult, op1=ALU.mult,
                     r=[("gl", r_), ("ssv", r3), "vg"], w=[("tmpn", r2)])
                P.op("act", "activation", out=vbns[:, :], in_=tmpn[0:64, r2, :], func=AF.Copy, r=[("tmpn", r2)],
                     w=["vbns"])
                P.op("sp", "dma_start", out=nbs_d[l], in_=tmpn[0:64, r2, :], r=[("tmpn", r2)], semkey=("tmpn", r2))
            else:
                P.op("dve", "scalar_tensor_tensor", out=vbn[:, ti, :], in0=gl[:, r_, :], scalar=ssv[:, r3:r3 + 1],
                     in1=vg[:, :], op0=ALU.mult, op1=ALU.mult, r=[("gl", r_), ("ssv", r3), "vg"], w=[("vbn", ti)])
        ws_done()
        sample = False
        ck(3)
        for s in range(4):
            slot = ws_get()
            for m in range(4):
                c = s * 4 + m
                b = nb()
                proj_fm(b, slot, 8, 512, m * 128, hr, hk, T)
                P.op("act", "activation", out=SA[:, c, 0:T], in_=ps[b][:, 0:T], func=AF.Sigmoid, r=bk(b),
                     w=[("SA", c)])
            ws_done()

        ck(4)
        if ms:
            items = [(bl, hp) for bl in range(4) for hp in range(4)]

            def ss1(it, st):
                bl, hp = it
                cr = 0
                for j in range(5):
                    for hh in range(2):
                        hs = slice(hh * 64, (hh + 1) * 64)
                        b = st * 4 + hh
                        if j < 4:
                            P.op("pe", "matmul", ps[b][:, j * 16:(j + 1) * 16],
                                 lhsT=ckb[hs, cr, hp, j * 128:(j + 1) * 128], rhs=qz[hs, hp, 3, hh, bl * 16:(bl + 1) * 16],
                                 start=True, stop=True, r=[("ckb", cr), ("qT", hp)], w=[("ps", b)])
                        else:
                            P.op("pe", "matmul", ps[b][0:16, 64:80],
                                 lhsT=kTs[hs, hp, bl * 16:(bl + 1) * 16], rhs=qz[hs, hp, 3, hh, bl * 16:(bl + 1) * 16],
                                 start=True, stop=True, r=["kTs", ("qT", hp)], w=[("ps", b)])
                for hh in range(2):
                    b = st * 4 + hh
                    h = 2 * hp + hh
                    P.op("act", "activation", out=Pts[:, st, hh, 0:48], in_=ps[b][:, 0:48], func=AF.Exp,
                         r=[("ps", b)], w=[("Pts", st, hh)])
                    P.op("act", "activation", out=exs[:, st, hh, 0:16], in_=ps[b][:, 48:64], func=AF.Exp,
                         r=[("ps", b)], w=[("exs", st, hh)])
                    P.op("act", "activation", out=exs[0:16, st, hh, 16:32], in_=ps[b][0:16, 64:80], func=AF.Exp,
                         r=[("ps", b)], w=[("exs", st, hh)])
                    P.op("dve", "tensor_tensor", out=Pts[:, st, hh, 48:64], in0=exs[:, st, hh, 0:16],
                         in1=E[:, 0, h, 0:16], op=ALU.mult, r=[("exs", st, hh), "E"], w=[("Pts", st, hh)])
                    P.op("dve", "tensor_tensor", out=Pts[0:16, st, hh, 64:80], in0=exs[0:16, st, hh, 16:32],
                         in1=E[0:16, 1, h, 0:16], op=ALU.mult, r=[("exs", st, hh), "E"], w=[("Pts", st, hh)])

            def ss2(it, st):
                bl, hp = it
                cr = 0
                bd, bo = st * 4 + 2, st * 4 + 3
                for hh in range(2):
                    for j in range(5):
                        if j < 4:
                            P.op("pe", "matmul", ps[bd][:, hh * 16:(hh + 1) * 16], lhsT=ones_bf[:, :],
                                 rhs=Pts[:, st, hh, j * 16:(j + 1) * 16], start=(j == 0), stop=False,
                                 r=[("Pts", st, hh), "ones_bf"], w=[("ps", bd)])
                        else:
                            P.op("pe", "matmul", ps[bd][:, hh * 16:(hh + 1) * 16], lhsT=ones_bf[0:16, :],
                                 rhs=Pts[0:16, st, hh, 64:80], start=False, stop=True,
                                 r=[("Pts", st, hh), "ones_bf"], w=[("ps", bd)])
                for hh in range(2):
                    hs = slice(hh * 64, (hh + 1) * 64)
                    fc = (2 * hp + hh) * 64
                    for j in range(5):
                        if j < 4:
                            P.op("pe", "matmul", ps[bo][hs, 0:16], lhsT=cvb[:, cr, j, fc:fc + 64],
                                 rhs=Pts[:, st, hh, j * 16:(j + 1) * 16], start=(j == 0), stop=False,
                                 r=[("Pts", st, hh), ("cvb", cr)], w=[("ps", bo)])
                        else:
                            P.op("pe", "matmul", ps[bo][hs, 0:16], lhsT=SA[0:16, 16 + bl, fc:fc + 64],
                                 rhs=Pts[0:16, st, hh, 64:80], start=False, stop=True,
                                 r=[("Pts", st, hh), ("SA", 16 + bl)], w=[("ps", bo)])
                P.op("dve", "reciprocal", out=rdens[:, st, :], in_=ps[bd][:, 0:32], r=[("ps", bd)],
                     w=[("rdens", st)])
                for hh in range(2):
                    hs = slice(hh * 64, (hh + 1) * 64)
                    P.op("dve", "tensor_tensor", out=oaT[hs, hp, C0 + bl * 16:C0 + (bl + 1) * 16], in0=ps[bo][hs, 0:16],
                         in1=rdens[hs, st, hh * 16:(hh + 1) * 16], op=ALU.mult, r=[("ps", bo), ("rdens", st)],
                         w=[("oaT", hp)])
        if True:
            items = [(qi, hp) for qi in range(ntile) for hp in range(4)]

            def s1(it, st):
                qi, hp = it
                qt = t0 + qi
                bS, b4 = st * 4, st * 4 + 2
                for j in range(5):
                    sl = (qt - 4 + j) % 8
                    if j < 4:
                        out_ = psd[st * 2][:, j * 256:(j + 1) * 256]
                        wkey = ("ps", bS + j // 2)
                    else:
                        out_ = ps[b4][:, 0:256]
                        wkey = ("ps", b4)
                    P.op("pe", "matmul", out_, lhsT=kT[:, hp, sl * 128:(sl + 1) * 128],
                         rhs=qz[:, hp, qi, :, :].rearrange("p h q -> p (h q)"), start=True, stop=True,
                         r=[("kT", sl), ("qT", hp)], w=[wkey])
                pk = [("Pt", st)]
                rk2 = [("ps", bS), ("ps", bS + 1)]
                two = "p (h q) -> p h q"
                P.op("act", "activation", out=Pt[:, st, 0:4, :].rearrange("p j c -> p (j c)"),
                     in_=psd[st * 2][:, 0:1024], func=AF.Exp, r=rk2, w=pk)
                P.op("act", "activation", out=Pt[:, st, 4, :], in_=ps[b4][:, 0:256], func=AF.Exp,
                     r=[("ps", b4)], w=pk)
                P.op("dve", "tensor_tensor", out=Pt[:, st, 0, :], in0=Pt[:, st, 0, :], in1=M0[:, :], op=ALU.mult,
                     r=pk + ["M0"], w=pk)
                P.op("dve", "tensor_tensor", out=Pt[:, st, 3:5, :].rearrange("p t (h q) -> p t h q", h=2),
                     in0=Pt[:, st, 3:5, :].rearrange("p t (h q) -> p t h q", h=2),
                     in1=E[:, :, 2 * hp:2 * hp + 2, :], op=ALU.mult, r=pk + ["E"], w=pk)

            def s2(it, st):
                qi, hp = it
                qt = t0 + qi
                bd, bo = st * 4 + 2, st * 4 + 3
                for j in range(5):
                    kt = qt - 4 + j
                    lw = vones[:, kt, :] if kt < 9 else ones_bf[:, :]
                    P.op("pe", "matmul", ps[bd][:, 256:512], lhsT=lw, rhs=Pt[:, st, j, :], start=(j == 0),
                         stop=(j == 4), r=[("Pt", st), "vones", "ones_bf"], w=[("ps", bd)])
                for hh in range(2):
                    hs = slice(hh * 64, (hh + 1) * 64)
                    fc = (2 * hp + hh) * 64
                    for j in range(5):
                        sl = (qt - 4 + j) % 8
                        P.op("pe", "matmul", ps[bo][hs, 0:128], lhsT=vt[:, sl, fc:fc + 64],
                             rhs=Pt[:, st, j, hh * 128:(hh + 1) * 128], start=(j == 0), stop=(j == 4),
                             r=[("Pt", st), ("vt", sl)], w=[("ps", bo)])
                P.op("dve", "tensor_scalar_max", rden[:, st, :], ps[bd][:, 256:512], 1e-30, r=[("ps", bd)],
                     w=[("rden", st)])
                P.op("dve", "reciprocal", out=rden[:, st, :], in_=rden[:, st, :], r=[("rden", st)],
                     w=[("rden", st)])
                for hh in range(2):
                    hs = slice(hh * 64, (hh + 1) * 64)
                    P.op("dve", "tensor_tensor", out=oaT[hs, hp, qi * 128:(qi + 1) * 128], in0=ps[bo][hs, 0:128],
                         in1=rden[hs, st, hh * 128:(hh + 1) * 128], op=ALU.mult, r=[("ps", bo), ("rden", st)],
                         w=[("oaT", hp)])

        for i in range(len(items) + 1):
            if i < len(items):
                s1(items[i], i % 2)
            if i >= 1:
                s2(items[i - 1], (i - 1) % 2)
        if ms:
            for bl in range(4):
                its = [(bl, hp) for hp in range(4)]
                P.op("pool", "dma_start", out=ckb[:, 0, :, :], in_=ckT_d[l, :, :, bl, :], w=[("ckb", 0)],
                     semkey=("ckb", 0))
                P.op("pool", "dma_start", out=cvb[:, 0, :, :], in_=cv_d[l, bl], w=[("cvb", 0)],
                     semkey=("cvb", 0))
                for i in range(len(its) + 1):
                    if i < len(its):
                        ss1(its[i], i % 2)
                    if i >= 1:
                        ss2(its[i - 1], (i - 1) % 2)

        ck(5)
        if ms:
            b = nb()
            for g in range(4):
                P.op("pe", "matmul", ps[b][:, g * 64:(g + 1) * 64], lhsT=vbns[0:64, g * 128:(g + 1) * 128],
                     rhs=Wblk[0:64, g, :], start=True, stop=False, r=["vbns", "Wblk"], w=bk(b))
                P.op("pe", "matmul", ps[b][:, g * 64:(g + 1) * 64], lhsT=ones_row[0:1, :],
                     rhs=bsps[0:1, g * 64:(g + 1) * 64], start=False, stop=True, r=["ones_row", "bsps"], w=bk(b))
            P.op("dve", "tensor_tensor", out=obT[:, :, C0:C0 + 64], in0=ps[b][:, 0:256].rearrange("p (g t) -> p g t", g=4),
                 in1=ubT[:, :, C0:C0 + 64], op=ALU.mult, r=bk(b) + [("ubT", c) for c in range(4)],
                 w=[("obT", c) for c in range(4)])
        if True:
            for ti in range(ntile):
                b = nb()
                for g in range(4):
                    P.op("pe", "matmul", ps[b][:, g * 128:(g + 1) * 128], lhsT=vbn[:, ti, g * 128:(g + 1) * 128],
                         rhs=WcT[:, g, :], start=True, stop=False, r=[("vbn", ti), "WcT"], w=bk(b))
                    P.op("pe", "matmul", ps[b][:, g * 128:(g + 1) * 128], lhsT=ones_row[0:1, :],
                         rhs=bsp[0:1, g * 128:(g + 1) * 128], start=False, stop=True, r=["ones_row", "bsp"], w=bk(b))
                P.op("dve", "tensor_tensor", out=obT[:, :, ti * 128:(ti + 1) * 128],
                     in0=ps[b][:, 0:512].rearrange("p (g t) -> p g t", g=4), in1=ubT[:, :, ti * 128:(ti + 1) * 128],
                     op=ALU.mult, r=bk(b) + [("ubT", c) for c in range(4)], w=[("obT", c) for c in range(4)])

        ck(6)
        sa_, sb_ = ws_get(), ws_get()
        for c in range(8):
            ba, bb_ = nb(), nb()
            proj_fm(ba, sa_, 4, 1024, c * 128, lambda kc: oaT[:, kc, 0:T], lambda kc: [("oaT", kc)], T)
            proj_fm(bb_, sb_, 4, 1024, c * 128, lambda kc: obT[:, kc, 0:T], lambda kc: [("obT", kc)], T)
            P.op("dve", "tensor_tensor", out=tmpn[:, 0, 0:T], in0=ps[ba][:, 0:T], in1=SA[:, c, 0:T], op=ALU.mult,
                 r=bk(ba) + [("SA", c)], w=[("tmpn", 0)])
            P.op("dve", "tensor_tensor", out=tmpn[:, 1, 0:T], in0=ps[bb_][:, 0:T], in1=SA[:, 8 + c, 0:T], op=ALU.mult,
                 r=bk(bb_) + [("SA", 8 + c)], w=[("tmpn", 1)])
            P.op("dve", "tensor_tensor", out=SA[:, 16 + c, 0:T], in0=tmpn[:, 0, 0:T], in1=tmpn[:, 1, 0:T], op=ALU.add,
                 r=[("tmpn", 0), ("tmpn", 1)], w=[("SA", 16 + c)])
        ws_done()
        ws_done()
        ck(7)
        for s in range(2):
            slot = ws_get()
            for m in range(4):
                c = s * 4 + m
                b = nb()
                proj_fm(b, slot, 8, 512, m * 128, lambda kc: SA[:, 16 + kc, 0:T], lambda kc: [("SA", 16 + kc)], T)
                for (c0, n, bb) in groups:
                    P.op("dve", "scalar_tensor_tensor", out=cur["xT"][:, c, c0:c0 + n], in0=ps[b][:, c0:c0 + n],
                         scalar=modTt[:, cur["l"], 16 + c, bb:bb + 1], in1=cur["xT"][:, c, c0:c0 + n], op0=ALU.mult, op1=ALU.add,
                         r=bk(b) + [("xT", cur["par"], c), ("modT", cur["l"])], w=[("xT", cur["par"], c)])
            ws_done()

        ck(8)
        norm_mod(1, T, groups)
        nxt = bi + 1 if pro_ok(bi + 1) else None
        if nxt is not None:
            l2, T2, g2 = blk_geom(nxt)
            bss = nb()
            reserved.add(bss)
            sqr = {}
        fst = {}

        def ffn_a(j):
            jj, sub = divmod(j, 2)
            if sub == 0:
                if nxt is not None:
                    with _NextCtx(nxt):
                        if 1 <= jj <= 8:
                            sqr[jj - 1] = norm_sq(bss, jj - 1, T2)
                        if 2 <= jj <= 9:
                            norm_ss(bss, jj - 2, sqr[jj - 2], T2)
                if jj > 0:
                    ws_done()
                fst["slot"] = ws_get()
            slot = fst["slot"]
            bg, bu = nb(), nb()
            proj_fm(bg, slot, 8, 512, sub * 128, hr, hk, T)
            proj_fm(bu, slot, 8, 512, 256 + sub * 128, hr, hk, T)
            r_ = nrot("gb", 3)
            r2 = nrot("gc", 2)
            P.op("act", "activation", out=gb[:, r_, 2:2 + Tp], in_=ps[bg][:, 0:Tp], func=AF.Copy, r=bk(bg),
                 w=[("gb", r_)])
            P.op("dve", "tensor_copy", gb[:, r_, 0:2], halo[:, j, :], r=["halo"], w=[("gb", r_)])
            if t0 == 8:
                P.op("pool", "tensor_scalar_mul", gb[:, r_, 128:130], gb[:, r_, 128:130], flag[:, 0:1],
                     r=[("gb", r_), "flag"], w=[("gb", r_)])
            P.op("pool", "tensor_copy", halo[:, j, :], gb[:, r_, Tp:Tp + 2], r=[("gb", r_)], w=["halo"])
            convs = [([gb[:, r_, k:k + Tp] for k in range(3)], gc[:, r2, 0:Tp])]
            if ms:
                g3 = gb[:, r_, Tp + 2:Tp + 74].rearrange("p (b t) -> p b t", t=18)
                P.op("act", "activation", out=g3[:, :, 2:18],
                     in_=ps[bg][:, C0:C0 + 64].rearrange("p (b t) -> p b t", t=16), func=AF.Copy, r=bk(bg),
                     w=[("gb", r_)])
                P.op("dve", "tensor_copy", g3[:, :, 0:2], halos[:, j, :, :], r=["halos"], w=[("gb", r_)])
                P.op("pool", "tensor_copy", ncs_st[:, j, :, :], g3[:, :, 16:18], r=[("gb", r_)], w=["ncs_st"])
                convs.append(([g3[:, :, k:k + 16] for k in range(3)],
                              gc[:, r2, C0:C0 + 64].rearrange("p (b t) -> p b t", t=16)))
            for views, gcv in convs:
                P.op("act", "activation", out=gcv, in_=views[0], func=AF.Identity, scale=cw[:, j, 0:1],
                     bias=cb[:, j:j + 1], r=[("gb", r_), "cw", "cb"], w=[("gc", r2)])
            return bu, r_, r2, convs

        def ffn_b(j, bu, r_, r2, convs):
            for views, gcv in convs:
                for k in (1, 2):
                    P.op("dve", "scalar_tensor_tensor", out=gcv, in0=views[k], scalar=cw[:, j, k:k + 1], in1=gcv,
                         op0=ALU.mult, op1=ALU.add, r=[("gb", r_), ("gc", r2), "cw"], w=[("gc", r2)])
            r3 = nrot("ge", 2)
            P.op("act", "activation", out=ge[:, r3, 0:T], in_=gc[:, r2, 0:T], func=AF.Gelu_apprx_tanh,
                 r=[("gc", r2)], w=[("ge", r3)])
            P.op("dve", "tensor_tensor", out=SA[:, j, 0:T], in0=ps[bu][:, 0:T], in1=ge[:, r3, 0:T], op=ALU.mult,
                 r=bk(bu) + [("ge", r3)], w=[("SA", j)])

        pend = ffn_a(0)
        for j in range(22):
            nxt_a = ffn_a(j + 1) if j + 1 < 22 else None
            ffn_b(j, *pend)
            pend = nxt_a
        ws_done()
        if nxt is not None:
            with _NextCtx(nxt):
                norm_fin(bss, T2)
                norm_apply(0, T2, g2)
            reserved.discard(bss)
            pro_done.add(nxt)
        for c in range(8):
            slot = ws_get()
            b = nb()
            proj_fm(b, slot, 22, 128, 0, lambda kc: SA[:, kc, 0:T], lambda kc: [("SA", kc)], T)
            for (c0, n, bb) in groups:
                P.op("dve", "scalar_tensor_tensor", out=cur["xT"][:, c, c0:c0 + n], in0=ps[b][:, c0:c0 + n],
                     scalar=modTt[:, cur["l"], 40 + c, bb:bb + 1], in1=cur["xT"][:, c, c0:c0 + n], op0=ALU.mult, op1=ALU.add,
                     r=bk(b) + [("xT", cur["par"], c), ("modT", cur["l"])], w=[("xT", cur["par"], c)])
            ws_done()

        ck(9)
        xk = [("xT", cur["par"], c) for c in range(8)]
        if ms:
            if l == 0:
                P.op("sp", "dma_start", out=xs1, in_=cur["xT"][:, :, C0:C0 + 64], r=xk, w=["xs1"], semkey="xst")
            else:
                P.op("sp", "dma_start", out=ysT_d, in_=cur["xT"][:, :, C0:C0 + 64], r=xk, semkey="xst")
            P.op("sp", "dma_start", out=ncs_d[l], in_=ncs_st[:, :, :, :], r=["ncs_st"], semkey="ncs")
            P.op("sp", "dma_start", out=ncv_d[l], in_=halo[:, :, :], r=["halo"], semkey="ncv")
        if l == 0:
            P.op("sp", "dma_start", out=x1s[:, :, (t0 - 4) * 128:(t0 - 4) * 128 + Tp], in_=cur["xT"][:, :, 0:Tp], r=xk,
                 w=[("x1s", t) for t in tiles], semkey="xst")
        else:
            lo = max(t0, OUT_T0)
            c0 = (lo - t0) * 128
            P.op("sp", "dma_start", out=yT_d[:, :, (lo - OUT_T0) * 128:(lo - OUT_T0) * 128 + Tp - c0],
                 in_=cur["xT"][:, :, c0:Tp], r=xk, semkey="xst")

    BLOCKS = []
    for l in range(2):
        BLOCKS.append((l, "kv", 0 if l == 0 else 4, 4))
        bl_ = (L0_BLOCKS if l == 0 else L1_BLOCKS)
        for k_, (t0, n) in enumerate(bl_):
            BLOCKS.append((l, "fullms" if k_ == len(bl_) - 1 else "full", t0, n))

    try:
        for l in range(2):
            if stop is not None and stop == -1:
                raise _Stop()
            if l == 0:
                ada_setup(0)
            layer_setup(l)
            if stop is not None and stop == 0:
                raise _Stop()
            if l == 1:
                P.op("pool", "memset", halo[:, :, :], 0.0, r=["halo"], w=["halo"])
            for (l_, kind, t0, n) in [b for b in BLOCKS if b[0] == l]:
                if kind == "fullms" and l == 0:
                    ada_setup(1)
                block(l_, kind, t0, n)
                nblk[0] += 1
                if stop is not None and nblk[0] >= stop:
                    raise _Stop()
        assert ws["consumed"] == len(ws["seq"]), (ws["consumed"], len(ws["seq"]))
    except _Stop:
        dbg = dout("dbgx", (128, 8, 512))
        P.op("sp", "dma_start", out=dbg, in_=cur["xT"][:, :, :], r=[("xT", cur["par"], c) for c in range(8)], semkey="dbg")
        dbg2 = dout("dbgh", (128, 8, 512))
        P.op("pool", "dma_start", out=dbg2, in_=tmpn[:, :, :].rearrange("p a b -> p (a b)"), r=[("tmpn", 0), ("tmpn", 1)], semkey="dbg") if False else None

    P.emit(nc, stack)
    stack.close()
    return nc


def _fm(a):
    t, f = a.shape
    return np.ascontiguousarray(a.reshape(t, f // 128, 128).transpose(2, 1, 0))


def _slab(wm):
    k, m = wm.shape
    return np.ascontiguousarray(wm.reshape(k // 128, 128, m).transpose(1, 0, 2)).reshape(128, (k // 128) * m)


_NC_CACHE = {}


def kernel(x_prompt, x_sample, cache_attn_k, cache_attn_v, cache_ffn_conv, c_prompt, c_sample,
           norm1_g, norm2_g, w_ada, b_ada, w_in, q_norm_g, k_norm_g, rel_bias, v_norm_g,
           w_spatial, b_spatial, w_out_a, w_out_b, w_out, w_ffn_in, ffn_conv_w, ffn_conv_b, w_ffn_out):
    f = lambda a: np.asarray(a, dtype=np.float32)
    x_prompt, x_sample, cache_attn_k, cache_attn_v, cache_ffn_conv = map(f, (x_prompt, x_sample, cache_attn_k, cache_attn_v, cache_ffn_conv))
    c_prompt, c_sample, norm1_g, norm2_g, w_ada, b_ada, w_in = map(f, (c_prompt, c_sample, norm1_g, norm2_g, w_ada, b_ada, w_in))
    q_norm_g, k_norm_g, rel_bias, v_norm_g, w_spatial, b_spatial = map(f, (q_norm_g, k_norm_g, rel_bias, v_norm_g, w_spatial, b_spatial))
    w_out_a, w_out_b, w_out, w_ffn_in, ffn_conv_w, ffn_conv_b, w_ffn_out = map(f, (w_out_a, w_out_b, w_out, w_ffn_in, ffn_conv_w, ffn_conv_b, w_ffn_out))

    in_maps = _prep(x_prompt, x_sample, cache_attn_k, cache_attn_v, cache_ffn_conv, c_prompt, c_sample,
                    norm1_g, norm2_g, w_ada, b_ada, w_in, q_norm_g, k_norm_g, rel_bias, v_norm_g,
                    w_spatial, b_spatial, w_out_a, w_out_b, w_out, w_ffn_in, ffn_conv_w, ffn_conv_b, w_ffn_out)
    if "nc" not in _NC_CACHE:
        _NC_CACHE["nc"] = build_nc()
    nc = _NC_CACHE["nc"]
    res = run_bass_kernel_spmd(nc, in_maps, core_ids=list(range(NCORES)))
    return _post(res.results)


def _prep(x_prompt, x_sample, cache_attn_k, cache_attn_v, cache_ffn_conv, c_prompt, c_sample,
          norm1_g, norm2_g, w_ada, b_ada, w_in, q_norm_g, k_norm_g, rel_bias, v_norm_g,
          w_spatial, b_spatial, w_out_a, w_out_b, w_out, w_ffn_in, ffn_conv_w, ffn_conv_b, w_ffn_out):

    wall = np.empty((2, 24, 128, 4096), np.float32)
    wfo = np.empty((2, 8, 128, 2816), np.float32)
    wada = np.empty((2, 12, 128, 4096), np.float32)
    for l in range(2):
        for s in range(9):
            wall[l, s] = _slab(w_in[l][:, s * 512:(s + 1) * 512])
        wall[l, 9] = _slab(w_out_a[l])
        wall[l, 10] = _slab(w_out_b[l])
        for s in range(2):
            wall[l, 11 + s] = _slab(w_out[l][:, s * 512:(s + 1) * 512])
        for jj in range(11):
            idx = np.concatenate([np.arange(2 * jj * 128, (2 * jj + 2) * 128), DFF + np.arange(2 * jj * 128, (2 * jj + 2) * 128)])
            wall[l, 13 + jj] = _slab(w_ffn_in[l][:, idx])
        for c in range(8):
            wfo[l, c] = _slab(w_ffn_out[l][:, c * 128:(c + 1) * 128])
        for s in range(12):
            wada[l, s] = _slab(w_ada[l][:, s * 512:(s + 1) * 512])
    col = lambda v: np.ascontiguousarray(v.reshape(-1, 128).T)
    nrm = np.stack([np.concatenate([col(norm1_g[l]), col(norm2_g[l])], 1) for l in range(2)])
    bada = np.stack([col(b_ada[l]) for l in range(2)])
    qkg = np.stack([np.stack([np.tile(q_norm_g[l], 2), np.tile(k_norm_g[l], 2)], 1) for l in range(2)])
    vg = np.stack([np.broadcast_to(v_norm_g[l][None, :], (128, 512)) for l in range(2)]).copy()
    bsp = b_spatial.reshape(2, 1, 512).copy()
    bsps = np.stack([np.tile(b_spatial[l][:, None, :16], (1, 4, 1)).reshape(1, 256) for l in range(2)])
    wspT = np.ascontiguousarray(w_spatial.transpose(0, 3, 1, 2))
    wspb = np.zeros((2, 64, 4, 64), np.float32)
    for bl in range(4):
        wspb[:, bl * 16:(bl + 1) * 16, :, bl * 16:(bl + 1) * 16] = wspT[:, 0:16, :, 0:16]
    s_i = np.arange(128)[:, None]
    t_i = np.arange(128)[None, :]
    tril = np.broadcast_to((s_i <= t_i).astype(np.float32)[:, None, :], (128, 4, 128)).copy()
    trilb = np.zeros((64, 4, 64), np.float32)
    for bl in range(4):
        trilb[bl * 16:(bl + 1) * 16, :, bl * 16:(bl + 1) * 16] = tril[0:16, :, 0:16]
    cw = np.ascontiguousarray(ffn_conv_w.reshape(2, 3, 22, 128).transpose(0, 3, 2, 1))
    cb = np.ascontiguousarray(ffn_conv_b.reshape(2, 22, 128).transpose(0, 2, 1))
    ki = np.arange(128)[:, None]
    qi = np.arange(128)[None, :]
    idx1 = np.clip(128 + qi - ki, -128, 128) + 128
    idx0 = np.clip(qi - ki, -128, 128) + 128
    Tb = np.stack([np.stack([rel_bias[l][:, idx1].transpose(1, 0, 2), rel_bias[l][:, idx0].transpose(1, 0, 2)], 1)
                   for l in range(2)])
    b256 = np.stack([np.broadcast_to(rel_bias[l][None, :, 256], (128, 8)) for l in range(2)]).copy()

    shared = dict(wall=wall, wfo=wfo, wada=wada, nrm=nrm, bada=bada, qkg=qkg, vg=vg, bsp=bsp, bsps=bsps, wspT=wspT,
                  wspb=wspb, tril=tril, trilb=trilb, cw=cw, cb=cb, Tb=np.ascontiguousarray(Tb), b256=b256)
    shared = {k: np.ascontiguousarray(v, dtype=np.float32) for k, v in shared.items()}

    in_maps = []
    for core in range(NCORES):
        b, seg = core // 4, core % 4
        s0 = seg * SEG
        a = s0 - HALO
        xw = np.zeros((W, D), np.float32)
        lo = max(a, 0)
        xw[lo - a:] = x_prompt[b, lo:s0 + SEG]
        valid = (np.arange(W) + a >= 0).astype(np.float32)
        m = dict(shared)
        m["xin"] = _fm(xw)
        m["xs"] = _fm(x_sample[4 * core:4 * core + 4].reshape(64, D))
        m["vtok"] = np.ascontiguousarray(valid.reshape(NT, 128).T)
        m["flag"] = np.full((128, 1), 1.0 if a >= 0 else 0.0, np.float32)
        cc = np.concatenate([c_prompt[b:b + 1], c_sample[4 * core:4 * core + 4]], 0)
        m["cT"] = np.ascontiguousarray(cc.reshape(5, 8, 128).transpose(2, 1, 0))
        ck = cache_attn_k[:, 4 * core:4 * core + 4]
        m["ckT"] = np.ascontiguousarray(ck.reshape(2, 4, 512, 4, 2, 64).transpose(0, 4, 5, 3, 1, 2)).reshape(2, 128, 4, 4, 512)
        cvv = cache_attn_v[:, 4 * core:4 * core + 4].reshape(2, 4, 4, 128, 512)
        m["cv"] = np.ascontiguousarray(cvv.transpose(0, 1, 3, 2, 4))
        cc2 = cache_ffn_conv[:, 4 * core:4 * core + 4].reshape(2, 4, 2, 22, 128)
        m["cconv"] = np.ascontiguousarray(cc2.transpose(0, 4, 3, 1, 2))
        in_maps.append(m)
    return in_maps


def _post(R):

    y_prompt = np.empty((2, 8192, D), np.float32)
    y_sample = np.empty((32, 16, D), np.float32)
    nkp = np.empty((2, 2, 512, 8, 64), np.float32)
    nvp = np.empty((2, 2, 512, 8, 64), np.float32)
    ncp = np.empty((2, 2, 2, DFF), np.float32)
    nks = np.empty((2, 32, 16, 8, 64), np.float32)
    nvs = np.empty((2, 32, 16, 8, 64), np.float32)
    nbs = np.empty((2, 32, 16, 4, 128), np.float32)
    ncs = np.empty((2, 32, 2, DFF), np.float32)
    for core in range(NCORES):
        r = R[core]
        b, seg = core // 4, core % 4
        y_prompt[b, seg * SEG:(seg + 1) * SEG] = r["yT"].transpose(2, 1, 0).reshape(SEG, D)
        y_sample[4 * core:4 * core + 4] = r["ysT"].transpose(2, 1, 0).reshape(4, 16, D)
        sl = slice(4 * core, 4 * core + 4)
        for l in range(2):
            if seg == 3:
                nkp[l, b] = r["nkT"][l].reshape(2, 64, 4, 512).transpose(3, 2, 0, 1).reshape(512, 8, 64)
                nvp[l, b] = r["nv"][l].reshape(512, 8, 64)
                ncp[l, b] = r["ncv"][l].transpose(2, 1, 0).reshape(2, DFF)
            nks[l, sl] = r["nksT"][l].reshape(2, 64, 4, 4, 16).transpose(3, 4, 2, 0, 1).reshape(4, 16, 8, 64)
            nvs[l, sl] = r["nvs"][l].transpose(1, 0, 2).reshape(4, 16, 8, 64)
            nbs[l, sl] = r["nbs"][l].reshape(4, 16, 4, 128)
            ncs[l, sl] = r["ncs"][l].transpose(2, 3, 1, 0).reshape(4, 2, DFF)
    return (y_prompt, y_sample, nkp, nvp, ncp, nks, nvs, nbs, ncs)
```

```python
import contextlib
import numpy as np
import concourse.bass as bass
import concourse.mybir as mybir
from concourse.bass_utils import run_bass_kernel_spmd

F32 = mybir.dt.float32
BF16 = mybir.dt.bfloat16
AF = mybir.ActivationFunctionType
ALU = mybir.AluOpType
AX = mybir.AxisListType

NCORES = 8
D = 1024
SEG = 2048
HALO = 1152
W = SEG + HALO
NT = W // 128
DFF = 2816
EPS = 1e-6
NB = 4
L0_BLOCKS = [(4, 4), (8, 4), (12, 4), (16, 3), (19, 3), (22, 3)]
L1_BLOCKS = [(8, 4), (12, 4), (16, 3), (19, 3), (22, 3)]
OUT_T0 = 9
KEEP_T0 = 21
_DBG = {}


class Prog:
    STREAMS = ("sp", "act", "dve", "pool", "pe")

    def __init__(self):
        self.ops = []
        self.keyw = {}
        self.keyr = {}
        self.dcnt = {}

    def op(self, stream, method, *args, r=(), w=(), semkey=None, **kw):
        idx = len(self.ops)
        dom = ("d", semkey) if semkey is not None else ("s", stream)
        deps = {}

        def add(d, i):
            if d[0] == "d":
                i = self.dcnt[d]
            if deps.get(d, -1) < i:
                deps[d] = i

        for k in r:
            for d, i in self.keyw.get(k, {}).items():
                add(d, i)
        skip_same = dom[0] == "d" or stream == "pe"
        for k in w:
            for d, i in self.keyr.get(k, {}).items():
                if d == dom and (skip_same or i == idx):
                    continue
                add(d, i)
            for d, i in self.keyw.get(k, {}).items():
                if d == dom and skip_same:
                    continue
                add(d, i)
        for k in r:
            self.keyr.setdefault(k, {})[dom] = idx
        for k in w:
            if self.keyr.get(k):
                self.keyw[k] = {dom: idx}
                self.keyr[k] = {}
            else:
                self.keyw.setdefault(k, {})[dom] = idx
        if dom[0] == "d":
            self.dcnt[dom] = self.dcnt.get(dom, 0) + 16
        self.ops.append(dict(stream=stream, method=method, args=args, kw=kw, dom=dom,
                             deps=list(deps.items()), stage=_DBG.get("stage", "")))
        return idx

    def emit(self, nc, stack):
        ops = self.ops
        needs = set()
        for o in ops:
            for d, i in o["deps"]:
                if d[0] == "s":
                    needs.add(i)
        cnt = {}
        sems = {}
        issuer = {}
        for i, o in enumerate(ops):
            d = o["dom"]
            if d[0] == "d":
                cnt[d] = cnt.get(d, 0) + 16
                o["done"] = cnt[d]
                assert issuer.setdefault(d, o["stream"]) == o["stream"], d
            elif i in needs:
                cnt[d] = cnt.get(d, 0) + 1
                o["done"] = cnt[d]
            else:
                o["done"] = None
        for d in cnt:
            sems[d] = stack.enter_context(nc.semaphore("s%d" % len(sems)))
        block = stack.enter_context(nc.Block())
        self.nsem = len(sems)

        def run(stream, eng):
            waited = {}
            for o in ops:
                if o["stream"] != stream:
                    continue
                for d, i in o["deps"]:
                    v = i if d[0] == "d" else ops[i]["done"]
                    if waited.get(d, 0) < v:
                        eng.wait_ge(sems[d], v)
                        waited[d] = v
                        _DBG.setdefault("waits", {}).setdefault(stream, []).append((o["stage"], d, (ops[i]["method"], ops[i]["stage"], str(ops[i]["kw"].get("out", ops[i]["args"][:1]))[:90]) if d[0] == "s" else None, o["method"]))
                ins = getattr(eng, o["method"])(*o["args"], **o["kw"])
                if o["done"] is not None:
                    d = o["dom"]
                    ins.then_inc(sems[d], 16 if d[0] == "d" else 1)
            for d, v in cnt.items():
                if d[0] == "d" and issuer[d] == stream and waited.get(d, 0) < v:
                    eng.wait_ge(sems[d], v)

        @block.sync
        def _(e):
            run("sp", e)

        @block.scalar
        def _(e):
            run("act", e)

        @block.vector
        def _(e):
            run("dve", e)

        @block.gpsimd
        def _(e):
            run("pool", e)

        @block.tensor
        def _(e):
            run("pe", e)


def build_nc(stop=None):
    class _Stop(Exception):
        pass

    nblk = [0]

    def ck(st):
        _DBG["stage"] = st + (10 if _DBG.get("stage", 0) >= 10 else 0)
        if stop is not None and nblk[0] + st / 10.0 >= stop - 1e-9 and stop > 0:
            raise _Stop()

    nc = bass.Bass("TRN2", target_bir_lowering=False)
    P = Prog()
    stack = contextlib.ExitStack()

    def din(name, shape):
        return nc.dram_tensor(name, list(shape), F32, kind="ExternalInput").ap()

    def dout(name, shape):
        return nc.dram_tensor(name, list(shape), F32, kind="ExternalOutput").ap()

    def dint(name, shape, dt):
        return nc.dram_tensor(name, list(shape), dt, kind="Internal").ap()

    xin = din("xin", (128, 8, W))
    xs_d = din("xs", (128, 8, 64))
    vtok_d = din("vtok", (128, NT))
    flag_d = din("flag", (128, 1))
    cT_d = din("cT", (128, 8, 5))
    ckT_d = din("ckT", (2, 128, 4, 4, 512))
    cv_d = din("cv", (2, 4, 128, 4, 512))
    cconv_d = din("cconv", (2, 128, 22, 4, 2))
    wall_d = din("wall", (2, 24, 128, 4096))
    wfo_d = din("wfo", (2, 8, 128, 2816))
    wada_d = din("wada", (2, 12, 128, 4096))
    nrm_d = din("nrm", (2, 128, 16))
    bada_d = din("bada", (2, 128, 48))
    qkg_d = din("qkg", (2, 128, 2))
    vg_d = din("vg", (2, 128, 512))
    bsp_d = din("bsp", (2, 1, 512))
    bsps_d = din("bsps", (2, 1, 256))
    wspT_d = din("wspT", (2, 128, 4, 128))
    wspb_d = din("wspb", (2, 64, 4, 64))
    tril_d = din("tril", (128, 4, 128))
    trilb_d = din("trilb", (64, 4, 64))
    cw_d = din("cw", (2, 128, 22, 3))
    cb_d = din("cb", (2, 128, 22))
    Tb_d = din("Tb", (2, 128, 2, 8, 128))
    b256_d = din("b256", (2, 128, 8))

    yT_d = dout("yT", (128, 8, SEG))
    ysT_d = dout("ysT", (128, 8, 64))
    nkT_d = dout("nkT", (2, 128, 4, 512))
    nv_d = dout("nv", (2, 4, 128, 512))
    ncv_d = dout("ncv", (2, 128, 22, 2))
    nksT_d = dout("nksT", (2, 128, 4, 64))
    nvs_d = dout("nvs", (2, 16, 4, 512))
    nbs_d = dout("nbs", (2, 64, 512))
    ncs_d = dout("ncs", (2, 128, 22, 4, 2))

    wsc = dint("wsc", (2, 24, 128, 4096), BF16)
    wsc_fo = dint("wscfo", (2, 8, 128, 2816), BF16)
    x1s = dint("x1s", (128, 8, 21 * 128), F32)
    xs1 = dint("xs1", (128, 8, 64), F32)

    def sb(name, shape, dt=F32):
        return stack.enter_context(nc.sbuf_tensor("sb_" + name, list(shape), dt))

    xTt = sb("xT", (128, 2, 8, 512))
    cur = {"xT": xTt[:, 0], "par": 0, "idx": 0}
    hT = sb("hT", (128, 8, 512), BF16)
    sq = sb("sq", (128, 3, 512), BF16)
    rstd = sb("rstd", (128, 512))
    tmpn = sb("tmpn", (128, 2, 512))
    qz = sb("qz", (128, 4, 4, 2, 128), BF16)
    kst = sb("kst", (128, 2, 512))
    rs = sb("rs", (128, 2, 512))
    kT = sb("kT", (128, 4, 1024), BF16)
    vt = sb("vt", (128, 8, 512), BF16)
    ubT = sb("ubT", (128, 4, 512), BF16)
    vbn = sb("vbn", (128, 4, 512), BF16)
    gl = sb("gl", (128, 2, 512))
    ssv = sb("ssv", (128, 4))
    SA = sb("SA", (128, 24, 512), BF16)
    oaT = sb("oaT", (128, 4, 512), BF16)
    obT = sb("obT", (128, 4, 512), BF16)
    Pt = sb("Pt", (128, 2, 5, 256), BF16)
    rden = sb("rden", (128, 2, 256))
    gb = sb("gb", (128, 3, 516))
    gc = sb("gc", (128, 2, 512))
    ge = sb("ge", (128, 2, 512))
    halo = sb("halo", (128, 22, 2))
    halos = sb("halos", (128, 22, 4, 2))
    ncs_st = sb("ncs_st", (128, 22, 4, 2))
    wslab = sb("wslab", (128, NB, 4096), BF16)
    ones_bf = sb("ones_bf", (128, 128), BF16)
    M0 = sb("M0", (128, 256), BF16)
    blk64 = sb("blk64", (128, 128), BF16)
    ones_row = sb("ones_row", (1, 128), BF16)
    eps_row = sb("eps_row", (1, 256), BF16)
    vones = sb("vones", (128, 9, 128), BF16)
    vtok = sb("vtok", (128, NT))
    flag = sb("flag", (128, 1))
    cTs = sb("cTs", (128, 8, 5))
    cs = sb("cs", (128, 8, 5), BF16)
    E = sb("E", (128, 2, 8, 128), BF16)
    negb = sb("negb", (128, 8))
    modTt = sb("modT", (128, 2, 48, 5))
    A12t = sb("A12", (128, 2, 2, 8, 5))
    nrmt = sb("nrm", (128, 2, 16))
    badat = sb("bada", (128, 2, 48))
    qkg = sb("qkg", (128, 2))
    vg = sb("vg", (128, 512))
    bsp = sb("bsp", (1, 512), BF16)
    bsps = sb("bsps", (1, 256), BF16)
    WcT = sb("WcT", (128, 4, 128), BF16)
    Wblk = sb("Wblk", (64, 4, 64), BF16)
    cw = sb("cw", (128, 22, 3))
    cb = sb("cb", (128, 22))
    kTs = sb("kTs", (128, 4, 64), BF16)
    vbns = sb("vbns", (64, 512), BF16)
    ckb = sb("ckb", (128, 1, 4, 512), BF16)
    cvb = sb("cvb", (128, 1, 4, 512), BF16)
    Pts = sb("Pts", (128, 2, 2, 80), BF16)
    exs = sb("exs", (128, 2, 2, 32))
    rdens = sb("rdens", (128, 2, 32))

    psd = [stack.enter_context(nc.psum_tensor("ps%d" % i, [128, 1024], F32)) for i in range(4)]
    ps = [psd[i // 2][:, (i % 2) * 512:(i % 2 + 1) * 512] for i in range(8)]

    def bk(i):
        return [("ps", i)]

    bank_ctr = [0]

    reserved = set()

    def nb():
        while True:
            b = bank_ctr[0] % 8
            bank_ctr[0] += 1
            if b not in reserved:
                return b

    rot = {}

    def nrot(name, n):
        v = rot.get(name, 0)
        rot[name] = v + 1
        return v % n

    ws = dict(seq=[], issued=0, consumed=0)

    casted = set()

    def ws_issue(upto):
        while ws["issued"] < min(upto, len(ws["seq"])):
            kind_, l_, s_ = ws["seq"][ws["issued"]]
            slot = ws["issued"] % NB
            wk, sk = [("wslab", slot)], ("w", slot)
            if kind_ == "ada":
                P.op("pool", "dma_start", out=wslab[:, slot, :], in_=wada_d[l_, s_], w=wk, semkey=sk)
            else:
                ncols = 4096 if kind_ == "wall" else 2816
                src32 = wall_d[l_, s_] if kind_ == "wall" else wfo_d[l_, s_]
                scr = wsc[l_, s_] if kind_ == "wall" else wsc_fo[l_, s_]
                key = ("wsc", kind_, l_, s_)
                if key not in casted:
                    casted.add(key)
                    P.op("pool", "dma_start", out=wslab[:, slot, 0:ncols], in_=src32, w=wk, semkey=sk)
                    P.op("sp", "dma_start", out=scr, in_=wslab[:, slot, 0:ncols], r=wk, w=[key],
                         semkey=("wst", slot))
                else:
                    P.op("pool", "dma_start", out=wslab[:, slot, 0:ncols], in_=scr, r=[key], w=wk, semkey=sk)
            ws["issued"] += 1

    def ws_get():
        assert ws["consumed"] < len(ws["seq"])
        ws_issue(ws["consumed"] + 1)
        slot = ws["consumed"] % NB
        ws["consumed"] += 1
        return slot

    def ws_done():
        ws_issue(ws["consumed"] + NB)

    def seq_ada(l):
        for s_ in range(12):
            ws["seq"].append(("ada", l, s_))

    def seq_block(l, kind):
        if kind == "kv":
            for s_ in (1, 2):
                ws["seq"].append(("wall", l, s_))
            return
        for s_ in range(24):
            ws["seq"].append(("wall", l, s_))
        for s_ in range(8):
            ws["seq"].append(("fo", l, s_))

    seq_ada(0)
    for l in range(2):
        seq_block(l, "kv")
        nfull = len(L0_BLOCKS if l == 0 else L1_BLOCKS)
        for k_ in range(nfull):
            if l == 0 and k_ == nfull - 1:
                seq_ada(1)
            seq_block(l, "full")

    P.op("pool", "memset", ones_bf[:, :], 1.0, w=["ones_bf"])
    P.op("pool", "memset", blk64[:, :], 0.0, w=["blk64"])
    P.op("pool", "memset", blk64[0:64, 0:64], 1.0, w=["blk64"])
    P.op("pool", "memset", blk64[64:128, 64:128], 1.0, w=["blk64"])
    P.op("pool", "memset", ones_row[:, :], 1.0, w=["ones_row"])
    P.op("pool", "memset", eps_row[:, :], 1e-30, w=["eps_row"])
    P.op("pool", "memset", M0[:, :], 1.0, w=["M0"])
    P.op("pool", "memset", M0[0:64, :].rearrange("p (h q) -> p h q", h=2)[:, :, 64:128], 0.0, w=["M0"])
    P.op("pool", "memset", Pt[:, :, :, :].rearrange("p a b c -> p (a b c)"), 0.0, w=[("Pt", 0), ("Pt", 1)])
    P.op("pool", "memset", qz[:, :, :, :, :].rearrange("p a b c d -> p (a b c d)"), 0.0, w=[("qT", c) for c in range(4)])
    P.op("pool", "memset", halo[:, :, :], 0.0, w=["halo"])
    P.op("sp", "dma_start", out=vtok[:, :], in_=vtok_d, w=["vtok"], semkey="c0")
    P.op("sp", "dma_start", out=flag[:, :], in_=flag_d, w=["flag"], semkey="c0")
    P.op("sp", "dma_start", out=cTs[:, :, :], in_=cT_d, w=["cTs"], semkey="c0")
    P.op("act", "activation", out=cs[:, :, :], in_=cTs[:, :, :], func=AF.Silu, r=["cTs"], w=["cs"])
    for t in range(9):
        P.op("act", "activation", out=vones[:, t, :], in_=ones_bf[:, :], func=AF.Copy,
             scale=vtok[:, t:t + 1], r=["ones_bf", "vtok"], w=["vones"])
    def layer_setup(l):
        for dst, src, key in ((qkg, qkg_d[l], "qkg"),
                              (vg, vg_d[l], "vg"), (cw, cw_d[l], "cw"), (cb, cb_d[l], "cb"),
                              (negb, b256_d[l], "negb")):
            full = tuple(slice(None) for _ in dst.shape)
            P.op("sp", "dma_start", out=dst[full], in_=src, w=[key], semkey="ls")
        P.op("sp", "dma_start", out=tmpn[:, 0, :].rearrange("p (h q) -> p h q", h=4), in_=Tb_d[l][:, 0, 0:4, :],
             w=[("tmpn", 0)], semkey="ls")
        P.op("sp", "dma_start", out=tmpn[:, 1, :].rearrange("p (h q) -> p h q", h=4), in_=Tb_d[l][:, 0, 4:8, :],
             w=[("tmpn", 1)], semkey="ls")
        P.op("sp", "dma_start", out=rs[:, 0, :].rearrange("p (h q) -> p h q", h=4), in_=Tb_d[l][:, 1, 0:4, :],
             w=[("rs", 0)], semkey="ls")
        P.op("sp", "dma_start", out=rs[:, 1, :].rearrange("p (h q) -> p h q", h=4), in_=Tb_d[l][:, 1, 4:8, :],
             w=[("rs", 1)], semkey="ls")
        P.op("sp", "dma_start", out=halos[:, :, :, :], in_=cconv_d[l], w=["halos"], semkey="ls")
        P.op("sp", "dma_start", out=gl[0:1, 0, :], in_=bsp_d[l], w=[("gl", 0)], semkey="ls")
        P.op("sp", "dma_start", out=gl[0:1, 1, 0:256], in_=bsps_d[l], w=[("gl", 1)], semkey="ls")
        P.op("sp", "dma_start", out=gc[:, 0, :], in_=wspT_d[l].rearrange("p g t -> p (g t)"),
             w=[("gc", 0)], semkey="ls3")
        P.op("sp", "dma_start", out=gc[:, 1, :], in_=tril_d.rearrange("p g t -> p (g t)"),
             w=[("gc", 1)], semkey="ls3")
        P.op("sp", "dma_start", out=ge[0:64, 0, 0:256], in_=wspb_d[l].rearrange("p g t -> p (g t)"),
             w=[("ge", 0)], semkey="ls3")
        P.op("sp", "dma_start", out=ge[0:64, 1, 0:256], in_=trilb_d.rearrange("p g t -> p (g t)"),
             w=[("ge", 1)], semkey="ls3")
        P.op("dve", "tensor_tensor", out=WcT[:, :, :].rearrange("p g t -> p (g t)"), in0=gc[:, 0, :],
             in1=gc[:, 1, :], op=ALU.mult, r=[("gc", 0), ("gc", 1)], w=["WcT"])
        P.op("dve", "tensor_tensor", out=Wblk[:, :, :].rearrange("p g t -> p (g t)"), in0=ge[0:64, 0, 0:256],
             in1=ge[0:64, 1, 0:256], op=ALU.mult, r=[("ge", 0), ("ge", 1)], w=["Wblk"])
        P.op("dve", "tensor_copy", bsp[:, :], gl[0:1, 0, :], r=[("gl", 0)], w=["bsp"])
        P.op("dve", "tensor_copy", bsps[:, :], gl[0:1, 1, 0:256], r=[("gl", 1)], w=["bsps"])
        P.op("dve", "tensor_scalar_mul", qkg[:, 1:2], qkg[:, 1:2], 8.0, r=["qkg"], w=["qkg"])
        P.op("dve", "tensor_scalar_mul", negb[:, :], negb[:, :], -1.0, r=["negb"], w=["negb"])
        for t in range(2):
            for h in range(8):
                stg = (tmpn if t == 0 else rs)
                skey = ("tmpn" if t == 0 else "rs", h // 4)
                P.op("act", "activation", out=E[:, t, h, :], in_=stg[:, h // 4, (h % 4) * 128:(h % 4 + 1) * 128],
                     func=AF.Exp, bias=negb[:, h:h + 1], scale=1.0, r=[skey, "negb"], w=["E"])
        P.op("pool", "memset", E[64:128, 1, :, 0:64], 0.0, r=["E"], w=["E"])

    def ada_setup(l):
        modT, A12, nrm, bada = modTt[:, l], A12t[:, l], nrmt[:, l], badat[:, l]
        P.op("sp", "dma_start", out=nrm, in_=nrm_d[l], w=[("nrm", l)], semkey=("lsa", l))
        P.op("sp", "dma_start", out=bada, in_=bada_d[l], w=[("bada", l)], semkey=("lsa", l))
        b = nb()
        for s_ in range(12):
            slot = ws_get()
            for m in range(4):
                ci = s_ * 4 + m
                for kc in range(8):
                    P.op("pe", "matmul", ps[b][:, ci * 5:(ci + 1) * 5],
                         lhsT=wslab[:, slot, kc * 512 + m * 128: kc * 512 + (m + 1) * 128],
                         rhs=cs[:, kc, :], start=(kc == 0), stop=(kc == 7),
                         r=[("wslab", slot), "cs"], w=bk(b))
            ws_done()
        for bb in range(5):
            P.op("dve", "tensor_tensor", out=modT[:, :, bb],
                 in0=ps[b][:, 0:240].rearrange("p (c b) -> p c b", b=5)[:, :, bb], in1=bada,
                 op=ALU.add, r=bk(b) + [("bada", l)], w=[("modT", l)])
        for which in range(2):
            sc0 = 8 if which == 0 else 32
            for bb in range(5):
                P.op("dve", "tensor_scalar", out=A12[:, which, :, bb], in0=modT[:, sc0:sc0 + 8, bb],
                     scalar1=1.0, scalar2=32.0, op0=ALU.add, op1=ALU.mult, r=[("modT", l)], w=[("A12", l)])
                P.op("dve", "tensor_tensor", out=A12[:, which, :, bb], in0=A12[:, which, :, bb],
                     in1=nrm[:, which * 8:(which + 1) * 8], op=ALU.mult, r=[("A12", l), ("nrm", l)],
                     w=[("A12", l)])

    def norm_sq(b, c, T):
        r_ = nrot("sq", 3)
        P.op("act", "activation", out=sq[:, r_, 0:T], in_=cur["xT"][:, c, 0:T], func=AF.Square,
             r=[("xT", cur["par"], c)], w=[("sq", r_)])
        return r_

    def norm_ss(b, c, r_, T):
        P.op("pe", "matmul", ps[b][:, 0:T], lhsT=ones_bf[:, :], rhs=sq[:, r_, 0:T],
             start=(c == 0), stop=(c == 7), r=[("sq", r_), "ones_bf"], w=bk(b))

    def norm_fin(b, T):
        P.op("act", "activation", out=rstd[:, 0:T], in_=ps[b][:, 0:T], func=AF.Sqrt, bias=float(D * EPS),
             scale=1.0, r=bk(b), w=["rstd"])
        P.op("dve", "reciprocal", out=rstd[:, 0:T], in_=rstd[:, 0:T], r=["rstd"], w=["rstd"])

    def norm_apply(which, T, groups):
        shc = 0 if which == 0 else 24
        for c in range(8):
            r_ = nrot("tmpn", 2)
            for (c0, n, bb) in groups:
                P.op("dve", "scalar_tensor_tensor", out=tmpn[:, r_, c0:c0 + n], in0=cur["xT"][:, c, c0:c0 + n],
                     scalar=A12t[:, cur["l"], which, c, bb:bb + 1], in1=rstd[:, c0:c0 + n], op0=ALU.mult, op1=ALU.mult,
                     r=[("xT", cur["par"], c), ("A12", cur["l"]), "rstd"], w=[("tmpn", r_)])
            for (c0, n, bb) in groups:
                P.op("act", "activation", out=hT[:, c, c0:c0 + n], in_=tmpn[:, r_, c0:c0 + n],
                     func=AF.Identity, bias=modTt[:, cur["l"], shc + c, bb:bb + 1], scale=1.0,
                     r=[("tmpn", r_), ("modT", cur["l"])], w=[("hT", c)])

    def norm_mod(which, T, groups):
        b = nb()
        for c in range(8):
            r_ = norm_sq(b, c, T)
            norm_ss(b, c, r_, T)
        norm_fin(b, T)
        norm_apply(which, T, groups)

    pro_done = set()

    def blk_geom(i):
        l_, kind_, t0_, n_ = BLOCKS[i]
        Tp_ = n_ * 128
        if kind_ == "fullms":
            return l_, Tp_ + 64, [(0, Tp_, 0)] + [(Tp_ + bl * 16, 16, 1 + bl) for bl in range(4)]
        return l_, Tp_, [(0, Tp_, 0)]

    def pro_ok(i):
        return i < len(BLOCKS)

    class _NextCtx:
        def __init__(self, i):
            self.i = i

        def __enter__(self):
            self.save = dict(cur)
            cur["l"] = BLOCKS[self.i][0]
            cur["par"] = self.i % 2
            cur["xT"] = xTt[:, self.i % 2]

        def __exit__(self, *a):
            cur.update(self.save)

    def proj_fm(b, slot, nkc, ms, col0, rhs_t, rkeys, T):
        for kc in range(nkc):
            P.op("pe", "matmul", ps[b][:, 0:T], lhsT=wslab[:, slot, kc * ms + col0: kc * ms + col0 + 128],
                 rhs=rhs_t(kc), start=(kc == 0), stop=(kc == nkc - 1),
                 r=[("wslab", slot)] + rkeys(kc), w=bk(b))

    def hn_a(b, T):
        r_ = nrot("sq", 3)
        P.op("act", "activation", out=sq[:, r_, 0:T], in_=ps[b][:, 0:T], func=AF.Square, r=bk(b), w=[("sq", r_)])
        return r_

    def hn_b(b, r_, T, gcol, out_ap, out_keys, qchunk=None, Tp=None, ms=False):
        b2 = nb()
        P.op("pe", "matmul", ps[b2][:, 0:T], lhsT=blk64[:, :], rhs=sq[:, r_, 0:T], start=True, stop=True,
             r=[("sq", r_), "blk64"], w=bk(b2))
        r2 = nrot("rs", 2)
        P.op("act", "activation", out=rs[:, r2, 0:T], in_=ps[b2][:, 0:T], func=AF.Sqrt, bias=float(64 * EPS),
             scale=1.0, r=bk(b2), w=[("rs", r2)])
        P.op("dve", "reciprocal", out=rs[:, r2, 0:T], in_=rs[:, r2, 0:T], r=[("rs", r2)], w=[("rs", r2)])
        if qchunk is not None:
            nq = Tp // 128
            for hh in range(2):
                hs = slice(hh * 64, (hh + 1) * 64)
                P.op("dve", "scalar_tensor_tensor", out=qz[hs, qchunk, 0:nq, hh, :],
                     in0=ps[b][hs, 0:Tp].rearrange("p (t q) -> p t q", t=nq), scalar=qkg[hs, gcol:gcol + 1],
                     in1=rs[hs, r2, 0:Tp].rearrange("p (t q) -> p t q", t=nq), op0=ALU.mult, op1=ALU.mult,
                     r=bk(b) + [("rs", r2), "qkg"], w=out_keys)
                if ms:
                    P.op("dve", "scalar_tensor_tensor", out=qz[hs, qchunk, 3, hh, 0:64], in0=ps[b][hs, Tp:Tp + 64],
                         scalar=qkg[hs, gcol:gcol + 1], in1=rs[hs, r2, Tp:Tp + 64], op0=ALU.mult, op1=ALU.mult,
                         r=bk(b) + [("rs", r2), "qkg"], w=out_keys)
            return
        P.op("dve", "scalar_tensor_tensor", out=out_ap, in0=ps[b][:, 0:T], scalar=qkg[:, gcol:gcol + 1],
             in1=rs[:, r2, 0:T], op0=ALU.mult, op1=ALU.mult, r=bk(b) + [("rs", r2), "qkg"], w=out_keys)

    def hT_r(T):
        return (lambda kc: hT[:, kc, 0:T]), (lambda kc: [("hT", kc)])

    def block(l, kind, t0, ntile):
        sample = False
        ms = kind == "fullms"
        Tp = ntile * 128
        C0 = Tp
        T = Tp + (64 if ms else 0)
        tiles = list(range(t0, t0 + ntile))
        groups = [(0, Tp, 0)] + ([(C0 + bl * 16, 16, 1 + bl) for bl in range(4)] if ms else [])
        hr, hk = hT_r(T)
        bi = cur["idx"]
        cur["l"] = l
        _DBG["stage"] = 0
        cur["par"] = bi % 2
        cur["xT"] = xTt[:, bi % 2]

        def xload(i):
            l_, kind_, t0_, n_ = BLOCKS[i]
            par_ = i % 2
            wk = [("xT", par_, c) for c in range(8)]
            T_ = n_ * 128
            if kind_ == "fullms":
                if l_ == 1:
                    P.op("sp", "dma_start", out=xTt[:, par_, :, T_:T_ + 64], in_=xs1, r=["xs1"], w=wk,
                         semkey=("xld", par_))
                else:
                    P.op("sp", "dma_start", out=xTt[:, par_, :, T_:T_ + 64], in_=xs_d, w=wk, semkey=("xld", par_))
            if l_ == 0:
                P.op("sp", "dma_start", out=xTt[:, par_, :, 0:T_], in_=xin[:, :, t0_ * 128: t0_ * 128 + T_], w=wk,
                     semkey=("xld", par_))
            else:
                P.op("sp", "dma_start", out=xTt[:, par_, :, 0:T_],
                     in_=x1s[:, :, (t0_ - 4) * 128: (t0_ - 4) * 128 + T_],
                     r=[("x1s", t) for t in range(t0_, t0_ + n_)], w=wk, semkey=("xld", par_))
            return True

        if bi == 0:
            xload(0)
        if bi + 1 < len(BLOCKS):
            xload(bi + 1)
        cur["idx"] = bi + 1
        if bi not in pro_done:
            norm_mod(0, T, groups)

        qk = ([("q", c) for c in range(4)] if kind != "kv" else []) + [("k", c) for c in range(4)]
        st_ = {}

        def qk_a(n):
            which, c = qk[n]
            if c == 0:
                if which == "k" and kind != "kv":
                    ws_done()
                st_["slot"] = ws_get()
            b = nb()
            proj_fm(b, st_["slot"], 8, 512, c * 128, hr, hk, T)
            return b, hn_a(b, T)

        def qk_b(n, b, r_sq):
            which, c = qk[n]
            if which == "q":
                hn_b(b, r_sq, T, 0, None, [("qT", c)], qchunk=c, Tp=Tp, ms=ms)
                return
            r_ = nrot("kst", 2)
            hn_b(b, r_sq, T, 1, kst[:, r_, 0:T], [("kst", r_)])
            if ms:
                P.op("act", "activation", out=kTs[:, c, :], in_=kst[:, r_, C0:C0 + 64], func=AF.Copy,
                     r=[("kst", r_)], w=["kTs"])
                P.op("sp", "dma_start", out=nksT_d[l, :, c, :], in_=kst[:, r_, C0:C0 + 64], r=[("kst", r_)],
                     semkey=("kst", r_))
            for ti, t in enumerate(tiles):
                sl = t % 8
                P.op("pool", "tensor_copy", kT[:, c, sl * 128:(sl + 1) * 128],
                     kst[:, r_, ti * 128:(ti + 1) * 128], r=[("kst", r_)], w=[("kT", sl)])
                if t >= KEEP_T0:
                    P.op("sp", "dma_start", out=nkT_d[l, :, c, (t - KEEP_T0) * 128:(t - KEEP_T0 + 1) * 128],
                         in_=kst[:, r_, ti * 128:(ti + 1) * 128], r=[("kst", r_)], semkey=("kst", r_))

        pend = qk_a(0)
        for n in range(len(qk)):
            nxt_ = qk_a(n + 1) if n + 1 < len(qk) else None
            qk_b(n, *pend)
            pend = nxt_
        ws_done()
        slot = ws_get()
        if ms:
            for bl in range(4):
                b = nb()
                for kc in range(8):
                    P.op("pe", "matmul", ps[b][0:16, 0:512], lhsT=hT[:, kc, C0 + bl * 16:C0 + (bl + 1) * 16],
                         rhs=wslab[:, slot, kc * 512:(kc + 1) * 512], start=(kc == 0), stop=(kc == 7),
                         r=[("wslab", slot), ("hT", kc)], w=bk(b))
                r_ = nrot("gl", 2)
                P.op("act", "activation", out=gl[0:16, r_, :], in_=ps[b][0:16, 0:512], func=AF.Copy, r=bk(b),
                     w=[("gl", r_)])
                P.op("dve", "tensor_copy", SA[0:16, 16 + bl, :], gl[0:16, r_, :], r=[("gl", r_)], w=[("SA", 16 + bl)])
                P.op("sp", "dma_start", out=nvs_d[l, :, bl, :], in_=gl[0:16, r_, :], r=[("gl", r_)],
                     semkey=("gl", r_))
        if True:
            for ti, t in enumerate(tiles):
                b = nb()
                for kc in range(8):
                    P.op("pe", "matmul", ps[b][:, 0:512], lhsT=hT[:, kc, ti * 128:(ti + 1) * 128],
                         rhs=wslab[:, slot, kc * 512:(kc + 1) * 512], start=(kc == 0), stop=(kc == 7),
                         r=[("wslab", slot), ("hT", kc)], w=bk(b))
                sl = t % 8
                if t >= KEEP_T0:
                    r_ = nrot("gl", 2)
                    P.op("act", "activation", out=gl[:, r_, :], in_=ps[b][:, 0:512], func=AF.Copy, r=bk(b),
                         w=[("gl", r_)])
                    P.op("dve", "tensor_scalar_mul", vt[:, sl, :], gl[:, r_, :], vtok[:, t:t + 1],
                         r=[("gl", r_), "vtok"], w=[("vt", sl)])
                    P.op("sp", "dma_start", out=nv_d[l, t - KEEP_T0], in_=gl[:, r_, :], r=[("gl", r_)],
                         semkey=("gl", r_))
                else:
                    P.op("dve", "tensor_scalar_mul", vt[:, sl, :], ps[b][:, 0:512], vtok[:, t:t + 1],
                         r=bk(b) + ["vtok"], w=[("vt", sl)])
        ws_done()
        if kind == "kv":
            if pro_ok(bi + 1):
                with _NextCtx(bi + 1):
                    l2, T2, g2 = blk_geom(bi + 1)
                    norm_mod(0, T2, g2)
                pro_done.add(bi + 1)
            return

        ck(1)
        slot = ws_get()
        for c in range(4):
            b = nb()
            proj_fm(b, slot, 8, 512, c * 128, hr, hk, T)
            P.op("act", "activation", out=ubT[:, c, 0:T], in_=ps[b][:, 0:T], func=AF.Gelu_apprx_tanh, r=bk(b),
                 w=[("ubT", c)])
        ws_done()
        ck(2)
        slot = ws_get()
        tl = [(ti, 128, False) for ti in range(ntile)] + ([(ntile, 64, True)] if ms else [])
        for ti, M, sample in tl:
            b = nb()
            for kc in range(8):
                P.op("pe", "matmul", ps[b][0:M, 0:512], lhsT=hT[:, kc, ti * 128: ti * 128 + M],
                     rhs=wslab[:, slot, kc * 512:(kc + 1) * 512], start=(kc == 0), stop=(kc == 7),
                     r=[("wslab", slot), ("hT", kc)], w=bk(b))
            r_ = nrot("gl", 2)
            P.op("act", "activation", out=gl[0:M, r_, :], in_=ps[b][0:M, 0:512], func=AF.Gelu_apprx_tanh, r=bk(b),
                 w=[("gl", r_)])
            r2 = nrot("tmpn", 2)
            P.op("dve", "tensor_tensor", out=tmpn[0:M, r2, :], in0=gl[0:M, r_, :], in1=gl[0:M, r_, :], op=ALU.mult,
                 r=[("gl", r_)], w=[("tmpn", r2)])
            r3 = nrot("ssv", 4)
            P.op("dve", "reduce_sum", out=ssv[0:M, r3:r3 + 1], in_=tmpn[0:M, r2, :], axis=AX.X,
                 r=[("tmpn", r2)], w=[("ssv", r3)])
            P.op("act", "activation", out=ssv[0:M, r3:r3 + 1], in_=ssv[0:M, r3:r3 + 1], func=AF.Sqrt,
                 bias=float(EPS), scale=1.0 / 512.0, r=[("ssv", r3)], w=[("ssv", r3)])
            P.op("dve", "reciprocal", out=ssv[0:M, r3:r3 + 1], in_=ssv[0:M, r3:r3 + 1], r=[("ssv", r3)],
                 w=[("ssv", r3)])
            if sample:
                P.op("dve", "scalar_tensor_tensor", out=tmpn[0:64, r2, :], in0=gl[0:64, r_, :],
                     scalar=ssv[0:64, r3:r3 + 1], in1=vg[0:64, :], op0=ALU.mult, op1=ALU.mult,
                     r=[("gl", r_), ("ssv", r3), "vg"], w=[("tmpn", r2)])
                P.op("act", "activation", out=vbns[:, :], in_=tmpn[0:64, r2, :], func=AF.Copy, r=[("tmpn", r2)],
                     w=["vbns"])
                P.op("sp", "dma_start", out=nbs_d[l], in_=tmpn[0:64, r2, :], r=[("tmpn", r2)], semkey=("tmpn", r2))
            else:
                P.op("dve", "scalar_tensor_tensor", out=vbn[:, ti, :], in0=gl[:, r_, :], scalar=ssv[:, r3:r3 + 1],
                     in1=vg[:, :], op0=ALU.mult, op1=ALU.mult, r=[("gl", r_), ("ssv", r3), "vg"], w=[("vbn", ti)])
        ws_done()
        sample = False
        ck(3)
        def gate_chunks():
            for s_ in range(4):
                slot = ws_get()
                for m in range(4):
                    c = s_ * 4 + m

                    def emit(b, slot=slot, m=m, c=c):
                        proj_fm(b, slot, 8, 512, m * 128, hr, hk, T)
                        P.op("act", "activation", out=SA[:, c, 0:T], in_=ps[b][:, 0:T], func=AF.Sigmoid, r=bk(b),
                             w=[("SA", c)])
                    yield emit
                ws_done()

        gates = gate_chunks()

        ck(4)
        if ms:
            items = [(bl, hp) for bl in range(4) for hp in range(4)]

            def ss1(it, st):
                bl, hp = it
                cr = 0
                for j in range(5):
                    for hh in range(2):
                        hs = slice(hh * 64, (hh + 1) * 64)
                        b = st * 4 + hh
                        if j < 4:
                            P.op("pe", "matmul", ps[b][:, j * 16:(j + 1) * 16],
                                 lhsT=ckb[hs, cr, hp, j * 128:(j + 1) * 128], rhs=qz[hs, hp, 3, hh, bl * 16:(bl + 1) * 16],
                                 start=True, stop=True, r=[("ckb", cr), ("qT", hp)], w=[("ps", b)])
                        else:
                            P.op("pe", "matmul", ps[b][0:16, 64:80],
                                 lhsT=kTs[hs, hp, bl * 16:(bl + 1) * 16], rhs=qz[hs, hp, 3, hh, bl * 16:(bl + 1) * 16],
                                 start=True, stop=True, r=["kTs", ("qT", hp)], w=[("ps", b)])
                for hh in range(2):
                    b = st * 4 + hh
                    h = 2 * hp + hh
                    P.op("act", "activation", out=Pts[:, st, hh, 0:48], in_=ps[b][:, 0:48], func=AF.Exp,
                         r=[("ps", b)], w=[("Pts", st, hh)])
                    P.op("act", "activation", out=exs[:, st, hh, 0:16], in_=ps[b][:, 48:64], func=AF.Exp,
                         r=[("ps", b)], w=[("exs", st, hh)])
                    P.op("act", "activation", out=exs[0:16, st, hh, 16:32], in_=ps[b][0:16, 64:80], func=AF.Exp,
                         r=[("ps", b)], w=[("exs", st, hh)])
                    P.op("dve", "tensor_tensor", out=Pts[:, st, hh, 48:64], in0=exs[:, st, hh, 0:16],
                         in1=E[:, 0, h, 0:16], op=ALU.mult, r=[("exs", st, hh), "E"], w=[("Pts", st, hh)])
                    P.op("dve", "tensor_tensor", out=Pts[0:16, st, hh, 64:80], in0=exs[0:16, st, hh, 16:32],
                         in1=E[0:16, 1, h, 0:16], op=ALU.mult, r=[("exs", st, hh), "E"], w=[("Pts", st, hh)])

            def ss2(it, st):
                bl, hp = it
                cr = 0
                bd, bo = st * 4 + 2, 3
                oc = slice(st * 128, st * 128 + 16)
                for hh in range(2):
                    for j in range(5):
                        if j < 4:
                            P.op("pe", "matmul", ps[bd][:, hh * 16:(hh + 1) * 16], lhsT=ones_bf[:, :],
                                 rhs=Pts[:, st, hh, j * 16:(j + 1) * 16], start=(j == 0), stop=False,
                                 r=[("Pts", st, hh), "ones_bf"], w=[("ps", bd)])
                        else:
                            P.op("pe", "matmul", ps[bd][:, hh * 16:(hh + 1) * 16], lhsT=ones_bf[0:16, :],
                                 rhs=Pts[0:16, st, hh, 64:80], start=False, stop=True,
                                 r=[("Pts", st, hh), "ones_bf"], w=[("ps", bd)])
                for hh in range(2):
                    hs = slice(hh * 64, (hh + 1) * 64)
                    fc = (2 * hp + hh) * 64
                    for j in range(5):
                        if j < 4:
                            P.op("pe", "matmul", ps[bo][hs, oc], lhsT=cvb[:, cr, j, fc:fc + 64],
                                 rhs=Pts[:, st, hh, j * 16:(j + 1) * 16], start=(j == 0), stop=False,
                                 r=[("Pts", st, hh), ("cvb", cr)], w=[("ps", bo)])
                        else:
                            P.op("pe", "matmul", ps[bo][hs, oc], lhsT=SA[0:16, 16 + bl, fc:fc + 64],
                                 rhs=Pts[0:16, st, hh, 64:80], start=False, stop=True,
                                 r=[("Pts", st, hh), ("SA", 16 + bl)], w=[("ps", bo)])
                P.op("dve", "reciprocal", out=rdens[:, st, :], in_=ps[bd][:, 0:32], r=[("ps", bd)],
                     w=[("rdens", st)])
                for hh in range(2):
                    hs = slice(hh * 64, (hh + 1) * 64)
                    P.op("dve", "tensor_tensor", out=oaT[hs, hp, C0 + bl * 16:C0 + (bl + 1) * 16], in0=ps[bo][hs, oc],
                         in1=rdens[hs, st, hh * 16:(hh + 1) * 16], op=ALU.mult, r=[("ps", bo), ("rdens", st)],
                         w=[("oaT", hp)])
        if True:
            items = [(qi, hp) for qi in range(ntile) for hp in range(4)]

            def s1(it, st):
                qi, hp = it
                qt = t0 + qi
                bS, b4 = st * 4, st * 4 + 2
                for j in range(5):
                    sl = (qt - 4 + j) % 8
                    if j < 4:
                        out_ = psd[st * 2][:, j * 256:(j + 1) * 256]
                        wkey = ("ps", bS + j // 2)
                    else:
                        out_ = ps[b4][:, 0:256]
                        wkey = ("ps", b4)
                    P.op("pe", "matmul", out_, lhsT=kT[:, hp, sl * 128:(sl + 1) * 128],
                         rhs=qz[:, hp, qi, :, :].rearrange("p h q -> p (h q)"), start=True, stop=True,
                         r=[("kT", sl), ("qT", hp)], w=[wkey])
                pk = [("Pt", st)]
                rk2 = [("ps", bS), ("ps", bS + 1)]
                two = "p (h q) -> p h q"
                P.op("act", "activation", out=Pt[:, st, 0:4, :].rearrange("p j c -> p (j c)"),
                     in_=psd[st * 2][:, 0:1024], func=AF.Exp, r=rk2, w=pk)
                P.op("act", "activation", out=Pt[:, st, 4, :], in_=ps[b4][:, 0:256], func=AF.Exp,
                     r=[("ps", b4)], w=pk)
                P.op("dve", "tensor_tensor", out=Pt[:, st, 0, :], in0=Pt[:, st, 0, :], in1=M0[:, :], op=ALU.mult,
                     r=pk + ["M0"], w=pk)
                P.op("dve", "tensor_tensor", out=Pt[:, st, 3:5, :].rearrange("p t (h q) -> p t h q", h=2),
                     in0=Pt[:, st, 3:5, :].rearrange("p t (h q) -> p t h q", h=2),
                     in1=E[:, :, 2 * hp:2 * hp + 2, :], op=ALU.mult, r=pk + ["E"], w=pk)

            def s2(it, st):
                qi, hp = it
                qt = t0 + qi
                bd, bo = st * 4 + 2, 3
                oc = slice(st * 128, (st + 1) * 128)
                for j in range(5):
                    kt = qt - 4 + j
                    lw = vones[:, kt, :] if kt < 9 else ones_bf[:, :]
                    P.op("pe", "matmul", ps[bd][:, 256:512], lhsT=lw, rhs=Pt[:, st, j, :], start=(j == 0),
                         stop=False, r=[("Pt", st), "vones", "ones_bf"], w=[("ps", bd)])
                P.op("pe", "matmul", ps[bd][:, 256:512], lhsT=ones_row[0:1, :], rhs=eps_row[0:1, :], start=False,
                     stop=True, r=["ones_row", "eps_row"], w=[("ps", bd)])
                for hh in range(2):
                    hs = slice(hh * 64, (hh + 1) * 64)
                    fc = (2 * hp + hh) * 64
                    for j in range(5):
                        sl = (qt - 4 + j) % 8
                        P.op("pe", "matmul", ps[bo][hs, oc], lhsT=vt[:, sl, fc:fc + 64],
                             rhs=Pt[:, st, j, hh * 128:(hh + 1) * 128], start=(j == 0), stop=(j == 4),
                             r=[("Pt", st), ("vt", sl)], w=[("ps", bo)])
                P.op("dve", "reciprocal", out=rden[:, st, :], in_=ps[bd][:, 256:512], r=[("ps", bd)],
                     w=[("rden", st)])
                for hh in range(2):
                    hs = slice(hh * 64, (hh + 1) * 64)
                    P.op("dve", "tensor_tensor", out=oaT[hs, hp, qi * 128:(qi + 1) * 128], in0=ps[bo][hs, oc],
                         in1=rden[hs, st, hh * 128:(hh + 1) * 128], op=ALU.mult, r=[("ps", bo), ("rden", st)],
                         w=[("oaT", hp)])

        for i in range(len(items) + 1):
            if i < len(items):
                s1(items[i], i % 2)
                g_ = next(gates, None)
                if g_ is not None:
                    g_(7)
            if i >= 1:
                s2(items[i - 1], (i - 1) % 2)
        for g_ in gates:
            g_(nb())
        if ms:
            for bl in range(4):
                its = [(bl, hp) for hp in range(4)]
                P.op("pool", "dma_start", out=ckb[:, 0, :, :], in_=ckT_d[l, :, :, bl, :], w=[("ckb", 0)],
                     semkey=("ckb", 0))
                P.op("pool", "dma_start", out=cvb[:, 0, :, :], in_=cv_d[l, bl], w=[("cvb", 0)],
                     semkey=("cvb", 0))
                for i in range(len(its) + 1):
                    if i < len(its):
                        ss1(its[i], i % 2)
                    if i >= 1:
                        ss2(its[i - 1], (i - 1) % 2)

        ck(5)
        if ms:
            b = nb()
            for g in range(4):
                P.op("pe", "matmul", ps[b][:, g * 64:(g + 1) * 64], lhsT=vbns[0:64, g * 128:(g + 1) * 128],
                     rhs=Wblk[0:64, g, :], start=True, stop=False, r=["vbns", "Wblk"], w=bk(b))
                P.op("pe", "matmul", ps[b][:, g * 64:(g + 1) * 64], lhsT=ones_row[0:1, :],
                     rhs=bsps[0:1, g * 64:(g + 1) * 64], start=False, stop=True, r=["ones_row", "bsps"], w=bk(b))
            P.op("dve", "tensor_tensor", out=obT[:, :, C0:C0 + 64], in0=ps[b][:, 0:256].rearrange("p (g t) -> p g t", g=4),
                 in1=ubT[:, :, C0:C0 + 64], op=ALU.mult, r=bk(b) + [("ubT", c) for c in range(4)],
                 w=[("obT", c) for c in range(4)])
        if True:
            for ti in range(ntile):
                b = nb()
                for g in range(4):
                    P.op("pe", "matmul", ps[b][:, g * 128:(g + 1) * 128], lhsT=vbn[:, ti, g * 128:(g + 1) * 128],
                         rhs=WcT[:, g, :], start=True, stop=False, r=[("vbn", ti), "WcT"], w=bk(b))
                    P.op("pe", "matmul", ps[b][:, g * 128:(g + 1) * 128], lhsT=ones_row[0:1, :],
                         rhs=bsp[0:1, g * 128:(g + 1) * 128], start=False, stop=True, r=["ones_row", "bsp"], w=bk(b))
                P.op("dve", "tensor_tensor", out=obT[:, :, ti * 128:(ti + 1) * 128],
                     in0=ps[b][:, 0:512].rearrange("p (g t) -> p g t", g=4), in1=ubT[:, :, ti * 128:(ti + 1) * 128],
                     op=ALU.mult, r=bk(b) + [("ubT", c) for c in range(4)], w=[("obT", c) for c in range(4)])

        ck(6)
        sa_, sb_ = ws_get(), ws_get()
        for c in range(8):
            ba, bb_ = nb(), nb()
            proj_fm(ba, sa_, 4, 1024, c * 128, lambda kc: oaT[:, kc, 0:T], lambda kc: [("oaT", kc)], T)
            proj_fm(bb_, sb_, 4, 1024, c * 128, lambda kc: obT[:, kc, 0:T], lambda kc: [("obT", kc)], T)
            P.op("dve", "tensor_tensor", out=tmpn[:, 0, 0:T], in0=ps[ba][:, 0:T], in1=SA[:, c, 0:T], op=ALU.mult,
                 r=bk(ba) + [("SA", c)], w=[("tmpn", 0)])
            P.op("dve", "tensor_tensor", out=tmpn[:, 1, 0:T], in0=ps[bb_][:, 0:T], in1=SA[:, 8 + c, 0:T], op=ALU.mult,
                 r=bk(bb_) + [("SA", 8 + c)], w=[("tmpn", 1)])
            P.op("dve", "tensor_tensor", out=SA[:, 16 + c, 0:T], in0=tmpn[:, 0, 0:T], in1=tmpn[:, 1, 0:T], op=ALU.add,
                 r=[("tmpn", 0), ("tmpn", 1)], w=[("SA", 16 + c)])
        ws_done()
        ws_done()
        ck(7)
        for s in range(2):
            slot = ws_get()
            for m in range(4):
                c = s * 4 + m
                b = nb()
                proj_fm(b, slot, 8, 512, m * 128, lambda kc: SA[:, 16 + kc, 0:T], lambda kc: [("SA", 16 + kc)], T)
                for (c0, n, bb) in groups:
                    P.op("dve", "scalar_tensor_tensor", out=cur["xT"][:, c, c0:c0 + n], in0=ps[b][:, c0:c0 + n],
                         scalar=modTt[:, cur["l"], 16 + c, bb:bb + 1], in1=cur["xT"][:, c, c0:c0 + n], op0=ALU.mult, op1=ALU.add,
                         r=bk(b) + [("xT", cur["par"], c), ("modT", cur["l"])], w=[("xT", cur["par"], c)])
            ws_done()

        ck(8)
        norm_mod(1, T, groups)
        nxt = bi + 1 if pro_ok(bi + 1) else None
        if nxt is not None:
            l2, T2, g2 = blk_geom(nxt)
            bss = nb()
            reserved.add(bss)
            sqr = {}
        fst = {}

        def ffn_a(j):
            jj, sub = divmod(j, 2)
            if sub == 0:
                if nxt is not None:
                    with _NextCtx(nxt):
                        if 1 <= jj <= 8:
                            sqr[jj - 1] = norm_sq(bss, jj - 1, T2)
                        if 2 <= jj <= 9:
                            norm_ss(bss, jj - 2, sqr[jj - 2], T2)
                if jj > 0:
                    ws_done()
                fst["slot"] = ws_get()
            slot = fst["slot"]
            bg, bu = nb(), nb()
            proj_fm(bg, slot, 8, 512, sub * 128, hr, hk, T)
            proj_fm(bu, slot, 8, 512, 256 + sub * 128, hr, hk, T)
            r_ = nrot("gb", 3)
            r2 = nrot("gc", 2)
            P.op("act", "activation", out=gb[:, r_, 2:2 + Tp], in_=ps[bg][:, 0:Tp], func=AF.Copy, r=bk(bg),
                 w=[("gb", r_)])
            P.op("dve", "tensor_copy", gb[:, r_, 0:2], halo[:, j, :], r=["halo"], w=[("gb", r_)])
            if t0 == 8:
                P.op("pool", "tensor_scalar_mul", gb[:, r_, 128:130], gb[:, r_, 128:130], flag[:, 0:1],
                     r=[("gb", r_), "flag"], w=[("gb", r_)])
            P.op("pool", "tensor_copy", halo[:, j, :], gb[:, r_, Tp:Tp + 2], r=[("gb", r_)], w=["halo"])
            convs = [([gb[:, r_, k:k + Tp] for k in range(3)], gc[:, r2, 0:Tp])]
            if ms:
                g3 = gb[:, r_, Tp + 2:Tp + 74].rearrange("p (b t) -> p b t", t=18)
                P.op("act", "activation", out=g3[:, :, 2:18],
                     in_=ps[bg][:, C0:C0 + 64].rearrange("p (b t) -> p b t", t=16), func=AF.Copy, r=bk(bg),
                     w=[("gb", r_)])
                P.op("dve", "tensor_copy", g3[:, :, 0:2], halos[:, j, :, :], r=["halos"], w=[("gb", r_)])
                P.op("pool", "tensor_copy", ncs_st[:, j, :, :], g3[:, :, 16:18], r=[("gb", r_)], w=["ncs_st"])
                convs.append(([g3[:, :, k:k + 16] for k in range(3)],
                              gc[:, r2, C0:C0 + 64].rearrange("p (b t) -> p b t", t=16)))
            for views, gcv in convs:
                P.op("act", "activation", out=gcv, in_=views[0], func=AF.Identity, scale=cw[:, j, 0:1],
                     bias=cb[:, j:j + 1], r=[("gb", r_), "cw", "cb"], w=[("gc", r2)])
            return bu, r_, r2, convs

        def ffn_b(j, bu, r_, r2, convs):
            for views, gcv in convs:
                for k in (1, 2):
                    P.op("dve", "scalar_tensor_tensor", out=gcv, in0=views[k], scalar=cw[:, j, k:k + 1], in1=gcv,
                         op0=ALU.mult, op1=ALU.add, r=[("gb", r_), ("gc", r2), "cw"], w=[("gc", r2)])
            r3 = nrot("ge", 2)
            P.op("act", "activation", out=ge[:, r3, 0:T], in_=gc[:, r2, 0:T], func=AF.Gelu_apprx_tanh,
                 r=[("gc", r2)], w=[("ge", r3)])
            P.op("dve", "tensor_tensor", out=SA[:, j, 0:T], in0=ps[bu][:, 0:T], in1=ge[:, r3, 0:T], op=ALU.mult,
                 r=bk(bu) + [("ge", r3)], w=[("SA", j)])

        pend = ffn_a(0)
        for j in range(22):
            nxt_a = ffn_a(j + 1) if j + 1 < 22 else None
            ffn_b(j, *pend)
            pend = nxt_a
        ws_done()
        if nxt is not None:
            with _NextCtx(nxt):
                norm_fin(bss, T2)
                norm_apply(0, T2, g2)
            reserved.discard(bss)
            pro_done.add(nxt)
        for c in range(8):
            slot = ws_get()
            b = nb()
            proj_fm(b, slot, 22, 128, 0, lambda kc: SA[:, kc, 0:T], lambda kc: [("SA", kc)], T)
            for (c0, n, bb) in groups:
                P.op("dve", "scalar_tensor_tensor", out=cur["xT"][:, c, c0:c0 + n], in0=ps[b][:, c0:c0 + n],
                     scalar=modTt[:, cur["l"], 40 + c, bb:bb + 1], in1=cur["xT"][:, c, c0:c0 + n], op0=ALU.mult, op1=ALU.add,
                     r=bk(b) + [("xT", cur["par"], c), ("modT", cur["l"])], w=[("xT", cur["par"], c)])
            ws_done()

        ck(9)
        xk = [("xT", cur["par"], c) for c in range(8)]
        if ms:
            if l == 0:
                P.op("sp", "dma_start", out=xs1, in_=cur["xT"][:, :, C0:C0 + 64], r=xk, w=["xs1"], semkey="xst")
            else:
                P.op("sp", "dma_start", out=ysT_d, in_=cur["xT"][:, :, C0:C0 + 64], r=xk, semkey="xst")
            P.op("sp", "dma_start", out=ncs_d[l], in_=ncs_st[:, :, :, :], r=["ncs_st"], semkey="ncs")
            P.op("sp", "dma_start", out=ncv_d[l], in_=halo[:, :, :], r=["halo"], semkey="ncv")
        if l == 0:
            P.op("sp", "dma_start", out=x1s[:, :, (t0 - 4) * 128:(t0 - 4) * 128 + Tp], in_=cur["xT"][:, :, 0:Tp], r=xk,
                 w=[("x1s", t) for t in tiles], semkey="xst")
        else:
            lo = max(t0, OUT_T0)
            c0 = (lo - t0) * 128
            P.op("sp", "dma_start", out=yT_d[:, :, (lo - OUT_T0) * 128:(lo - OUT_T0) * 128 + Tp - c0],
                 in_=cur["xT"][:, :, c0:Tp], r=xk, semkey="xst")

    BLOCKS = []
    for l in range(2):
        BLOCKS.append((l, "kv", 0 if l == 0 else 4, 4))
        bl_ = (L0_BLOCKS if l == 0 else L1_BLOCKS)
        for k_, (t0, n) in enumerate(bl_):
            BLOCKS.append((l, "fullms" if k_ == len(bl_) - 1 else "full", t0, n))

    try:
        for l in range(2):
            if stop is not None and stop == -1:
                raise _Stop()
            if l == 0:
                ada_setup(0)
            layer_setup(l)
            if stop is not None and stop == 0:
                raise _Stop()
            if l == 1:
                P.op("pool", "memset", halo[:, :, :], 0.0, r=["halo"], w=["halo"])
            for (l_, kind, t0, n) in [b for b in BLOCKS if b[0] == l]:
                if kind == "fullms" and l == 0:
                    ada_setup(1)
                block(l_, kind, t0, n)
                nblk[0] += 1
                if stop is not None and nblk[0] >= stop:
                    raise _Stop()
        assert ws["consumed"] == len(ws["seq"]), (ws["consumed"], len(ws["seq"]))
    except _Stop:
        dbg = dout("dbgx", (128, 8, 512))
        P.op("sp", "dma_start", out=dbg, in_=cur["xT"][:, :, :], r=[("xT", cur["par"], c) for c in range(8)], semkey="dbg")
        dbg2 = dout("dbgh", (128, 8, 512))
        P.op("pool", "dma_start", out=dbg2, in_=tmpn[:, :, :].rearrange("p a b -> p (a b)"), r=[("tmpn", 0), ("tmpn", 1)], semkey="dbg") if False else None

    P.emit(nc, stack)
    stack.close()
    return nc


def _fm(a):
    t, f = a.shape
    return np.ascontiguousarray(a.reshape(t, f // 128, 128).transpose(2, 1, 0))


def _slab(wm):
    k, m = wm.shape
    return np.ascontiguousarray(wm.reshape(k // 128, 128, m).transpose(1, 0, 2)).reshape(128, (k // 128) * m)


_NC_CACHE = {}


def kernel(x_prompt, x_sample, cache_attn_k, cache_attn_v, cache_ffn_conv, c_prompt, c_sample,
           norm1_g, norm2_g, w_ada, b_ada, w_in, q_norm_g, k_norm_g, rel_bias, v_norm_g,
           w_spatial, b_spatial, w_out_a, w_out_b, w_out, w_ffn_in, ffn_conv_w, ffn_conv_b, w_ffn_out):
    f = lambda a: np.asarray(a, dtype=np.float32)
    x_prompt, x_sample, cache_attn_k, cache_attn_v, cache_ffn_conv = map(f, (x_prompt, x_sample, cache_attn_k, cache_attn_v, cache_ffn_conv))
    c_prompt, c_sample, norm1_g, norm2_g, w_ada, b_ada, w_in = map(f, (c_prompt, c_sample, norm1_g, norm2_g, w_ada, b_ada, w_in))
    q_norm_g, k_norm_g, rel_bias, v_norm_g, w_spatial, b_spatial = map(f, (q_norm_g, k_norm_g, rel_bias, v_norm_g, w_spatial, b_spatial))
    w_out_a, w_out_b, w_out, w_ffn_in, ffn_conv_w, ffn_conv_b, w_ffn_out = map(f, (w_out_a, w_out_b, w_out, w_ffn_in, ffn_conv_w, ffn_conv_b, w_ffn_out))

    in_maps = _prep(x_prompt, x_sample, cache_attn_k, cache_attn_v, cache_ffn_conv, c_prompt, c_sample,
                    norm1_g, norm2_g, w_ada, b_ada, w_in, q_norm_g, k_norm_g, rel_bias, v_norm_g,
                    w_spatial, b_spatial, w_out_a, w_out_b, w_out, w_ffn_in, ffn_conv_w, ffn_conv_b, w_ffn_out)
    if "nc" not in _NC_CACHE:
        _NC_CACHE["nc"] = build_nc()
    nc = _NC_CACHE["nc"]
    res = run_bass_kernel_spmd(nc, in_maps, core_ids=list(range(NCORES)))
    return _post(res.results)


def _prep(x_prompt, x_sample, cache_attn_k, cache_attn_v, cache_ffn_conv, c_prompt, c_sample,
          norm1_g, norm2_g, w_ada, b_ada, w_in, q_norm_g, k_norm_g, rel_bias, v_norm_g,
          w_spatial, b_spatial, w_out_a, w_out_b, w_out, w_ffn_in, ffn_conv_w, ffn_conv_b, w_ffn_out):

    wall = np.empty((2, 24, 128, 4096), np.float32)
    wfo = np.empty((2, 8, 128, 2816), np.float32)
    wada = np.empty((2, 12, 128, 4096), np.float32)
    for l in range(2):
        for s in range(9):
            wall[l, s] = _slab(w_in[l][:, s * 512:(s + 1) * 512])
        wall[l, 9] = _slab(w_out_a[l])
        wall[l, 10] = _slab(w_out_b[l])
        for s in range(2):
            wall[l, 11 + s] = _slab(w_out[l][:, s * 512:(s + 1) * 512])
        for jj in range(11):
            idx = np.concatenate([np.arange(2 * jj * 128, (2 * jj + 2) * 128), DFF + np.arange(2 * jj * 128, (2 * jj + 2) * 128)])
            wall[l, 13 + jj] = _slab(w_ffn_in[l][:, idx])
        for c in range(8):
            wfo[l, c] = _slab(w_ffn_out[l][:, c * 128:(c + 1) * 128])
        for s in range(12):
            wada[l, s] = _slab(w_ada[l][:, s * 512:(s + 1) * 512])
    col = lambda v: np.ascontiguousarray(v.reshape(-1, 128).T)
    nrm = np.stack([np.concatenate([col(norm1_g[l]), col(norm2_g[l])], 1) for l in range(2)])
    bada = np.stack([col(b_ada[l]) for l in range(2)])
    qkg = np.stack([np.stack([np.tile(q_norm_g[l], 2), np.tile(k_norm_g[l], 2)], 1) for l in range(2)])
    vg = np.stack([np.broadcast_to(v_norm_g[l][None, :], (128, 512)) for l in range(2)]).copy()
    bsp = b_spatial.reshape(2, 1, 512).copy()
    bsps = np.stack([np.tile(b_spatial[l][:, None, :16], (1, 4, 1)).reshape(1, 256) for l in range(2)])
    wspT = np.ascontiguousarray(w_spatial.transpose(0, 3, 1, 2))
    wspb = np.zeros((2, 64, 4, 64), np.float32)
    for bl in range(4):
        wspb[:, bl * 16:(bl + 1) * 16, :, bl * 16:(bl + 1) * 16] = wspT[:, 0:16, :, 0:16]
    s_i = np.arange(128)[:, None]
    t_i = np.arange(128)[None, :]
    tril = np.broadcast_to((s_i <= t_i).astype(np.float32)[:, None, :], (128, 4, 128)).copy()
    trilb = np.zeros((64, 4, 64), np.float32)
    for bl in range(4):
        trilb[bl * 16:(bl + 1) * 16, :, bl * 16:(bl + 1) * 16] = tril[0:16, :, 0:16]
    cw = np.ascontiguousarray(ffn_conv_w.reshape(2, 3, 22, 128).transpose(0, 3, 2, 1))
    cb = np.ascontiguousarray(ffn_conv_b.reshape(2, 22, 128).transpose(0, 2, 1))
    ki = np.arange(128)[:, None]
    qi = np.arange(128)[None, :]
    idx1 = np.clip(128 + qi - ki, -128, 128) + 128
    idx0 = np.clip(qi - ki, -128, 128) + 128
    Tb = np.stack([np.stack([rel_bias[l][:, idx1].transpose(1, 0, 2), rel_bias[l][:, idx0].transpose(1, 0, 2)], 1)
                   for l in range(2)])
    b256 = np.stack([np.broadcast_to(rel_bias[l][None, :, 256], (128, 8)) for l in range(2)]).copy()

    shared = dict(wall=wall, wfo=wfo, wada=wada, nrm=nrm, bada=bada, qkg=qkg, vg=vg, bsp=bsp, bsps=bsps, wspT=wspT,
                  wspb=wspb, tril=tril, trilb=trilb, cw=cw, cb=cb, Tb=np.ascontiguousarray(Tb), b256=b256)
    shared = {k: np.ascontiguousarray(v, dtype=np.float32) for k, v in shared.items()}

    in_maps = []
    for core in range(NCORES):
        b, seg = core // 4, core % 4
        s0 = seg * SEG
        a = s0 - HALO
        xw = np.zeros((W, D), np.float32)
        lo = max(a, 0)
        xw[lo - a:] = x_prompt[b, lo:s0 + SEG]
        valid = (np.arange(W) + a >= 0).astype(np.float32)
        m = dict(shared)
        m["xin"] = _fm(xw)
        m["xs"] = _fm(x_sample[4 * core:4 * core + 4].reshape(64, D))
        m["vtok"] = np.ascontiguousarray(valid.reshape(NT, 128).T)
        m["flag"] = np.full((128, 1), 1.0 if a >= 0 else 0.0, np.float32)
        cc = np.concatenate([c_prompt[b:b + 1], c_sample[4 * core:4 * core + 4]], 0)
        m["cT"] = np.ascontiguousarray(cc.reshape(5, 8, 128).transpose(2, 1, 0))
        ck = cache_attn_k[:, 4 * core:4 * core + 4]
        m["ckT"] = np.ascontiguousarray(ck.reshape(2, 4, 512, 4, 2, 64).transpose(0, 4, 5, 3, 1, 2)).reshape(2, 128, 4, 4, 512)
        cvv = cache_attn_v[:, 4 * core:4 * core + 4].reshape(2, 4, 4, 128, 512)
        m["cv"] = np.ascontiguousarray(cvv.transpose(0, 1, 3, 2, 4))
        cc2 = cache_ffn_conv[:, 4 * core:4 * core + 4].reshape(2, 4, 2, 22, 128)
        m["cconv"] = np.ascontiguousarray(cc2.transpose(0, 4, 3, 1, 2))
        in_maps.append(m)
    return in_maps


def _post(R):

    y_prompt = np.empty((2, 8192, D), np.float32)
    y_sample = np.empty((32, 16, D), np.float32)
    nkp = np.empty((2, 2, 512, 8, 64), np.float32)
    nvp = np.empty((2, 2, 512, 8, 64), np.float32)
    ncp = np.empty((2, 2, 2, DFF), np.float32)
    nks = np.empty((2, 32, 16, 8, 64), np.float32)
    nvs = np.empty((2, 32, 16, 8, 64), np.float32)
    nbs = np.empty((2, 32, 16, 4, 128), np.float32)
    ncs = np.empty((2, 32, 2, DFF), np.float32)
    for core in range(NCORES):
        r = R[core]
        b, seg = core // 4, core % 4
        y_prompt[b, seg * SEG:(seg + 1) * SEG] = r["yT"].transpose(2, 1, 0).reshape(SEG, D)
        y_sample[4 * core:4 * core + 4] = r["ysT"].transpose(2, 1, 0).reshape(4, 16, D)
        sl = slice(4 * core, 4 * core + 4)
        for l in range(2):
            if seg == 3:
                nkp[l, b] = r["nkT"][l].reshape(2, 64, 4, 512).transpose(3, 2, 0, 1).reshape(512, 8, 64)
                nvp[l, b] = r["nv"][l].reshape(512, 8, 64)
                ncp[l, b] = r["ncv"][l].transpose(2, 1, 0).reshape(2, DFF)
            nks[l, sl] = r["nksT"][l].reshape(2, 64, 4, 4, 16).transpose(3, 4, 2, 0, 1).reshape(4, 16, 8, 64)
            nvs[l, sl] = r["nvs"][l].transpose(1, 0, 2).reshape(4, 16, 8, 64)
            nbs[l, sl] = r["nbs"][l].reshape(4, 16, 4, 128)
            ncs[l, sl] = r["ncs"][l].transpose(2, 3, 1, 0).reshape(4, 2, DFF)
    return (y_prompt, y_sample, nkp, nvp, ncp, nks, nvs, nbs, ncs)
```

```python
import contextlib
import numpy as np
import concourse.bass as bass
import concourse.mybir as mybir
from concourse.bass_utils import run_bass_kernel_spmd

F32 = mybir.dt.float32
BF16 = mybir.dt.bfloat16
AF = mybir.ActivationFunctionType
ALU = mybir.AluOpType
AX = mybir.AxisListType

NCORES = 8
D = 1024
SEG = 2048
HALO = 1152
W = SEG + HALO
NT = W // 128
DFF = 2816
EPS = 1e-6
NB = 4
L0_BLOCKS = [(4, 4), (8, 4), (12, 4), (16, 3), (19, 3), (22, 3)]
L1_BLOCKS = [(8, 4), (12, 4), (16, 3), (19, 3), (22, 3)]
OUT_T0 = 9
KEEP_T0 = 21
_DBG = {}


class Prog:
    STREAMS = ("sp", "act", "dve", "pool", "pe")

    def __init__(self):
        self.ops = []
        self.keyw = {}
        self.keyr = {}
        self.dcnt = {}

    def op(self, stream, method, *args, r=(), w=(), semkey=None, **kw):
        idx = len(self.ops)
        dom = ("d", semkey) if semkey is not None else ("s", stream)
        deps = {}

        def add(d, i):
            if d[0] == "d":
                i = self.dcnt[d]
            if deps.get(d, -1) < i:
                deps[d] = i

        for k in r:
            for d, i in self.keyw.get(k, {}).items():
                add(d, i)
        skip_same = dom[0] == "d" or stream == "pe"
        for k in w:
            for d, i in self.keyr.get(k, {}).items():
                if d == dom and (skip_same or i == idx):
                    continue
                add(d, i)
            for d, i in self.keyw.get(k, {}).items():
                if d == dom and skip_same:
                    continue
                add(d, i)
        for k in r:
            self.keyr.setdefault(k, {})[dom] = idx
        for k in w:
            if self.keyr.get(k):
                self.keyw[k] = {dom: idx}
                self.keyr[k] = {}
            else:
                self.keyw.setdefault(k, {})[dom] = idx
        if dom[0] == "d":
            self.dcnt[dom] = self.dcnt.get(dom, 0) + 16
        self.ops.append(dict(stream=stream, method=method, args=args, kw=kw, dom=dom,
                             deps=list(deps.items()), stage=_DBG.get("stage", "")))
        return idx

    def emit(self, nc, stack):
        ops = self.ops
        needs = set()
        for o in ops:
            for d, i in o["deps"]:
                if d[0] == "s":
                    needs.add(i)
        cnt = {}
        sems = {}
        issuer = {}
        for i, o in enumerate(ops):
            d = o["dom"]
            if d[0] == "d":
                cnt[d] = cnt.get(d, 0) + 16
                o["done"] = cnt[d]
                assert issuer.setdefault(d, o["stream"]) == o["stream"], d
            elif i in needs:
                cnt[d] = cnt.get(d, 0) + 1
                o["done"] = cnt[d]
            else:
                o["done"] = None
        for d in cnt:
            sems[d] = stack.enter_context(nc.semaphore("s%d" % len(sems)))
        block = stack.enter_context(nc.Block())
        self.nsem = len(sems)

        def run(stream, eng):
            waited = {}
            for o in ops:
                if o["stream"] != stream:
                    continue
                for d, i in o["deps"]:
                    v = i if d[0] == "d" else ops[i]["done"]
                    if waited.get(d, 0) < v:
                        eng.wait_ge(sems[d], v)
                        waited[d] = v
                        _DBG.setdefault("waits", {}).setdefault(stream, []).append((o["stage"], d, (ops[i]["method"], ops[i]["stage"], str(ops[i]["kw"].get("out", ops[i]["args"][:1]))[:90]) if d[0] == "s" else None, o["method"]))
                ins = getattr(eng, o["method"])(*o["args"], **o["kw"])
                if o["done"] is not None:
                    d = o["dom"]
                    ins.then_inc(sems[d], 16 if d[0] == "d" else 1)
            for d, v in cnt.items():
                if d[0] == "d" and issuer[d] == stream and waited.get(d, 0) < v:
                    eng.wait_ge(sems[d], v)

        @block.sync
        def _(e):
            run("sp", e)

        @block.scalar
        def _(e):
            run("act", e)

        @block.vector
        def _(e):
            run("dve", e)

        @block.gpsimd
        def _(e):
            run("pool", e)

        @block.tensor
        def _(e):
            run("pe", e)


def build_nc(stop=None):
    class _Stop(Exception):
        pass

    nblk = [0]

    def ck(st):
        _DBG["stage"] = st + (10 if _DBG.get("stage", 0) >= 10 else 0)
        if stop is not None and nblk[0] + st / 10.0 >= stop - 1e-9 and stop > 0:
            raise _Stop()

    nc = bass.Bass("TRN2", target_bir_lowering=False)
    P = Prog()
    stack = contextlib.ExitStack()

    def din(name, shape):
        return nc.dram_tensor(name, list(shape), F32, kind="ExternalInput").ap()

    def dout(name, shape):
        return nc.dram_tensor(name, list(shape), F32, kind="ExternalOutput").ap()

    def dint(name, shape, dt):
        return nc.dram_tensor(name, list(shape), dt, kind="Internal").ap()

    xin = din("xin", (128, 8, W))
    xs_d = din("xs", (128, 8, 64))
    vtok_d = din("vtok", (128, NT))
    flag_d = din("flag", (128, 1))
    cT_d = din("cT", (128, 8, 5))
    ckT_d = din("ckT", (2, 128, 4, 4, 512))
    cv_d = din("cv", (2, 4, 128, 4, 512))
    cconv_d = din("cconv", (2, 128, 22, 4, 2))
    wall_d = din("wall", (2, 24, 128, 4096))
    wfo_d = din("wfo", (2, 8, 128, 2816))
    wada_d = din("wada", (2, 12, 128, 4096))
    nrm_d = din("nrm", (2, 128, 16))
    bada_d = din("bada", (2, 128, 48))
    qkg_d = din("qkg", (2, 128, 2))
    vg_d = din("vg", (2, 128, 512))
    bsp_d = din("bsp", (2, 1, 512))
    bsps_d = din("bsps", (2, 1, 256))
    wspT_d = din("wspT", (2, 128, 4, 128))
    wspb_d = din("wspb", (2, 64, 4, 64))
    tril_d = din("tril", (128, 4, 128))
    trilb_d = din("trilb", (64, 4, 64))
    cw_d = din("cw", (2, 128, 22, 3))
    cb_d = din("cb", (2, 128, 22))
    Tb_d = din("Tb", (2, 128, 2, 8, 128))
    b256_d = din("b256", (2, 128, 8))

    yT_d = dout("yT", (128, 8, SEG))
    ysT_d = dout("ysT", (128, 8, 64))
    nkT_d = dout("nkT", (2, 128, 4, 512))
    nv_d = dout("nv", (2, 4, 128, 512))
    ncv_d = dout("ncv", (2, 128, 22, 2))
    nksT_d = dout("nksT", (2, 128, 4, 64))
    nvs_d = dout("nvs", (2, 16, 4, 512))
    nbs_d = dout("nbs", (2, 64, 512))
    ncs_d = dout("ncs", (2, 128, 22, 4, 2))

    wsc = dint("wsc", (2, 24, 128, 4096), BF16)
    wsc_fo = dint("wscfo", (2, 8, 128, 2816), BF16)
    x1s = dint("x1s", (128, 8, 21 * 128), F32)
    xs1 = dint("xs1", (128, 8, 64), F32)

    def sb(name, shape, dt=F32):
        return stack.enter_context(nc.sbuf_tensor("sb_" + name, list(shape), dt))

    xTt = sb("xT", (128, 2, 8, 512))
    cur = {"xT": xTt[:, 0], "par": 0, "idx": 0}
    hT = sb("hT", (128, 8, 512), BF16)
    sq = sb("sq", (128, 3, 512), BF16)
    rstd = sb("rstd", (128, 512))
    tmpn = sb("tmpn", (128, 2, 512))
    qz = sb("qz", (128, 4, 4, 2, 128), BF16)
    kst = sb("kst", (128, 2, 512))
    rs = sb("rs", (128, 2, 512))
    kT = sb("kT", (128, 4, 1024), BF16)
    vt = sb("vt", (128, 8, 512), BF16)
    ubT = sb("ubT", (128, 4, 512), BF16)
    vbn = sb("vbn", (128, 4, 512), BF16)
    gl = sb("gl", (128, 2, 512))
    ssv = sb("ssv", (128, 4))
    SA = sb("SA", (128, 24, 512), BF16)
    oaT = sb("oaT", (128, 4, 512), BF16)
    obT = sb("obT", (128, 4, 512), BF16)
    Pt = sb("Pt", (128, 2, 5, 256), BF16)
    rden = sb("rden", (128, 2, 256))
    gb = sb("gb", (128, 3, 516))
    gc = sb("gc", (128, 2, 512))
    ge = sb("ge", (128, 2, 512))
    halo = sb("halo", (128, 22, 2))
    halos = sb("halos", (128, 22, 4, 2))
    ncs_st = sb("ncs_st", (128, 22, 4, 2))
    wslab = sb("wslab", (128, NB, 4096), BF16)
    ones_bf = sb("ones_bf", (128, 128), BF16)
    M0 = sb("M0", (128, 256), BF16)
    blk64 = sb("blk64", (128, 128), BF16)
    ones_row = sb("ones_row", (1, 128), BF16)
    vones = sb("vones", (128, 9, 128), BF16)
    vtok = sb("vtok", (128, NT))
    flag = sb("flag", (128, 1))
    cTs = sb("cTs", (128, 8, 5))
    cs = sb("cs", (128, 8, 5), BF16)
    E = sb("E", (128, 2, 8, 128), BF16)
    negb = sb("negb", (128, 8))
    modTt = sb("modT", (128, 2, 48, 5))
    A12t = sb("A12", (128, 2, 2, 8, 5))
    nrmt = sb("nrm", (128, 2, 16))
    badat = sb("bada", (128, 2, 48))
    qkg = sb("qkg", (128, 2))
    vg = sb("vg", (128, 512))
    bsp = sb("bsp", (1, 512), BF16)
    bsps = sb("bsps", (1, 256), BF16)
    WcT = sb("WcT", (128, 4, 128), BF16)
    Wblk = sb("Wblk", (64, 4, 64), BF16)
    cw = sb("cw", (128, 22, 3))
    cb = sb("cb", (128, 22))
    kTs = sb("kTs", (128, 4, 64), BF16)
    vbns = sb("vbns", (64, 512), BF16)
    ckb = sb("ckb", (128, 1, 4, 512), BF16)
    cvb = sb("cvb", (128, 1, 4, 512), BF16)
    Pts = sb("Pts", (128, 2, 2, 80), BF16)
    exs = sb("exs", (128, 2, 2, 32))
    rdens = sb("rdens", (128, 2, 32))

    psd = [stack.enter_context(nc.psum_tensor("ps%d" % i, [128, 1024], F32)) for i in range(4)]
    ps = [psd[i // 2][:, (i % 2) * 512:(i % 2 + 1) * 512] for i in range(8)]

    def bk(i):
        return [("ps", i)]

    bank_ctr = [0]

    reserved = set()

    def nb():
        while True:
            b = bank_ctr[0] % 8
            bank_ctr[0] += 1
            if b not in reserved:
                return b

    rot = {}

    def nrot(name, n):
        v = rot.get(name, 0)
        rot[name] = v + 1
        return v % n

    ws = dict(seq=[], issued=0, consumed=0)

    casted = set()

    def ws_issue(upto):
        while ws["issued"] < min(upto, len(ws["seq"])):
            kind_, l_, s_ = ws["seq"][ws["issued"]]
            slot = ws["issued"] % NB
            wk, sk = [("wslab", slot)], ("w", slot)
            if kind_ == "ada":
                P.op("pool", "dma_start", out=wslab[:, slot, :], in_=wada_d[l_, s_], w=wk, semkey=sk)
            else:
                ncols = 4096 if kind_ == "wall" else 2816
                src32 = wall_d[l_, s_] if kind_ == "wall" else wfo_d[l_, s_]
                scr = wsc[l_, s_] if kind_ == "wall" else wsc_fo[l_, s_]
                key = ("wsc", kind_, l_, s_)
                if key not in casted:
                    casted.add(key)
                    P.op("pool", "dma_start", out=wslab[:, slot, 0:ncols], in_=src32, w=wk, semkey=sk)
                    P.op("sp", "dma_start", out=scr, in_=wslab[:, slot, 0:ncols], r=wk, w=[key],
                         semkey=("wst", slot))
                else:
                    P.op("pool", "dma_start", out=wslab[:, slot, 0:ncols], in_=scr, r=[key], w=wk, semkey=sk)
            ws["issued"] += 1

    def ws_get():
        assert ws["consumed"] < len(ws["seq"])
        ws_issue(ws["consumed"] + 1)
        slot = ws["consumed"] % NB
        ws["consumed"] += 1
        return slot

    def ws_done():
        ws_issue(ws["consumed"] + NB)

    def seq_ada(l):
        for s_ in range(12):
            ws["seq"].append(("ada", l, s_))

    def seq_block(l, kind):
        if kind == "kv":
            for s_ in (1, 2):
                ws["seq"].append(("wall", l, s_))
            return
        for s_ in range(24):
            ws["seq"].append(("wall", l, s_))
        for s_ in range(8):
            ws["seq"].append(("fo", l, s_))

    seq_ada(0)
    for l in range(2):
        seq_block(l, "kv")
        nfull = len(L0_BLOCKS if l == 0 else L1_BLOCKS)
        for k_ in range(nfull):
            if l == 0 and k_ == nfull - 1:
                seq_ada(1)
            seq_block(l, "full")

    P.op("pool", "memset", ones_bf[:, :], 1.0, w=["ones_bf"])
    P.op("pool", "memset", blk64[:, :], 0.0, w=["blk64"])
    P.op("pool", "memset", blk64[0:64, 0:64], 1.0, w=["blk64"])
    P.op("pool", "memset", blk64[64:128, 64:128], 1.0, w=["blk64"])
    P.op("pool", "memset", ones_row[:, :], 1.0, w=["ones_row"])
    P.op("pool", "memset", M0[:, :], 1.0, w=["M0"])
    P.op("pool", "memset", M0[0:64, :].rearrange("p (h q) -> p h q", h=2)[:, :, 64:128], 0.0, w=["M0"])
    P.op("pool", "memset", Pt[:, :, :, :].rearrange("p a b c -> p (a b c)"), 0.0, w=[("Pt", 0), ("Pt", 1)])
    P.op("pool", "memset", qz[:, :, :, :, :].rearrange("p a b c d -> p (a b c d)"), 0.0, w=[("qT", c) for c in range(4)])
    P.op("pool", "memset", halo[:, :, :], 0.0, w=["halo"])
    P.op("sp", "dma_start", out=vtok[:, :], in_=vtok_d, w=["vtok"], semkey="c0")
    P.op("sp", "dma_start", out=flag[:, :], in_=flag_d, w=["flag"], semkey="c0")
    P.op("sp", "dma_start", out=cTs[:, :, :], in_=cT_d, w=["cTs"], semkey="c0")
    P.op("act", "activation", out=cs[:, :, :], in_=cTs[:, :, :], func=AF.Silu, r=["cTs"], w=["cs"])
    for t in range(9):
        P.op("act", "activation", out=vones[:, t, :], in_=ones_bf[:, :], func=AF.Copy,
             scale=vtok[:, t:t + 1], r=["ones_bf", "vtok"], w=["vones"])
    def layer_setup(l):
        for dst, src, key in ((qkg, qkg_d[l], "qkg"),
                              (vg, vg_d[l], "vg"), (cw, cw_d[l], "cw"), (cb, cb_d[l], "cb"),
                              (negb, b256_d[l], "negb")):
            full = tuple(slice(None) for _ in dst.shape)
            P.op("sp", "dma_start", out=dst[full], in_=src, w=[key], semkey="ls")
        P.op("sp", "dma_start", out=tmpn[:, 0, :].rearrange("p (h q) -> p h q", h=4), in_=Tb_d[l][:, 0, 0:4, :],
             w=[("tmpn", 0)], semkey="ls")
        P.op("sp", "dma_start", out=tmpn[:, 1, :].rearrange("p (h q) -> p h q", h=4), in_=Tb_d[l][:, 0, 4:8, :],
             w=[("tmpn", 1)], semkey="ls")
        P.op("sp", "dma_start", out=rs[:, 0, :].rearrange("p (h q) -> p h q", h=4), in_=Tb_d[l][:, 1, 0:4, :],
             w=[("rs", 0)], semkey="ls")
        P.op("sp", "dma_start", out=rs[:, 1, :].rearrange("p (h q) -> p h q", h=4), in_=Tb_d[l][:, 1, 4:8, :],
             w=[("rs", 1)], semkey="ls")
        P.op("sp", "dma_start", out=halos[:, :, :, :], in_=cconv_d[l], w=["halos"], semkey="ls")
        P.op("sp", "dma_start", out=gl[0:1, 0, :], in_=bsp_d[l], w=[("gl", 0)], semkey="ls")
        P.op("sp", "dma_start", out=gl[0:1, 1, 0:256], in_=bsps_d[l], w=[("gl", 1)], semkey="ls")
        P.op("sp", "dma_start", out=gc[:, 0, :], in_=wspT_d[l].rearrange("p g t -> p (g t)"),
             w=[("gc", 0)], semkey="ls3")
        P.op("sp", "dma_start", out=gc[:, 1, :], in_=tril_d.rearrange("p g t -> p (g t)"),
             w=[("gc", 1)], semkey="ls3")
        P.op("sp", "dma_start", out=ge[0:64, 0, 0:256], in_=wspb_d[l].rearrange("p g t -> p (g t)"),
             w=[("ge", 0)], semkey="ls3")
        P.op("sp", "dma_start", out=ge[0:64, 1, 0:256], in_=trilb_d.rearrange("p g t -> p (g t)"),
             w=[("ge", 1)], semkey="ls3")
        P.op("dve", "tensor_tensor", out=WcT[:, :, :].rearrange("p g t -> p (g t)"), in0=gc[:, 0, :],
             in1=gc[:, 1, :], op=ALU.mult, r=[("gc", 0), ("gc", 1)], w=["WcT"])
        P.op("dve", "tensor_tensor", out=Wblk[:, :, :].rearrange("p g t -> p (g t)"), in0=ge[0:64, 0, 0:256],
             in1=ge[0:64, 1, 0:256], op=ALU.mult, r=[("ge", 0), ("ge", 1)], w=["Wblk"])
        P.op("dve", "tensor_copy", bsp[:, :], gl[0:1, 0, :], r=[("gl", 0)], w=["bsp"])
        P.op("dve", "tensor_copy", bsps[:, :], gl[0:1, 1, 0:256], r=[("gl", 1)], w=["bsps"])
        P.op("dve", "tensor_scalar_mul", qkg[:, 1:2], qkg[:, 1:2], 8.0, r=["qkg"], w=["qkg"])
        P.op("dve", "tensor_scalar_mul", negb[:, :], negb[:, :], -1.0, r=["negb"], w=["negb"])
        for t in range(2):
            for h in range(8):
                stg = (tmpn if t == 0 else rs)
                skey = ("tmpn" if t == 0 else "rs", h // 4)
                P.op("act", "activation", out=E[:, t, h, :], in_=stg[:, h // 4, (h % 4) * 128:(h % 4 + 1) * 128],
                     func=AF.Exp, bias=negb[:, h:h + 1], scale=1.0, r=[skey, "negb"], w=["E"])
        P.op("pool", "memset", E[64:128, 1, :, 0:64], 0.0, r=["E"], w=["E"])

    def ada_setup(l):
        modT, A12, nrm, bada = modTt[:, l], A12t[:, l], nrmt[:, l], badat[:, l]
        P.op("sp", "dma_start", out=nrm, in_=nrm_d[l], w=[("nrm", l)], semkey=("lsa", l))
        P.op("sp", "dma_start", out=bada, in_=bada_d[l], w=[("bada", l)], semkey=("lsa", l))
        b = nb()
        for s_ in range(12):
            slot = ws_get()
            for m in range(4):
                ci = s_ * 4 + m
                for kc in range(8):
                    P.op("pe", "matmul", ps[b][:, ci * 5:(ci + 1) * 5],
                         lhsT=wslab[:, slot, kc * 512 + m * 128: kc * 512 + (m + 1) * 128],
                         rhs=cs[:, kc, :], start=(kc == 0), stop=(kc == 7),
                         r=[("wslab", slot), "cs"], w=bk(b))
            ws_done()
        for bb in range(5):
            P.op("dve", "tensor_tensor", out=modT[:, :, bb],
                 in0=ps[b][:, 0:240].rearrange("p (c b) -> p c b", b=5)[:, :, bb], in1=bada,
                 op=ALU.add, r=bk(b) + [("bada", l)], w=[("modT", l)])
        P.op("dve", "tensor_scalar_mul", modT[:, 16:24, :], modT[:, 16:24, :], 0.5, r=[("modT", l)], w=[("modT", l)])
        for which in range(2):
            sc0 = 8 if which == 0 else 32
            for bb in range(5):
                P.op("dve", "tensor_scalar", out=A12[:, which, :, bb], in0=modT[:, sc0:sc0 + 8, bb],
                     scalar1=1.0, scalar2=32.0, op0=ALU.add, op1=ALU.mult, r=[("modT", l)], w=[("A12", l)])
                P.op("dve", "tensor_tensor", out=A12[:, which, :, bb], in0=A12[:, which, :, bb],
                     in1=nrm[:, which * 8:(which + 1) * 8], op=ALU.mult, r=[("A12", l), ("nrm", l)],
                     w=[("A12", l)])

    def norm_sq(b, c, T):
        r_ = nrot("sq", 3)
        P.op("act", "activation", out=sq[:, r_, 0:T], in_=cur["xT"][:, c, 0:T], func=AF.Square,
             r=[("xT", cur["par"], c)], w=[("sq", r_)])
        return r_

    def norm_ss(b, c, r_, T):
        P.op("pe", "matmul", ps[b][:, 0:T], lhsT=ones_bf[:, :], rhs=sq[:, r_, 0:T],
             start=(c == 0), stop=(c == 7), r=[("sq", r_), "ones_bf"], w=bk(b))

    def norm_fin(b, T):
        P.op("act", "activation", out=rstd[:, 0:T], in_=ps[b][:, 0:T], func=AF.Sqrt, bias=float(D * EPS),
             scale=1.0, r=bk(b), w=["rstd"])
        P.op("dve", "reciprocal", out=rstd[:, 0:T], in_=rstd[:, 0:T], r=["rstd"], w=["rstd"])

    def norm_apply(which, T, groups):
        shc = 0 if which == 0 else 24
        for c in range(8):
            r_ = nrot("tmpn", 2)
            for (c0, n, bb) in groups:
                P.op("dve", "scalar_tensor_tensor", out=tmpn[:, r_, c0:c0 + n], in0=cur["xT"][:, c, c0:c0 + n],
                     scalar=A12t[:, cur["l"], which, c, bb:bb + 1], in1=rstd[:, c0:c0 + n], op0=ALU.mult, op1=ALU.mult,
                     r=[("xT", cur["par"], c), ("A12", cur["l"]), "rstd"], w=[("tmpn", r_)])
            for (c0, n, bb) in groups:
                P.op("act", "activation", out=hT[:, c, c0:c0 + n], in_=tmpn[:, r_, c0:c0 + n],
                     func=AF.Identity, bias=modTt[:, cur["l"], shc + c, bb:bb + 1], scale=1.0,
                     r=[("tmpn", r_), ("modT", cur["l"])], w=[("hT", c)])

    def norm_mod(which, T, groups):
        b = nb()
        for c in range(8):
            r_ = norm_sq(b, c, T)
            norm_ss(b, c, r_, T)
        norm_fin(b, T)
        norm_apply(which, T, groups)

    pro_done = set()

    def blk_geom(i):
        l_, kind_, t0_, n_ = BLOCKS[i]
        Tp_ = n_ * 128
        if kind_ == "fullms":
            return l_, Tp_ + 64, [(0, Tp_, 0)] + [(Tp_ + bl * 16, 16, 1 + bl) for bl in range(4)]
        return l_, Tp_, [(0, Tp_, 0)]

    def pro_ok(i):
        return i < len(BLOCKS)

    class _NextCtx:
        def __init__(self, i):
            self.i = i

        def __enter__(self):
            self.save = dict(cur)
            cur["l"] = BLOCKS[self.i][0]
            cur["par"] = self.i % 2
            cur["xT"] = xTt[:, self.i % 2]

        def __exit__(self, *a):
            cur.update(self.save)

    def proj_fm(b, slot, nkc, ms, col0, rhs_t, rkeys, T):
        for kc in range(nkc):
            P.op("pe", "matmul", ps[b][:, 0:T], lhsT=wslab[:, slot, kc * ms + col0: kc * ms + col0 + 128],
                 rhs=rhs_t(kc), start=(kc == 0), stop=(kc == nkc - 1),
                 r=[("wslab", slot)] + rkeys(kc), w=bk(b))

    def hn_a(b, T):
        r_ = nrot("sq", 3)
        P.op("act", "activation", out=sq[:, r_, 0:T], in_=ps[b][:, 0:T], func=AF.Square, r=bk(b), w=[("sq", r_)])
        return r_

    def hn_b(b, r_, T, gcol, out_ap, out_keys, qchunk=None, Tp=None, ms=False):
        b2 = nb()
        P.op("pe", "matmul", ps[b2][:, 0:T], lhsT=blk64[:, :], rhs=sq[:, r_, 0:T], start=True, stop=True,
             r=[("sq", r_), "blk64"], w=bk(b2))
        r2 = nrot("rs", 2)
        P.op("act", "activation", out=rs[:, r2, 0:T], in_=ps[b2][:, 0:T], func=AF.Sqrt, bias=float(64 * EPS),
             scale=1.0, r=bk(b2), w=[("rs", r2)])
        P.op("dve", "reciprocal", out=rs[:, r2, 0:T], in_=rs[:, r2, 0:T], r=[("rs", r2)], w=[("rs", r2)])
        if qchunk is not None:
            nq = Tp // 128
            for hh in range(2):
                hs = slice(hh * 64, (hh + 1) * 64)
                P.op("dve", "scalar_tensor_tensor", out=qz[hs, qchunk, 0:nq, hh, :],
                     in0=ps[b][hs, 0:Tp].rearrange("p (t q) -> p t q", t=nq), scalar=qkg[hs, gcol:gcol + 1],
                     in1=rs[hs, r2, 0:Tp].rearrange("p (t q) -> p t q", t=nq), op0=ALU.mult, op1=ALU.mult,
                     r=bk(b) + [("rs", r2), "qkg"], w=out_keys)
                if ms:
                    P.op("dve", "scalar_tensor_tensor", out=qz[hs, qchunk, 3, hh, 0:64], in0=ps[b][hs, Tp:Tp + 64],
                         scalar=qkg[hs, gcol:gcol + 1], in1=rs[hs, r2, Tp:Tp + 64], op0=ALU.mult, op1=ALU.mult,
                         r=bk(b) + [("rs", r2), "qkg"], w=out_keys)
            return
        P.op("dve", "scalar_tensor_tensor", out=out_ap, in0=ps[b][:, 0:T], scalar=qkg[:, gcol:gcol + 1],
             in1=rs[:, r2, 0:T], op0=ALU.mult, op1=ALU.mult, r=bk(b) + [("rs", r2), "qkg"], w=out_keys)

    def hT_r(T):
        return (lambda kc: hT[:, kc, 0:T]), (lambda kc: [("hT", kc)])

    def block(l, kind, t0, ntile):
        sample = False
        ms = kind == "fullms"
        Tp = ntile * 128
        C0 = Tp
        T = Tp + (64 if ms else 0)
        tiles = list(range(t0, t0 + ntile))
        groups = [(0, Tp, 0)] + ([(C0 + bl * 16, 16, 1 + bl) for bl in range(4)] if ms else [])
        hr, hk = hT_r(T)
        bi = cur["idx"]
        cur["l"] = l
        _DBG["stage"] = 0
        cur["par"] = bi % 2
        cur["xT"] = xTt[:, bi % 2]

        def xload(i):
            l_, kind_, t0_, n_ = BLOCKS[i]
            par_ = i % 2
            wk = [("xT", par_, c) for c in range(8)]
            T_ = n_ * 128
            if kind_ == "fullms":
                if l_ == 1:
                    P.op("sp", "dma_start", out=xTt[:, par_, :, T_:T_ + 64], in_=xs1, r=["xs1"], w=wk,
                         semkey=("xld", par_))
                else:
                    P.op("sp", "dma_start", out=xTt[:, par_, :, T_:T_ + 64], in_=xs_d, w=wk, semkey=("xld", par_))
            if l_ == 0:
                P.op("sp", "dma_start", out=xTt[:, par_, :, 0:T_], in_=xin[:, :, t0_ * 128: t0_ * 128 + T_], w=wk,
                     semkey=("xld", par_))
            else:
                P.op("sp", "dma_start", out=xTt[:, par_, :, 0:T_],
                     in_=x1s[:, :, (t0_ - 4) * 128: (t0_ - 4) * 128 + T_],
                     r=[("x1s", t) for t in range(t0_, t0_ + n_)], w=wk, semkey=("xld", par_))
            return True

        if bi == 0:
            xload(0)
        if bi + 1 < len(BLOCKS):
            xload(bi + 1)
        cur["idx"] = bi + 1
        if bi not in pro_done:
            norm_mod(0, T, groups)

        qk = ([("q", c) for c in range(4)] if kind != "kv" else []) + [("k", c) for c in range(4)]
        st_ = {}

        def qk_a(n):
            which, c = qk[n]
            if c == 0:
                if which == "k" and kind != "kv":
                    ws_done()
                st_["slot"] = ws_get()
            b = nb()
            proj_fm(b, st_["slot"], 8, 512, c * 128, hr, hk, T)
            return b, hn_a(b, T)

        def qk_b(n, b, r_sq):
            which, c = qk[n]
            if which == "q":
                hn_b(b, r_sq, T, 0, None, [("qT", c)], qchunk=c, Tp=Tp, ms=ms)
                return
            r_ = nrot("kst", 2)
            hn_b(b, r_sq, T, 1, kst[:, r_, 0:T], [("kst", r_)])
            if ms:
                P.op("act", "activation", out=kTs[:, c, :], in_=kst[:, r_, C0:C0 + 64], func=AF.Copy,
                     r=[("kst", r_)], w=["kTs"])
                P.op("sp", "dma_start", out=nksT_d[l, :, c, :], in_=kst[:, r_, C0:C0 + 64], r=[("kst", r_)],
                     semkey=("kst", r_))
            for ti, t in enumerate(tiles):
                sl = t % 8
                P.op("pool", "tensor_copy", kT[:, c, sl * 128:(sl + 1) * 128],
                     kst[:, r_, ti * 128:(ti + 1) * 128], r=[("kst", r_)], w=[("kT", sl)])
                if t >= KEEP_T0:
                    P.op("sp", "dma_start", out=nkT_d[l, :, c, (t - KEEP_T0) * 128:(t - KEEP_T0 + 1) * 128],
                         in_=kst[:, r_, ti * 128:(ti + 1) * 128], r=[("kst", r_)], semkey=("kst", r_))

        pend = qk_a(0)
        for n in range(len(qk)):
            nxt_ = qk_a(n + 1) if n + 1 < len(qk) else None
            qk_b(n, *pend)
            pend = nxt_
        ws_done()
        slot = ws_get()
        if ms:
            for bl in range(4):
                b = nb()
                for kc in range(8):
                    P.op("pe", "matmul", ps[b][0:16, 0:512], lhsT=hT[:, kc, C0 + bl * 16:C0 + (bl + 1) * 16],
                         rhs=wslab[:, slot, kc * 512:(kc + 1) * 512], start=(kc == 0), stop=(kc == 7),
                         r=[("wslab", slot), ("hT", kc)], w=bk(b))
                r_ = nrot("gl", 2)
                P.op("act", "activation", out=gl[0:16, r_, :], in_=ps[b][0:16, 0:512], func=AF.Copy, r=bk(b),
                     w=[("gl", r_)])
                P.op("dve", "tensor_copy", SA[0:16, 16 + bl, :], gl[0:16, r_, :], r=[("gl", r_)], w=[("SA", 16 + bl)])
                P.op("sp", "dma_start", out=nvs_d[l, :, bl, :], in_=gl[0:16, r_, :], r=[("gl", r_)],
                     semkey=("gl", r_))
        if True:
            for ti, t in enumerate(tiles):
                b = nb()
                for kc in range(8):
                    P.op("pe", "matmul", ps[b][:, 0:512], lhsT=hT[:, kc, ti * 128:(ti + 1) * 128],
                         rhs=wslab[:, slot, kc * 512:(kc + 1) * 512], start=(kc == 0), stop=(kc == 7),
                         r=[("wslab", slot), ("hT", kc)], w=bk(b))
                sl = t % 8
                if t >= KEEP_T0:
                    r_ = nrot("gl", 2)
                    P.op("act", "activation", out=gl[:, r_, :], in_=ps[b][:, 0:512], func=AF.Copy, r=bk(b),
                         w=[("gl", r_)])
                    P.op("dve", "tensor_scalar_mul", vt[:, sl, :], gl[:, r_, :], vtok[:, t:t + 1],
                         r=[("gl", r_), "vtok"], w=[("vt", sl)])
                    P.op("sp", "dma_start", out=nv_d[l, t - KEEP_T0], in_=gl[:, r_, :], r=[("gl", r_)],
                         semkey=("gl", r_))
                else:
                    P.op("dve", "tensor_scalar_mul", vt[:, sl, :], ps[b][:, 0:512], vtok[:, t:t + 1],
                         r=bk(b) + ["vtok"], w=[("vt", sl)])
        ws_done()
        if kind == "kv":
            if pro_ok(bi + 1):
                with _NextCtx(bi + 1):
                    l2, T2, g2 = blk_geom(bi + 1)
                    norm_mod(0, T2, g2)
                pro_done.add(bi + 1)
            return

        ck(1)
        slot = ws_get()
        for c in range(4):
            b = nb()
            proj_fm(b, slot, 8, 512, c * 128, hr, hk, T)
            P.op("act", "activation", out=ubT[:, c, 0:T], in_=ps[b][:, 0:T], func=AF.Gelu_apprx_tanh, r=bk(b),
                 w=[("ubT", c)])
        ws_done()
        ck(2)
        slot = ws_get()
        tl = [(ti, 128, False) for ti in range(ntile)] + ([(ntile, 64, True)] if ms else [])
        for ti, M, sample in tl:
            b = nb()
            for kc in range(8):
                P.op("pe", "matmul", ps[b][0:M, 0:512], lhsT=hT[:, kc, ti * 128: ti * 128 + M],
                     rhs=wslab[:, slot, kc * 512:(kc + 1) * 512], start=(kc == 0), stop=(kc == 7),
                     r=[("wslab", slot), ("hT", kc)], w=bk(b))
            r_ = nrot("gl", 2)
            P.op("act", "activation", out=gl[0:M, r_, :], in_=ps[b][0:M, 0:512], func=AF.Gelu_apprx_tanh, r=bk(b),
                 w=[("gl", r_)])
            r2 = nrot("tmpn", 2)
            P.op("dve", "tensor_tensor", out=tmpn[0:M, r2, :], in0=gl[0:M, r_, :], in1=gl[0:M, r_, :], op=ALU.mult,
                 r=[("gl", r_)], w=[("tmpn", r2)])
            r3 = nrot("ssv", 4)
            P.op("dve", "reduce_sum", out=ssv[0:M, r3:r3 + 1], in_=tmpn[0:M, r2, :], axis=AX.X,
                 r=[("tmpn", r2)], w=[("ssv", r3)])
            P.op("act", "activation", out=ssv[0:M, r3:r3 + 1], in_=ssv[0:M, r3:r3 + 1], func=AF.Sqrt,
                 bias=float(EPS), scale=1.0 / 512.0, r=[("ssv", r3)], w=[("ssv", r3)])
            P.op("dve", "reciprocal", out=ssv[0:M, r3:r3 + 1], in_=ssv[0:M, r3:r3 + 1], r=[("ssv", r3)],
                 w=[("ssv", r3)])
            if sample:
                P.op("dve", "scalar_tensor_tensor", out=tmpn[0:64, r2, :], in0=gl[0:64, r_, :],
                     scalar=ssv[0:64, r3:r3 + 1], in1=vg[0:64, :], op0=ALU.mult, op1=ALU.mult,
                     r=[("gl", r_), ("ssv", r3), "vg"], w=[("tmpn", r2)])
                P.op("act", "activation", out=vbns[:, :], in_=tmpn[0:64, r2, :], func=AF.Copy, r=[("tmpn", r2)],
                     w=["vbns"])
                P.op("sp", "dma_start", out=nbs_d[l], in_=tmpn[0:64, r2, :], r=[("tmpn", r2)], semkey=("tmpn", r2))
            else:
                P.op("dve", "scalar_tensor_tensor", out=vbn[:, ti, :], in0=gl[:, r_, :], scalar=ssv[:, r3:r3 + 1],
                     in1=vg[:, :], op0=ALU.mult, op1=ALU.mult, r=[("gl", r_), ("ssv", r3), "vg"], w=[("vbn", ti)])
        ws_done()
        sample = False
        ck(3)
        def gate_chunks():
            for s_ in range(4):
                slot = ws_get()
                for m in range(4):
                    c = s_ * 4 + m

                    def emit(b, slot=slot, m=m, c=c):
                        proj_fm(b, slot, 8, 512, m * 128, hr, hk, T)
                        P.op("act", "activation", out=SA[:, c, 0:T], in_=ps[b][:, 0:T], func=AF.Tanh, scale=0.5,
                             r=bk(b), w=[("SA", c)])
                    yield emit
                ws_done()

        gates = gate_chunks()

        ck(4)
        if ms:
            items = [(bl, hp) for bl in range(4) for hp in range(4)]

            def ss1(it, st):
                bl, hp = it
                cr = 0
                for j in range(5):
                    for hh in range(2):
                        hs = slice(hh * 64, (hh + 1) * 64)
                        b = st * 4 + hh
                        if j < 4:
                            P.op("pe", "matmul", ps[b][:, j * 16:(j + 1) * 16],
                                 lhsT=ckb[hs, cr, hp, j * 128:(j + 1) * 128], rhs=qz[hs, hp, 3, hh, bl * 16:(bl + 1) * 16],
                                 start=True, stop=True, r=[("ckb", cr), ("qT", hp)], w=[("ps", b)])
                        else:
                            P.op("pe", "matmul", ps[b][0:16, 64:80],
                                 lhsT=kTs[hs, hp, bl * 16:(bl + 1) * 16], rhs=qz[hs, hp, 3, hh, bl * 16:(bl + 1) * 16],
                                 start=True, stop=True, r=["kTs", ("qT", hp)], w=[("ps", b)])
                for hh in range(2):
                    b = st * 4 + hh
                    h = 2 * hp + hh
                    P.op("act", "activation", out=Pts[:, st, hh, 0:48], in_=ps[b][:, 0:48], func=AF.Exp,
                         r=[("ps", b)], w=[("Pts", st, hh)])
                    P.op("act", "activation", out=exs[:, st, hh, 0:16], in_=ps[b][:, 48:64], func=AF.Exp,
                         r=[("ps", b)], w=[("exs", st, hh)])
                    P.op("act", "activation", out=exs[0:16, st, hh, 16:32], in_=ps[b][0:16, 64:80], func=AF.Exp,
                         r=[("ps", b)], w=[("exs", st, hh)])
                    P.op("dve", "tensor_tensor", out=Pts[:, st, hh, 48:64], in0=exs[:, st, hh, 0:16],
                         in1=E[:, 0, h, 0:16], op=ALU.mult, r=[("exs", st, hh), "E"], w=[("Pts", st, hh)])
                    P.op("dve", "tensor_tensor", out=Pts[0:16, st, hh, 64:80], in0=exs[0:16, st, hh, 16:32],
                         in1=E[0:16, 1, h, 0:16], op=ALU.mult, r=[("exs", st, hh), "E"], w=[("Pts", st, hh)])

            def ss2(it, st):
                bl, hp = it
                cr = 0
                bd, bo = st * 4 + 2, 3
                oc = slice(st * 128, st * 128 + 16)
                for hh in range(2):
                    for j in range(5):
                        if j < 4:
                            P.op("pe", "matmul", ps[bd][:, hh * 16:(hh + 1) * 16], lhsT=ones_bf[:, :],
                                 rhs=Pts[:, st, hh, j * 16:(j + 1) * 16], start=(j == 0), stop=False,
                                 r=[("Pts", st, hh), "ones_bf"], w=[("ps", bd)])
                        else:
                            P.op("pe", "matmul", ps[bd][:, hh * 16:(hh + 1) * 16], lhsT=ones_bf[0:16, :],
                                 rhs=Pts[0:16, st, hh, 64:80], start=False, stop=True,
                                 r=[("Pts", st, hh), "ones_bf"], w=[("ps", bd)])
                for hh in range(2):
                    hs = slice(hh * 64, (hh + 1) * 64)
                    fc = (2 * hp + hh) * 64
                    for j in range(5):
                        if j < 4:
                            P.op("pe", "matmul", ps[bo][hs, oc], lhsT=cvb[:, cr, j, fc:fc + 64],
                                 rhs=Pts[:, st, hh, j * 16:(j + 1) * 16], start=(j == 0), stop=False,
                                 r=[("Pts", st, hh), ("cvb", cr)], w=[("ps", bo)])
                        else:
                            P.op("pe", "matmul", ps[bo][hs, oc], lhsT=SA[0:16, 16 + bl, fc:fc + 64],
                                 rhs=Pts[0:16, st, hh, 64:80], start=False, stop=True,
                                 r=[("Pts", st, hh), ("SA", 16 + bl)], w=[("ps", bo)])
                P.op("dve", "reciprocal", out=rdens[:, st, :], in_=ps[bd][:, 0:32], r=[("ps", bd)],
                     w=[("rdens", st)])
                for hh in range(2):
                    hs = slice(hh * 64, (hh + 1) * 64)
                    P.op("dve", "tensor_tensor", out=oaT[hs, hp, C0 + bl * 16:C0 + (bl + 1) * 16], in0=ps[bo][hs, oc],
                         in1=rdens[hs, st, hh * 16:(hh + 1) * 16], op=ALU.mult, r=[("ps", bo), ("rdens", st)],
                         w=[("oaT", hp)])
        if True:
            items = [(qi, hp) for qi in range(ntile) for hp in range(4)]

            def s1(it, st):
                qi, hp = it
                qt = t0 + qi
                bS, b4 = st * 4, st * 4 + 2
                for j in range(5):
                    sl = (qt - 4 + j) % 8
                    if j < 4:
                        out_ = psd[st * 2][:, j * 256:(j + 1) * 256]
                        wkey = ("ps", bS + j // 2)
                    else:
                        out_ = ps[b4][:, 0:256]
                        wkey = ("ps", b4)
                    P.op("pe", "matmul", out_, lhsT=kT[:, hp, sl * 128:(sl + 1) * 128],
                         rhs=qz[:, hp, qi, :, :].rearrange("p h q -> p (h q)"), start=True, stop=True,
                         r=[("kT", sl), ("qT", hp)], w=[wkey])
                pk = [("Pt", st)]
                rk2 = [("ps", bS), ("ps", bS + 1)]
                two = "p (h q) -> p h q"
                P.op("act", "activation", out=Pt[:, st, 0:4, :].rearrange("p j c -> p (j c)"),
                     in_=psd[st * 2][:, 0:1024], func=AF.Exp, r=rk2, w=pk)
                P.op("act", "activation", out=Pt[:, st, 4, :], in_=ps[b4][:, 0:256], func=AF.Exp,
                     r=[("ps", b4)], w=pk)
                P.op("dve", "tensor_tensor", out=Pt[:, st, 0, :], in0=Pt[:, st, 0, :], in1=M0[:, :], op=ALU.mult,
                     r=pk + ["M0"], w=pk)
                P.op("dve", "tensor_tensor", out=Pt[:, st, 3:5, :].rearrange("p t (h q) -> p t h q", h=2),
                     in0=Pt[:, st, 3:5, :].rearrange("p t (h q) -> p t h q", h=2),
                     in1=E[:, :, 2 * hp:2 * hp + 2, :], op=ALU.mult, r=pk + ["E"], w=pk)

            def s2(it, st):
                qi, hp = it
                qt = t0 + qi
                bd, bo = st * 4 + 2, 3
                oc = slice(st * 128, (st + 1) * 128)
                for j in range(5):
                    kt = qt - 4 + j
                    lw = vones[:, kt, :] if kt < 9 else ones_bf[:, :]
                    P.op("pe", "matmul", ps[bd][:, 256:512], lhsT=lw, rhs=Pt[:, st, j, :], start=(j == 0),
                         stop=(j == 4), r=[("Pt", st), "vones", "ones_bf"], w=[("ps", bd)])
                for hh in range(2):
                    hs = slice(hh * 64, (hh + 1) * 64)
                    fc = (2 * hp + hh) * 64
                    for j in range(5):
                        sl = (qt - 4 + j) % 8
                        P.op("pe", "matmul", ps[bo][hs, oc], lhsT=vt[:, sl, fc:fc + 64],
                             rhs=Pt[:, st, j, hh * 128:(hh + 1) * 128], start=(j == 0), stop=(j == 4),
                             r=[("Pt", st), ("vt", sl)], w=[("ps", bo)])
                P.op("dve", "tensor_scalar_max", rden[:, st, :], ps[bd][:, 256:512], 1e-30, r=[("ps", bd)],
                     w=[("rden", st)])
                P.op("dve", "reciprocal", out=rden[:, st, :], in_=rden[:, st, :], r=[("rden", st)],
                     w=[("rden", st)])
                for hh in range(2):
                    hs = slice(hh * 64, (hh + 1) * 64)
                    P.op("dve", "tensor_tensor", out=oaT[hs, hp, qi * 128:(qi + 1) * 128], in0=ps[bo][hs, oc],
                         in1=rden[hs, st, hh * 128:(hh + 1) * 128], op=ALU.mult, r=[("ps", bo), ("rden", st)],
                         w=[("oaT", hp)])

        for i in range(len(items) + 1):
            if i < len(items):
                s1(items[i], i % 2)
                g_ = next(gates, None)
                if g_ is not None:
                    g_(7)
            if i >= 1:
                s2(items[i - 1], (i - 1) % 2)
        for g_ in gates:
            g_(nb())
        if ms:
            for bl in range(4):
                its = [(bl, hp) for hp in range(4)]
                P.op("pool", "dma_start", out=ckb[:, 0, :, :], in_=ckT_d[l, :, :, bl, :], w=[("ckb", 0)],
                     semkey=("ckb", 0))
                P.op("pool", "dma_start", out=cvb[:, 0, :, :], in_=cv_d[l, bl], w=[("cvb", 0)],
                     semkey=("cvb", 0))
                for i in range(len(its) + 1):
                    if i < len(its):
                        ss1(its[i], i % 2)
                    if i >= 1:
                        ss2(its[i - 1], (i - 1) % 2)

        ck(5)
        if ms:
            b = nb()
            for g in range(4):
                P.op("pe", "matmul", ps[b][:, g * 64:(g + 1) * 64], lhsT=vbns[0:64, g * 128:(g + 1) * 128],
                     rhs=Wblk[0:64, g, :], start=True, stop=False, r=["vbns", "Wblk"], w=bk(b))
                P.op("pe", "matmul", ps[b][:, g * 64:(g + 1) * 64], lhsT=ones_row[0:1, :],
                     rhs=bsps[0:1, g * 64:(g + 1) * 64], start=False, stop=True, r=["ones_row", "bsps"], w=bk(b))
            P.op("dve", "tensor_tensor", out=obT[:, :, C0:C0 + 64], in0=ps[b][:, 0:256].rearrange("p (g t) -> p g t", g=4),
                 in1=ubT[:, :, C0:C0 + 64], op=ALU.mult, r=bk(b) + [("ubT", c) for c in range(4)],
                 w=[("obT", c) for c in range(4)])
        if True:
            for ti in range(ntile):
                b = nb()
                for g in range(4):
                    P.op("pe", "matmul", ps[b][:, g * 128:(g + 1) * 128], lhsT=vbn[:, ti, g * 128:(g + 1) * 128],
                         rhs=WcT[:, g, :], start=True, stop=False, r=[("vbn", ti), "WcT"], w=bk(b))
                    P.op("pe", "matmul", ps[b][:, g * 128:(g + 1) * 128], lhsT=ones_row[0:1, :],
                         rhs=bsp[0:1, g * 128:(g + 1) * 128], start=False, stop=True, r=["ones_row", "bsp"], w=bk(b))
                P.op("dve", "tensor_tensor", out=obT[:, :, ti * 128:(ti + 1) * 128],
                     in0=ps[b][:, 0:512].rearrange("p (g t) -> p g t", g=4), in1=ubT[:, :, ti * 128:(ti + 1) * 128],
                     op=ALU.mult, r=bk(b) + [("ubT", c) for c in range(4)], w=[("obT", c) for c in range(4)])

        ck(6)
        sa_, sb_ = ws_get(), ws_get()
        for c in range(8):
            ba, bb_ = nb(), nb()
            proj_fm(ba, sa_, 4, 1024, c * 128, lambda kc: oaT[:, kc, 0:T], lambda kc: [("oaT", kc)], T)
            proj_fm(bb_, sb_, 4, 1024, c * 128, lambda kc: obT[:, kc, 0:T], lambda kc: [("obT", kc)], T)
            P.op("dve", "scalar_tensor_tensor", out=tmpn[:, 0, 0:T], in0=SA[:, c, 0:T], scalar=1.0, in1=ps[ba][:, 0:T],
                 op0=ALU.add, op1=ALU.mult, r=bk(ba) + [("SA", c)], w=[("tmpn", 0)])
            P.op("dve", "scalar_tensor_tensor", out=tmpn[:, 1, 0:T], in0=SA[:, 8 + c, 0:T], scalar=1.0,
                 in1=ps[bb_][:, 0:T], op0=ALU.add, op1=ALU.mult, r=bk(bb_) + [("SA", 8 + c)], w=[("tmpn", 1)])
            P.op("dve", "tensor_tensor", out=SA[:, 16 + c, 0:T], in0=tmpn[:, 0, 0:T], in1=tmpn[:, 1, 0:T], op=ALU.add,
                 r=[("tmpn", 0), ("tmpn", 1)], w=[("SA", 16 + c)])
        ws_done()
        ws_done()
        ck(7)
        for s in range(2):
            slot = ws_get()
            for m in range(4):
                c = s * 4 + m
                b = nb()
                proj_fm(b, slot, 8, 512, m * 128, lambda kc: SA[:, 16 + kc, 0:T], lambda kc: [("SA", 16 + kc)], T)
                for (c0, n, bb) in groups:
                    P.op("dve", "scalar_tensor_tensor", out=cur["xT"][:, c, c0:c0 + n], in0=ps[b][:, c0:c0 + n],
                         scalar=modTt[:, cur["l"], 16 + c, bb:bb + 1], in1=cur["xT"][:, c, c0:c0 + n], op0=ALU.mult, op1=ALU.add,
                         r=bk(b) + [("xT", cur["par"], c), ("modT", cur["l"])], w=[("xT", cur["par"], c)])
            ws_done()

        ck(8)
        norm_mod(1, T, groups)
        nxt = bi + 1 if pro_ok(bi + 1) else None
        if nxt is not None:
            l2, T2, g2 = blk_geom(nxt)
            bss = nb()
            reserved.add(bss)
            sqr = {}
        fst = {}

        def ffn_a(j):
            jj, sub = divmod(j, 2)
            if sub == 0:
                if nxt is not None:
                    with _NextCtx(nxt):
                        if 1 <= jj <= 8:
                            sqr[jj - 1] = norm_sq(bss, jj - 1, T2)
                        if 2 <= jj <= 9:
                            norm_ss(bss, jj - 2, sqr[jj - 2], T2)
                if jj > 0:
                    ws_done()
                fst["slot"] = ws_get()
            slot = fst["slot"]
            bg, bu = nb(), nb()
            proj_fm(bg, slot, 8, 512, sub * 128, hr, hk, T)
            proj_fm(bu, slot, 8, 512, 256 + sub * 128, hr, hk, T)
            r_ = nrot("gb", 3)
            r2 = nrot("gc", 2)
            P.op("act", "activation", out=gb[:, r_, 2:2 + Tp], in_=ps[bg][:, 0:Tp], func=AF.Copy, r=bk(bg),
                 w=[("gb", r_)])
            P.op("dve", "tensor_copy", gb[:, r_, 0:2], halo[:, j, :], r=["halo"], w=[("gb", r_)])
            if t0 == 8:
                P.op("pool", "tensor_scalar_mul", gb[:, r_, 128:130], gb[:, r_, 128:130], flag[:, 0:1],
                     r=[("gb", r_), "flag"], w=[("gb", r_)])
            P.op("pool", "tensor_copy", halo[:, j, :], gb[:, r_, Tp:Tp + 2], r=[("gb", r_)], w=["halo"])
            convs = [([gb[:, r_, k:k + Tp] for k in range(3)], gc[:, r2, 0:Tp])]
            if ms:
                g3 = gb[:, r_, Tp + 2:Tp + 74].rearrange("p (b t) -> p b t", t=18)
                P.op("act", "activation", out=g3[:, :, 2:18],
                     in_=ps[bg][:, C0:C0 + 64].rearrange("p (b t) -> p b t", t=16), func=AF.Copy, r=bk(bg),
                     w=[("gb", r_)])
                P.op("dve", "tensor_copy", g3[:, :, 0:2], halos[:, j, :, :], r=["halos"], w=[("gb", r_)])
                P.op("pool", "tensor_copy", ncs_st[:, j, :, :], g3[:, :, 16:18], r=[("gb", r_)], w=["ncs_st"])
                convs.append(([g3[:, :, k:k + 16] for k in range(3)],
                              gc[:, r2, C0:C0 + 64].rearrange("p (b t) -> p b t", t=16)))
            for views, gcv in convs:
                P.op("act", "activation", out=gcv, in_=views[0], func=AF.Identity, scale=cw[:, j, 0:1],
                     bias=cb[:, j:j + 1], r=[("gb", r_), "cw", "cb"], w=[("gc", r2)])
            return bu, r_, r2, convs

        def ffn_b(j, bu, r_, r2, convs):
            for views, gcv in convs:
                for k in (1, 2):
                    P.op("dve", "scalar_tensor_tensor", out=gcv, in0=views[k], scalar=cw[:, j, k:k + 1], in1=gcv,
                         op0=ALU.mult, op1=ALU.add, r=[("gb", r_), ("gc", r2), "cw"], w=[("gc", r2)])
            r3 = nrot("ge", 2)
            P.op("act", "activation", out=ge[:, r3, 0:T], in_=gc[:, r2, 0:T], func=AF.Gelu_apprx_tanh,
                 r=[("gc", r2)], w=[("ge", r3)])
            P.op("dve", "tensor_tensor", out=SA[:, j, 0:T], in0=ps[bu][:, 0:T], in1=ge[:, r3, 0:T], op=ALU.mult,
                 r=bk(bu) + [("ge", r3)], w=[("SA", j)])

        pend = ffn_a(0)
        for j in range(22):
            nxt_a = ffn_a(j + 1) if j + 1 < 22 else None
            ffn_b(j, *pend)
            pend = nxt_a
        ws_done()
        if nxt is not None:
            with _NextCtx(nxt):
                norm_fin(bss, T2)
                norm_apply(0, T2, g2)
            reserved.discard(bss)
            pro_done.add(nxt)
        for c in range(8):
            slot = ws_get()
            b = nb()
            proj_fm(b, slot, 22, 128, 0, lambda kc: SA[:, kc, 0:T], lambda kc: [("SA", kc)], T)
            for (c0, n, bb) in groups:
                P.op("dve", "scalar_tensor_tensor", out=cur["xT"][:, c, c0:c0 + n], in0=ps[b][:, c0:c0 + n],
                     scalar=modTt[:, cur["l"], 40 + c, bb:bb + 1], in1=cur["xT"][:, c, c0:c0 + n], op0=ALU.mult, op1=ALU.add,
                     r=bk(b) + [("xT", cur["par"], c), ("modT", cur["l"])], w=[("xT", cur["par"], c)])
            ws_done()

        ck(9)
        xk = [("xT", cur["par"], c) for c in range(8)]
        if ms:
            if l == 0:
                P.op("sp", "dma_start", out=xs1, in_=cur["xT"][:, :, C0:C0 + 64], r=xk, w=["xs1"], semkey="xst")
            else:
                P.op("sp", "dma_start", out=ysT_d, in_=cur["xT"][:, :, C0:C0 + 64], r=xk, semkey="xst")
            P.op("sp", "dma_start", out=ncs_d[l], in_=ncs_st[:, :, :, :], r=["ncs_st"], semkey="ncs")
            P.op("sp", "dma_start", out=ncv_d[l], in_=halo[:, :, :], r=["halo"], semkey="ncv")
        if l == 0:
            P.op("sp", "dma_start", out=x1s[:, :, (t0 - 4) * 128:(t0 - 4) * 128 + Tp], in_=cur["xT"][:, :, 0:Tp], r=xk,
                 w=[("x1s", t) for t in tiles], semkey="xst")
        else:
            lo = max(t0, OUT_T0)
            c0 = (lo - t0) * 128
            P.op("sp", "dma_start", out=yT_d[:, :, (lo - OUT_T0) * 128:(lo - OUT_T0) * 128 + Tp - c0],
                 in_=cur["xT"][:, :, c0:Tp], r=xk, semkey="xst")

    BLOCKS = []
    for l in range(2):
        BLOCKS.append((l, "kv", 0 if l == 0 else 4, 4))
        bl_ = (L0_BLOCKS if l == 0 else L1_BLOCKS)
        for k_, (t0, n) in enumerate(bl_):
            BLOCKS.append((l, "fullms" if k_ == len(bl_) - 1 else "full", t0, n))

    try:
        for l in range(2):
            if stop is not None and stop == -1:
                raise _Stop()
            if l == 0:
                ada_setup(0)
            layer_setup(l)
            if stop is not None and stop == 0:
                raise _Stop()
            if l == 1:
                P.op("pool", "memset", halo[:, :, :], 0.0, r=["halo"], w=["halo"])
            for (l_, kind, t0, n) in [b for b in BLOCKS if b[0] == l]:
                if kind == "fullms" and l == 0:
                    ada_setup(1)
                block(l_, kind, t0, n)
                nblk[0] += 1
                if stop is not None and nblk[0] >= stop:
                    raise _Stop()
        assert ws["consumed"] == len(ws["seq"]), (ws["consumed"], len(ws["seq"]))
    except _Stop:
        dbg = dout("dbgx", (128, 8, 512))
        P.op("sp", "dma_start", out=dbg, in_=cur["xT"][:, :, :], r=[("xT", cur["par"], c) for c in range(8)], semkey="dbg")
        dbg2 = dout("dbgh", (128, 8, 512))
        P.op("pool", "dma_start", out=dbg2, in_=tmpn[:, :, :].rearrange("p a b -> p (a b)"), r=[("tmpn", 0), ("tmpn", 1)], semkey="dbg") if False else None

    P.emit(nc, stack)
    stack.close()
    return nc


def _fm(a):
    t, f = a.shape
    return np.ascontiguousarray(a.reshape(t, f // 128, 128).transpose(2, 1, 0))


def _slab(wm):
    k, m = wm.shape
    return np.ascontiguousarray(wm.reshape(k // 128, 128, m).transpose(1, 0, 2)).reshape(128, (k // 128) * m)


_NC_CACHE = {}


def kernel(x_prompt, x_sample, cache_attn_k, cache_attn_v, cache_ffn_conv, c_prompt, c_sample,
           norm1_g, norm2_g, w_ada, b_ada, w_in, q_norm_g, k_norm_g, rel_bias, v_norm_g,
           w_spatial, b_spatial, w_out_a, w_out_b, w_out, w_ffn_in, ffn_conv_w, ffn_conv_b, w_ffn_out):
    f = lambda a: np.asarray(a, dtype=np.float32)
    x_prompt, x_sample, cache_attn_k, cache_attn_v, cache_ffn_conv = map(f, (x_prompt, x_sample, cache_attn_k, cache_attn_v, cache_ffn_conv))
    c_prompt, c_sample, norm1_g, norm2_g, w_ada, b_ada, w_in = map(f, (c_prompt, c_sample, norm1_g, norm2_g, w_ada, b_ada, w_in))
    q_norm_g, k_norm_g, rel_bias, v_norm_g, w_spatial, b_spatial = map(f, (q_norm_g, k_norm_g, rel_bias, v_norm_g, w_spatial, b_spatial))
    w_out_a, w_out_b, w_out, w_ffn_in, ffn_conv_w, ffn_conv_b, w_ffn_out = map(f, (w_out_a, w_out_b, w_out, w_ffn_in, ffn_conv_w, ffn_conv_b, w_ffn_out))

    in_maps = _prep(x_prompt, x_sample, cache_attn_k, cache_attn_v, cache_ffn_conv, c_prompt, c_sample,
                    norm1_g, norm2_g, w_ada, b_ada, w_in, q_norm_g, k_norm_g, rel_bias, v_norm_g,
                    w_spatial, b_spatial, w_out_a, w_out_b, w_out, w_ffn_in, ffn_conv_w, ffn_conv_b, w_ffn_out)
    if "nc" not in _NC_CACHE:
        _NC_CACHE["nc"] = build_nc()
    nc = _NC_CACHE["nc"]
    res = run_bass_kernel_spmd(nc, in_maps, core_ids=list(range(NCORES)))
    return _post(res.results)


def _prep(x_prompt, x_sample, cache_attn_k, cache_attn_v, cache_ffn_conv, c_prompt, c_sample,
          norm1_g, norm2_g, w_ada, b_ada, w_in, q_norm_g, k_norm_g, rel_bias, v_norm_g,
          w_spatial, b_spatial, w_out_a, w_out_b, w_out, w_ffn_in, ffn_conv_w, ffn_conv_b, w_ffn_out):

    wall = np.empty((2, 24, 128, 4096), np.float32)
    wfo = np.empty((2, 8, 128, 2816), np.float32)
    wada = np.empty((2, 12, 128, 4096), np.float32)
    for l in range(2):
        for s in range(9):
            wall[l, s] = _slab(w_in[l][:, s * 512:(s + 1) * 512])
        wall[l, 9] = _slab(w_out_a[l])
        wall[l, 10] = _slab(w_out_b[l])
        for s in range(2):
            wall[l, 11 + s] = _slab(w_out[l][:, s * 512:(s + 1) * 512])
        for jj in range(11):
            idx = np.concatenate([np.arange(2 * jj * 128, (2 * jj + 2) * 128), DFF + np.arange(2 * jj * 128, (2 * jj + 2) * 128)])
            wall[l, 13 + jj] = _slab(w_ffn_in[l][:, idx])
        for c in range(8):
            wfo[l, c] = _slab(w_ffn_out[l][:, c * 128:(c + 1) * 128])
        for s in range(12):
            wada[l, s] = _slab(w_ada[l][:, s * 512:(s + 1) * 512])
    col = lambda v: np.ascontiguousarray(v.reshape(-1, 128).T)
    nrm = np.stack([np.concatenate([col(norm1_g[l]), col(norm2_g[l])], 1) for l in range(2)])
    bada = np.stack([col(b_ada[l]) for l in range(2)])
    qkg = np.stack([np.stack([np.tile(q_norm_g[l], 2), np.tile(k_norm_g[l], 2)], 1) for l in range(2)])
    vg = np.stack([np.broadcast_to(v_norm_g[l][None, :], (128, 512)) for l in range(2)]).copy()
    bsp = b_spatial.reshape(2, 1, 512).copy()
    bsps = np.stack([np.tile(b_spatial[l][:, None, :16], (1, 4, 1)).reshape(1, 256) for l in range(2)])
    wspT = np.ascontiguousarray(w_spatial.transpose(0, 3, 1, 2))
    wspb = np.zeros((2, 64, 4, 64), np.float32)
    for bl in range(4):
        wspb[:, bl * 16:(bl + 1) * 16, :, bl * 16:(bl + 1) * 16] = wspT[:, 0:16, :, 0:16]
    s_i = np.arange(128)[:, None]
    t_i = np.arange(128)[None, :]
    tril = np.broadcast_to((s_i <= t_i).astype(np.float32)[:, None, :], (128, 4, 128)).copy()
    trilb = np.zeros((64, 4, 64), np.float32)
    for bl in range(4):
        trilb[bl * 16:(bl + 1) * 16, :, bl * 16:(bl + 1) * 16] = tril[0:16, :, 0:16]
    cw = np.ascontiguousarray(ffn_conv_w.reshape(2, 3, 22, 128).transpose(0, 3, 2, 1))
    cb = np.ascontiguousarray(ffn_conv_b.reshape(2, 22, 128).transpose(0, 2, 1))
    ki = np.arange(128)[:, None]
    qi = np.arange(128)[None, :]
    idx1 = np.clip(128 + qi - ki, -128, 128) + 128
    idx0 = np.clip(qi - ki, -128, 128) + 128
    Tb = np.stack([np.stack([rel_bias[l][:, idx1].transpose(1, 0, 2), rel_bias[l][:, idx0].transpose(1, 0, 2)], 1)
                   for l in range(2)])
    b256 = np.stack([np.broadcast_to(rel_bias[l][None, :, 256], (128, 8)) for l in range(2)]).copy()

    shared = dict(wall=wall, wfo=wfo, wada=wada, nrm=nrm, bada=bada, qkg=qkg, vg=vg, bsp=bsp, bsps=bsps, wspT=wspT,
                  wspb=wspb, tril=tril, trilb=trilb, cw=cw, cb=cb, Tb=np.ascontiguousarray(Tb), b256=b256)
    shared = {k: np.ascontiguousarray(v, dtype=np.float32) for k, v in shared.items()}

    in_maps = []
    for core in range(NCORES):
        b, seg = core // 4, core % 4
        s0 = seg * SEG
        a = s0 - HALO
        xw = np.zeros((W, D), np.float32)
        lo = max(a, 0)
        xw[lo - a:] = x_prompt[b, lo:s0 + SEG]
        valid = (np.arange(W) + a >= 0).astype(np.float32)
        m = dict(shared)
        m["xin"] = _fm(xw)
        m["xs"] = _fm(x_sample[4 * core:4 * core + 4].reshape(64, D))
        m["vtok"] = np.ascontiguousarray(valid.reshape(NT, 128).T)
        m["flag"] = np.full((128, 1), 1.0 if a >= 0 else 0.0, np.float32)
        cc = np.concatenate([c_prompt[b:b + 1], c_sample[4 * core:4 * core + 4]], 0)
        m["cT"] = np.ascontiguousarray(cc.reshape(5, 8, 128).transpose(2, 1, 0))
        ck = cache_attn_k[:, 4 * core:4 * core + 4]
        m["ckT"] = np.ascontiguousarray(ck.reshape(2, 4, 512, 4, 2, 64).transpose(0, 4, 5, 3, 1, 2)).reshape(2, 128, 4, 4, 512)
        cvv = cache_attn_v[:, 4 * core:4 * core + 4].reshape(2, 4, 4, 128, 512)
        m["cv"] = np.ascontiguousarray(cvv.transpose(0, 1, 3, 2, 4))
        cc2 = cache_ffn_conv[:, 4 * core:4 * core + 4].reshape(2, 4, 2, 22, 128)
        m["cconv"] = np.ascontiguousarray(cc2.transpose(0, 4, 3, 1, 2))
        in_maps.append(m)
    return in_maps


def _post(R):

    y_prompt = np.empty((2, 8192, D), np.float32)
    y_sample = np.empty((32, 16, D), np.float32)
    nkp = np.empty((2, 2, 512, 8, 64), np.float32)
    nvp = np.empty((2, 2, 512, 8, 64), np.float32)
    ncp = np.empty((2, 2, 2, DFF), np.float32)
    nks = np.empty((2, 32, 16, 8, 64), np.float32)
    nvs = np.empty((2, 32, 16, 8, 64), np.float32)
    nbs = np.empty((2, 32, 16, 4, 128), np.float32)
    ncs = np.empty((2, 32, 2, DFF), np.float32)
    for core in range(NCORES):
        r = R[core]
        b, seg = core // 4, core % 4
        y_prompt[b, seg * SEG:(seg + 1) * SEG] = r["yT"].transpose(2, 1, 0).reshape(SEG, D)
        y_sample[4 * core:4 * core + 4] = r["ysT"].transpose(2, 1, 0).reshape(4, 16, D)
        sl = slice(4 * core, 4 * core + 4)
        for l in range(2):
            if seg == 3:
                nkp[l, b] = r["nkT"][l].reshape(2, 64, 4, 512).transpose(3, 2, 0, 1).reshape(512, 8, 64)
                nvp[l, b] = r["nv"][l].reshape(512, 8, 64)
                ncp[l, b] = r["ncv"][l].transpose(2, 1, 0).reshape(2, DFF)
            nks[l, sl] = r["nksT"][l].reshape(2, 64, 4, 4, 16).transpose(3, 4, 2, 0, 1).reshape(4, 16, 8, 64)
            nvs[l, sl] = r["nvs"][l].transpose(1, 0, 2).reshape(4, 16, 8, 64)
            nbs[l, sl] = r["nbs"][l].reshape(4, 16, 4, 128)
            ncs[l, sl] = r["ncs"][l].transpose(2, 3, 1, 0).reshape(4, 2, DFF)
    return (y_prompt, y_sample, nkp, nvp, ncp, nks, nvs, nbs, ncs)
```

```python
import contextlib
import numpy as np
import concourse.bass as bass
import concourse.mybir as mybir
from concourse.bass_utils import run_bass_kernel_spmd

F32 = mybir.dt.float32
BF16 = mybir.dt.bfloat16
AF = mybir.ActivationFunctionType
ALU = mybir.AluOpType
AX = mybir.AxisListType

NCORES = 8
D = 1024
SEG = 2048
HALO = 1152
W = SEG + HALO
NT = W // 128
DFF = 2816
EPS = 1e-6
NB = 4
L0_BLOCKS = [(4, 4), (8, 4), (12, 4), (16, 3), (19, 3), (22, 3)]
L1_BLOCKS = [(8, 4), (12, 4), (16, 3), (19, 3), (22, 3)]
OUT_T0 = 9
KEEP_T0 = 21
_DBG = {}


class Prog:
    STREAMS = ("sp", "act", "dve", "pool", "pe")

    def __init__(self):
        self.ops = []
        self.keyw = {}
        self.keyr = {}
        self.dcnt = {}

    def op(self, stream, method, *args, r=(), w=(), semkey=None, **kw):
        idx = len(self.ops)
        dom = ("d", semkey) if semkey is not None else ("s", stream)
        deps = {}

        def add(d, i):
            if d[0] == "d":
                i = self.dcnt[d]
            if deps.get(d, -1) < i:
                deps[d] = i

        for k in r:
            for d, i in self.keyw.get(k, {}).items():
                add(d, i)
        skip_same = dom[0] == "d" or stream == "pe"
        for k in w:
            for d, i in self.keyr.get(k, {}).items():
                if d == dom and (skip_same or i == idx):
                    continue
                add(d, i)
            for d, i in self.keyw.get(k, {}).items():
                if d == dom and skip_same:
                    continue
                add(d, i)
        for k in r:
            self.keyr.setdefault(k, {})[dom] = idx
        for k in w:
            if self.keyr.get(k):
                self.keyw[k] = {dom: idx}
                self.keyr[k] = {}
            else:
                self.keyw.setdefault(k, {})[dom] = idx
        if dom[0] == "d":
            self.dcnt[dom] = self.dcnt.get(dom, 0) + 16
        self.ops.append(dict(stream=stream, method=method, args=args, kw=kw, dom=dom,
                             deps=list(deps.items()), stage=_DBG.get("stage", "")))
        return idx

    def emit(self, nc, stack):
        ops = self.ops
        needs = set()
        for o in ops:
            for d, i in o["deps"]:
                if d[0] == "s":
                    needs.add(i)
        cnt = {}
        sems = {}
        issuer = {}
        for i, o in enumerate(ops):
            d = o["dom"]
            if d[0] == "d":
                cnt[d] = cnt.get(d, 0) + 16
                o["done"] = cnt[d]
                assert issuer.setdefault(d, o["stream"]) == o["stream"], d
            elif i in needs:
                cnt[d] = cnt.get(d, 0) + 1
                o["done"] = cnt[d]
            else:
                o["done"] = None
        for d in cnt:
            sems[d] = stack.enter_context(nc.semaphore("s%d" % len(sems)))
        block = stack.enter_context(nc.Block())
        self.nsem = len(sems)

        def run(stream, eng):
            waited = {}
            for o in ops:
                if o["stream"] != stream:
                    continue
                for d, i in o["deps"]:
                    v = i if d[0] == "d" else ops[i]["done"]
                    if waited.get(d, 0) < v:
                        eng.wait_ge(sems[d], v)
                        waited[d] = v
                        _DBG.setdefault("waits", {}).setdefault(stream, []).append((o["stage"], d, (ops[i]["method"], ops[i]["stage"], str(ops[i]["kw"].get("out", ops[i]["args"][:1]))[:90]) if d[0] == "s" else None, o["method"]))
                ins = getattr(eng, o["method"])(*o["args"], **o["kw"])
                if o["done"] is not None:
                    d = o["dom"]
                    ins.then_inc(sems[d], 16 if d[0] == "d" else 1)
            for d, v in cnt.items():
                if d[0] == "d" and issuer[d] == stream and waited.get(d, 0) < v:
                    eng.wait_ge(sems[d], v)

        @block.sync
        def _(e):
            run("sp", e)

        @block.scalar
        def _(e):
            run("act", e)

        @block.vector
        def _(e):
            run("dve", e)

        @block.gpsimd
        def _(e):
            run("pool", e)

        @block.tensor
        def _(e):
            run("pe", e)


def build_nc(stop=None):
    class _Stop(Exception):
        pass

    nblk = [0]

    def ck(st):
        _DBG["stage"] = st + (10 if _DBG.get("stage", 0) >= 10 else 0)
        if stop is not None and nblk[0] + st / 10.0 >= stop - 1e-9 and stop > 0:
            raise _Stop()

    nc = bass.Bass("TRN2", target_bir_lowering=False)
    P = Prog()
    stack = contextlib.ExitStack()

    def din(name, shape):
        return nc.dram_tensor(name, list(shape), F32, kind="ExternalInput").ap()

    def dout(name, shape):
        return nc.dram_tensor(name, list(shape), F32, kind="ExternalOutput").ap()

    def dint(name, shape, dt):
        return nc.dram_tensor(name, list(shape), dt, kind="Internal").ap()

    xin = din("xin", (128, 8, W))
    xs_d = din("xs", (128, 8, 64))
    vtok_d = din("vtok", (128, NT))
    flag_d = din("flag", (128, 1))
    cT_d = din("cT", (128, 8, 5))
    ckT_d = din("ckT", (2, 128, 4, 4, 512))
    cv_d = din("cv", (2, 4, 128, 4, 512))
    cconv_d = din("cconv", (2, 128, 22, 4, 2))
    wall_d = din("wall", (2, 24, 128, 4096))
    wfo_d = din("wfo", (2, 8, 128, 2816))
    wada_d = din("wada", (2, 12, 128, 4096))
    nrm_d = din("nrm", (2, 128, 16))
    bada_d = din("bada", (2, 128, 48))
    qkg_d = din("qkg", (2, 128, 2))
    vg_d = din("vg", (2, 128, 512))
    bsp_d = din("bsp", (2, 1, 512))
    bsps_d = din("bsps", (2, 1, 256))
    wspT_d = din("wspT", (2, 128, 4, 128))
    wspb_d = din("wspb", (2, 64, 4, 64))
    tril_d = din("tril", (128, 4, 128))
    trilb_d = din("trilb", (64, 4, 64))
    cw_d = din("cw", (2, 128, 22, 3))
    cb_d = din("cb", (2, 128, 22))
    Tb_d = din("Tb", (2, 128, 2, 8, 128))
    b256_d = din("b256", (2, 128, 8))

    yT_d = dout("yT", (128, 8, SEG))
    ysT_d = dout("ysT", (128, 8, 64))
    nkT_d = dout("nkT", (2, 128, 4, 512))
    nv_d = dout("nv", (2, 4, 128, 512))
    ncv_d = dout("ncv", (2, 128, 22, 2))
    nksT_d = dout("nksT", (2, 128, 4, 64))
    nvs_d = dout("nvs", (2, 16, 4, 512))
    nbs_d = dout("nbs", (2, 64, 512))
    ncs_d = dout("ncs", (2, 128, 22, 4, 2))

    wsc = dint("wsc", (2, 24, 128, 4096), BF16)
    wsc_fo = dint("wscfo", (2, 8, 128, 2816), BF16)
    x1s = dint("x1s", (128, 8, 21 * 128), F32)
    xs1 = dint("xs1", (128, 8, 64), F32)

    def sb(name, shape, dt=F32):
        return stack.enter_context(nc.sbuf_tensor("sb_" + name, list(shape), dt))

    xTt = sb("xT", (128, 2, 8, 512))
    cur = {"xT": xTt[:, 0], "par": 0, "idx": 0}
    hT = sb("hT", (128, 8, 512), BF16)
    sq = sb("sq", (128, 3, 512), BF16)
    rstd = sb("rstd", (128, 512))
    tmpn = sb("tmpn", (128, 2, 512))
    qz = sb("qz", (128, 4, 4, 2, 128), BF16)
    kst = sb("kst", (128, 2, 512))
    rs = sb("rs", (128, 2, 512))
    kT = sb("kT", (128, 4, 1024), BF16)
    vt = sb("vt", (128, 8, 512), BF16)
    ubT = sb("ubT", (128, 4, 512), BF16)
    vbn = sb("vbn", (128, 4, 512), BF16)
    gl = sb("gl", (128, 2, 512))
    ssv = sb("ssv", (128, 4))
    SA = sb("SA", (128, 24, 512), BF16)
    oaT = sb("oaT", (128, 4, 512), BF16)
    obT = sb("obT", (128, 4, 512), BF16)
    Pt = sb("Pt", (128, 2, 5, 256), BF16)
    rden = sb("rden", (128, 2, 256))
    gb = sb("gb", (128, 3, 516))
    gc = sb("gc", (128, 2, 512))
    ge = sb("ge", (128, 2, 512))
    halo = sb("halo", (128, 22, 2))
    halos = sb("halos", (128, 22, 4, 2))
    ncs_st = sb("ncs_st", (128, 22, 4, 2))
    wslab = sb("wslab", (128, NB, 4096), BF16)
    ones_bf = sb("ones_bf", (128, 128), BF16)
    M0 = sb("M0", (128, 256), BF16)
    blk64 = sb("blk64", (128, 128), BF16)
    ones_row = sb("ones_row", (1, 128), BF16)
    vones = sb("vones", (128, 9, 128), BF16)
    vtok = sb("vtok", (128, NT))
    flag = sb("flag", (128, 1))
    cTs = sb("cTs", (128, 8, 5))
    cs = sb("cs", (128, 8, 5), BF16)
    E = sb("E", (128, 2, 8, 128), BF16)
    negb = sb("negb", (128, 8))
    modTt = sb("modT", (128, 2, 48, 5))
    A12t = sb("A12", (128, 2, 2, 8, 5))
    nrmt = sb("nrm", (128, 2, 16))
    badat = sb("bada", (128, 2, 48))
    qkg = sb("qkg", (128, 2))
    vg = sb("vg", (128, 512))
    bsp = sb("bsp", (1, 512), BF16)
    bsps = sb("bsps", (1, 256), BF16)
    WcT = sb("WcT", (128, 4, 128), BF16)
    Wblk = sb("Wblk", (64, 4, 64), BF16)
    cw = sb("cw", (128, 22, 3))
    cb = sb("cb", (128, 22))
    kTs = sb("kTs", (128, 4, 64), BF16)
    vbns = sb("vbns", (64, 512), BF16)
    ckb = sb("ckb", (128, 1, 4, 512), BF16)
    cvb = sb("cvb", (128, 1, 4, 512), BF16)
    Pts = sb("Pts", (128, 2, 2, 80), BF16)
    exs = sb("exs", (128, 2, 2, 32))
    rdens = sb("rdens", (128, 2, 32))

    psd = [stack.enter_context(nc.psum_tensor("ps%d" % i, [128, 1024], F32)) for i in range(4)]
    ps = [psd[i // 2][:, (i % 2) * 512:(i % 2 + 1) * 512] for i in range(8)]

    def bk(i):
        return [("ps", i)]

    bank_ctr = [0]

    reserved = set()

    def nb():
        while True:
            b = bank_ctr[0] % 8
            bank_ctr[0] += 1
            if b not in reserved:
                return b

    rot = {}

    def nrot(name, n):
        v = rot.get(name, 0)
        rot[name] = v + 1
        return v % n

    ws = dict(seq=[], issued=0, consumed=0)

    casted = set()

    def ws_issue(upto):
        while ws["issued"] < min(upto, len(ws["seq"])):
            kind_, l_, s_ = ws["seq"][ws["issued"]]
            slot = ws["issued"] % NB
            wk, sk = [("wslab", slot)], ("w", slot)
            if kind_ == "ada":
                P.op("pool", "dma_start", out=wslab[:, slot, :], in_=wada_d[l_, s_], w=wk, semkey=sk)
            else:
                ncols = 4096 if kind_ == "wall" else 2816
                src32 = wall_d[l_, s_] if kind_ == "wall" else wfo_d[l_, s_]
                scr = wsc[l_, s_] if kind_ == "wall" else wsc_fo[l_, s_]
                key = ("wsc", kind_, l_, s_)
                if key not in casted:
                    casted.add(key)
                    P.op("pool", "dma_start", out=wslab[:, slot, 0:ncols], in_=src32, w=wk, semkey=sk)
                    P.op("sp", "dma_start", out=scr, in_=wslab[:, slot, 0:ncols], r=wk, w=[key],
                         semkey=("wst", slot))
                else:
                    P.op("pool", "dma_start", out=wslab[:, slot, 0:ncols], in_=scr, r=[key], w=wk, semkey=sk)
            ws["issued"] += 1

    def ws_get():
        assert ws["consumed"] < len(ws["seq"])
        ws_issue(ws["consumed"] + 1)
        slot = ws["consumed"] % NB
        ws["consumed"] += 1
        return slot

    def ws_done():
        ws_issue(ws["consumed"] + NB)

    def seq_ada(l):
        for s_ in range(12):
            ws["seq"].append(("ada", l, s_))

    def seq_block(l, kind):
        if kind == "kv":
            for s_ in (1, 2):
                ws["seq"].append(("wall", l, s_))
            return
        for s_ in range(24):
            ws["seq"].append(("wall", l, s_))
        for s_ in range(8):
            ws["seq"].append(("fo", l, s_))

    seq_ada(0)
    for l in range(2):
        seq_block(l, "kv")
        nfull = len(L0_BLOCKS if l == 0 else L1_BLOCKS)
        for k_ in range(nfull):
            if l == 0 and k_ == nfull - 1:
                seq_ada(1)
            seq_block(l, "full")

    P.op("pool", "memset", ones_bf[:, :], 1.0, w=["ones_bf"])
    P.op("pool", "memset", blk64[:, :], 0.0, w=["blk64"])
    P.op("pool", "memset", blk64[0:64, 0:64], 1.0, w=["blk64"])
    P.op("pool", "memset", blk64[64:128, 64:128], 1.0, w=["blk64"])
    P.op("pool", "memset", ones_row[:, :], 1.0, w=["ones_row"])
    P.op("pool", "memset", M0[:, :], 1.0, w=["M0"])
    P.op("pool", "memset", M0[0:64, :].rearrange("p (h q) -> p h q", h=2)[:, :, 64:128], 0.0, w=["M0"])
    P.op("pool", "memset", Pt[:, :, :, :].rearrange("p a b c -> p (a b c)"), 0.0, w=[("Pt", 0), ("Pt", 1)])
    P.op("pool", "memset", qz[:, :, :, :, :].rearrange("p a b c d -> p (a b c d)"), 0.0, w=[("qT", c) for c in range(4)])
    P.op("pool", "memset", halo[:, :, :], 0.0, w=["halo"])
    P.op("sp", "dma_start", out=vtok[:, :], in_=vtok_d, w=["vtok"], semkey="c0")
    P.op("sp", "dma_start", out=flag[:, :], in_=flag_d, w=["flag"], semkey="c0")
    P.op("sp", "dma_start", out=cTs[:, :, :], in_=cT_d, w=["cTs"], semkey="c0")
    P.op("act", "activation", out=cs[:, :, :], in_=cTs[:, :, :], func=AF.Silu, r=["cTs"], w=["cs"])
    for t in range(9):
        P.op("act", "activation", out=vones[:, t, :], in_=ones_bf[:, :], func=AF.Copy,
             scale=vtok[:, t:t + 1], r=["ones_bf", "vtok"], w=["vones"])
    def layer_setup(l):
        for dst, src, key in ((qkg, qkg_d[l], "qkg"),
                              (vg, vg_d[l], "vg"), (cw, cw_d[l], "cw"), (cb, cb_d[l], "cb"),
                              (negb, b256_d[l], "negb")):
            full = tuple(slice(None) for _ in dst.shape)
            P.op("sp", "dma_start", out=dst[full], in_=src, w=[key], semkey="ls")
        P.op("sp", "dma_start", out=tmpn[:, 0, :].rearrange("p (h q) -> p h q", h=4), in_=Tb_d[l][:, 0, 0:4, :],
             w=[("tmpn", 0)], semkey="ls")
        P.op("sp", "dma_start", out=tmpn[:, 1, :].rearrange("p (h q) -> p h q", h=4), in_=Tb_d[l][:, 0, 4:8, :],
             w=[("tmpn", 1)], semkey="ls")
        P.op("sp", "dma_start", out=rs[:, 0, :].rearrange("p (h q) -> p h q", h=4), in_=Tb_d[l][:, 1, 0:4, :],
             w=[("rs", 0)], semkey="ls")
        P.op("sp", "dma_start", out=rs[:, 1, :].rearrange("p (h q) -> p h q", h=4), in_=Tb_d[l][:, 1, 4:8, :],
             w=[("rs", 1)], semkey="ls")
        P.op("sp", "dma_start", out=halos[:, :, :, :], in_=cconv_d[l], w=["halos"], semkey="ls")
        P.op("sp", "dma_start", out=gl[0:1, 0, :], in_=bsp_d[l], w=[("gl", 0)], semkey="ls")
        P.op("sp", "dma_start", out=gl[0:1, 1, 0:256], in_=bsps_d[l], w=[("gl", 1)], semkey="ls")
        P.op("sp", "dma_start", out=gc[:, 0, :], in_=wspT_d[l].rearrange("p g t -> p (g t)"),
             w=[("gc", 0)], semkey="ls3")
        P.op("sp", "dma_start", out=gc[:, 1, :], in_=tril_d.rearrange("p g t -> p (g t)"),
             w=[("gc", 1)], semkey="ls3")
        P.op("sp", "dma_start", out=ge[0:64, 0, 0:256], in_=wspb_d[l].rearrange("p g t -> p (g t)"),
             w=[("ge", 0)], semkey="ls3")
        P.op("sp", "dma_start", out=ge[0:64, 1, 0:256], in_=trilb_d.rearrange("p g t -> p (g t)"),
             w=[("ge", 1)], semkey="ls3")
        P.op("dve", "tensor_tensor", out=WcT[:, :, :].rearrange("p g t -> p (g t)"), in0=gc[:, 0, :],
             in1=gc[:, 1, :], op=ALU.mult, r=[("gc", 0), ("gc", 1)], w=["WcT"])
        P.op("dve", "tensor_tensor", out=Wblk[:, :, :].rearrange("p g t -> p (g t)"), in0=ge[0:64, 0, 0:256],
             in1=ge[0:64, 1, 0:256], op=ALU.mult, r=[("ge", 0), ("ge", 1)], w=["Wblk"])
        P.op("dve", "tensor_copy", bsp[:, :], gl[0:1, 0, :], r=[("gl", 0)], w=["bsp"])
        P.op("dve", "tensor_copy", bsps[:, :], gl[0:1, 1, 0:256], r=[("gl", 1)], w=["bsps"])
        P.op("dve", "tensor_scalar_mul", qkg[:, 1:2], qkg[:, 1:2], 8.0, r=["qkg"], w=["qkg"])
        P.op("dve", "tensor_scalar_mul", negb[:, :], negb[:, :], -1.0, r=["negb"], w=["negb"])
        for t in range(2):
            for h in range(8):
                stg = (tmpn if t == 0 else rs)
                skey = ("tmpn" if t == 0 else "rs", h // 4)
                P.op("act", "activation", out=E[:, t, h, :], in_=stg[:, h // 4, (h % 4) * 128:(h % 4 + 1) * 128],
                     func=AF.Exp, bias=negb[:, h:h + 1], scale=1.0, r=[skey, "negb"], w=["E"])
        P.op("pool", "memset", E[64:128, 1, :, 0:64], 0.0, r=["E"], w=["E"])

    def ada_setup(l):
        modT, A12, nrm, bada = modTt[:, l], A12t[:, l], nrmt[:, l], badat[:, l]
        P.op("sp", "dma_start", out=nrm, in_=nrm_d[l], w=[("nrm", l)], semkey=("lsa", l))
        P.op("sp", "dma_start", out=bada, in_=bada_d[l], w=[("bada", l)], semkey=("lsa", l))
        b = nb()
        for s_ in range(12):
            slot = ws_get()
            for m in range(4):
                ci = s_ * 4 + m
                for kc in range(8):
                    P.op("pe", "matmul", ps[b][:, ci * 5:(ci + 1) * 5],
                         lhsT=wslab[:, slot, kc * 512 + m * 128: kc * 512 + (m + 1) * 128],
                         rhs=cs[:, kc, :], start=(kc == 0), stop=(kc == 7),
                         r=[("wslab", slot), "cs"], w=bk(b))
            ws_done()
        for bb in range(5):
            P.op("dve", "tensor_tensor", out=modT[:, :, bb],
                 in0=ps[b][:, 0:240].rearrange("p (c b) -> p c b", b=5)[:, :, bb], in1=bada,
                 op=ALU.add, r=bk(b) + [("bada", l)], w=[("modT", l)])
        P.op("dve", "tensor_scalar_mul", modT[:, 16:24, :], modT[:, 16:24, :], 0.5, r=[("modT", l)], w=[("modT", l)])
        for which in range(2):
            sc0 = 8 if which == 0 else 32
            for bb in range(5):
                P.op("dve", "tensor_scalar", out=A12[:, which, :, bb], in0=modT[:, sc0:sc0 + 8, bb],
                     scalar1=1.0, scalar2=32.0, op0=ALU.add, op1=ALU.mult, r=[("modT", l)], w=[("A12", l)])
                P.op("dve", "tensor_tensor", out=A12[:, which, :, bb], in0=A12[:, which, :, bb],
                     in1=nrm[:, which * 8:(which + 1) * 8], op=ALU.mult, r=[("A12", l), ("nrm", l)],
                     w=[("A12", l)])

    def norm_sq(b, c, T):
        r_ = nrot("sq", 3)
        P.op("act", "activation", out=sq[:, r_, 0:T], in_=cur["xT"][:, c, 0:T], func=AF.Square,
             r=[("xT", cur["par"], c)], w=[("sq", r_)])
        return r_

    def norm_ss(b, c, r_, T):
        P.op("pe", "matmul", ps[b][:, 0:T], lhsT=ones_bf[:, :], rhs=sq[:, r_, 0:T],
             start=(c == 0), stop=(c == 7), r=[("sq", r_), "ones_bf"], w=bk(b))

    def norm_fin(b, T):
        P.op("act", "activation", out=rstd[:, 0:T], in_=ps[b][:, 0:T], func=AF.Sqrt, bias=float(D * EPS),
             scale=1.0, r=bk(b), w=["rstd"])
        P.op("dve", "reciprocal", out=rstd[:, 0:T], in_=rstd[:, 0:T], r=["rstd"], w=["rstd"])

    def norm_apply(which, T, groups):
        shc = 0 if which == 0 else 24
        for c in range(8):
            r_ = nrot("tmpn", 2)
            for (c0, n, bb) in groups:
                P.op("dve", "scalar_tensor_tensor", out=tmpn[:, r_, c0:c0 + n], in0=cur["xT"][:, c, c0:c0 + n],
                     scalar=A12t[:, cur["l"], which, c, bb:bb + 1], in1=rstd[:, c0:c0 + n], op0=ALU.mult, op1=ALU.mult,
                     r=[("xT", cur["par"], c), ("A12", cur["l"]), "rstd"], w=[("tmpn", r_)])
            for (c0, n, bb) in groups:
                P.op("act", "activation", out=hT[:, c, c0:c0 + n], in_=tmpn[:, r_, c0:c0 + n],
                     func=AF.Identity, bias=modTt[:, cur["l"], shc + c, bb:bb + 1], scale=1.0,
                     r=[("tmpn", r_), ("modT", cur["l"])], w=[("hT", c)])

    def norm_mod(which, T, groups):
        b = nb()
        for c in range(8):
            r_ = norm_sq(b, c, T)
            norm_ss(b, c, r_, T)
        norm_fin(b, T)
        norm_apply(which, T, groups)

    pro_done = set()

    def blk_geom(i):
        l_, kind_, t0_, n_ = BLOCKS[i]
        Tp_ = n_ * 128
        if kind_ == "fullms":
            return l_, Tp_ + 64, [(0, Tp_, 0)] + [(Tp_ + bl * 16, 16, 1 + bl) for bl in range(4)]
        return l_, Tp_, [(0, Tp_, 0)]

    def pro_ok(i):
        return i < len(BLOCKS)

    class _NextCtx:
        def __init__(self, i):
            self.i = i

        def __enter__(self):
            self.save = dict(cur)
            cur["l"] = BLOCKS[self.i][0]
            cur["par"] = self.i % 2
            cur["xT"] = xTt[:, self.i % 2]

        def __exit__(self, *a):
            cur.update(self.save)

    def proj_fm(b, slot, nkc, ms, col0, rhs_t, rkeys, T):
        for kc in range(nkc):
            P.op("pe", "matmul", ps[b][:, 0:T], lhsT=wslab[:, slot, kc * ms + col0: kc * ms + col0 + 128],
                 rhs=rhs_t(kc), start=(kc == 0), stop=(kc == nkc - 1),
                 r=[("wslab", slot)] + rkeys(kc), w=bk(b))

    def hn_a(b, T):
        r_ = nrot("sq", 3)
        P.op("act", "activation", out=sq[:, r_, 0:T], in_=ps[b][:, 0:T], func=AF.Square, r=bk(b), w=[("sq", r_)])
        return r_

    def hn_b(b, r_, T, gcol, out_ap, out_keys, qchunk=None, Tp=None, ms=False):
        b2 = nb()
        P.op("pe", "matmul", ps[b2][:, 0:T], lhsT=blk64[:, :], rhs=sq[:, r_, 0:T], start=True, stop=True,
             r=[("sq", r_), "blk64"], w=bk(b2))
        r2 = nrot("rs", 2)
        P.op("act", "activation", out=rs[:, r2, 0:T], in_=ps[b2][:, 0:T], func=AF.Sqrt, bias=float(64 * EPS),
             scale=1.0, r=bk(b2), w=[("rs", r2)])
        P.op("dve", "reciprocal", out=rs[:, r2, 0:T], in_=rs[:, r2, 0:T], r=[("rs", r2)], w=[("rs", r2)])
        if qchunk is not None:
            nq = Tp // 128
            for hh in range(2):
                hs = slice(hh * 64, (hh + 1) * 64)
                P.op("dve", "scalar_tensor_tensor", out=qz[hs, qchunk, 0:nq, hh, :],
                     in0=ps[b][hs, 0:Tp].rearrange("p (t q) -> p t q", t=nq), scalar=qkg[hs, gcol:gcol + 1],
                     in1=rs[hs, r2, 0:Tp].rearrange("p (t q) -> p t q", t=nq), op0=ALU.mult, op1=ALU.mult,
                     r=bk(b) + [("rs", r2), "qkg"], w=out_keys)
                if ms:
                    P.op("dve", "scalar_tensor_tensor", out=qz[hs, qchunk, 3, hh, 0:64], in0=ps[b][hs, Tp:Tp + 64],
                         scalar=qkg[hs, gcol:gcol + 1], in1=rs[hs, r2, Tp:Tp + 64], op0=ALU.mult, op1=ALU.mult,
                         r=bk(b) + [("rs", r2), "qkg"], w=out_keys)
            return
        P.op("dve", "scalar_tensor_tensor", out=out_ap, in0=ps[b][:, 0:T], scalar=qkg[:, gcol:gcol + 1],
             in1=rs[:, r2, 0:T], op0=ALU.mult, op1=ALU.mult, r=bk(b) + [("rs", r2), "qkg"], w=out_keys)

    def hT_r(T):
        return (lambda kc: hT[:, kc, 0:T]), (lambda kc: [("hT", kc)])

    def block(l, kind, t0, ntile):
        sample = False
        ms = kind == "fullms"
        Tp = ntile * 128
        C0 = Tp
        T = Tp + (64 if ms else 0)
        tiles = list(range(t0, t0 + ntile))
        groups = [(0, Tp, 0)] + ([(C0 + bl * 16, 16, 1 + bl) for bl in range(4)] if ms else [])
        hr, hk = hT_r(T)
        bi = cur["idx"]
        cur["l"] = l
        _DBG["stage"] = 0
        cur["par"] = bi % 2
        cur["xT"] = xTt[:, bi % 2]

        def xload(i):
            l_, kind_, t0_, n_ = BLOCKS[i]
            par_ = i % 2
            wk = [("xT", par_, c) for c in range(8)]
            T_ = n_ * 128
            if kind_ == "fullms":
                if l_ == 1:
                    P.op("sp", "dma_start", out=xTt[:, par_, :, T_:T_ + 64], in_=xs1, r=["xs1"], w=wk,
                         semkey=("xld", par_))
                else:
                    P.op("sp", "dma_start", out=xTt[:, par_, :, T_:T_ + 64], in_=xs_d, w=wk, semkey=("xld", par_))
            if l_ == 0:
                P.op("sp", "dma_start", out=xTt[:, par_, :, 0:T_], in_=xin[:, :, t0_ * 128: t0_ * 128 + T_], w=wk,
                     semkey=("xld", par_))
            else:
                P.op("sp", "dma_start", out=xTt[:, par_, :, 0:T_],
                     in_=x1s[:, :, (t0_ - 4) * 128: (t0_ - 4) * 128 + T_],
                     r=[("x1s", t) for t in range(t0_, t0_ + n_)], w=wk, semkey=("xld", par_))
            return True

        if bi == 0:
            xload(0)
        if bi + 1 < len(BLOCKS):
            xload(bi + 1)
        cur["idx"] = bi + 1
        if bi not in pro_done:
            norm_mod(0, T, groups)

        qk = ([("q", c) for c in range(4)] if kind != "kv" else []) + [("k", c) for c in range(4)]
        st_ = {}

        def qk_a(n):
            which, c = qk[n]
            if c == 0:
                if which == "k" and kind != "kv":
                    ws_done()
                st_["slot"] = ws_get()
            b = nb()
            proj_fm(b, st_["slot"], 8, 512, c * 128, hr, hk, T)
            return b, hn_a(b, T)

        def qk_b(n, b, r_sq):
            which, c = qk[n]
            if which == "q":
                hn_b(b, r_sq, T, 0, None, [("qT", c)], qchunk=c, Tp=Tp, ms=ms)
                return
            r_ = nrot("kst", 2)
            hn_b(b, r_sq, T, 1, kst[:, r_, 0:T], [("kst", r_)])
            if ms:
                P.op("act", "activation", out=kTs[:, c, :], in_=kst[:, r_, C0:C0 + 64], func=AF.Copy,
                     r=[("kst", r_)], w=["kTs"])
                P.op("sp", "dma_start", out=nksT_d[l, :, c, :], in_=kst[:, r_, C0:C0 + 64], r=[("kst", r_)],
                     semkey=("kst", r_))
            for ti, t in enumerate(tiles):
                sl = t % 8
                P.op("pool", "tensor_copy", kT[:, c, sl * 128:(sl + 1) * 128],
                     kst[:, r_, ti * 128:(ti + 1) * 128], r=[("kst", r_)], w=[("kT", sl)])
                if t >= KEEP_T0:
                    P.op("sp", "dma_start", out=nkT_d[l, :, c, (t - KEEP_T0) * 128:(t - KEEP_T0 + 1) * 128],
                         in_=kst[:, r_, ti * 128:(ti + 1) * 128], r=[("kst", r_)], semkey=("kst", r_))

        pend = qk_a(0)
        for n in range(len(qk)):
            nxt_ = qk_a(n + 1) if n + 1 < len(qk) else None
            qk_b(n, *pend)
            pend = nxt_
        ws_done()
        slot = ws_get()
        if ms:
            for bl in range(4):
                b = nb()
                for kc in range(8):
                    P.op("pe", "matmul", ps[b][0:16, 0:512], lhsT=hT[:, kc, C0 + bl * 16:C0 + (bl + 1) * 16],
                         rhs=wslab[:, slot, kc * 512:(kc + 1) * 512], start=(kc == 0), stop=(kc == 7),
                         r=[("wslab", slot), ("hT", kc)], w=bk(b))
                r_ = nrot("gl", 2)
                P.op("act", "activation", out=gl[0:16, r_, :], in_=ps[b][0:16, 0:512], func=AF.Copy, r=bk(b),
                     w=[("gl", r_)])
                P.op("dve", "tensor_copy", SA[0:16, 16 + bl, :], gl[0:16, r_, :], r=[("gl", r_)], w=[("SA", 16 + bl)])
                P.op("sp", "dma_start", out=nvs_d[l, :, bl, :], in_=gl[0:16, r_, :], r=[("gl", r_)],
                     semkey=("gl", r_))
        if True:
            for ti, t in enumerate(tiles):
                b = nb()
                for kc in range(8):
                    P.op("pe", "matmul", ps[b][:, 0:512], lhsT=hT[:, kc, ti * 128:(ti + 1) * 128],
                         rhs=wslab[:, slot, kc * 512:(kc + 1) * 512], start=(kc == 0), stop=(kc == 7),
                         r=[("wslab", slot), ("hT", kc)], w=bk(b))
                sl = t % 8
                if t >= KEEP_T0:
                    r_ = nrot("gl", 2)
                    P.op("act", "activation", out=gl[:, r_, :], in_=ps[b][:, 0:512], func=AF.Copy, r=bk(b),
                         w=[("gl", r_)])
                    P.op("dve", "tensor_scalar_mul", vt[:, sl, :], gl[:, r_, :], vtok[:, t:t + 1],
                         r=[("gl", r_), "vtok"], w=[("vt", sl)])
                    P.op("sp", "dma_start", out=nv_d[l, t - KEEP_T0], in_=gl[:, r_, :], r=[("gl", r_)],
                         semkey=("gl", r_))
                else:
                    P.op("dve", "tensor_scalar_mul", vt[:, sl, :], ps[b][:, 0:512], vtok[:, t:t + 1],
                         r=bk(b) + ["vtok"], w=[("vt", sl)])
        ws_done()
        if kind == "kv":
            if pro_ok(bi + 1):
                with _NextCtx(bi + 1):
                    l2, T2, g2 = blk_geom(bi + 1)
                    norm_mod(0, T2, g2)
                pro_done.add(bi + 1)
            return

        ck(1)
        slot = ws_get()
        for c in range(4):
            b = nb()
            proj_fm(b, slot, 8, 512, c * 128, hr, hk, T)
            P.op("act", "activation", out=ubT[:, c, 0:T], in_=ps[b][:, 0:T], func=AF.Gelu_apprx_tanh, r=bk(b),
                 w=[("ubT", c)])
        ws_done()
        ck(2)
        slot = ws_get()
        tl = [(ti, 128, False) for ti in range(ntile)] + ([(ntile, 64, True)] if ms else [])
        def vb_a(ti, M):
            b = nb()
            for kc in range(8):
                P.op("pe", "matmul", ps[b][0:M, 0:512], lhsT=hT[:, kc, ti * 128: ti * 128 + M],
                     rhs=wslab[:, slot, kc * 512:(kc + 1) * 512], start=(kc == 0), stop=(kc == 7),
                     r=[("wslab", slot), ("hT", kc)], w=bk(b))
            r_ = nrot("gl", 2)
            P.op("act", "activation", out=gl[0:M, r_, :], in_=ps[b][0:M, 0:512], func=AF.Gelu_apprx_tanh, r=bk(b),
                 w=[("gl", r_)])
            r2 = nrot("tmpn", 2)
            P.op("dve", "tensor_tensor", out=tmpn[0:M, r2, :], in0=gl[0:M, r_, :], in1=gl[0:M, r_, :], op=ALU.mult,
                 r=[("gl", r_)], w=[("tmpn", r2)])
            r3 = nrot("ssv", 4)
            P.op("dve", "reduce_sum", out=ssv[0:M, r3:r3 + 1], in_=tmpn[0:M, r2, :], axis=AX.X,
                 r=[("tmpn", r2)], w=[("ssv", r3)])
            return r_, r2, r3

        def vb_b(ti, M, sample, r_, r2, r3):
            P.op("act", "activation", out=ssv[0:M, r3:r3 + 1], in_=ssv[0:M, r3:r3 + 1], func=AF.Sqrt,
                 bias=float(EPS), scale=1.0 / 512.0, r=[("ssv", r3)], w=[("ssv", r3)])
            P.op("dve", "reciprocal", out=ssv[0:M, r3:r3 + 1], in_=ssv[0:M, r3:r3 + 1], r=[("ssv", r3)],
                 w=[("ssv", r3)])
            if sample:
                P.op("dve", "scalar_tensor_tensor", out=tmpn[0:64, r2, :], in0=gl[0:64, r_, :],
                     scalar=ssv[0:64, r3:r3 + 1], in1=vg[0:64, :], op0=ALU.mult, op1=ALU.mult,
                     r=[("gl", r_), ("ssv", r3), "vg"], w=[("tmpn", r2)])
                P.op("act", "activation", out=vbns[:, :], in_=tmpn[0:64, r2, :], func=AF.Copy, r=[("tmpn", r2)],
                     w=["vbns"])
                P.op("sp", "dma_start", out=nbs_d[l], in_=tmpn[0:64, r2, :], r=[("tmpn", r2)], semkey=("tmpn", r2))
            else:
                P.op("dve", "scalar_tensor_tensor", out=vbn[:, ti, :], in0=gl[:, r_, :], scalar=ssv[:, r3:r3 + 1],
                     in1=vg[:, :], op0=ALU.mult, op1=ALU.mult, r=[("gl", r_), ("ssv", r3), "vg"], w=[("vbn", ti)])

        for p0 in range(0, len(tl), 2):
            pair = tl[p0:p0 + 2]
            sts = [vb_a(ti, M) for ti, M, smp in pair]
            for (ti, M, smp), st3 in zip(pair, sts):
                vb_b(ti, M, smp, *st3)
        ws_done()
        sample = False
        ck(3)
        def gate_chunks():
            for s_ in range(4):
                slot = ws_get()
                for m in range(4):
                    c = s_ * 4 + m

                    def emit(b, slot=slot, m=m, c=c):
                        proj_fm(b, slot, 8, 512, m * 128, hr, hk, T)
                        P.op("act", "activation", out=SA[:, c, 0:T], in_=ps[b][:, 0:T], func=AF.Tanh, scale=0.5,
                             r=bk(b), w=[("SA", c)])
                    yield emit
                ws_done()

        gates = gate_chunks()

        ck(4)
        if ms:
            items = [(bl, hp) for bl in range(4) for hp in range(4)]

            def ss1(it, st):
                bl, hp = it
                cr = 0
                for j in range(5):
                    for hh in range(2):
                        hs = slice(hh * 64, (hh + 1) * 64)
                        b = st * 4 + hh
                        if j < 4:
                            P.op("pe", "matmul", ps[b][:, j * 16:(j + 1) * 16],
                                 lhsT=ckb[hs, cr, hp, j * 128:(j + 1) * 128], rhs=qz[hs, hp, 3, hh, bl * 16:(bl + 1) * 16],
                                 start=True, stop=True, r=[("ckb", cr), ("qT", hp)], w=[("ps", b)])
                        else:
                            P.op("pe", "matmul", ps[b][0:16, 64:80],
                                 lhsT=kTs[hs, hp, bl * 16:(bl + 1) * 16], rhs=qz[hs, hp, 3, hh, bl * 16:(bl + 1) * 16],
                                 start=True, stop=True, r=["kTs", ("qT", hp)], w=[("ps", b)])
                for hh in range(2):
                    b = st * 4 + hh
                    h = 2 * hp + hh
                    P.op("act", "activation", out=Pts[:, st, hh, 0:48], in_=ps[b][:, 0:48], func=AF.Exp,
                         r=[("ps", b)], w=[("Pts", st, hh)])
                    P.op("act", "activation", out=exs[:, st, hh, 0:16], in_=ps[b][:, 48:64], func=AF.Exp,
                         r=[("ps", b)], w=[("exs", st, hh)])
                    P.op("act", "activation", out=exs[0:16, st, hh, 16:32], in_=ps[b][0:16, 64:80], func=AF.Exp,
                         r=[("ps", b)], w=[("exs", st, hh)])
                    P.op("dve", "tensor_tensor", out=Pts[:, st, hh, 48:64], in0=exs[:, st, hh, 0:16],
                         in1=E[:, 0, h, 0:16], op=ALU.mult, r=[("exs", st, hh), "E"], w=[("Pts", st, hh)])
                    P.op("dve", "tensor_tensor", out=Pts[0:16, st, hh, 64:80], in0=exs[0:16, st, hh, 16:32],
                         in1=E[0:16, 1, h, 0:16], op=ALU.mult, r=[("exs", st, hh), "E"], w=[("Pts", st, hh)])

            def ss2(it, st):
                bl, hp = it
                cr = 0
                bd, bo = st * 4 + 2, 3
                oc = slice(st * 128, st * 128 + 16)
                for hh in range(2):
                    for j in range(5):
                        if j < 4:
                            P.op("pe", "matmul", ps[bd][:, hh * 16:(hh + 1) * 16], lhsT=ones_bf[:, :],
                                 rhs=Pts[:, st, hh, j * 16:(j + 1) * 16], start=(j == 0), stop=False,
                                 r=[("Pts", st, hh), "ones_bf"], w=[("ps", bd)])
                        else:
                            P.op("pe", "matmul", ps[bd][:, hh * 16:(hh + 1) * 16], lhsT=ones_bf[0:16, :],
                                 rhs=Pts[0:16, st, hh, 64:80], start=False, stop=True,
                                 r=[("Pts", st, hh), "ones_bf"], w=[("ps", bd)])
                for hh in range(2):
                    hs = slice(hh * 64, (hh + 1) * 64)
                    fc = (2 * hp + hh) * 64
                    for j in range(5):
                        if j < 4:
                            P.op("pe", "matmul", ps[bo][hs, oc], lhsT=cvb[:, cr, j, fc:fc + 64],
                                 rhs=Pts[:, st, hh, j * 16:(j + 1) * 16], start=(j == 0), stop=False,
                                 r=[("Pts", st, hh), ("cvb", cr)], w=[("ps", bo)])
                        else:
                            P.op("pe", "matmul", ps[bo][hs, oc], lhsT=SA[0:16, 16 + bl, fc:fc + 64],
                                 rhs=Pts[0:16, st, hh, 64:80], start=False, stop=True,
                                 r=[("Pts", st, hh), ("SA", 16 + bl)], w=[("ps", bo)])
                P.op("dve", "reciprocal", out=rdens[:, st, :], in_=ps[bd][:, 0:32], r=[("ps", bd)],
                     w=[("rdens", st)])
                for hh in range(2):
                    hs = slice(hh * 64, (hh + 1) * 64)
                    P.op("dve", "tensor_tensor", out=oaT[hs, hp, C0 + bl * 16:C0 + (bl + 1) * 16], in0=ps[bo][hs, oc],
                         in1=rdens[hs, st, hh * 16:(hh + 1) * 16], op=ALU.mult, r=[("ps", bo), ("rdens", st)],
                         w=[("oaT", hp)])
        if True:
            items = [(qi, hp) for qi in range(ntile) for hp in range(4)]

            def s1(it, st):
                qi, hp = it
                qt = t0 + qi
                bS, b4 = st * 4, st * 4 + 2
                for j in range(5):
                    sl = (qt - 4 + j) % 8
                    if j < 4:
                        out_ = psd[st * 2][:, j * 256:(j + 1) * 256]
                        wkey = ("ps", bS + j // 2)
                    else:
                        out_ = ps[b4][:, 0:256]
                        wkey = ("ps", b4)
                    P.op("pe", "matmul", out_, lhsT=kT[:, hp, sl * 128:(sl + 1) * 128],
                         rhs=qz[:, hp, qi, :, :].rearrange("p h q -> p (h q)"), start=True, stop=True,
                         r=[("kT", sl), ("qT", hp)], w=[wkey])
                pk = [("Pt", st)]
                rk2 = [("ps", bS), ("ps", bS + 1)]
                two = "p (h q) -> p h q"
                P.op("act", "activation", out=Pt[:, st, 0:4, :].rearrange("p j c -> p (j c)"),
                     in_=psd[st * 2][:, 0:1024], func=AF.Exp, r=rk2, w=pk)
                P.op("act", "activation", out=Pt[:, st, 4, :], in_=ps[b4][:, 0:256], func=AF.Exp,
                     r=[("ps", b4)], w=pk)
                P.op("dve", "tensor_tensor", out=Pt[:, st, 0, :], in0=Pt[:, st, 0, :], in1=M0[:, :], op=ALU.mult,
                     r=pk + ["M0"], w=pk)
                P.op("dve", "tensor_tensor", out=Pt[:, st, 3:5, :].rearrange("p t (h q) -> p t h q", h=2),
                     in0=Pt[:, st, 3:5, :].rearrange("p t (h q) -> p t h q", h=2),
                     in1=E[:, :, 2 * hp:2 * hp + 2, :], op=ALU.mult, r=pk + ["E"], w=pk)

            def s2(it, st):
                qi, hp = it
                qt = t0 + qi
                bd, bo = st * 4 + 2, 3
                oc = slice(st * 128, (st + 1) * 128)
                for j in range(5):
                    kt = qt - 4 + j
                    lw = vones[:, kt, :] if kt < 9 else ones_bf[:, :]
                    P.op("pe", "matmul", ps[bd][:, 256:512], lhsT=lw, rhs=Pt[:, st, j, :], start=(j == 0),
                         stop=(j == 4), r=[("Pt", st), "vones", "ones_bf"], w=[("ps", bd)])
                for hh in range(2):
                    hs = slice(hh * 64, (hh + 1) * 64)
                    fc = (2 * hp + hh) * 64
                    for j in range(5):
                        sl = (qt - 4 + j) % 8
                        P.op("pe", "matmul", ps[bo][hs, oc], lhsT=vt[:, sl, fc:fc + 64],
                             rhs=Pt[:, st, j, hh * 128:(hh + 1) * 128], start=(j == 0), stop=(j == 4),
                             r=[("Pt", st), ("vt", sl)], w=[("ps", bo)])
                P.op("dve", "tensor_scalar_max", rden[:, st, :], ps[bd][:, 256:512], 1e-30, r=[("ps", bd)],
                     w=[("rden", st)])
                P.op("dve", "reciprocal", out=rden[:, st, :], in_=rden[:, st, :], r=[("rden", st)],
                     w=[("rden", st)])
                for hh in range(2):
                    hs = slice(hh * 64, (hh + 1) * 64)
                    P.op("dve", "tensor_tensor", out=oaT[hs, hp, qi * 128:(qi + 1) * 128], in0=ps[bo][hs, oc],
                         in1=rden[hs, st, hh * 128:(hh + 1) * 128], op=ALU.mult, r=[("ps", bo), ("rden", st)],
                         w=[("oaT", hp)])

        for i in range(len(items) + 1):
            if i < len(items):
                s1(items[i], i % 2)
                g_ = next(gates, None)
                if g_ is not None:
                    g_(7)
            if i >= 1:
                s2(items[i - 1], (i - 1) % 2)
        for g_ in gates:
            g_(nb())
        if ms:
            for bl in range(4):
                its = [(bl, hp) for hp in range(4)]
                P.op("pool", "dma_start", out=ckb[:, 0, :, :], in_=ckT_d[l, :, :, bl, :], w=[("ckb", 0)],
                     semkey=("ckb", 0))
                P.op("pool", "dma_start", out=cvb[:, 0, :, :], in_=cv_d[l, bl], w=[("cvb", 0)],
                     semkey=("cvb", 0))
                for i in range(len(its) + 1):
                    if i < len(its):
                        ss1(its[i], i % 2)
                    if i >= 1:
                        ss2(its[i - 1], (i - 1) % 2)

        ck(5)
        if ms:
            b = nb()
            for g in range(4):
                P.op("pe", "matmul", ps[b][:, g * 64:(g + 1) * 64], lhsT=vbns[0:64, g * 128:(g + 1) * 128],
                     rhs=Wblk[0:64, g, :], start=True, stop=False, r=["vbns", "Wblk"], w=bk(b))
                P.op("pe", "matmul", ps[b][:, g * 64:(g + 1) * 64], lhsT=ones_row[0:1, :],
                     rhs=bsps[0:1, g * 64:(g + 1) * 64], start=False, stop=True, r=["ones_row", "bsps"], w=bk(b))
            P.op("dve", "tensor_tensor", out=obT[:, :, C0:C0 + 64], in0=ps[b][:, 0:256].rearrange("p (g t) -> p g t", g=4),
                 in1=ubT[:, :, C0:C0 + 64], op=ALU.mult, r=bk(b) + [("ubT", c) for c in range(4)],
                 w=[("obT", c) for c in range(4)])
        if True:
            for ti in range(ntile):
                b = nb()
                for g in range(4):
                    P.op("pe", "matmul", ps[b][:, g * 128:(g + 1) * 128], lhsT=vbn[:, ti, g * 128:(g + 1) * 128],
                         rhs=WcT[:, g, :], start=True, stop=False, r=[("vbn", ti), "WcT"], w=bk(b))
                    P.op("pe", "matmul", ps[b][:, g * 128:(g + 1) * 128], lhsT=ones_row[0:1, :],
                         rhs=bsp[0:1, g * 128:(g + 1) * 128], start=False, stop=True, r=["ones_row", "bsp"], w=bk(b))
                P.op("dve", "tensor_tensor", out=obT[:, :, ti * 128:(ti + 1) * 128],
                     in0=ps[b][:, 0:512].rearrange("p (g t) -> p g t", g=4), in1=ubT[:, :, ti * 128:(ti + 1) * 128],
                     op=ALU.mult, r=bk(b) + [("ubT", c) for c in range(4)], w=[("obT", c) for c in range(4)])

        ck(6)
        sa_, sb_ = ws_get(), ws_get()
        for c in range(8):
            ba, bb_ = nb(), nb()
            proj_fm(ba, sa_, 4, 1024, c * 128, lambda kc: oaT[:, kc, 0:T], lambda kc: [("oaT", kc)], T)
            proj_fm(bb_, sb_, 4, 1024, c * 128, lambda kc: obT[:, kc, 0:T], lambda kc: [("obT", kc)], T)
            P.op("dve", "scalar_tensor_tensor", out=tmpn[:, 0, 0:T], in0=SA[:, c, 0:T], scalar=1.0, in1=ps[ba][:, 0:T],
                 op0=ALU.add, op1=ALU.mult, r=bk(ba) + [("SA", c)], w=[("tmpn", 0)])
            P.op("dve", "scalar_tensor_tensor", out=tmpn[:, 1, 0:T], in0=SA[:, 8 + c, 0:T], scalar=1.0,
                 in1=ps[bb_][:, 0:T], op0=ALU.add, op1=ALU.mult, r=bk(bb_) + [("SA", 8 + c)], w=[("tmpn", 1)])
            P.op("dve", "tensor_tensor", out=SA[:, 16 + c, 0:T], in0=tmpn[:, 0, 0:T], in1=tmpn[:, 1, 0:T], op=ALU.add,
                 r=[("tmpn", 0), ("tmpn", 1)], w=[("SA", 16 + c)])
        ws_done()
        ws_done()
        ck(7)
        for s in range(2):
            slot = ws_get()
            for m in range(4):
                c = s * 4 + m
                b = nb()
                proj_fm(b, slot, 8, 512, m * 128, lambda kc: SA[:, 16 + kc, 0:T], lambda kc: [("SA", 16 + kc)], T)
                for (c0, n, bb) in groups:
                    P.op("dve", "scalar_tensor_tensor", out=cur["xT"][:, c, c0:c0 + n], in0=ps[b][:, c0:c0 + n],
                         scalar=modTt[:, cur["l"], 16 + c, bb:bb + 1], in1=cur["xT"][:, c, c0:c0 + n], op0=ALU.mult, op1=ALU.add,
                         r=bk(b) + [("xT", cur["par"], c), ("modT", cur["l"])], w=[("xT", cur["par"], c)])
            ws_done()

        ck(8)
        norm_mod(1, T, groups)
        nxt = bi + 1 if pro_ok(bi + 1) else None
        if nxt is not None:
            l2, T2, g2 = blk_geom(nxt)
            bss = nb()
            reserved.add(bss)
            sqr = {}
        fst = {}

        def ffn_a(j):
            jj, sub = divmod(j, 2)
            if sub == 0:
                if nxt is not None:
                    with _NextCtx(nxt):
                        if 1 <= jj <= 8:
                            sqr[jj - 1] = norm_sq(bss, jj - 1, T2)
                        if 2 <= jj <= 9:
                            norm_ss(bss, jj - 2, sqr[jj - 2], T2)
                if jj > 0:
                    ws_done()
                fst["slot"] = ws_get()
            slot = fst["slot"]
            bg, bu = nb(), nb()
            proj_fm(bg, slot, 8, 512, sub * 128, hr, hk, T)
            proj_fm(bu, slot, 8, 512, 256 + sub * 128, hr, hk, T)
            r_ = nrot("gb", 3)
            r2 = nrot("gc", 2)
            P.op("act", "activation", out=gb[:, r_, 2:2 + Tp], in_=ps[bg][:, 0:Tp], func=AF.Copy, r=bk(bg),
                 w=[("gb", r_)])
            P.op("dve", "tensor_copy", gb[:, r_, 0:2], halo[:, j, :], r=["halo"], w=[("gb", r_)])
            if t0 == 8:
                P.op("pool", "tensor_scalar_mul", gb[:, r_, 128:130], gb[:, r_, 128:130], flag[:, 0:1],
                     r=[("gb", r_), "flag"], w=[("gb", r_)])
            P.op("pool", "tensor_copy", halo[:, j, :], gb[:, r_, Tp:Tp + 2], r=[("gb", r_)], w=["halo"])
            convs = [([gb[:, r_, k:k + Tp] for k in range(3)], gc[:, r2, 0:Tp])]
            if ms:
                g3 = gb[:, r_, Tp + 2:Tp + 74].rearrange("p (b t) -> p b t", t=18)
                P.op("act", "activation", out=g3[:, :, 2:18],
                     in_=ps[bg][:, C0:C0 + 64].rearrange("p (b t) -> p b t", t=16), func=AF.Copy, r=bk(bg),
                     w=[("gb", r_)])
                P.op("dve", "tensor_copy", g3[:, :, 0:2], halos[:, j, :, :], r=["halos"], w=[("gb", r_)])
                P.op("pool", "tensor_copy", ncs_st[:, j, :, :], g3[:, :, 16:18], r=[("gb", r_)], w=["ncs_st"])
                convs.append(([g3[:, :, k:k + 16] for k in range(3)],
                              gc[:, r2, C0:C0 + 64].rearrange("p (b t) -> p b t", t=16)))
            for views, gcv in convs:
                P.op("act", "activation", out=gcv, in_=views[0], func=AF.Identity, scale=cw[:, j, 0:1],
                     bias=cb[:, j:j + 1], r=[("gb", r_), "cw", "cb"], w=[("gc", r2)])
            return bu, r_, r2, convs

        def ffn_b(j, bu, r_, r2, convs):
            for views, gcv in convs:
                for k in (1, 2):
                    P.op("dve", "scalar_tensor_tensor", out=gcv, in0=views[k], scalar=cw[:, j, k:k + 1], in1=gcv,
                         op0=ALU.mult, op1=ALU.add, r=[("gb", r_), ("gc", r2), "cw"], w=[("gc", r2)])
            r3 = nrot("ge", 2)
            P.op("act", "activation", out=ge[:, r3, 0:T], in_=gc[:, r2, 0:T], func=AF.Gelu_apprx_tanh,
                 r=[("gc", r2)], w=[("ge", r3)])
            P.op("dve", "tensor_tensor", out=SA[:, j, 0:T], in0=ps[bu][:, 0:T], in1=ge[:, r3, 0:T], op=ALU.mult,
                 r=bk(bu) + [("ge", r3)], w=[("SA", j)])

        pend = ffn_a(0)
        for j in range(22):
            nxt_a = ffn_a(j + 1) if j + 1 < 22 else None
            ffn_b(j, *pend)
            pend = nxt_a
        ws_done()
        if nxt is not None:
            with _NextCtx(nxt):
                norm_fin(bss, T2)
                norm_apply(0, T2, g2)
            reserved.discard(bss)
            pro_done.add(nxt)
        for c in range(8):
            slot = ws_get()
            b = nb()
            proj_fm(b, slot, 22, 128, 0, lambda kc: SA[:, kc, 0:T], lambda kc: [("SA", kc)], T)
            for (c0, n, bb) in groups:
                P.op("dve", "scalar_tensor_tensor", out=cur["xT"][:, c, c0:c0 + n], in0=ps[b][:, c0:c0 + n],
                     scalar=modTt[:, cur["l"], 40 + c, bb:bb + 1], in1=cur["xT"][:, c, c0:c0 + n], op0=ALU.mult, op1=ALU.add,
                     r=bk(b) + [("xT", cur["par"], c), ("modT", cur["l"])], w=[("xT", cur["par"], c)])
            ws_done()

        ck(9)
        xk = [("xT", cur["par"], c) for c in range(8)]
        if ms:
            if l == 0:
                P.op("sp", "dma_start", out=xs1, in_=cur["xT"][:, :, C0:C0 + 64], r=xk, w=["xs1"], semkey="xst")
            else:
                P.op("sp", "dma_start", out=ysT_d, in_=cur["xT"][:, :, C0:C0 + 64], r=xk, semkey="xst")
            P.op("sp", "dma_start", out=ncs_d[l], in_=ncs_st[:, :, :, :], r=["ncs_st"], semkey="ncs")
            P.op("sp", "dma_start", out=ncv_d[l], in_=halo[:, :, :], r=["halo"], semkey="ncv")
        if l == 0:
            P.op("sp", "dma_start", out=x1s[:, :, (t0 - 4) * 128:(t0 - 4) * 128 + Tp], in_=cur["xT"][:, :, 0:Tp], r=xk,
                 w=[("x1s", t) for t in tiles], semkey="xst")
        else:
            lo = max(t0, OUT_T0)
            c0 = (lo - t0) * 128
            P.op("sp", "dma_start", out=yT_d[:, :, (lo - OUT_T0) * 128:(lo - OUT_T0) * 128 + Tp - c0],
                 in_=cur["xT"][:, :, c0:Tp], r=xk, semkey="xst")

    BLOCKS = []
    for l in range(2):
        BLOCKS.append((l, "kv", 0 if l == 0 else 4, 4))
        bl_ = (L0_BLOCKS if l == 0 else L1_BLOCKS)
        for k_, (t0, n) in enumerate(bl_):
            BLOCKS.append((l, "fullms" if k_ == len(bl_) - 1 else "full", t0, n))

    try:
        for l in range(2):
            if stop is not None and stop == -1:
                raise _Stop()
            if l == 0:
                ada_setup(0)
            layer_setup(l)
            if stop is not None and stop == 0:
                raise _Stop()
            if l == 1:
                P.op("pool", "memset", halo[:, :, :], 0.0, r=["halo"], w=["halo"])
            for (l_, kind, t0, n) in [b for b in BLOCKS if b[0] == l]:
                if kind == "fullms" and l == 0:
                    ada_setup(1)
                block(l_, kind, t0, n)
                nblk[0] += 1
                if stop is not None and nblk[0] >= stop:
                    raise _Stop()
        assert ws["consumed"] == len(ws["seq"]), (ws["consumed"], len(ws["seq"]))
    except _Stop:
        dbg = dout("dbgx", (128, 8, 512))
        P.op("sp", "dma_start", out=dbg, in_=cur["xT"][:, :, :], r=[("xT", cur["par"], c) for c in range(8)], semkey="dbg")
        dbg2 = dout("dbgh", (128, 8, 512))
        P.op("pool", "dma_start", out=dbg2, in_=tmpn[:, :, :].rearrange("p a b -> p (a b)"), r=[("tmpn", 0), ("tmpn", 1)], semkey="dbg") if False else None

    P.emit(nc, stack)
    stack.close()
    return nc


def _fm(a):
    t, f = a.shape
    return np.ascontiguousarray(a.reshape(t, f // 128, 128).transpose(2, 1, 0))


def _slab(wm):
    k, m = wm.shape
    return np.ascontiguousarray(wm.reshape(k // 128, 128, m).transpose(1, 0, 2)).reshape(128, (k // 128) * m)


_NC_CACHE = {}


def kernel(x_prompt, x_sample, cache_attn_k, cache_attn_v, cache_ffn_conv, c_prompt, c_sample,
           norm1_g, norm2_g, w_ada, b_ada, w_in, q_norm_g, k_norm_g, rel_bias, v_norm_g,
           w_spatial, b_spatial, w_out_a, w_out_b, w_out, w_ffn_in, ffn_conv_w, ffn_conv_b, w_ffn_out):
    f = lambda a: np.asarray(a, dtype=np.float32)
    x_prompt, x_sample, cache_attn_k, cache_attn_v, cache_ffn_conv = map(f, (x_prompt, x_sample, cache_attn_k, cache_attn_v, cache_ffn_conv))
    c_prompt, c_sample, norm1_g, norm2_g, w_ada, b_ada, w_in = map(f, (c_prompt, c_sample, norm1_g, norm2_g, w_ada, b_ada, w_in))
    q_norm_g, k_norm_g, rel_bias, v_norm_g, w_spatial, b_spatial = map(f, (q_norm_g, k_norm_g, rel_bias, v_norm_g, w_spatial, b_spatial))
    w_out_a, w_out_b, w_out, w_ffn_in, ffn_conv_w, ffn_conv_b, w_ffn_out = map(f, (w_out_a, w_out_b, w_out, w_ffn_in, ffn_conv_w, ffn_conv_b, w_ffn_out))

    in_maps = _prep(x_prompt, x_sample, cache_attn_k, cache_attn_v, cache_ffn_conv, c_prompt, c_sample,
                    norm1_g, norm2_g, w_ada, b_ada, w_in, q_norm_g, k_norm_g, rel_bias, v_norm_g,
                    w_spatial, b_spatial, w_out_a, w_out_b, w_out, w_ffn_in, ffn_conv_w, ffn_conv_b, w_ffn_out)
    if "nc" not in _NC_CACHE:
        _NC_CACHE["nc"] = build_nc()
    nc = _NC_CACHE["nc"]
    res = run_bass_kernel_spmd(nc, in_maps, core_ids=list(range(NCORES)))
    return _post(res.results)


def _prep(x_prompt, x_sample, cache_attn_k, cache_attn_v, cache_ffn_conv, c_prompt, c_sample,
          norm1_g, norm2_g, w_ada, b_ada, w_in, q_norm_g, k_norm_g, rel_bias, v_norm_g,
          w_spatial, b_spatial, w_out_a, w_out_b, w_out, w_ffn_in, ffn_conv_w, ffn_conv_b, w_ffn_out):

    wall = np.empty((2, 24, 128, 4096), np.float32)
    wfo = np.empty((2, 8, 128, 2816), np.float32)
    wada = np.empty((2, 12, 128, 4096), np.float32)
    for l in range(2):
        for s in range(9):
            wall[l, s] = _slab(w_in[l][:, s * 512:(s + 1) * 512])
        wall[l, 9] = _slab(w_out_a[l])
        wall[l, 10] = _slab(w_out_b[l])
        for s in range(2):
            wall[l, 11 + s] = _slab(w_out[l][:, s * 512:(s + 1) * 512])
        for jj in range(11):
            idx = np.concatenate([np.arange(2 * jj * 128, (2 * jj + 2) * 128), DFF + np.arange(2 * jj * 128, (2 * jj + 2) * 128)])
            wall[l, 13 + jj] = _slab(w_ffn_in[l][:, idx])
        for c in range(8):
            wfo[l, c] = _slab(w_ffn_out[l][:, c * 128:(c + 1) * 128])
        for s in range(12):
            wada[l, s] = _slab(w_ada[l][:, s * 512:(s + 1) * 512])
    col = lambda v: np.ascontiguousarray(v.reshape(-1, 128).T)
    nrm = np.stack([np.concatenate([col(norm1_g[l]), col(norm2_g[l])], 1) for l in range(2)])
    bada = np.stack([col(b_ada[l]) for l in range(2)])
    qkg = np.stack([np.stack([np.tile(q_norm_g[l], 2), np.tile(k_norm_g[l], 2)], 1) for l in range(2)])
    vg = np.stack([np.broadcast_to(v_norm_g[l][None, :], (128, 512)) for l in range(2)]).copy()
    bsp = b_spatial.reshape(2, 1, 512).copy()
    bsps = np.stack([np.tile(b_spatial[l][:, None, :16], (1, 4, 1)).reshape(1, 256) for l in range(2)])
    wspT = np.ascontiguousarray(w_spatial.transpose(0, 3, 1, 2))
    wspb = np.zeros((2, 64, 4, 64), np.float32)
    for bl in range(4):
        wspb[:, bl * 16:(bl + 1) * 16, :, bl * 16:(bl + 1) * 16] = wspT[:, 0:16, :, 0:16]
    s_i = np.arange(128)[:, None]
    t_i = np.arange(128)[None, :]
    tril = np.broadcast_to((s_i <= t_i).astype(np.float32)[:, None, :], (128, 4, 128)).copy()
    trilb = np.zeros((64, 4, 64), np.float32)
    for bl in range(4):
        trilb[bl * 16:(bl + 1) * 16, :, bl * 16:(bl + 1) * 16] = tril[0:16, :, 0:16]
    cw = np.ascontiguousarray(ffn_conv_w.reshape(2, 3, 22, 128).transpose(0, 3, 2, 1))
    cb = np.ascontiguousarray(ffn_conv_b.reshape(2, 22, 128).transpose(0, 2, 1))
    ki = np.arange(128)[:, None]
    qi = np.arange(128)[None, :]
    idx1 = np.clip(128 + qi - ki, -128, 128) + 128
    idx0 = np.clip(qi - ki, -128, 128) + 128
    Tb = np.stack([np.stack([rel_bias[l][:, idx1].transpose(1, 0, 2), rel_bias[l][:, idx0].transpose(1, 0, 2)], 1)
                   for l in range(2)])
    b256 = np.stack([np.broadcast_to(rel_bias[l][None, :, 256], (128, 8)) for l in range(2)]).copy()

    shared = dict(wall=wall, wfo=wfo, wada=wada, nrm=nrm, bada=bada, qkg=qkg, vg=vg, bsp=bsp, bsps=bsps, wspT=wspT,
                  wspb=wspb, tril=tril, trilb=trilb, cw=cw, cb=cb, Tb=np.ascontiguousarray(Tb), b256=b256)
    shared = {k: np.ascontiguousarray(v, dtype=np.float32) for k, v in shared.items()}

    in_maps = []
    for core in range(NCORES):
        b, seg = core // 4, core % 4
        s0 = seg * SEG
        a = s0 - HALO
        xw = np.zeros((W, D), np.float32)
        lo = max(a, 0)
        xw[lo - a:] = x_prompt[b, lo:s0 + SEG]
        valid = (np.arange(W) + a >= 0).astype(np.float32)
        m = dict(shared)
        m["xin"] = _fm(xw)
        m["xs"] = _fm(x_sample[4 * core:4 * core + 4].reshape(64, D))
        m["vtok"] = np.ascontiguousarray(valid.reshape(NT, 128).T)
        m["flag"] = np.full((128, 1), 1.0 if a >= 0 else 0.0, np.float32)
        cc = np.concatenate([c_prompt[b:b + 1], c_sample[4 * core:4 * core + 4]], 0)
        m["cT"] = np.ascontiguousarray(cc.reshape(5, 8, 128).transpose(2, 1, 0))
        ck = cache_attn_k[:, 4 * core:4 * core + 4]
        m["ckT"] = np.ascontiguousarray(ck.reshape(2, 4, 512, 4, 2, 64).transpose(0, 4, 5, 3, 1, 2)).reshape(2, 128, 4, 4, 512)
        cvv = cache_attn_v[:, 4 * core:4 * core + 4].reshape(2, 4, 4, 128, 512)
        m["cv"] = np.ascontiguousarray(cvv.transpose(0, 1, 3, 2, 4))
        cc2 = cache_ffn_conv[:, 4 * core:4 * core + 4].reshape(2, 4, 2, 22, 128)
        m["cconv"] = np.ascontiguousarray(cc2.transpose(0, 4, 3, 1, 2))
        in_maps.append(m)
    return in_maps


def _post(R):

    y_prompt = np.empty((2, 8192, D), np.float32)
    y_sample = np.empty((32, 16, D), np.float32)
    nkp = np.empty((2, 2, 512, 8, 64), np.float32)
    nvp = np.empty((2, 2, 512, 8, 64), np.float32)
    ncp = np.empty((2, 2, 2, DFF), np.float32)
    nks = np.empty((2, 32, 16, 8, 64), np.float32)
    nvs = np.empty((2, 32, 16, 8, 64), np.float32)
    nbs = np.empty((2, 32, 16, 4, 128), np.float32)
    ncs = np.empty((2, 32, 2, DFF), np.float32)
    for core in range(NCORES):
        r = R[core]
        b, seg = core // 4, core % 4
        y_prompt[b, seg * SEG:(seg + 1) * SEG] = r["yT"].transpose(2, 1, 0).reshape(SEG, D)
        y_sample[4 * core:4 * core + 4] = r["ysT"].transpose(2, 1, 0).reshape(4, 16, D)
        sl = slice(4 * core, 4 * core + 4)
        for l in range(2):
            if seg == 3:
                nkp[l, b] = r["nkT"][l].reshape(2, 64, 4, 512).transpose(3, 2, 0, 1).reshape(512, 8, 64)
                nvp[l, b] = r["nv"][l].reshape(512, 8, 64)
                ncp[l, b] = r["ncv"][l].transpose(2, 1, 0).reshape(2, DFF)
            nks[l, sl] = r["nksT"][l].reshape(2, 64, 4, 4, 16).transpose(3, 4, 2, 0, 1).reshape(4, 16, 8, 64)
            nvs[l, sl] = r["nvs"][l].transpose(1, 0, 2).reshape(4, 16, 8, 64)
            nbs[l, sl] = r["nbs"][l].reshape(4, 16, 4, 128)
            ncs[l, sl] = r["ncs"][l].transpose(2, 3, 1, 0).reshape(4, 2, DFF)
    return (y_prompt, y_sample, nkp, nvp, ncp, nks, nvs, nbs, ncs)
```
